# Optimizing a Trainium2 kernel written in Bass

```python
import math
import jax, jax.numpy as jnp
from jax import lax
import numpy as np

D_MODEL = 1024
BATCH = 4
SEQ = 8192
DEPTH = 2

N_MIXERS = 2
N_HEADS = 16
HEAD_DIM = 64
N_KV_HEADS = 4
GROUP = N_HEADS // N_KV_HEADS
IDX_HEADS = 8
IDX_DIM = 64
TOPK_MAX = 256
Q_BLOCK = 128
ROPE_THETA = 10000.0
CONV_WIDTH = 3
FFN_HIDDEN = int(math.ceil(8 * D_MODEL / 3 / 256) * 256)
NORM_EPS = 1e-6
N_A = (DEPTH + 1) // 2
N_B = DEPTH // 2

Q_COLS = N_HEADS * HEAD_DIM
K_COLS = N_KV_HEADS * HEAD_DIM
V_COLS = N_KV_HEADS * HEAD_DIM
QI_COLS = IDX_HEADS * IDX_DIM
KI_COLS = IDX_DIM
WI_COLS = IDX_HEADS
ATTN_IN_COLS = Q_COLS + K_COLS + V_COLS + QI_COLS + KI_COLS + WI_COLS

kernel_name = "hybrid_dsa_shortconv_adaln_block"


def _rmsnorm(x, g):
    xf = x.astype(jnp.float32)
    y = xf * lax.rsqrt(jnp.mean(xf * xf, axis=-1, keepdims=True) + NORM_EPS)
    return (y * g.astype(jnp.float32)).astype(x.dtype)


def _layernorm(x, g, b):
    xf = x.astype(jnp.float32)
    mu = jnp.mean(xf, axis=-1, keepdims=True)
    var = jnp.mean(jnp.square(xf - mu), axis=-1, keepdims=True)
    y = (xf - mu) * lax.rsqrt(var + NORM_EPS)
    return (y * g.astype(jnp.float32) + b.astype(jnp.float32)).astype(x.dtype)


def _rope(x, pos):
    d = x.shape[-1]
    half = d // 2
    inv_freq = ROPE_THETA ** (-(jnp.arange(half, dtype=jnp.float32) * 2.0 / d))
    ang = pos.astype(jnp.float32)[..., None] * inv_freq
    cos = jnp.cos(ang)[:, :, None, :]
    sin = jnp.sin(ang)[:, :, None, :]
    xf = x.astype(jnp.float32)
    x1, x2 = xf[..., :half], xf[..., half:]
    out = jnp.concatenate([x1 * cos - x2 * sin, x2 * cos + x1 * sin], axis=-1)
    return out.astype(x.dtype)


def _dsa_mixer(h, positions, w_in, q_norm_g, k_norm_g, idx_k_ln_g, idx_k_ln_b, w_out):
    Bn, T, _ = h.shape
    proj = h @ w_in
    o0 = 0
    q = proj[..., o0:o0 + Q_COLS].reshape(Bn, T, N_HEADS, HEAD_DIM); o0 += Q_COLS
    k = proj[..., o0:o0 + K_COLS].reshape(Bn, T, N_KV_HEADS, HEAD_DIM); o0 += K_COLS
    v = proj[..., o0:o0 + V_COLS].reshape(Bn, T, N_KV_HEADS, HEAD_DIM); o0 += V_COLS
    qi = proj[..., o0:o0 + QI_COLS].reshape(Bn, T, IDX_HEADS, IDX_DIM); o0 += QI_COLS
    ki = proj[..., o0:o0 + KI_COLS]; o0 += KI_COLS
    wi = proj[..., o0:o0 + WI_COLS]

    q = _rope(_rmsnorm(q, q_norm_g), positions)
    k = _rope(_rmsnorm(k, k_norm_g), positions)
    qi = _rope(qi, positions)
    ki = _rope(_layernorm(ki, idx_k_ln_g, idx_k_ln_b)[:, :, None, :], positions)[:, :, 0, :]
    wi = wi * (IDX_HEADS ** -0.5 * IDX_DIM ** -0.5)

    topk = min(TOPK_MAX, T // 4)
    n_blk = T // Q_BLOCK
    key_idx = jnp.arange(T)
    scale = HEAD_DIM ** -0.5

    def to_blocks(a):
        return jnp.moveaxis(a.reshape(Bn, n_blk, Q_BLOCK, *a.shape[2:]), 1, 0)

    def block_fn(args):
        qb, qib, wib, start = args
        t_idx = start + jnp.arange(Q_BLOCK)
        s = jnp.einsum('bqhd,bsd->bqhs', qib, ki)
        score = jnp.einsum('bqhs,bqh->bqs', jax.nn.relu(s), wib).astype(jnp.float32)
        causal = key_idx[None, :] <= t_idx[:, None]
        score = jnp.where(causal[None], score, -jnp.inf)
        _, sel = lax.top_k(score, topk)
        sel_ok = sel <= t_idx[None, :, None]
        kg = jax.vmap(lambda a, i: a[i])(k, sel)
        vg = jax.vmap(lambda a, i: a[i])(v, sel)
        qg = qb.reshape(Bn, Q_BLOCK, N_KV_HEADS, GROUP, HEAD_DIM)
        logits = jnp.einsum('bqgrd,bqkgd->bqgrk', qg, kg).astype(jnp.float32) * scale
        logits = jnp.where(sel_ok[:, :, None, None, :], logits, -1e30)
        p = jax.nn.softmax(logits, axis=-1).astype(vg.dtype)
        o = jnp.einsum('bqgrk,bqkgd->bqgrd', p, vg)
        return o.reshape(Bn, Q_BLOCK, N_HEADS * HEAD_DIM)

    starts = jnp.arange(n_blk) * Q_BLOCK
    o = lax.map(block_fn, (to_blocks(q), to_blocks(qi), to_blocks(wi), starts))
    o = jnp.moveaxis(o, 0, 1).reshape(Bn, T, N_HEADS * HEAD_DIM)
    return o @ w_out


def _short_conv_mixer(h, w_in, conv_w, w_out):
    proj = h @ w_in
    b_gate, c_gate, u = jnp.split(proj, 3, axis=-1)
    z = c_gate * u
    z = lax.conv_general_dilated(
        z, conv_w[:, None, :].astype(z.dtype),
        window_strides=(1,), padding=[(CONV_WIDTH - 1, 0)],
        dimension_numbers=('NWC', 'WIO', 'NWC'),
        feature_group_count=z.shape[-1])
    return (b_gate * z) @ w_out


def _swiglu(h, w_gate, w_up, w_down):
    return (jax.nn.silu(h @ w_gate) * (h @ w_up)) @ w_down


def setup_inputs(seed: int = 0) -> dict:
    key = jax.random.key(seed)
    ks = jax.random.split(key, 20)
    D = D_MODEL

    def nrm(k, shape, s):
        return jax.random.normal(k, shape, dtype=jnp.float32) * s

    x = nrm(ks[0], (BATCH, SEQ, D), 1.0)
    c = nrm(ks[1], (BATCH, D), 1.0)
    positions = (jnp.arange(SEQ, dtype=jnp.int32)[None, :]
                 + jax.random.randint(ks[2], (BATCH, 1), 0, 1024, dtype=jnp.int32))
    ada_w = nrm(ks[3], (DEPTH, D, 6 * D), 0.5 * D ** -0.5)
    ada_b = nrm(ks[4], (DEPTH, 6 * D), 0.02)
    norm1_g = 1.0 + nrm(ks[5], (DEPTH, D), 0.02)
    norm2_g = 1.0 + nrm(ks[6], (DEPTH, D), 0.02)
    attn_w_in = nrm(ks[7], (N_A, D, ATTN_IN_COLS), D ** -0.5)
    attn_q_norm_g = 1.0 + nrm(ks[8], (N_A, HEAD_DIM), 0.02)
    attn_k_norm_g = 1.0 + nrm(ks[9], (N_A, HEAD_DIM), 0.02)
    idx_k_ln_g = 1.0 + nrm(ks[10], (N_A, IDX_DIM), 0.02)
    idx_k_ln_b = nrm(ks[11], (N_A, IDX_DIM), 0.02)
    attn_w_out = nrm(ks[12], (N_A, N_HEADS * HEAD_DIM, D), (N_HEADS * HEAD_DIM) ** -0.5)
    conv_w_in = nrm(ks[13], (N_B, D, 3 * D), D ** -0.5)
    conv_w = nrm(ks[14], (N_B, CONV_WIDTH, D), CONV_WIDTH ** -0.5)
    conv_w_out = nrm(ks[15], (N_B, D, D), D ** -0.5)
    ffn_w_gate = nrm(ks[16], (DEPTH, D, FFN_HIDDEN), D ** -0.5)
    ffn_w_up = nrm(ks[17], (DEPTH, D, FFN_HIDDEN), D ** -0.5)
    ffn_w_down = nrm(ks[18], (DEPTH, FFN_HIDDEN, D), FFN_HIDDEN ** -0.5)
    return {"x": x, "c": c, "positions": positions,
            "ada_w": ada_w, "ada_b": ada_b, "norm1_g": norm1_g, "norm2_g": norm2_g,
            "attn_w_in": attn_w_in, "attn_q_norm_g": attn_q_norm_g,
            "attn_k_norm_g": attn_k_norm_g, "idx_k_ln_g": idx_k_ln_g,
            "idx_k_ln_b": idx_k_ln_b, "attn_w_out": attn_w_out,
            "conv_w_in": conv_w_in, "conv_w": conv_w, "conv_w_out": conv_w_out,
            "ffn_w_gate": ffn_w_gate, "ffn_w_up": ffn_w_up, "ffn_w_down": ffn_w_down}


def reference(x, c, positions, ada_w, ada_b, norm1_g, norm2_g,
              attn_w_in, attn_q_norm_g, attn_k_norm_g, idx_k_ln_g, idx_k_ln_b, attn_w_out,
              conv_w_in, conv_w, conv_w_out, ffn_w_gate, ffn_w_up, ffn_w_down):
    c_act = jax.nn.silu(c)
    for i in range(DEPTH):
        mod = c_act @ ada_w[i] + ada_b[i]
        sh1, sc1, g1, sh2, sc2, g2 = [m[:, None, :] for m in jnp.split(mod, 6, axis=-1)]
        h = _rmsnorm(x, norm1_g[i]) * (1.0 + sc1) + sh1
        if i % N_MIXERS == 0:
            j = i // N_MIXERS
            y = _dsa_mixer(h, positions, attn_w_in[j], attn_q_norm_g[j], attn_k_norm_g[j],
                           idx_k_ln_g[j], idx_k_ln_b[j], attn_w_out[j])
        else:
            j = i // N_MIXERS
            y = _short_conv_mixer(h, conv_w_in[j], conv_w[j], conv_w_out[j])
        x = x + g1 * y
        h = _rmsnorm(x, norm2_g[i]) * (1.0 + sc2) + sh2
        x = x + g2 * _swiglu(h, ffn_w_gate[i], ffn_w_up[i], ffn_w_down[i])
    return x
```

```python
import math
import numpy as np
import ml_dtypes
import concourse.bass as bass
import concourse.mybir as mybir
from concourse.bass_utils import run_bass_kernel_spmd

F32 = mybir.dt.float32
BF16 = mybir.dt.bfloat16
I32 = mybir.dt.int32
U8 = mybir.dt.uint8
ALU = mybir.AluOpType
AF = mybir.ActivationFunctionType
AX = mybir.AxisListType

D = 1024
FF = 2816
NFC = FF // 128
WCOLS = 2184
EPS = 1e-6
NIT = 25
TWO_PI = 2.0 * math.pi
C1 = 6.28125
C2 = TWO_PI - C1


class Tok:
    __slots__ = ("w", "r", "rd")

    def __init__(self):
        self.w = None
        self.r = {}
        self.rd = []


class Op:
    __slots__ = ("eng", "fn", "deps", "dma", "sig", "cnt", "dsem")

    def __init__(self, eng, fn, dma):
        self.eng = eng
        self.fn = fn
        self.deps = set()
        self.dma = dma
        self.sig = False
        self.cnt = 0
        self.dsem = None


class Prog:
    def __init__(self, nc):
        self.nc = nc
        self.ops = []
        self.engs = {"pe": nc.tensor, "act": nc.scalar, "dve": nc.vector,
                     "pool": nc.gpsimd, "sp": nc.sync}
        self.last = {}
        self.dmas_since_bar = []
        self.bar = {}

    def add(self, eng, fn, R=(), W=(), dma=None):
        idx = len(self.ops)
        op = Op(eng, fn, dma)
        deps = op.deps
        if eng in self.bar:
            deps.update(self.bar.pop(eng))
        for t in R:
            if t.w is not None:
                deps.add(t.w)
        for t in W:
            if t.w is not None:
                deps.add(t.w)
            deps.update(t.r.values())
            deps.update(t.rd)
        deps.discard(idx)
        for t in W:
            t.w = idx
            t.r = {}
            t.rd = []
        for t in R:
            if t.w == idx:
                continue
            if dma is not None:
                t.rd.append(idx)
            else:
                t.r[eng] = idx
        self.ops.append(op)
        if dma is None:
            self.last[eng] = idx
        else:
            self.dmas_since_bar.append(idx)
        return idx

    def barrier(self):
        s = set(self.last.values()) | set(self.dmas_since_bar)
        self.dmas_since_bar = []
        for e in self.engs:
            self.bar[e] = set(s) | self.bar.get(e, set())

    def emit(self, final_wait_eng="sp"):
        nc = self.nc
        ops = self.ops
        for op in ops:
            nd = set()
            for d in op.deps:
                dop = ops[d]
                if dop.dma is None and op.dma is None and dop.eng == op.eng == "pe":
                    continue
                nd.add(d)
                dop.sig = True
            op.deps = nd
        esem = {e: nc.semaphore("se_" + e).__enter__() for e in self.engs}
        dsem, dcnt = {}, {}
        ecnt = {e: 0 for e in self.engs}
        for op in ops:
            if op.dma is not None:
                if op.dma not in dsem:
                    dsem[op.dma] = nc.semaphore("sd_%d" % len(dsem)).__enter__()
                    dcnt[op.dma] = 0
                dcnt[op.dma] += 16
                op.cnt = dcnt[op.dma]
                op.dsem = dsem[op.dma]
                op.sig = True
            elif op.sig:
                ecnt[op.eng] += 1
                op.cnt = ecnt[op.eng]
        waited = {e: {} for e in self.engs}
        for op in ops:
            E = self.engs[op.eng]
            wd = waited[op.eng]
            need = {}
            for d in op.deps:
                dop = ops[d]
                if dop.dma is not None:
                    key, sem = ("d", dop.dma), dop.dsem
                else:
                    key, sem = ("e", dop.eng), esem[dop.eng]
                if need.get(key, (None, 0))[1] < dop.cnt:
                    need[key] = (sem, dop.cnt)
            for key, (sem, cnt) in need.items():
                if wd.get(key, 0) < cnt:
                    E.wait_ge(sem, cnt)
                    wd[key] = cnt
            ins = op.fn()
            if op.sig:
                ins.then_inc(op.dsem if op.dma is not None else esem[op.eng], 16 if op.dma is not None else 1)
        E = self.engs[final_wait_eng]
        for k, sem in dsem.items():
            E.wait_ge(sem, dcnt[k])
        for e in self.engs:
            if ecnt[e] > 0 and e != final_wait_eng:
                E.wait_ge(esem[e], ecnt[e])
        self.stats = (len(ops), ecnt, len(dsem))


def _dsize(dt):
    if dt in (F32, I32):
        return 4
    if dt == BF16:
        return 2
    return 1


class Arena:
    def __init__(self, nc, nbytes):
        self.t = nc.sbuf_tensor("arena", [128, nbytes // 4], F32).__enter__()
        self.cap = nbytes
        self.off = 0
        self.peak = 0

    def mark(self):
        return self.off

    def release(self, m):
        self.off = m

    def alloc(self, free, dt=F32, parts=128):
        if isinstance(free, int):
            free = [free]
        n = 1
        for f in free:
            n *= f
        sz = (n * _dsize(dt) + 63) // 64 * 64
        assert self.off + sz <= self.cap, "SBUF arena overflow: need %d have %d" % (self.off + sz, self.cap)
        ap = self.t[0:parts, self.off // 4:(self.off + sz) // 4]
        self.off += sz
        self.peak = max(self.peak, self.off)
        if dt != F32:
            ap = ap.bitcast(dt)
        ap = ap[:, 0:n]
        if len(free) == 2:
            ap = ap.rearrange("p (a b) -> p a b", a=free[0])
        elif len(free) == 3:
            ap = ap.rearrange("p (a b c) -> p a b c", a=free[0], b=free[1])
        return ap


class Cfg:
    def __init__(self, T):
        self.T = T
        self.NT = T // 128
        assert self.NT % 4 == 0
        CH = self.NT // 4
        self.CH = CH
        self.NS = 2 * CH + 1
        self.tilesA = list(range(CH)) + [3 * CH - 1] + list(range(3 * CH, 4 * CH))
        self.tilesB = [CH - 1] + list(range(CH, 3 * CH))
        self.ext = [max(a, b) + 1 for a, b in zip(self.tilesA, self.tilesB)]
        self.dchunk = [min(a, b) // 4 for a, b in zip(self.tilesA, self.tilesB)]
        self.haloA = CH
        self.haloB = 0
        self.topk = min(256, T // 4)


def build(cfg, stop=None):
    T, NT, NS = cfg.T, cfg.NT, cfg.NS
    nc = bass.Bass("TRN2", target_bir_lowering=False)
    P = Prog(nc)

    def din(name, shape, dt=F32):
        return nc.dram_tensor(name, list(shape), dt, kind="ExternalInput").ap()

    x_seq = din("x_seq", [T, D])
    x_own = din("x_own", [NS * 128, D])
    pos_seq = din("pos_seq", [128, NT], I32)
    pos_own = din("pos_own", [128, NS], I32)
    qpos_d = din("qpos", [128, NS])
    cT_d = din("cT", [128, 8])
    w_in = din("w_in", [D, WCOLS])
    w_out = din("w_out", [D, D])
    ada_w = din("ada_w", [2, D, 6 * D])
    vecs_d = din("vecs", [128, 128])
    hv_d = din("hv", [128, 256])
    cw_in = din("cw_in", [D, 3 * D])
    cwT_d = din("cwT", [128, 24])
    cw_out = din("cw_out", [D, D])
    wg_d = din("wg", [2, D, FF])
    wu_d = din("wu", [2, D, FF])
    wd_d = din("wd", [2, FF, D])
    ident_d = din("ident", [128, 128])
    invf_d = din("invf", [128, 32])
    iota_d = din("iota", [128, 512])
    out_d = nc.dram_tensor("out", [NS * 128, D], F32, kind="ExternalOutput").ap()
    dbg_d = nc.dram_tensor("dbg", [128, 8192], F32, kind="ExternalOutput").ap() if stop else None
    qTs = nc.dram_tensor("qTs", [NS, 128, 1024], BF16, kind="Internal").ap()
    qiTs = nc.dram_tensor("qiTs", [NS, 128, 512], BF16, kind="Internal").ap()
    x1s = nc.dram_tensor("x1s", [NS * 128, D], F32, kind="Internal").ap()
    t_qTs = [Tok() for _ in range(NS)]
    t_qiTs = [Tok() for _ in range(NS)]
    t_x1s = [Tok() for _ in range(NS)]
    t_out = [Tok() for _ in range(NS)]

    AR = Arena(nc, 206 * 1024)
    pbank = [nc.psum_tensor("pb%d" % k, [128, 512], F32).__enter__() for k in range(8)]
    tpb = [Tok() for _ in range(8)]

    def pbf(k):
        return pbank[k][:].bitcast(BF16)

    V, S, G, PE = nc.vector, nc.scalar, nc.gpsimd, nc.tensor

    def A(eng, fn, R=(), W=()):
        P.add(eng, fn, R, W)

    dma_ctr = [0]

    def DMA(q, out, in_, R=(), W=(), key=None, **kw):
        if key is None:
            dma_ctr[0] += 1
            key = "k%d" % dma_ctr[0]
        e = {"sp": nc.sync, "pool": nc.gpsimd, "act": nc.scalar}[q]
        P.add(q, lambda: e.dma_start(out=out, in_=in_, **kw), R, W, dma=key)

    dbg_off = [0]

    def dump(ap2d, toks, n, bf=False):
        if stop.endswith("x"):
            return
        o = dbg_off[0]
        if bf:
            tmp = AR.alloc(n)
            tt = Tok()
            A("dve", lambda: V.tensor_copy(out=tmp, in_=ap2d), toks, [tt])
            DMA("sp", dbg_d[:, o:o + n], tmp, R=[tt], W=[Tok()])
        else:
            DMA("sp", dbg_d[:, o:o + n], ap2d, R=toks, W=[Tok()])
        dbg_off[0] += n

    identF = AR.alloc(128); t_identF = Tok()
    identB = AR.alloc(128, BF16); t_identB = Tok()
    onesF = AR.alloc(128); t_onesF = Tok()
    iota = AR.alloc(512); t_iota = Tok()
    vecT = AR.alloc(128); t_vecT = Tok()
    modT = AR.alloc(96); t_modT = Tok()
    gscT = AR.alloc(32); t_gscT = Tok()
    wsign = AR.alloc([NS, 8]); t_wsign = Tok()
    cwT = AR.alloc(24); t_cwT = Tok()
    qpos = AR.alloc(NS); t_qpos = Tok()
    hv = AR.alloc(256); t_hv = Tok()
    G0 = AR.alloc(1024); t_G0 = Tok()

    DMA("sp", identF, ident_d, W=[t_identF])
    DMA("sp", iota, iota_d, W=[t_iota])
    DMA("sp", cwT, cwT_d, W=[t_cwT])
    DMA("sp", qpos, qpos_d, W=[t_qpos])
    DMA("sp", hv, hv_d, W=[t_hv])
    A("dve", lambda: V.tensor_copy(out=identB, in_=identF), [t_identF], [t_identB])
    A("dve", lambda: V.memset(onesF, 1.0), [], [t_onesF])

    if stop == "pre":
        dump(identF, [t_identF], 128)
        dump(identB, [t_identB], 128, bf=True)
        dump(hv, [t_hv], 256)
        P.emit()
        return nc, P, AR
    m0 = AR.mark()
    vecs_sb = AR.alloc(128); t_vecs = Tok()
    cT_sb = AR.alloc(8); t_cT = Tok()
    cact2 = AR.alloc([8, 2]); t_cact = Tok()
    adaw = [AR.alloc([8, 512]) for _ in range(2)]
    t_adaw = [Tok(), Tok()]
    DMA("sp", vecs_sb, vecs_d, W=[t_vecs])
    DMA("sp", cT_sb, cT_d, W=[t_cT])
    A("pe", lambda: PE.transpose(out=pbank[0][:, 0:128], in_=vecs_sb, identity=identF), [t_vecs, t_identF], [tpb[0]])
    A("act", lambda: S.copy(out=vecT, in_=pbank[0][:, 0:128]), [tpb[0]], [t_vecT])
    A("act", lambda: S.activation(out=cact2[:, :, 0], in_=cT_sb, func=AF.Silu), [t_cT], [t_cact])
    A("act", lambda: S.activation(out=cact2[:, :, 1], in_=cT_sb, func=AF.Silu), [t_cT], [t_cact])
    n = 0
    for L in range(2):
        src = ada_w[L].rearrange("(k p) n -> p k n", p=128)
        for cg in range(12):
            b = n % 2
            n += 1
            DMA("sp", adaw[b], src[:, :, cg * 512:(cg + 1) * 512], W=[t_adaw[b]], key="adaw%d" % b)
            for m4 in range(4):
                m = L * 48 + cg * 4 + m4
                for k in range(8):
                    A("pe", (lambda b=b, m=m, m4=m4, k=k: PE.matmul(
                        pbank[1][:, 2 * m:2 * m + 2], lhsT=adaw[b][:, k, m4 * 128:(m4 + 1) * 128],
                        rhs=cact2[:, k, :], start=(k == 0), stop=(k == 7))),
                      [t_adaw[b], t_cact], [tpb[1]])
    if stop == "p0a":
        A("act", lambda: S.copy(out=modT, in_=pbank[1][:, 0:96]), [tpb[1]], [t_modT])
        dump(modT, [t_modT], 96)
        dump(vecT, [t_vecT], 128)
        dump(cact2.rearrange("p a b -> p (a b)"), [t_cact], 16)
        P.emit()
        return nc, P, AR
    A("dve", lambda: V.tensor_tensor(out=modT, in0=pbank[1][:, 0:192].rearrange("p (m t) -> p m t", t=2)[:, :, 0],
                                     in1=vecT[:, 0:96], op=ALU.add), [tpb[1], t_vecT], [t_modT])
    for L in range(2):
        for s in range(2):
            o = (2 * L + s) * 8
            sc = modT[:, L * 48 + (8 if s == 0 else 32): L * 48 + (16 if s == 0 else 40)]
            ng = vecT[:, 96 + s * 16 + L * 8: 96 + s * 16 + L * 8 + 8]
            A("dve", (lambda o=o, sc=sc, ng=ng: V.scalar_tensor_tensor(
                out=gscT[:, o:o + 8], in0=sc, scalar=1.0, in1=ng, op0=ALU.add, op1=ALU.mult)),
              [t_modT, t_vecT], [t_gscT])

    def mod_cols(L, which):
        return modT[:, L * 48 + which * 8: L * 48 + which * 8 + 8]

    def make_gate(gcols, dst, t_dst, dg_bufs, t_dg, banks):
        for j in range(8):
            b = j % 2
            A("dve", (lambda j=j, b=b: V.tensor_scalar(out=dg_bufs[b], in0=identF, scalar1=gcols[:, j:j + 1],
                                                      scalar2=None, op0=ALU.mult)),
              [t_identF, t_modT], [t_dg[b]])
            bk = banks[j // 4]
            A("pe", (lambda j=j, b=b, bk=bk: PE.matmul(pbank[bk][:, (j % 4) * 128:(j % 4 + 1) * 128], lhsT=onesF,
                                                      rhs=dg_bufs[b], start=True, stop=True)),
              [t_onesF, t_dg[b]], [tpb[bk]])
        for h in range(2):
            A("act", (lambda h=h: S.copy(out=dst[:, h * 512:(h + 1) * 512], in_=pbank[banks[h]][:])),
              [tpb[banks[h]]], [t_dst])

    if stop == "p0b":
        dump(modT, [t_modT], 96)
        dump(gscT, [t_gscT], 32)
        P.emit()
        return nc, P, AR
    dgb = [AR.alloc(128), AR.alloc(128)]
    t_dgb = [Tok(), Tok()]
    make_gate(mod_cols(0, 2), G0, t_G0, dgb, t_dgb, [2, 3])
    if stop == "p0c":
        dump(G0, [t_G0], 1024)
        P.emit()
        return nc, P, AR
    P.barrier()
    if stop == "p0":
        dump(modT, [t_modT], 96)
        dump(gscT, [t_gscT], 32)
        dump(G0, [t_G0], 1024)
        dump(vecT, [t_vecT], 128)
        P.emit()
        return nc, P, AR
    AR.release(m0)

    def norm_T(xt, t_xt, gs, shc, hT_dst, t_hT, scr, bank):
        junkb, t_junkb, ss, t_ss, xn, t_xn = scr
        A("act", lambda: S.activation(out=junkb, in_=xt, func=AF.Square, accum_out=ss[:, 0:1]), [t_xt], [t_junkb, t_ss])
        A("act", lambda: S.activation(out=ss[:, 1:2], in_=ss[:, 0:1], func=AF.Sqrt, scale=1.0 / D, bias=eps_t[:, 0:1]),
          [t_ss, t_eps], [t_ss])
        A("dve", lambda: V.reciprocal(out=ss[:, 2:3], in_=ss[:, 1:2]), [t_ss], [t_ss])
        A("act", lambda: S.activation(out=xn, in_=xt, func=AF.Identity, scale=ss[:, 2:3]), [t_xt, t_ss], [t_xn])
        pv = pbf(bank)
        for j in range(8):
            A("pe", (lambda j=j: PE.transpose(out=pv[:, j * 128:(j + 1) * 128], in_=xn[:, j * 128:(j + 1) * 128],
                                              identity=identB)), [t_xn, t_identB], [tpb[bank]])
        for j in range(8):
            A("act", (lambda j=j: S.activation(out=hT_dst[:, j, :], in_=pv[:, j * 128:(j + 1) * 128], func=AF.Identity,
                                               scale=gs[:, j:j + 1], bias=shc[:, j:j + 1])),
              [tpb[bank], t_gscT, t_modT], [t_hT])

    eps_t = AR.alloc(4); t_eps = Tok()
    A("dve", lambda: V.memset(eps_t, EPS), [], [t_eps])

    mA = AR.mark()
    kT_all = AR.alloc([2, T], BF16); t_kT = [Tok() for _ in range(NT)]
    Vaug = AR.alloc([NT, 4, 66], BF16); t_V = [Tok() for _ in range(NT)]
    kiT = AR.alloc(T, BF16); t_kiT = [Tok() for _ in range(NT)]
    t_Vones = Tok()
    A("pool", lambda: G.memset(Vaug[:, :, :, 64:65], 1.0), [], [t_Vones] + t_V)

    mA1 = AR.mark()
    Win = AR.alloc([8, WCOLS], BF16); t_Win = [Tok() for _ in range(8)]
    wsrc = w_in.rearrange("(k p) n -> p k n", p=128)
    for k in range(8):
        DMA("pool", Win[:, k, :], wsrc[:, k, :], W=[t_Win[k]], key="win%d" % k, max_dma_last_dim=4096)

    def rope_tables(pos_d, n, cos_t, sin_t, t_tab):
        m = AR.mark()
        pi_ = AR.alloc(n, I32); t_pi = Tok()
        pf = AR.alloc(n); ang = AR.alloc([n, 32]); u = AR.alloc([n, 32]); ki_ = AR.alloc([n, 32], I32)
        kf = AR.alloc([n, 32]); r = AR.alloc([n, 32]); r2 = AR.alloc([n, 32]); tmp = AR.alloc([n, 32])
        invt = AR.alloc(32)
        tk = Tok()
        DMA("sp", pi_, pos_d, W=[t_pi])
        DMA("sp", invt, invf_d, W=[tk])
        A("dve", lambda: V.tensor_copy(out=pf, in_=pi_), [t_pi], [tk])
        A("dve", lambda: V.tensor_tensor(out=ang, in0=pf.unsqueeze(2).to_broadcast([128, n, 32]),
                                         in1=invt.unsqueeze(1).to_broadcast([128, n, 32]), op=ALU.mult), [tk], [tk])
        A("dve", lambda: V.tensor_scalar(out=u, in0=ang, scalar1=1.0 / TWO_PI, scalar2=None, op0=ALU.mult), [tk], [tk])
        A("dve", lambda: V.tensor_copy(out=ki_, in_=u), [tk], [tk])
        A("dve", lambda: V.tensor_copy(out=kf, in_=ki_), [tk], [tk])
        A("dve", lambda: V.scalar_tensor_tensor(out=r, in0=kf, scalar=-C1, in1=ang, op0=ALU.mult, op1=ALU.add), [tk], [tk])
        A("dve", lambda: V.scalar_tensor_tensor(out=r, in0=kf, scalar=-C2, in1=r, op0=ALU.mult, op1=ALU.add), [tk], [tk])
        A("dve", lambda: V.tensor_scalar(out=r, in0=r, scalar1=-3.1415925, scalar2=3.1415925, op0=ALU.max, op1=ALU.min), [tk], [tk])
        A("dve", lambda: V.tensor_scalar(out=r2, in0=r, scalar1=math.pi / 2, scalar2=None, op0=ALU.add), [tk], [tk])
        A("dve", lambda: V.tensor_scalar(out=tmp, in0=r2, scalar1=math.pi, scalar2=-TWO_PI, op0=ALU.is_gt, op1=ALU.mult), [tk], [tk])
        A("dve", lambda: V.tensor_tensor(out=r2, in0=r2, in1=tmp, op=ALU.add), [tk], [tk])
        A("dve", lambda: V.tensor_scalar(out=r2, in0=r2, scalar1=-3.1415925, scalar2=3.1415925, op0=ALU.max, op1=ALU.min), [tk], [tk])
        A("act", lambda: S.activation(out=sin_t, in_=r, func=AF.Sin), [tk], [t_tab])
        A("act", lambda: S.activation(out=cos_t, in_=r2, func=AF.Sin), [tk], [t_tab])
        return m

    cos_s = AR.alloc([NT, 32]); sin_s = AR.alloc([NT, 32]); t_tabs = Tok()
    cos_o = AR.alloc([NS, 32]); sin_o = AR.alloc([NS, 32]); t_tabo = Tok()
    hn = NT // 2
    for hf in range(2):
        mm_ = rope_tables(pos_seq[:, hf * hn:(hf + 1) * hn], hn, cos_s[:, hf * hn:(hf + 1) * hn, :], sin_s[:, hf * hn:(hf + 1) * hn, :], t_tabs)
        P.barrier()
        AR.release(mm_)
    mm_ = rope_tables(pos_own, NS, cos_o, sin_o, t_tabo)
    P.barrier()
    AR.release(mm_)

    xt = [AR.alloc(1024), AR.alloc(1024)]; t_xt = [Tok(), Tok()]
    hT = [AR.alloc([8, 128], BF16), AR.alloc([8, 128], BF16)]; t_hT = [Tok(), Tok()]
    scr = (AR.alloc(1024, BF16), Tok(), AR.alloc(4), Tok(), AR.alloc(1024, BF16), Tok())
    sq = AR.alloc(1024); t_sq = Tok()
    qn = AR.alloc(1024); t_qn = Tok()
    r1 = AR.alloc(1024); t_r1 = Tok()
    r2_ = AR.alloc(1024); t_r2 = Tok()
    qb = AR.alloc(1024, BF16); t_qb = Tok()
    sm = AR.alloc(64); t_sm = Tok()
    qTt = [AR.alloc(1024, BF16), AR.alloc(1024, BF16)]; t_qTt = [Tok(), Tok()]
    qiTt = [AR.alloc(512, BF16), AR.alloc(512, BF16)]; t_qiTt = [Tok(), Tok()]
    kib = AR.alloc(128, BF16); t_kib = Tok()
    qr = AR.alloc(512); t_qr = Tok()
    wab = AR.alloc(8); t_wab = Tok()

    def headnorm_rope(src_ps, t_src, H, gcol, cosv, sinv, t_tab, outb, t_outb):
        W_ = H * 64
        s3 = src_ps.rearrange("p (h d) -> p h d", h=H)
        A("act", lambda: S.activation(out=sq[:, 0:W_], in_=src_ps, func=AF.Square), [t_src], [t_sq])
        A("dve", lambda: V.tensor_reduce(out=sm[:, 0:H], in_=sq[:, 0:W_].rearrange("p (h d) -> p h d", h=H), axis=AX.X, op=ALU.add),
          [t_sq], [t_sm])
        A("act", lambda: S.activation(out=sm[:, 16:16 + H], in_=sm[:, 0:H], func=AF.Sqrt, scale=1.0 / 64, bias=eps_t[:, 0:1]),
          [t_sm, t_eps], [t_sm])
        A("dve", lambda: V.reciprocal(out=sm[:, 32:32 + H], in_=sm[:, 16:16 + H]), [t_sm], [t_sm])
        q3 = qn[:, 0:W_].rearrange("p (h d) -> p h d", h=H)
        A("dve", lambda: V.tensor_tensor(out=q3, in0=s3, in1=sm[:, 32:32 + H].unsqueeze(2).to_broadcast([128, H, 64]), op=ALU.mult),
          [t_src, t_sm], [t_qn])
        A("pool", lambda: G.tensor_tensor(out=q3, in0=q3, in1=hv[:, gcol:gcol + 64].unsqueeze(1).to_broadcast([128, H, 64]), op=ALU.mult),
          [t_qn, t_hv], [t_qn])
        rope(q3, t_qn, H, cosv, sinv, t_tab, outb, t_outb)

    def rope(q3, t_q3, H, cosv, sinv, t_tab, outb, t_outb):
        W_ = H * 64
        a3 = r1[:, 0:W_].rearrange("p (h d) -> p h d", h=H)
        b3 = r2_[:, 0:W_].rearrange("p (h d) -> p h d", h=H)
        o3 = outb[:, 0:W_].rearrange("p (h d) -> p h d", h=H)
        cb = cosv.unsqueeze(1).to_broadcast([128, H, 32])
        sb_ = sinv.unsqueeze(1).to_broadcast([128, H, 32])
        A("pool", lambda: G.tensor_tensor(out=a3[:, :, 0:32], in0=q3[:, :, 0:32], in1=cb, op=ALU.mult), [t_q3, t_tab], [t_r1])
        A("pool", lambda: G.tensor_tensor(out=a3[:, :, 32:64], in0=q3[:, :, 32:64], in1=cb, op=ALU.mult), [t_q3, t_tab], [t_r1])
        A("dve", lambda: V.tensor_tensor(out=b3[:, :, 0:32], in0=q3[:, :, 32:64], in1=sb_, op=ALU.mult), [t_q3, t_tab], [t_r2])
        A("dve", lambda: V.tensor_tensor(out=b3[:, :, 32:64], in0=q3[:, :, 0:32], in1=sb_, op=ALU.mult), [t_q3, t_tab], [t_r2])
        A("pool", lambda: G.tensor_tensor(out=o3[:, :, 0:32], in0=a3[:, :, 0:32], in1=b3[:, :, 0:32], op=ALU.subtract), [t_r1, t_r2], [t_outb])
        A("pool", lambda: G.tensor_tensor(out=o3[:, :, 32:64], in0=a3[:, :, 32:64], in1=b3[:, :, 32:64], op=ALU.add), [t_r1, t_r2], [t_outb])

    def proj(dst_bank, ncols, c0, hTb, t_hTb):
        for k in range(8):
            A("pe", (lambda k=k: PE.matmul(pbank[dst_bank][:, 0:ncols], lhsT=hTb[:, k, :], rhs=Win[:, k, c0:c0 + ncols],
                                           start=(k == 0), stop=(k == 7))), [t_hTb, t_Win[k]], [tpb[dst_bank]])

    gs1_0 = gscT[:, 0:8]
    sh1_0 = mod_cols(0, 0)
    for j in range(NT):
        b = j % 2
        DMA("sp", xt[b], x_seq[j * 128:(j + 1) * 128, :], W=[t_xt[b]], key="xt%d" % b)
        norm_T(xt[b], t_xt[b], gs1_0, sh1_0, hT[b], t_hT[b], scr, 0)
        proj(1, 512, 1024, hT[b], t_hT[b])
        proj(2, 128, 2048, hT[b], t_hT[b])
        cj, sj = cos_s[:, j, :], sin_s[:, j, :]
        headnorm_rope(pbank[1][:, 0:256], tpb[1], 4, 64, cj, sj, t_tabs, qb, t_qb)
        A("act", (lambda j=j: S.copy(out=Vaug[:, j, :, 0:64], in_=pbank[1][:, 256:512].rearrange("p (g d) -> p g d", g=4))),
          [tpb[1]], [t_V[j]])
        pv3 = pbf(3)
        for i in range(2):
            A("pe", (lambda i=i: PE.transpose(out=pv3[:, i * 128:(i + 1) * 128], in_=qb[:, i * 128:(i + 1) * 128], identity=identB)),
              [t_qb, t_identB], [tpb[3]])
        A("act", (lambda j=j: S.copy(out=kT_all[:, :, j * 128:(j + 1) * 128], in_=pv3[:, 0:256].rearrange("p (i t) -> p i t", i=2))),
          [tpb[3]], [t_kT[j]])
        A("dve", lambda: V.bn_stats(out=sm[:, 48:54], in_=pbank[2][:, 0:64]), [tpb[2]], [t_sm])
        A("dve", lambda: V.bn_aggr(out=sm[:, 54:56], in_=sm[:, 48:54]), [t_sm], [t_sm])
        A("act", lambda: S.activation(out=sm[:, 56:57], in_=sm[:, 55:56], func=AF.Sqrt, scale=1.0, bias=eps_t[:, 0:1]), [t_sm, t_eps], [t_sm])
        A("dve", lambda: V.reciprocal(out=sm[:, 57:58], in_=sm[:, 56:57]), [t_sm], [t_sm])
        A("dve", lambda: V.tensor_scalar(out=qn[:, 0:64], in0=pbank[2][:, 0:64], scalar1=sm[:, 54:55], scalar2=sm[:, 57:58],
                                         op0=ALU.subtract, op1=ALU.mult), [tpb[2], t_sm], [t_qn])
        A("pool", lambda: G.tensor_tensor(out=qn[:, 0:64], in0=qn[:, 0:64], in1=hv[:, 128:192], op=ALU.mult), [t_qn, t_hv], [t_qn])
        A("pool", lambda: G.tensor_tensor(out=qn[:, 0:64], in0=qn[:, 0:64], in1=hv[:, 192:256], op=ALU.add), [t_qn, t_hv], [t_qn])
        rope(qn[:, 0:64].rearrange("p (h d) -> p h d", h=1), t_qn, 1, cj, sj, t_tabs, kib, t_kib)
        A("pool", lambda: G.tensor_copy(out=kib[:, 64:128], in_=kib[:, 0:64]), [t_kib], [t_kib])
        A("pe", lambda: PE.transpose(out=pv3[:, 256:384], in_=kib, identity=identB), [t_kib, t_identB], [tpb[3]])
        A("act", (lambda j=j: S.copy(out=kiT[:, j * 128:(j + 1) * 128], in_=pv3[:, 256:384])), [tpb[3]], [t_kiT[j]])

    WSC = (8 ** -0.5) * (64 ** -0.5)
    for i in range(NS):
        b = i % 2
        DMA("sp", xt[b], x_own[i * 128:(i + 1) * 128, :], W=[t_xt[b]], key="xt%d" % b)
        norm_T(xt[b], t_xt[b], gs1_0, sh1_0, hT[b], t_hT[b], scr, 0)
        proj(1, 512, 0, hT[b], t_hT[b])
        proj(2, 512, 512, hT[b], t_hT[b])
        proj(4, 512, 1536, hT[b], t_hT[b])
        proj(5, 8, 2176, hT[b], t_hT[b])
        ci, si = cos_o[:, i, :], sin_o[:, i, :]
        for hh in range(2):
            headnorm_rope(pbank[1 + hh][:], tpb[1 + hh], 8, 0, ci, si, t_tabo, qb, t_qb)
            pv3 = pbf(3)
            for jj in range(4):
                A("pe", (lambda jj=jj, hh=hh: PE.transpose(out=pv3[:, (hh * 4 + jj) * 128:(hh * 4 + jj + 1) * 128],
                                                          in_=qb[:, jj * 128:(jj + 1) * 128], identity=identB)),
                  [t_qb, t_identB], [tpb[3]])
        A("act", (lambda b=b: S.copy(out=qTt[b], in_=pbf(3))), [tpb[3]], [t_qTt[b]])
        DMA("sp", qTs[i], qTt[b], R=[t_qTt[b]], W=[t_qTs[i]], key="qTs")
        A("act", (lambda i=i: S.activation(out=wsign[:, i, :], in_=pbank[5][:, 0:8], func=AF.Sign)), [tpb[5]], [t_wsign])
        A("act", lambda: S.activation(out=wab, in_=pbank[5][:, 0:8], func=AF.Abs, scale=WSC), [tpb[5]], [t_wab])
        A("act", lambda: S.copy(out=qn[:, 0:512], in_=pbank[4][:]), [tpb[4]], [t_qn])
        rope(qn[:, 0:512].rearrange("p (h d) -> p h d", h=8), t_qn, 8, ci, si, t_tabo, qr, t_qr)
        A("dve", lambda: V.tensor_tensor(out=qb[:, 0:512].rearrange("p (h d) -> p h d", h=8),
                                         in0=qr[:, 0:512].rearrange("p (h d) -> p h d", h=8),
                                         in1=wab.unsqueeze(2).to_broadcast([128, 8, 64]), op=ALU.mult),
          [t_qr, t_wab], [t_qb])
        pv6 = pbf(6)
        for jj in range(4):
            A("pe", (lambda jj=jj: PE.transpose(out=pv6[:, jj * 128:(jj + 1) * 128], in_=qb[:, jj * 128:(jj + 1) * 128], identity=identB)),
              [t_qb, t_identB], [tpb[6]])
        A("act", (lambda b=b: S.copy(out=qiTt[b], in_=pv6[:, 0:512])), [tpb[6]], [t_qiTt[b]])
        DMA("sp", qiTs[i], qiTt[b], R=[t_qiTt[b]], W=[t_qiTs[i]], key="qiTs")
    P.barrier()
    if stop in ("a1", "a1x"):
        dump(kT_all[:, 0, 0:512], t_kT, 512, bf=True)
        dump(kT_all[:, 1, 0:512], t_kT, 512, bf=True)
        dump(kiT[:, 0:512], t_kiT, 512, bf=True)
        dump(Vaug[:, 0:4, :, :].rearrange("p a g d -> p (a g d)"), t_V, 1056, bf=True)
        dump(qTt[(NS - 1) % 2], t_qTt, 1024, bf=True)
        dump(qiTt[(NS - 1) % 2], t_qiTt, 512, bf=True)
        dump(wsign.rearrange("p a h -> p (a h)"), [t_wsign], NS * 8)
        dump(cos_s.rearrange("p a h -> p (a h)"), [t_tabs], NT * 32)
        dump(sin_s.rearrange("p a h -> p (a h)"), [t_tabs], NT * 32)
        P.emit()
        return nc, P, AR
    AR.release(mA1)

    Wo = AR.alloc([8, 1024], BF16); t_Wo = Tok()
    for hh in range(2):
        DMA("pool", Wo[hh * 64:(hh + 1) * 64, :, :], w_out[hh * 512:(hh + 1) * 512, :].rearrange("(j d) c -> d j c", d=64),
            W=[t_Wo], key="wo", max_dma_last_dim=4096)
    I_ = AR.alloc(T); t_I = Tok()
    mask01 = AR.alloc(T, BF16); t_mask = Tok()
    junk8 = AR.alloc(T, U8); t_junk8 = Tok()
    qTb = [[AR.alloc(1024, BF16), AR.alloc(1024, BF16)] for _ in range(2)]; t_qTb = [Tok(), Tok()]
    qiTb = [[AR.alloc(512, BF16), AR.alloc(512, BF16)] for _ in range(2)]; t_qiTb = [Tok(), Tok()]
    for b_ in range(2):
        for hf_ in range(2):
            A("pool", (lambda b_=b_, hf_=hf_: G.memset(qTb[b_][hf_], 0.0)), [], [t_qTb[b_]])
            A("pool", (lambda b_=b_, hf_=hf_: G.memset(qiTb[b_][hf_], 0.0)), [], [t_qiTb[b_]])
    NR = 4
    Rb = [AR.alloc(512, BF16) for _ in range(NR)]; t_Rb = [Tok() for _ in range(NR)]
    Dh = AR.alloc([8, 128], BF16); t_Dh = Tok()
    biasb = [AR.alloc(512, BF16), AR.alloc(512, BF16)]; t_biasb = [Tok(), Tok()]
    NPB = 3
    Pexp = [AR.alloc(512, BF16) for _ in range(NPB)]; t_Pexp = [Tok() for _ in range(NPB)]
    NPM = 4
    Pm = [AR.alloc(512, BF16) for _ in range(NPM)]; t_Pm = [Tok() for _ in range(NPM)]
    maskT = [AR.alloc(128, BF16) for _ in range(3)]; t_maskT = [Tok() for _ in range(3)]
    rs = AR.alloc(512); t_rs = Tok()
    bcS = AR.alloc(512); t_bcS = Tok()
    ys = AR.alloc(512); t_ys = Tok()
    numS = ys; t_numS = t_ys
    oT_all = AR.alloc([2, 512], BF16); t_oTlo = Tok(); t_oThi = Tok()
    oT_tmp = AR.alloc([2, 512], BF16); t_oTtmp = Tok()
    xa = AR.alloc(512); t_xa = Tok()
    ta = AR.alloc(512); t_ta = Tok()
    bs = AR.alloc(16); t_bs = Tok()
    tr_banks = [0, 1, 2]
    trc = [0]

    def tbank():
        k = tr_banks[trc[0] % 3]
        trc[0] += 1
        return k

    rbc = [0]
    SCALE = 64 ** -0.5

    def indexer(i):
        b = i % 2
        E = cfg.ext[i]
        nch = (E + 3) // 4
        for hf_ in range(2):
            DMA("sp", qiTb[b][hf_][hf_ * 64:(hf_ + 1) * 64, :], qiTs[i][hf_ * 64:(hf_ + 1) * 64, :], R=[t_qiTs[i]], W=[t_qiTb[b]],
                key="qiTb%d" % b)
        for hf_ in range(2):
            DMA("sp", qTb[b][hf_][hf_ * 64:(hf_ + 1) * 64, :], qTs[i][hf_ * 64:(hf_ + 1) * 64, :], R=[t_qTs[i]], W=[t_qTb[b]],
                key="qTb%d" % b)
        for h in range(8):
            A("pool", (lambda h=h, i=i: G.tensor_scalar(out=Dh[:, h, :], in0=identB, scalar1=wsign[:, i, h:h + 1], scalar2=None, op0=ALU.mult)),
              [t_identB, t_wsign], [t_Dh])
        for c in range(nch):
            kts = [t_kiT[jj] for jj in range(c * 4, c * 4 + 4)]
            need_bias = c >= cfg.dchunk[i]
            if need_bias:
                bb = c % 2
                A("dve", (lambda c=c, i=i: V.tensor_scalar(out=bs[:, 6:7], in0=qpos[:, i:i + 1], scalar1=float(-512 * c), scalar2=None, op0=ALU.add)),
                  [t_qpos], [t_bs])
                A("dve", (lambda bb=bb: V.tensor_scalar(out=biasb[bb], in0=iota, scalar1=bs[:, 6:7], scalar2=-1e30, op0=ALU.is_gt, op1=ALU.mult)),
                  [t_iota, t_bs], [t_biasb[bb]])
            banks = []
            LA = 2

            def acc(hh, nb=need_bias):
                A("pe", (lambda hh=hh, rbb=banks[hh], nb=nb: PE.matmul(pbank[3][:], lhsT=Dh[:, hh, :], rhs=Rb[rbb], start=(hh == 0),
                                                                     stop=(hh == 7 and not nb))),
                  [t_Dh, t_Rb[banks[hh]]], [tpb[3]])

            for h in range(8):
                bk = tbank()
                hp, pr = h % 2, h // 2
                A("pe", (lambda bk=bk, hp=hp, pr=pr, c=c, b=b: PE.matmul(
                    pbank[bk][:], lhsT=qiTb[b][hp][:, pr * 128:(pr + 1) * 128],
                    rhs=kiT[:, c * 512:(c + 1) * 512], start=True, stop=True)),
                  [t_qiTb[b]] + kts, [tpb[bk]])
                rb = rbc[0] % NR
                rbc[0] += 1
                if h % 2 == 0:
                    A("act", (lambda bk=bk, rb=rb: S.activation(out=Rb[rb], in_=pbank[bk][:], func=AF.Relu)), [tpb[bk]], [t_Rb[rb]])
                else:
                    A("dve", (lambda bk=bk, rb=rb: V.tensor_scalar(out=Rb[rb], in0=pbank[bk][:], scalar1=0.0, scalar2=None, op0=ALU.max)),
                      [tpb[bk]], [t_Rb[rb]])
                banks.append(rb)
                if h >= LA:
                    acc(h - LA)
            for hh in range(8 - LA, 8):
                acc(hh)
            if need_bias:
                A("pe", (lambda bb=bb: PE.matmul(pbank[3][:], lhsT=identB, rhs=biasb[bb], start=False, stop=True)),
                  [t_identB, t_biasb[bb]], [tpb[3]])
            A("act", (lambda c=c: S.copy(out=I_[:, c * 512:(c + 1) * 512], in_=pbank[3][:])), [tpb[3]], [t_I])

    def topk_parts(i):
        E = cfg.ext[i]
        Sn = ((E + 3) // 4) * 512
        K0 = min(256, Sn)

        def pro():
            A("dve", lambda: V.tensor_reduce(out=bs[:, 0:1], in_=I_[:, 0:Sn], axis=AX.X, op=ALU.max), [t_I], [t_bs])
            A("dve", lambda: V.tensor_reduce(out=bs[:, 1:2], in_=I_[:, 0:K0], axis=AX.X, op=ALU.min), [t_I], [t_bs])
            A("dve", lambda: V.tensor_scalar(out=bs[:, 1:2], in0=bs[:, 1:2], scalar1=-1e29, scalar2=None, op0=ALU.max), [t_bs], [t_bs])
            A("dve", lambda: V.tensor_tensor(out=bs[:, 2:3], in0=bs[:, 0:1], in1=bs[:, 1:2], op=ALU.subtract), [t_bs], [t_bs])

        def mk_it(it):
            f = 2.0 ** (-(it + 1))

            def run():
                A("dve", lambda: V.scalar_tensor_tensor(out=bs[:, 3:4], in0=bs[:, 2:3], scalar=f, in1=bs[:, 1:2], op0=ALU.mult, op1=ALU.add),
                  [t_bs], [t_bs])
                A("dve", lambda: V.tensor_scalar(out=junk8[:, 0:Sn], in0=I_[:, 0:Sn], scalar1=bs[:, 3:4], scalar2=None, op0=ALU.is_ge,
                                                 op1=ALU.add, accum_out=bs[:, 4:5]), [t_I, t_bs], [t_junk8, t_bs])
                A("dve", lambda: V.tensor_scalar(out=bs[:, 5:6], in0=bs[:, 4:5], scalar1=cfg.topk - 0.5, scalar2=f, op0=ALU.is_gt, op1=ALU.mult),
                  [t_bs], [t_bs])
                A("dve", lambda: V.scalar_tensor_tensor(out=bs[:, 1:2], in0=bs[:, 5:6], scalar=bs[:, 2:3], in1=bs[:, 1:2], op0=ALU.mult, op1=ALU.add),
                  [t_bs], [t_bs])
            return run

        def epi():
            A("dve", lambda: V.tensor_scalar(out=mask01[:, 0:E * 128], in0=I_[:, 0:E * 128], scalar1=bs[:, 1:2], scalar2=None, op0=ALU.is_ge),
              [t_I, t_bs], [t_mask])
        return pro, [mk_it(it) for it in range(NIT)], epi

    pbc = [0]

    def attention_parts(i):
        b = i % 2
        E = cfg.ext[i]
        steps = [(kb, g) for kb in range(E) for g in range(4)]
        DL = 2
        pmof = {}

        def front(n):
            kb, g = steps[n]
            mt = kb % 3
            if g == 0:
                bk = tbank()
                A("pe", (lambda bk=bk, kb=kb: PE.transpose(out=pbf(bk)[:, 0:128], in_=mask01[:, kb * 128:(kb + 1) * 128], identity=identB)),
                  [t_mask, t_identB], [tpb[bk]])
                A("act", (lambda bk=bk, mt=mt: S.copy(out=maskT[mt], in_=pbf(bk)[:, 0:128])), [tpb[bk]], [t_maskT[mt]])
            hp, gi = g // 2, g % 2
            bk = tbank()
            A("pe", (lambda bk=bk, hp=hp, gi=gi, kb=kb, b=b: PE.matmul(
                pbank[bk][:], lhsT=kT_all[:, gi, kb * 128:(kb + 1) * 128],
                rhs=qTb[b][hp][:, gi * 512:(gi + 1) * 512], start=True, stop=True)),
              [t_kT[kb], t_qTb[b]], [tpb[bk]])
            pe_ = pbc[0] % NPB
            pm_ = pbc[0] % NPM
            pbc[0] += 1
            pmof[n] = pm_
            A("act", (lambda bk=bk, pe_=pe_: S.activation(out=Pexp[pe_], in_=pbank[bk][:], func=AF.Exp, scale=SCALE)),
              [tpb[bk]], [t_Pexp[pe_]])
            A("dve", (lambda pe_=pe_, pm_=pm_, mt=mt: V.tensor_tensor(
                out=Pm[pm_].rearrange("p (r t) -> p r t", r=4), in0=Pexp[pe_].rearrange("p (r t) -> p r t", r=4),
                in1=maskT[mt].unsqueeze(1).to_broadcast([128, 4, 128]), op=ALU.mult)),
              [t_Pexp[pe_], t_maskT[mt]], [t_Pm[pm_]])

        def back(n):
            kb, g = steps[n]
            pm_ = pmof[n]
            A("pe", (lambda g=g, kb=kb, pm_=pm_, E=E: PE.matmul(pbank[4 + g][0:65, :], lhsT=Vaug[:, kb, g, 0:65], rhs=Pm[pm_],
                                                               start=(kb == 0), stop=(kb == E - 1))),
              [t_V[kb], t_Vones, t_Pm[pm_]], [tpb[4 + g]])

        def mk_step(n):
            def run():
                if n < len(steps):
                    front(n)
                if n - DL >= 0:
                    back(n - DL)
            return run

        return [mk_step(n) for n in range(len(steps) + DL)], (lambda: att_tail(i))

    def att_tail(i):
        b = i % 2
        E = cfg.ext[i]
        if ATT_PARTS < 2:
            return
        for g in range(4):
            A("act", (lambda g=g: S.activation(out=rs[64:65, :], in_=pbank[4 + g][64:65, :], func=AF.Ln)), [tpb[4 + g]], [t_rs])
            A("act", lambda: S.activation(out=rs[64:65, :], in_=rs[64:65, :], func=AF.Exp, scale=-1.0), [t_rs], [t_rs])
            bk = tbank()
            A("pe", (lambda bk=bk: PE.matmul(pbank[bk][0:64, :], lhsT=onesF[64:65, 0:64], rhs=rs[64:65, :], start=True, stop=True)),
              [t_onesF, t_rs], [tpb[bk]])
            A("act", (lambda bk=bk: S.copy(out=bcS[0:64, :], in_=pbank[bk][0:64, :])), [tpb[bk]], [t_bcS])
            A("act", (lambda g=g: S.copy(out=numS[0:64, :], in_=pbank[4 + g][0:64, :])), [tpb[4 + g]], [t_numS])
            if g < 2:
                A("pool", (lambda g=g: G.tensor_tensor(out=oT_all[0:64, g, :], in0=numS[0:64, :], in1=bcS[0:64, :], op=ALU.mult)),
                  [t_numS, t_bcS], [t_oTlo])
            else:
                A("pool", (lambda g=g: G.tensor_tensor(out=oT_tmp[0:64, g - 2, :], in0=numS[0:64, :], in1=bcS[0:64, :], op=ALU.mult)),
                  [t_numS, t_bcS], [t_oTtmp])
        if ATT_PARTS < 3:
            return
        DMA("sp", oT_all[64:128, :, :], oT_tmp[0:64, :, :], R=[t_oTtmp], W=[t_oThi], key="oThi")
        if ATT_PARTS < 4:
            return
        for ch in range(2):
            DMA("sp", xa, x_own[i * 128:(i + 1) * 128, ch * 512:(ch + 1) * 512], W=[t_xa], key="xa")
            bk = tbank()
            for gi in range(2):
                for r in range(4):
                    j = gi * 4 + r
                    A("pe", (lambda bk=bk, gi=gi, r=r, j=j, ch=ch: PE.matmul(
                        pbank[bk][:], lhsT=oT_all[:, gi, r * 128:(r + 1) * 128], rhs=Wo[:, j, ch * 512:(ch + 1) * 512],
                        start=(j == 0), stop=(j == 7))), [t_oTlo, t_oThi, t_Wo], [tpb[bk]])
            A("act", (lambda bk=bk: S.copy(out=ys, in_=pbank[bk][:])), [tpb[bk]], [t_ys])
            A("pool", (lambda ch=ch: G.tensor_tensor(out=ta, in0=ys, in1=G0[:, ch * 512:(ch + 1) * 512], op=ALU.mult)), [t_ys, t_G0], [t_ta])
            A("pool", lambda: G.tensor_tensor(out=ta, in0=ta, in1=xa, op=ALU.add), [t_ta, t_xa], [t_ta])
            DMA("sp", x1s[i * 128:(i + 1) * 128, ch * 512:(ch + 1) * 512], ta, R=[t_ta], W=[t_x1s[i]], key="x1s")

    indexer(0)
    if stop in ("a2i", "a2t", "a2a"):
        pass
        dump(I_[:, 0:1024], [t_I], 1024)
        dump(mask01[:, 0:1024], [t_mask], 1024, bf=True)
        dump(bs, [t_bs], 16)
        dump(ta, [t_ta], 512)
        P.emit()
        return nc, P, AR
    pro, its, epi = topk_parts(0)
    pro()
    for f_ in its:
        f_()
    epi()
    for i in range(NS):
        steps_, tail_ = attention_parts(i)
        if i + 1 < NS:
            indexer(i + 1)
            pro, its, epi = topk_parts(i + 1)
            pro()
        else:
            its, epi = [], None
        per = max(1, len(steps_) // (len(its) + 1)) if its else len(steps_)
        k_it = 0
        for k_, st_ in enumerate(steps_):
            st_()
            if its and (k_ + 1) % per == 0 and k_it < len(its):
                its[k_it]()
                k_it += 1
        while k_it < len(its):
            its[k_it]()
            k_it += 1
        tail_()
        if epi is not None:
            epi()
    P.barrier()
    if stop in ("a2", "a2x"):
        dump(I_[:, 0:1024], [t_I], 1024)
        dump(mask01[:, 0:1024], [t_mask], 1024, bf=True)
        dump(bs, [t_bs], 16)
        dump(ta, [t_ta], 512)
        P.emit()
        return nc, P, AR
    AR.release(mA)

    Gt = [AR.alloc(1024) for _ in range(3)]; t_Gt = [Tok() for _ in range(3)]
    dgb = [AR.alloc(128), AR.alloc(128)]
    make_gate(mod_cols(0, 5), Gt[0], t_Gt[0], dgb, t_dgb, [0, 1])
    make_gate(mod_cols(1, 2), Gt[1], t_Gt[1], dgb, t_dgb, [2, 3])
    make_gate(mod_cols(1, 5), Gt[2], t_Gt[2], dgb, t_dgb, [0, 1])
    MS = 4
    xm = AR.alloc([MS, 1024]); t_xm = [Tok() for _ in range(MS)]
    hTm = AR.alloc([8, MS * 128], BF16); t_hTm = Tok()
    hTs = AR.alloc([8, 128], BF16); t_hTs = Tok()
    scrB = (AR.alloc(1024, BF16), Tok(), AR.alloc(4), Tok(), AR.alloc(1024, BF16), Tok())
    aT = AR.alloc([NFC, MS * 128], BF16); t_aT = Tok()
    Wd = AR.alloc([NFC, 1024], BF16); t_Wd = [Tok() for _ in range(4)]
    NWB = 2
    WA = [AR.alloc([8, 512], BF16) for _ in range(NWB)]; t_WA = [Tok() for _ in range(NWB)]
    WB = [AR.alloc([8, 512], BF16) for _ in range(NWB)]; t_WB = [Tok() for _ in range(NWB)]
    WC = [AR.alloc([8, 512], BF16) for _ in range(NWB)]; t_WC = [Tok() for _ in range(NWB)]
    sg = [AR.alloc(MS * 128), AR.alloc(MS * 128)]; t_sg = [Tok(), Tok()]
    zb = AR.alloc(MS * 128 + 2); t_zb = Tok()
    zc = AR.alloc(MS * 128); t_zc = Tok()
    carry = AR.alloc([8, 2]); t_carry = Tok()
    tb = AR.alloc(512); t_tb = Tok()
    A("dve", lambda: V.memset(carry, 0.0), [], [t_carry])
    wbc = [0]

    def norm_macro(ns, gs, shc):
        for s in range(ns):
            norm_T(xm[:, s, :], t_xm[s], gs, shc, hTs, t_hTs, scrB, 7)
            A("pool", (lambda s=s: G.tensor_copy(out=hTm[:, :, s * 128:(s + 1) * 128], in_=hTs)), [t_hTs], [t_hTm])

    def down(ns, nchunks, Gate, t_Gate):
        N = ns * 128
        for s in range(ns):
            for ch in range(2):
                bk = 4 + (s * 2 + ch) % 2
                for j in range(nchunks):
                    A("pe", (lambda bk=bk, j=j, s=s, ch=ch: PE.matmul(pbank[bk][:], lhsT=aT[:, j, s * 128:(s + 1) * 128],
                                                                     rhs=Wd[:, j, ch * 512:(ch + 1) * 512],
                                                                     start=(j == 0), stop=(j == nchunks - 1))),
                      [t_aT, t_Wd[j // 6]], [tpb[bk]])
                A("dve", (lambda bk=bk, ch=ch: V.tensor_tensor(out=tb, in0=pbank[bk][:], in1=Gate[:, ch * 512:(ch + 1) * 512], op=ALU.mult)),
                  [tpb[bk], t_Gate], [t_tb])
                A("pool", (lambda s=s, ch=ch: G.tensor_tensor(out=xm[:, s, ch * 512:(ch + 1) * 512], in0=xm[:, s, ch * 512:(ch + 1) * 512],
                                                             in1=tb, op=ALU.add)), [t_tb, t_xm[s]], [t_xm[s]])

    def ffn(L, ns, Gate, t_Gate):
        N = ns * 128
        norm_macro(ns, gscT[:, (2 * L + 1) * 8:(2 * L + 1) * 8 + 8], mod_cols(L, 3))
        gsrc = wg_d[L].rearrange("(k p) f -> p k f", p=128)
        usrc = wu_d[L].rearrange("(k p) f -> p k f", p=128)
        for fg in range(6):
            f0 = fg * 512
            fw = min(512, FF - f0)
            wb = wbc[0] % NWB
            wbc[0] += 1
            DMA("pool", WA[wb][:, :, 0:fw], gsrc[:, :, f0:f0 + fw], W=[t_WA[wb]], key="WA%d" % wb)
            DMA("pool", WB[wb][:, :, 0:fw], usrc[:, :, f0:f0 + fw], W=[t_WB[wb]], key="WB%d" % wb)
            for fc in range(fw // 128):
                j = fg * 4 + fc
                bg, bu = (0, 1) if j % 2 == 0 else (2, 3)
                for k in range(8):
                    A("pe", (lambda bg=bg, k=k, fc=fc, wb=wb: PE.matmul(pbank[bg][:, 0:N], lhsT=WA[wb][:, k, fc * 128:(fc + 1) * 128],
                                                                       rhs=hTm[:, k, 0:N], start=(k == 0), stop=(k == 7))),
                      [t_WA[wb], t_hTm], [tpb[bg]])
                for k in range(8):
                    A("pe", (lambda bu=bu, k=k, fc=fc, wb=wb: PE.matmul(pbank[bu][:, 0:N], lhsT=WB[wb][:, k, fc * 128:(fc + 1) * 128],
                                                                       rhs=hTm[:, k, 0:N], start=(k == 0), stop=(k == 7))),
                      [t_WB[wb], t_hTm], [tpb[bu]])
                sb_ = j % 2
                A("act", (lambda bg=bg, sb_=sb_: S.activation(out=sg[sb_][:, 0:N], in_=pbank[bg][:, 0:N], func=AF.Silu)),
                  [tpb[bg]], [t_sg[sb_]])
                A("dve", (lambda bu=bu, sb_=sb_, j=j: V.tensor_tensor(out=aT[:, j, 0:N], in0=pbank[bu][:, 0:N], in1=sg[sb_][:, 0:N], op=ALU.mult)),
                  [tpb[bu], t_sg[sb_]], [t_aT])
        dsrc = wd_d[L].rearrange("(j p) c -> p j c", p=128)
        for q4 in range(0, NFC, 6):
            q5 = min(NFC, q4 + 6)
            DMA("pool", Wd[:, q4:q5, :], dsrc[:, q4:q5, :], W=[t_Wd[q4 // 6]], key="Wd%d" % (q4 // 6), max_dma_last_dim=4096)
        down(ns, NFC, Gate, t_Gate)

    def convmix(ns):
        N = ns * 128
        norm_macro(ns, gscT[:, 16:24], mod_cols(1, 0))
        src = cw_in.rearrange("(k p) f -> p k f", p=128)
        for cg in range(2):
            wb = wbc[0] % NWB
            wbc[0] += 1
            DMA("pool", WA[wb], src[:, :, cg * 512:(cg + 1) * 512], W=[t_WA[wb]], key="WA%d" % wb)
            DMA("pool", WB[wb], src[:, :, 1024 + cg * 512:1024 + (cg + 1) * 512], W=[t_WB[wb]], key="WB%d" % wb)
            DMA("pool", WC[wb], src[:, :, 2048 + cg * 512:2048 + (cg + 1) * 512], W=[t_WC[wb]], key="WC%d" % wb)
            for cc in range(4):
                cj = cg * 4 + cc
                for (bk, Wt, tW) in ((0, WA, t_WA), (1, WB, t_WB), (2, WC, t_WC)):
                    for k in range(8):
                        A("pe", (lambda bk=bk, Wt=Wt, k=k, cc=cc, wb=wb: PE.matmul(
                            pbank[bk][:, 0:N], lhsT=Wt[wb][:, k, cc * 128:(cc + 1) * 128], rhs=hTm[:, k, 0:N],
                            start=(k == 0), stop=(k == 7))), [tW[wb], t_hTm], [tpb[bk]])
                A("act", lambda: S.copy(out=sg[0][:, 0:N], in_=pbank[1][:, 0:N]), [tpb[1]], [t_sg[0]])
                A("dve", (lambda cj=cj: V.tensor_copy(out=zb[:, 0:2], in_=carry[:, cj, :])), [t_carry], [t_zb])
                A("dve", lambda: V.tensor_tensor(out=zb[:, 2:2 + N], in0=pbank[2][:, 0:N], in1=sg[0][:, 0:N], op=ALU.mult),
                  [tpb[2], t_sg[0]], [t_zb])
                A("dve", (lambda cj=cj: V.tensor_copy(out=carry[:, cj, :], in_=zb[:, N:N + 2])), [t_zb], [t_carry])
                A("dve", (lambda cj=cj: V.tensor_scalar(out=zc[:, 0:N], in0=zb[:, 2:2 + N], scalar1=cwT[:, cj * 3 + 2:cj * 3 + 3],
                                                       scalar2=None, op0=ALU.mult)), [t_zb, t_cwT], [t_zc])
                A("dve", (lambda cj=cj: V.scalar_tensor_tensor(out=zc[:, 0:N], in0=zb[:, 1:1 + N], scalar=cwT[:, cj * 3 + 1:cj * 3 + 2],
                                                              in1=zc[:, 0:N], op0=ALU.mult, op1=ALU.add)), [t_zb, t_cwT, t_zc], [t_zc])
                A("dve", (lambda cj=cj: V.scalar_tensor_tensor(out=zc[:, 0:N], in0=zb[:, 0:N], scalar=cwT[:, cj * 3:cj * 3 + 1],
                                                              in1=zc[:, 0:N], op0=ALU.mult, op1=ALU.add)), [t_zb, t_cwT, t_zc], [t_zc])
                A("dve", (lambda cj=cj: V.tensor_tensor(out=aT[:, cj, 0:N], in0=pbank[0][:, 0:N], in1=zc[:, 0:N], op=ALU.mult)),
                  [tpb[0], t_zc], [t_aT])
        osrc = cw_out.rearrange("(j p) c -> p j c", p=128)
        DMA("pool", Wd[:, 0:6, :], osrc[:, 0:6, :], W=[t_Wd[0]], key="Wd0", max_dma_last_dim=4096)
        DMA("pool", Wd[:, 6:8, :], osrc[:, 6:8, :], W=[t_Wd[1]], key="Wd1", max_dma_last_dim=4096)
        down(ns, 8, Gt[1], t_Gt[1])

    nmac = (NS + MS - 1) // MS
    for m in range(nmac):
        s0 = m * MS
        ns = min(MS, NS - s0)
        for s in range(ns):
            DMA("sp", xm[:, s, :], x1s[(s0 + s) * 128:(s0 + s + 1) * 128, :], R=[t_x1s[s0 + s]], W=[t_xm[s]], key="xm%d" % s)
        ffn(0, ns, Gt[0], t_Gt[0])
        convmix(ns)
        ffn(1, ns, Gt[2], t_Gt[2])
        for s in range(ns):
            DMA("sp", out_d[(s0 + s) * 128:(s0 + s + 1) * 128, :], xm[:, s, :], R=[t_xm[s]], W=[t_out[s0 + s]], key="out%d" % s)

    P.emit()
    return nc, P, AR


import os
ATT_PARTS = int(os.environ.get('ATT_PARTS', '9'))
_CACHE = {}
STOP = None
LAST = None


def _host_inputs(cfg, r, x, c, positions, ada_w, ada_b, norm1_g, norm2_g, attn_w_in, attn_q_norm_g, attn_k_norm_g,
                 idx_k_ln_g, idx_k_ln_b, attn_w_out, conv_w_in, conv_w, conv_w_out, ffn_w_gate, ffn_w_up, ffn_w_down,
                 shared):
    b, role = r // 2, r % 2
    tiles = cfg.tilesA if role == 0 else cfg.tilesB
    T, NT, NS = cfg.T, cfg.NT, cfg.NS
    xs = np.ascontiguousarray(x[b])
    xo = np.ascontiguousarray(xs.reshape(NT, 128, D)[tiles].reshape(NS * 128, D))
    ps = np.ascontiguousarray(positions[b].reshape(NT, 128).T)
    po = np.ascontiguousarray(positions[b].reshape(NT, 128)[tiles].T)
    tok = np.arange(T, dtype=np.float32).reshape(NT, 128)
    qp = np.ascontiguousarray(tok[tiles].T)
    cT = np.ascontiguousarray(c[b].reshape(8, 128).T)
    d = dict(shared)
    d.update({"x_seq": xs, "x_own": xo, "pos_seq": ps.astype(np.int32), "pos_own": po.astype(np.int32), "qpos": qp, "cT": cT})
    return d


def _shared_inputs(ada_w, ada_b, norm1_g, norm2_g, attn_w_in, attn_q_norm_g, attn_k_norm_g, idx_k_ln_g, idx_k_ln_b,
                   attn_w_out, conv_w_in, conv_w, conv_w_out, ffn_w_gate, ffn_w_up, ffn_w_down):
    w = attn_w_in[0]
    qc = w[:, 0:1024].reshape(D, 16, 64)
    qperm = np.stack([qc[:, [j, 8 + j], :] for j in range(8)], axis=1).reshape(D, 1024)
    kc = w[:, 1024:1280].reshape(D, 4, 64)
    kperm = np.concatenate([kc[:, 0], kc[:, 2], kc[:, 1], kc[:, 3]], axis=1)
    vcol = w[:, 1280:1536]
    qic = w[:, 1536:2048]
    kic = w[:, 2048:2112]
    wic = w[:, 2112:2120]
    w_in = np.ascontiguousarray(np.concatenate([qperm, kperm, vcol, qic, kic, kic, wic], axis=1), dtype=np.float32)
    assert w_in.shape[1] == WCOLS
    vecs = np.concatenate([ada_b[0].reshape(48, 128), ada_b[1].reshape(48, 128), norm1_g.reshape(16, 128),
                           norm2_g.reshape(16, 128)], axis=0).astype(np.float32)
    hv = np.tile(np.concatenate([attn_q_norm_g[0], attn_k_norm_g[0], idx_k_ln_g[0], idx_k_ln_b[0]])[None, :], (128, 1)).astype(np.float32)
    cwT = np.ascontiguousarray(conv_w[0].T.reshape(8, 128, 3).transpose(1, 0, 2).reshape(128, 24)).astype(np.float32)
    invf = np.float32(10000.0) ** (-(np.arange(32, dtype=np.float32) * np.float32(2.0) / np.float32(64)))
    return {
        "w_in": w_in, "w_out": np.ascontiguousarray(attn_w_out[0]), "ada_w": np.ascontiguousarray(ada_w),
        "vecs": np.ascontiguousarray(vecs), "hv": np.ascontiguousarray(hv),
        "cw_in": np.ascontiguousarray(conv_w_in[0]), "cwT": cwT, "cw_out": np.ascontiguousarray(conv_w_out[0]),
        "wg": np.ascontiguousarray(ffn_w_gate), "wu": np.ascontiguousarray(ffn_w_up), "wd": np.ascontiguousarray(ffn_w_down),
        "ident": np.eye(128, dtype=np.float32), "invf": np.tile(invf.astype(np.float32)[None, :], (128, 1)),
        "iota": np.tile(np.arange(512, dtype=np.float32)[None, :], (128, 1)),
    }


def kernel(x, c, positions, ada_w, ada_b, norm1_g, norm2_g, attn_w_in, attn_q_norm_g, attn_k_norm_g, idx_k_ln_g,
           idx_k_ln_b, attn_w_out, conv_w_in, conv_w, conv_w_out, ffn_w_gate, ffn_w_up, ffn_w_down):
    args = [np.asarray(a) for a in (x, c, positions, ada_w, ada_b, norm1_g, norm2_g, attn_w_in, attn_q_norm_g,
                                     attn_k_norm_g, idx_k_ln_g, idx_k_ln_b, attn_w_out, conv_w_in, conv_w, conv_w_out,
                                     ffn_w_gate, ffn_w_up, ffn_w_down)]
    x = args[0]
    B, T, _ = x.shape
    cfg = Cfg(T)
    if (T, STOP) not in _CACHE:
        _CACHE[(T, STOP)] = build(cfg, STOP)[0]
    nc = _CACHE[(T, STOP)]
    shared = _shared_inputs(*args[3:])
    ncores = 2 * B
    in_maps = [_host_inputs(cfg, r, *args, shared) for r in range(ncores)]
    res = run_bass_kernel_spmd(nc, in_maps, core_ids=list(range(ncores)))
    global LAST
    LAST = res
    out = np.empty((B, T, D), dtype=np.float32)
    for r in range(ncores):
        b, role = r // 2, r % 2
        tiles = cfg.tilesA if role == 0 else cfg.tilesB
        halo = cfg.haloA if role == 0 else cfg.haloB
        o = np.asarray(res.results[r]["out"]).reshape(cfg.NS, 128, D)
        for s, t in enumerate(tiles):
            if s == halo:
                continue
            out[b, t * 128:(t + 1) * 128, :] = o[s]
    return out
```

```python
import math
import numpy as np
import ml_dtypes
import concourse.bass as bass
import concourse.mybir as mybir
from concourse.bass_utils import run_bass_kernel_spmd

F32 = mybir.dt.float32
BF16 = mybir.dt.bfloat16
I32 = mybir.dt.int32
U8 = mybir.dt.uint8
ALU = mybir.AluOpType
AF = mybir.ActivationFunctionType
AX = mybir.AxisListType

D = 1024
FF = 2816
NFC = FF // 128
WCOLS = 2184
EPS = 1e-6
NIT = 25
TWO_PI = 2.0 * math.pi
C1 = 6.28125
C2 = TWO_PI - C1


class Tok:
    __slots__ = ("w", "r", "rd")

    def __init__(self):
        self.w = None
        self.r = {}
        self.rd = []


class Op:
    __slots__ = ("eng", "fn", "deps", "dma", "sig", "cnt", "dsem")

    def __init__(self, eng, fn, dma):
        self.eng = eng
        self.fn = fn
        self.deps = set()
        self.dma = dma
        self.sig = False
        self.cnt = 0
        self.dsem = None


class Prog:
    def __init__(self, nc):
        self.nc = nc
        self.ops = []
        self.engs = {"pe": nc.tensor, "act": nc.scalar, "dve": nc.vector,
                     "pool": nc.gpsimd, "sp": nc.sync}
        self.last = {}
        self.dmas_since_bar = []
        self.bar = {}

    def add(self, eng, fn, R=(), W=(), dma=None):
        idx = len(self.ops)
        op = Op(eng, fn, dma)
        deps = op.deps
        if eng in self.bar:
            deps.update(self.bar.pop(eng))
        for t in R:
            if t.w is not None:
                deps.add(t.w)
        for t in W:
            if t.w is not None:
                deps.add(t.w)
            deps.update(t.r.values())
            deps.update(t.rd)
        deps.discard(idx)
        for t in W:
            t.w = idx
            t.r = {}
            t.rd = []
        for t in R:
            if t.w == idx:
                continue
            if dma is not None:
                t.rd.append(idx)
            else:
                t.r[eng] = idx
        self.ops.append(op)
        if dma is None:
            self.last[eng] = idx
        else:
            self.dmas_since_bar.append(idx)
        return idx

    def barrier(self):
        s = set(self.last.values()) | set(self.dmas_since_bar)
        self.dmas_since_bar = []
        for e in self.engs:
            self.bar[e] = set(s) | self.bar.get(e, set())

    def emit(self, final_wait_eng="sp"):
        nc = self.nc
        ops = self.ops
        for op in ops:
            nd = set()
            for d in op.deps:
                dop = ops[d]
                if dop.dma is None and op.dma is None and dop.eng == op.eng == "pe":
                    continue
                nd.add(d)
                dop.sig = True
            op.deps = nd
        esem = {e: nc.semaphore("se_" + e).__enter__() for e in self.engs}
        dsem, dcnt = {}, {}
        ecnt = {e: 0 for e in self.engs}
        for op in ops:
            if op.dma is not None:
                if op.dma not in dsem:
                    dsem[op.dma] = nc.semaphore("sd_%d" % len(dsem)).__enter__()
                    dcnt[op.dma] = 0
                dcnt[op.dma] += 16
                op.cnt = dcnt[op.dma]
                op.dsem = dsem[op.dma]
                op.sig = True
            elif op.sig:
                ecnt[op.eng] += 1
                op.cnt = ecnt[op.eng]
        waited = {e: {} for e in self.engs}
        for op in ops:
            E = self.engs[op.eng]
            wd = waited[op.eng]
            need = {}
            for d in op.deps:
                dop = ops[d]
                if dop.dma is not None:
                    key, sem = ("d", dop.dma), dop.dsem
                else:
                    key, sem = ("e", dop.eng), esem[dop.eng]
                if need.get(key, (None, 0))[1] < dop.cnt:
                    need[key] = (sem, dop.cnt)
            for key, (sem, cnt) in need.items():
                if wd.get(key, 0) < cnt:
                    E.wait_ge(sem, cnt)
                    wd[key] = cnt
            ins = op.fn()
            if op.sig:
                ins.then_inc(op.dsem if op.dma is not None else esem[op.eng], 16 if op.dma is not None else 1)
        E = self.engs[final_wait_eng]
        for k, sem in dsem.items():
            E.wait_ge(sem, dcnt[k])
        for e in self.engs:
            if ecnt[e] > 0 and e != final_wait_eng:
                E.wait_ge(esem[e], ecnt[e])
        self.stats = (len(ops), ecnt, len(dsem))


def _dsize(dt):
    if dt in (F32, I32):
        return 4
    if dt == BF16:
        return 2
    return 1


class Arena:
    def __init__(self, nc, nbytes):
        self.t = nc.sbuf_tensor("arena", [128, nbytes // 4], F32).__enter__()
        self.cap = nbytes
        self.off = 0
        self.peak = 0

    def mark(self):
        return self.off

    def release(self, m):
        self.off = m

    def alloc(self, free, dt=F32, parts=128):
        if isinstance(free, int):
            free = [free]
        n = 1
        for f in free:
            n *= f
        sz = (n * _dsize(dt) + 63) // 64 * 64
        assert self.off + sz <= self.cap, "SBUF arena overflow: need %d have %d" % (self.off + sz, self.cap)
        ap = self.t[0:parts, self.off // 4:(self.off + sz) // 4]
        self.off += sz
        self.peak = max(self.peak, self.off)
        if dt != F32:
            ap = ap.bitcast(dt)
        ap = ap[:, 0:n]
        if len(free) == 2:
            ap = ap.rearrange("p (a b) -> p a b", a=free[0])
        elif len(free) == 3:
            ap = ap.rearrange("p (a b c) -> p a b c", a=free[0], b=free[1])
        return ap


class Cfg:
    def __init__(self, T):
        self.T = T
        self.NT = T // 128
        assert self.NT % 4 == 0
        CH = self.NT // 4
        self.CH = CH
        self.NS = 2 * CH + 1
        self.tilesA = list(range(CH)) + [3 * CH - 1] + list(range(3 * CH, 4 * CH))
        self.tilesB = [CH - 1] + list(range(CH, 3 * CH))
        self.ext = [max(a, b) + 1 for a, b in zip(self.tilesA, self.tilesB)]
        self.dchunk = [min(a, b) // 4 for a, b in zip(self.tilesA, self.tilesB)]
        self.haloA = CH
        self.haloB = 0
        self.topk = min(256, T // 4)


def build(cfg, stop=None):
    T, NT, NS = cfg.T, cfg.NT, cfg.NS
    nc = bass.Bass("TRN2", target_bir_lowering=False)
    P = Prog(nc)

    def din(name, shape, dt=F32):
        return nc.dram_tensor(name, list(shape), dt, kind="ExternalInput").ap()

    x_seq = din("x_seq", [T, D])
    x_own = din("x_own", [NS * 128, D])
    pos_seq = din("pos_seq", [128, NT], I32)
    pos_own = din("pos_own", [128, NS], I32)
    qpos_d = din("qpos", [128, NS])
    cT_d = din("cT", [128, 8])
    w_in = din("w_in", [D, WCOLS])
    w_out = din("w_out", [D, D])
    ada_w = din("ada_w", [2, D, 6 * D])
    vecs_d = din("vecs", [128, 128])
    hv_d = din("hv", [128, 256])
    cw_in = din("cw_in", [D, 3 * D])
    cwT_d = din("cwT", [128, 24])
    cw_out = din("cw_out", [D, D])
    wg_d = din("wg", [2, D, FF])
    wu_d = din("wu", [2, D, FF])
    wd_d = din("wd", [2, FF, D])
    ident_d = din("ident", [128, 128])
    invf_d = din("invf", [128, 32])
    iota_d = din("iota", [128, 512])
    out_d = nc.dram_tensor("out", [NS * 128, D], F32, kind="ExternalOutput").ap()
    dbg_d = nc.dram_tensor("dbg", [128, 8192], F32, kind="ExternalOutput").ap() if stop else None
    qTs = nc.dram_tensor("qTs", [NS, 128, 1024], BF16, kind="Internal").ap()
    qiTs = nc.dram_tensor("qiTs", [NS, 128, 512], BF16, kind="Internal").ap()
    x1s = nc.dram_tensor("x1s", [NS * 128, D], F32, kind="Internal").ap()
    t_qTs = [Tok() for _ in range(NS)]
    t_qiTs = [Tok() for _ in range(NS)]
    t_x1s = [Tok() for _ in range(NS)]
    t_out = [Tok() for _ in range(NS)]

    AR = Arena(nc, 206 * 1024)
    pbank = [nc.psum_tensor("pb%d" % k, [128, 512], F32).__enter__() for k in range(8)]
    tpb = [Tok() for _ in range(8)]

    def pbf(k):
        return pbank[k][:].bitcast(BF16)

    V, S, G, PE = nc.vector, nc.scalar, nc.gpsimd, nc.tensor

    def A(eng, fn, R=(), W=()):
        P.add(eng, fn, R, W)

    dma_ctr = [0]

    def DMA(q, out, in_, R=(), W=(), key=None, **kw):
        if key is None:
            dma_ctr[0] += 1
            key = "k%d" % dma_ctr[0]
        e = {"sp": nc.sync, "pool": nc.gpsimd, "act": nc.scalar}[q]
        P.add(q, lambda: e.dma_start(out=out, in_=in_, **kw), R, W, dma=key)

    dbg_off = [0]

    def dump(ap2d, toks, n, bf=False):
        if stop.endswith("x"):
            return
        o = dbg_off[0]
        if bf:
            tmp = AR.alloc(n)
            tt = Tok()
            A("dve", lambda: V.tensor_copy(out=tmp, in_=ap2d), toks, [tt])
            DMA("sp", dbg_d[:, o:o + n], tmp, R=[tt], W=[Tok()])
        else:
            DMA("sp", dbg_d[:, o:o + n], ap2d, R=toks, W=[Tok()])
        dbg_off[0] += n

    identF = AR.alloc(128); t_identF = Tok()
    identB = AR.alloc(128, BF16); t_identB = Tok()
    onesF = AR.alloc(128); t_onesF = Tok()
    iota = AR.alloc(512); t_iota = Tok()
    vecT = AR.alloc(128); t_vecT = Tok()
    modT = AR.alloc(96); t_modT = Tok()
    gscT = AR.alloc(32); t_gscT = Tok()
    wsign = AR.alloc([NS, 8]); t_wsign = Tok()
    cwT = AR.alloc(24); t_cwT = Tok()
    qpos = AR.alloc(NS); t_qpos = Tok()
    hv = AR.alloc(256); t_hv = Tok()
    G0 = AR.alloc(1024); t_G0 = Tok()

    DMA("sp", identF, ident_d, W=[t_identF])
    DMA("sp", iota, iota_d, W=[t_iota])
    DMA("sp", cwT, cwT_d, W=[t_cwT])
    DMA("sp", qpos, qpos_d, W=[t_qpos])
    DMA("sp", hv, hv_d, W=[t_hv])
    A("dve", lambda: V.tensor_copy(out=identB, in_=identF), [t_identF], [t_identB])
    A("dve", lambda: V.memset(onesF, 1.0), [], [t_onesF])

    if stop == "pre":
        dump(identF, [t_identF], 128)
        dump(identB, [t_identB], 128, bf=True)
        dump(hv, [t_hv], 256)
        P.emit()
        return nc, P, AR
    m0 = AR.mark()
    vecs_sb = AR.alloc(128); t_vecs = Tok()
    cT_sb = AR.alloc(8); t_cT = Tok()
    cact2 = AR.alloc([8, 2]); t_cact = Tok()
    adaw = [AR.alloc([8, 512]) for _ in range(2)]
    t_adaw = [Tok(), Tok()]
    DMA("sp", vecs_sb, vecs_d, W=[t_vecs])
    DMA("sp", cT_sb, cT_d, W=[t_cT])
    A("pe", lambda: PE.transpose(out=pbank[0][:, 0:128], in_=vecs_sb, identity=identF), [t_vecs, t_identF], [tpb[0]])
    A("act", lambda: S.copy(out=vecT, in_=pbank[0][:, 0:128]), [tpb[0]], [t_vecT])
    A("act", lambda: S.activation(out=cact2[:, :, 0], in_=cT_sb, func=AF.Silu), [t_cT], [t_cact])
    A("act", lambda: S.activation(out=cact2[:, :, 1], in_=cT_sb, func=AF.Silu), [t_cT], [t_cact])
    n = 0
    for L in range(2):
        src = ada_w[L].rearrange("(k p) n -> p k n", p=128)
        for cg in range(12):
            b = n % 2
            n += 1
            DMA("sp", adaw[b], src[:, :, cg * 512:(cg + 1) * 512], W=[t_adaw[b]], key="adaw%d" % b)
            for m4 in range(4):
                m = L * 48 + cg * 4 + m4
                for k in range(8):
                    A("pe", (lambda b=b, m=m, m4=m4, k=k: PE.matmul(
                        pbank[1][:, 2 * m:2 * m + 2], lhsT=adaw[b][:, k, m4 * 128:(m4 + 1) * 128],
                        rhs=cact2[:, k, :], start=(k == 0), stop=(k == 7))),
                      [t_adaw[b], t_cact], [tpb[1]])
    if stop == "p0a":
        A("act", lambda: S.copy(out=modT, in_=pbank[1][:, 0:96]), [tpb[1]], [t_modT])
        dump(modT, [t_modT], 96)
        dump(vecT, [t_vecT], 128)
        dump(cact2.rearrange("p a b -> p (a b)"), [t_cact], 16)
        P.emit()
        return nc, P, AR
    A("dve", lambda: V.tensor_tensor(out=modT, in0=pbank[1][:, 0:192].rearrange("p (m t) -> p m t", t=2)[:, :, 0],
                                     in1=vecT[:, 0:96], op=ALU.add), [tpb[1], t_vecT], [t_modT])
    for L in range(2):
        for s in range(2):
            o = (2 * L + s) * 8
            sc = modT[:, L * 48 + (8 if s == 0 else 32): L * 48 + (16 if s == 0 else 40)]
            ng = vecT[:, 96 + s * 16 + L * 8: 96 + s * 16 + L * 8 + 8]
            A("dve", (lambda o=o, sc=sc, ng=ng: V.scalar_tensor_tensor(
                out=gscT[:, o:o + 8], in0=sc, scalar=1.0, in1=ng, op0=ALU.add, op1=ALU.mult)),
              [t_modT, t_vecT], [t_gscT])

    def mod_cols(L, which):
        return modT[:, L * 48 + which * 8: L * 48 + which * 8 + 8]

    def make_gate(gcols, dst, t_dst, dg_bufs, t_dg, banks):
        for j in range(8):
            b = j % 2
            A("dve", (lambda j=j, b=b: V.tensor_scalar(out=dg_bufs[b], in0=identF, scalar1=gcols[:, j:j + 1],
                                                      scalar2=None, op0=ALU.mult)),
              [t_identF, t_modT], [t_dg[b]])
            bk = banks[j // 4]
            A("pe", (lambda j=j, b=b, bk=bk: PE.matmul(pbank[bk][:, (j % 4) * 128:(j % 4 + 1) * 128], lhsT=onesF,
                                                      rhs=dg_bufs[b], start=True, stop=True)),
              [t_onesF, t_dg[b]], [tpb[bk]])
        for h in range(2):
            A("act", (lambda h=h: S.copy(out=dst[:, h * 512:(h + 1) * 512], in_=pbank[banks[h]][:])),
              [tpb[banks[h]]], [t_dst])

    if stop == "p0b":
        dump(modT, [t_modT], 96)
        dump(gscT, [t_gscT], 32)
        P.emit()
        return nc, P, AR
    dgb = [AR.alloc(128), AR.alloc(128)]
    t_dgb = [Tok(), Tok()]
    make_gate(mod_cols(0, 2), G0, t_G0, dgb, t_dgb, [2, 3])
    if stop == "p0c":
        dump(G0, [t_G0], 1024)
        P.emit()
        return nc, P, AR
    P.barrier()
    if stop == "p0":
        dump(modT, [t_modT], 96)
        dump(gscT, [t_gscT], 32)
        dump(G0, [t_G0], 1024)
        dump(vecT, [t_vecT], 128)
        P.emit()
        return nc, P, AR
    AR.release(m0)

    def norm_T(xt, t_xt, gs, shc, hT_dst, t_hT, scr, bank):
        junkb, t_junkb, ss, t_ss, xn, t_xn = scr
        A("act", lambda: S.activation(out=junkb, in_=xt, func=AF.Square, accum_out=ss[:, 0:1]), [t_xt], [t_junkb, t_ss])
        A("act", lambda: S.activation(out=ss[:, 1:2], in_=ss[:, 0:1], func=AF.Sqrt, scale=1.0 / D, bias=eps_t[:, 0:1]),
          [t_ss, t_eps], [t_ss])
        A("dve", lambda: V.reciprocal(out=ss[:, 2:3], in_=ss[:, 1:2]), [t_ss], [t_ss])
        A("act", lambda: S.activation(out=xn, in_=xt, func=AF.Identity, scale=ss[:, 2:3]), [t_xt, t_ss], [t_xn])
        pv = pbf(bank)
        for j in range(8):
            A("pe", (lambda j=j: PE.transpose(out=pv[:, j * 128:(j + 1) * 128], in_=xn[:, j * 128:(j + 1) * 128],
                                              identity=identB)), [t_xn, t_identB], [tpb[bank]])
        for j in range(8):
            A("act", (lambda j=j: S.activation(out=hT_dst[:, j, :], in_=pv[:, j * 128:(j + 1) * 128], func=AF.Identity,
                                               scale=gs[:, j:j + 1], bias=shc[:, j:j + 1])),
              [tpb[bank], t_gscT, t_modT], [t_hT])

    eps_t = AR.alloc(4); t_eps = Tok()
    A("dve", lambda: V.memset(eps_t, EPS), [], [t_eps])

    mA = AR.mark()
    kT_all = AR.alloc([2, T], BF16); t_kT = [Tok() for _ in range(NT)]
    Vaug = AR.alloc([NT, 4, 66], BF16); t_V = [Tok() for _ in range(NT)]
    kiT = AR.alloc(T, BF16); t_kiT = [Tok() for _ in range(NT)]
    t_Vones = Tok()
    A("pool", lambda: G.memset(Vaug[:, :, :, 64:65], 1.0), [], [t_Vones] + t_V)

    mA1 = AR.mark()
    Win = AR.alloc([8, WCOLS], BF16); t_Win = [Tok() for _ in range(8)]
    wsrc = w_in.rearrange("(k p) n -> p k n", p=128)
    for k in range(8):
        DMA("pool", Win[:, k, :], wsrc[:, k, :], W=[t_Win[k]], key="win%d" % k, max_dma_last_dim=4096)

    def rope_tables(pos_d, n, cos_t, sin_t, t_tab):
        m = AR.mark()
        pi_ = AR.alloc(n, I32); t_pi = Tok()
        pf = AR.alloc(n); ang = AR.alloc([n, 32]); u = AR.alloc([n, 32]); ki_ = AR.alloc([n, 32], I32)
        kf = AR.alloc([n, 32]); r = AR.alloc([n, 32]); r2 = AR.alloc([n, 32]); tmp = AR.alloc([n, 32])
        invt = AR.alloc(32)
        tk = Tok()
        DMA("sp", pi_, pos_d, W=[t_pi])
        DMA("sp", invt, invf_d, W=[tk])
        A("dve", lambda: V.tensor_copy(out=pf, in_=pi_), [t_pi], [tk])
        A("dve", lambda: V.tensor_tensor(out=ang, in0=pf.unsqueeze(2).to_broadcast([128, n, 32]),
                                         in1=invt.unsqueeze(1).to_broadcast([128, n, 32]), op=ALU.mult), [tk], [tk])
        A("dve", lambda: V.tensor_scalar(out=u, in0=ang, scalar1=1.0 / TWO_PI, scalar2=None, op0=ALU.mult), [tk], [tk])
        A("dve", lambda: V.tensor_copy(out=ki_, in_=u), [tk], [tk])
        A("dve", lambda: V.tensor_copy(out=kf, in_=ki_), [tk], [tk])
        A("dve", lambda: V.scalar_tensor_tensor(out=r, in0=kf, scalar=-C1, in1=ang, op0=ALU.mult, op1=ALU.add), [tk], [tk])
        A("dve", lambda: V.scalar_tensor_tensor(out=r, in0=kf, scalar=-C2, in1=r, op0=ALU.mult, op1=ALU.add), [tk], [tk])
        A("dve", lambda: V.tensor_scalar(out=r, in0=r, scalar1=-3.1415925, scalar2=3.1415925, op0=ALU.max, op1=ALU.min), [tk], [tk])
        A("dve", lambda: V.tensor_scalar(out=r2, in0=r, scalar1=math.pi / 2, scalar2=None, op0=ALU.add), [tk], [tk])
        A("dve", lambda: V.tensor_scalar(out=tmp, in0=r2, scalar1=math.pi, scalar2=-TWO_PI, op0=ALU.is_gt, op1=ALU.mult), [tk], [tk])
        A("dve", lambda: V.tensor_tensor(out=r2, in0=r2, in1=tmp, op=ALU.add), [tk], [tk])
        A("dve", lambda: V.tensor_scalar(out=r2, in0=r2, scalar1=-3.1415925, scalar2=3.1415925, op0=ALU.max, op1=ALU.min), [tk], [tk])
        A("act", lambda: S.activation(out=sin_t, in_=r, func=AF.Sin), [tk], [t_tab])
        A("act", lambda: S.activation(out=cos_t, in_=r2, func=AF.Sin), [tk], [t_tab])
        return m

    cos_s = AR.alloc([NT, 32]); sin_s = AR.alloc([NT, 32]); t_tabs = Tok()
    cos_o = AR.alloc([NS, 32]); sin_o = AR.alloc([NS, 32]); t_tabo = Tok()
    hn = NT // 2
    for hf in range(2):
        mm_ = rope_tables(pos_seq[:, hf * hn:(hf + 1) * hn], hn, cos_s[:, hf * hn:(hf + 1) * hn, :], sin_s[:, hf * hn:(hf + 1) * hn, :], t_tabs)
        P.barrier()
        AR.release(mm_)
    mm_ = rope_tables(pos_own, NS, cos_o, sin_o, t_tabo)
    P.barrier()
    AR.release(mm_)

    xt = [AR.alloc(1024), AR.alloc(1024)]; t_xt = [Tok(), Tok()]
    hT = [AR.alloc([8, 128], BF16), AR.alloc([8, 128], BF16)]; t_hT = [Tok(), Tok()]
    scr = (AR.alloc(1024, BF16), Tok(), AR.alloc(4), Tok(), AR.alloc(1024, BF16), Tok())
    sq = AR.alloc(1024); t_sq = Tok()
    qn = AR.alloc(1024); t_qn = Tok()
    r1 = AR.alloc(1024); t_r1 = Tok()
    r2_ = AR.alloc(1024); t_r2 = Tok()
    qb = AR.alloc(1024, BF16); t_qb = Tok()
    sm = AR.alloc(64); t_sm = Tok()
    qTt = [AR.alloc(1024, BF16), AR.alloc(1024, BF16)]; t_qTt = [Tok(), Tok()]
    qiTt = [AR.alloc(512, BF16), AR.alloc(512, BF16)]; t_qiTt = [Tok(), Tok()]
    kib = AR.alloc(128, BF16); t_kib = Tok()
    qr = AR.alloc(512); t_qr = Tok()
    wab = AR.alloc(8); t_wab = Tok()

    def headnorm_rope(src_ps, t_src, H, gcol, cosv, sinv, t_tab, outb, t_outb):
        W_ = H * 64
        s3 = src_ps.rearrange("p (h d) -> p h d", h=H)
        A("act", lambda: S.activation(out=sq[:, 0:W_], in_=src_ps, func=AF.Square), [t_src], [t_sq])
        A("dve", lambda: V.tensor_reduce(out=sm[:, 0:H], in_=sq[:, 0:W_].rearrange("p (h d) -> p h d", h=H), axis=AX.X, op=ALU.add),
          [t_sq], [t_sm])
        A("act", lambda: S.activation(out=sm[:, 16:16 + H], in_=sm[:, 0:H], func=AF.Sqrt, scale=1.0 / 64, bias=eps_t[:, 0:1]),
          [t_sm, t_eps], [t_sm])
        A("dve", lambda: V.reciprocal(out=sm[:, 32:32 + H], in_=sm[:, 16:16 + H]), [t_sm], [t_sm])
        q3 = qn[:, 0:W_].rearrange("p (h d) -> p h d", h=H)
        A("dve", lambda: V.tensor_tensor(out=q3, in0=s3, in1=sm[:, 32:32 + H].unsqueeze(2).to_broadcast([128, H, 64]), op=ALU.mult),
          [t_src, t_sm], [t_qn])
        A("pool", lambda: G.tensor_tensor(out=q3, in0=q3, in1=hv[:, gcol:gcol + 64].unsqueeze(1).to_broadcast([128, H, 64]), op=ALU.mult),
          [t_qn, t_hv], [t_qn])
        rope(q3, t_qn, H, cosv, sinv, t_tab, outb, t_outb)

    def rope(q3, t_q3, H, cosv, sinv, t_tab, outb, t_outb):
        W_ = H * 64
        a3 = r1[:, 0:W_].rearrange("p (h d) -> p h d", h=H)
        b3 = r2_[:, 0:W_].rearrange("p (h d) -> p h d", h=H)
        o3 = outb[:, 0:W_].rearrange("p (h d) -> p h d", h=H)
        cb = cosv.unsqueeze(1).to_broadcast([128, H, 32])
        sb_ = sinv.unsqueeze(1).to_broadcast([128, H, 32])
        A("pool", lambda: G.tensor_tensor(out=a3[:, :, 0:32], in0=q3[:, :, 0:32], in1=cb, op=ALU.mult), [t_q3, t_tab], [t_r1])
        A("pool", lambda: G.tensor_tensor(out=a3[:, :, 32:64], in0=q3[:, :, 32:64], in1=cb, op=ALU.mult), [t_q3, t_tab], [t_r1])
        A("dve", lambda: V.tensor_tensor(out=b3[:, :, 0:32], in0=q3[:, :, 32:64], in1=sb_, op=ALU.mult), [t_q3, t_tab], [t_r2])
        A("dve", lambda: V.tensor_tensor(out=b3[:, :, 32:64], in0=q3[:, :, 0:32], in1=sb_, op=ALU.mult), [t_q3, t_tab], [t_r2])
        A("pool", lambda: G.tensor_tensor(out=o3[:, :, 0:32], in0=a3[:, :, 0:32], in1=b3[:, :, 0:32], op=ALU.subtract), [t_r1, t_r2], [t_outb])
        A("pool", lambda: G.tensor_tensor(out=o3[:, :, 32:64], in0=a3[:, :, 32:64], in1=b3[:, :, 32:64], op=ALU.add), [t_r1, t_r2], [t_outb])

    def proj(dst_bank, ncols, c0, hTb, t_hTb):
        for k in range(8):
            A("pe", (lambda k=k: PE.matmul(pbank[dst_bank][:, 0:ncols], lhsT=hTb[:, k, :], rhs=Win[:, k, c0:c0 + ncols],
                                           start=(k == 0), stop=(k == 7))), [t_hTb, t_Win[k]], [tpb[dst_bank]])

    gs1_0 = gscT[:, 0:8]
    sh1_0 = mod_cols(0, 0)
    for j in range(NT):
        b = j % 2
        DMA("sp", xt[b], x_seq[j * 128:(j + 1) * 128, :], W=[t_xt[b]], key="xt%d" % b)
        norm_T(xt[b], t_xt[b], gs1_0, sh1_0, hT[b], t_hT[b], scr, 0)
        proj(1, 512, 1024, hT[b], t_hT[b])
        proj(2, 128, 2048, hT[b], t_hT[b])
        cj, sj = cos_s[:, j, :], sin_s[:, j, :]
        headnorm_rope(pbank[1][:, 0:256], tpb[1], 4, 64, cj, sj, t_tabs, qb, t_qb)
        A("act", (lambda j=j: S.copy(out=Vaug[:, j, :, 0:64], in_=pbank[1][:, 256:512].rearrange("p (g d) -> p g d", g=4))),
          [tpb[1]], [t_V[j]])
        pv3 = pbf(3)
        for i in range(2):
            A("pe", (lambda i=i: PE.transpose(out=pv3[:, i * 128:(i + 1) * 128], in_=qb[:, i * 128:(i + 1) * 128], identity=identB)),
              [t_qb, t_identB], [tpb[3]])
        A("act", (lambda j=j: S.copy(out=kT_all[:, :, j * 128:(j + 1) * 128], in_=pv3[:, 0:256].rearrange("p (i t) -> p i t", i=2))),
          [tpb[3]], [t_kT[j]])
        A("dve", lambda: V.bn_stats(out=sm[:, 48:54], in_=pbank[2][:, 0:64]), [tpb[2]], [t_sm])
        A("dve", lambda: V.bn_aggr(out=sm[:, 54:56], in_=sm[:, 48:54]), [t_sm], [t_sm])
        A("act", lambda: S.activation(out=sm[:, 56:57], in_=sm[:, 55:56], func=AF.Sqrt, scale=1.0, bias=eps_t[:, 0:1]), [t_sm, t_eps], [t_sm])
        A("dve", lambda: V.reciprocal(out=sm[:, 57:58], in_=sm[:, 56:57]), [t_sm], [t_sm])
        A("dve", lambda: V.tensor_scalar(out=qn[:, 0:64], in0=pbank[2][:, 0:64], scalar1=sm[:, 54:55], scalar2=sm[:, 57:58],
                                         op0=ALU.subtract, op1=ALU.mult), [tpb[2], t_sm], [t_qn])
        A("pool", lambda: G.tensor_tensor(out=qn[:, 0:64], in0=qn[:, 0:64], in1=hv[:, 128:192], op=ALU.mult), [t_qn, t_hv], [t_qn])
        A("pool", lambda: G.tensor_tensor(out=qn[:, 0:64], in0=qn[:, 0:64], in1=hv[:, 192:256], op=ALU.add), [t_qn, t_hv], [t_qn])
        rope(qn[:, 0:64].rearrange("p (h d) -> p h d", h=1), t_qn, 1, cj, sj, t_tabs, kib, t_kib)
        A("pool", lambda: G.tensor_copy(out=kib[:, 64:128], in_=kib[:, 0:64]), [t_kib], [t_kib])
        A("pe", lambda: PE.transpose(out=pv3[:, 256:384], in_=kib, identity=identB), [t_kib, t_identB], [tpb[3]])
        A("act", (lambda j=j: S.copy(out=kiT[:, j * 128:(j + 1) * 128], in_=pv3[:, 256:384])), [tpb[3]], [t_kiT[j]])

    WSC = (8 ** -0.5) * (64 ** -0.5)
    for i in range(NS):
        b = i % 2
        DMA("sp", xt[b], x_own[i * 128:(i + 1) * 128, :], W=[t_xt[b]], key="xt%d" % b)
        norm_T(xt[b], t_xt[b], gs1_0, sh1_0, hT[b], t_hT[b], scr, 0)
        proj(1, 512, 0, hT[b], t_hT[b])
        proj(2, 512, 512, hT[b], t_hT[b])
        proj(4, 512, 1536, hT[b], t_hT[b])
        proj(5, 8, 2176, hT[b], t_hT[b])
        ci, si = cos_o[:, i, :], sin_o[:, i, :]
        for hh in range(2):
            headnorm_rope(pbank[1 + hh][:], tpb[1 + hh], 8, 0, ci, si, t_tabo, qb, t_qb)
            pv3 = pbf(3)
            for jj in range(4):
                A("pe", (lambda jj=jj, hh=hh: PE.transpose(out=pv3[:, (hh * 4 + jj) * 128:(hh * 4 + jj + 1) * 128],
                                                          in_=qb[:, jj * 128:(jj + 1) * 128], identity=identB)),
                  [t_qb, t_identB], [tpb[3]])
        A("act", (lambda b=b: S.copy(out=qTt[b], in_=pbf(3))), [tpb[3]], [t_qTt[b]])
        DMA("sp", qTs[i], qTt[b], R=[t_qTt[b]], W=[t_qTs[i]], key="qTs")
        A("act", (lambda i=i: S.activation(out=wsign[:, i, :], in_=pbank[5][:, 0:8], func=AF.Sign)), [tpb[5]], [t_wsign])
        A("act", lambda: S.activation(out=wab, in_=pbank[5][:, 0:8], func=AF.Abs, scale=WSC), [tpb[5]], [t_wab])
        A("act", lambda: S.copy(out=qn[:, 0:512], in_=pbank[4][:]), [tpb[4]], [t_qn])
        rope(qn[:, 0:512].rearrange("p (h d) -> p h d", h=8), t_qn, 8, ci, si, t_tabo, qr, t_qr)
        A("dve", lambda: V.tensor_tensor(out=qb[:, 0:512].rearrange("p (h d) -> p h d", h=8),
                                         in0=qr[:, 0:512].rearrange("p (h d) -> p h d", h=8),
                                         in1=wab.unsqueeze(2).to_broadcast([128, 8, 64]), op=ALU.mult),
          [t_qr, t_wab], [t_qb])
        pv6 = pbf(6)
        for jj in range(4):
            A("pe", (lambda jj=jj: PE.transpose(out=pv6[:, jj * 128:(jj + 1) * 128], in_=qb[:, jj * 128:(jj + 1) * 128], identity=identB)),
              [t_qb, t_identB], [tpb[6]])
        A("act", (lambda b=b: S.copy(out=qiTt[b], in_=pv6[:, 0:512])), [tpb[6]], [t_qiTt[b]])
        DMA("sp", qiTs[i], qiTt[b], R=[t_qiTt[b]], W=[t_qiTs[i]], key="qiTs")
    P.barrier()
    if stop in ("a1", "a1x"):
        dump(kT_all[:, 0, 0:512], t_kT, 512, bf=True)
        dump(kT_all[:, 1, 0:512], t_kT, 512, bf=True)
        dump(kiT[:, 0:512], t_kiT, 512, bf=True)
        dump(Vaug[:, 0:4, :, :].rearrange("p a g d -> p (a g d)"), t_V, 1056, bf=True)
        dump(qTt[(NS - 1) % 2], t_qTt, 1024, bf=True)
        dump(qiTt[(NS - 1) % 2], t_qiTt, 512, bf=True)
        dump(wsign.rearrange("p a h -> p (a h)"), [t_wsign], NS * 8)
        dump(cos_s.rearrange("p a h -> p (a h)"), [t_tabs], NT * 32)
        dump(sin_s.rearrange("p a h -> p (a h)"), [t_tabs], NT * 32)
        P.emit()
        return nc, P, AR
    AR.release(mA1)

    Wo = AR.alloc([8, 1024], BF16); t_Wo = Tok()
    for hh in range(2):
        DMA("pool", Wo[hh * 64:(hh + 1) * 64, :, :], w_out[hh * 512:(hh + 1) * 512, :].rearrange("(j d) c -> d j c", d=64),
            W=[t_Wo], key="wo", max_dma_last_dim=4096)
    I_ = AR.alloc(T); t_I = Tok()
    mask01 = AR.alloc(T, BF16); t_mask = Tok()
    junk8 = AR.alloc(T, U8); t_junk8 = Tok()
    qTb = [[AR.alloc(1024, BF16), AR.alloc(1024, BF16)] for _ in range(2)]; t_qTb = [Tok(), Tok()]
    qiTb = [[AR.alloc(512, BF16), AR.alloc(512, BF16)] for _ in range(2)]; t_qiTb = [Tok(), Tok()]
    for b_ in range(2):
        for hf_ in range(2):
            A("pool", (lambda b_=b_, hf_=hf_: G.memset(qTb[b_][hf_], 0.0)), [], [t_qTb[b_]])
            A("pool", (lambda b_=b_, hf_=hf_: G.memset(qiTb[b_][hf_], 0.0)), [], [t_qiTb[b_]])
    NR = 4
    Rb = [AR.alloc(512, BF16) for _ in range(NR)]; t_Rb = [Tok() for _ in range(NR)]
    Dh = AR.alloc([8, 128], BF16); t_Dh = Tok()
    _bb = AR.alloc(512, BF16); _tb = Tok()
    biasb = [_bb, _bb]; t_biasb = [_tb, _tb]
    NPB = 6
    Pexp = [AR.alloc(512, BF16) for _ in range(NPB)]; t_Pexp = [Tok() for _ in range(NPB)]
    NPM = 6
    Pm = [AR.alloc(512, BF16) for _ in range(NPM)]; t_Pm = [Tok() for _ in range(NPM)]
    maskT = [AR.alloc(128, BF16) for _ in range(4)]; t_maskT = [Tok() for _ in range(4)]
    ys = AR.alloc(512); t_ys = Tok()
    numS = ys; t_numS = t_ys
    oT_all = AR.alloc([2, 512], BF16); t_oTlo = Tok(); t_oThi = Tok()
    oT_tmp = AR.alloc([2, 512], BF16); t_oTtmp = Tok()
    xa = AR.alloc(512); t_xa = Tok()
    ta = AR.alloc(512); t_ta = Tok()
    if os.environ.get('NOALIAS'):
        rs = AR.alloc(512); t_rs = Tok(); bcS = AR.alloc(512); t_bcS = Tok()
    else:
        rs = ta; t_rs = t_ta
        bcS = xa; t_bcS = t_xa
    bs = AR.alloc(16); t_bs = Tok()
    tr_banks = [0, 1, 2]
    trc = [0]

    def tbank():
        k = tr_banks[trc[0] % 3]
        trc[0] += 1
        return k

    rbc = [0]
    SCALE = 64 ** -0.5

    def indexer(i):
        b = i % 2
        E = cfg.ext[i]
        nch = (E + 3) // 4
        for hf_ in range(2):
            DMA("sp", qiTb[b][hf_][hf_ * 64:(hf_ + 1) * 64, :], qiTs[i][hf_ * 64:(hf_ + 1) * 64, :], R=[t_qiTs[i]], W=[t_qiTb[b]],
                key="qiTb%d" % b)
        for hf_ in range(2):
            DMA("sp", qTb[b][hf_][hf_ * 64:(hf_ + 1) * 64, :], qTs[i][hf_ * 64:(hf_ + 1) * 64, :], R=[t_qTs[i]], W=[t_qTb[b]],
                key="qTb%d" % b)
        for h in range(8):
            A("pool", (lambda h=h, i=i: G.tensor_scalar(out=Dh[:, h, :], in0=identB, scalar1=wsign[:, i, h:h + 1], scalar2=None, op0=ALU.mult)),
              [t_identB, t_wsign], [t_Dh])
        for c in range(nch):
            kts = [t_kiT[jj] for jj in range(c * 4, c * 4 + 4)]
            need_bias = c >= cfg.dchunk[i]
            if need_bias:
                bb = c % 2
                A("dve", (lambda c=c, i=i: V.tensor_scalar(out=bs[:, 6:7], in0=qpos[:, i:i + 1], scalar1=float(-512 * c), scalar2=None, op0=ALU.add)),
                  [t_qpos], [t_bs])
                A("dve", (lambda bb=bb: V.tensor_scalar(out=biasb[bb], in0=iota, scalar1=bs[:, 6:7], scalar2=-1e30, op0=ALU.is_gt, op1=ALU.mult)),
                  [t_iota, t_bs], [t_biasb[bb]])
            banks = []
            LA = 2

            def acc(hh, nb=need_bias):
                A("pe", (lambda hh=hh, rbb=banks[hh], nb=nb: PE.matmul(pbank[3][:], lhsT=Dh[:, hh, :], rhs=Rb[rbb], start=(hh == 0),
                                                                     stop=(hh == 7 and not nb))),
                  [t_Dh, t_Rb[banks[hh]]], [tpb[3]])

            for h in range(8):
                bk = tbank()
                hp, pr = h % 2, h // 2
                A("pe", (lambda bk=bk, hp=hp, pr=pr, c=c, b=b: PE.matmul(
                    pbank[bk][:], lhsT=qiTb[b][hp][:, pr * 128:(pr + 1) * 128],
                    rhs=kiT[:, c * 512:(c + 1) * 512], start=True, stop=True)),
                  [t_qiTb[b]] + kts, [tpb[bk]])
                rb = rbc[0] % NR
                rbc[0] += 1
                if h % 2 == 0:
                    A("act", (lambda bk=bk, rb=rb: S.activation(out=Rb[rb], in_=pbank[bk][:], func=AF.Relu)), [tpb[bk]], [t_Rb[rb]])
                else:
                    A("dve", (lambda bk=bk, rb=rb: V.tensor_scalar(out=Rb[rb], in0=pbank[bk][:], scalar1=0.0, scalar2=None, op0=ALU.max)),
                      [tpb[bk]], [t_Rb[rb]])
                banks.append(rb)
                if h >= LA:
                    acc(h - LA)
            for hh in range(8 - LA, 8):
                acc(hh)
            if need_bias:
                A("pe", (lambda bb=bb: PE.matmul(pbank[3][:], lhsT=identB, rhs=biasb[bb], start=False, stop=True)),
                  [t_identB, t_biasb[bb]], [tpb[3]])
            A("act", (lambda c=c: S.copy(out=I_[:, c * 512:(c + 1) * 512], in_=pbank[3][:])), [tpb[3]], [t_I])

    def topk_parts(i):
        E = cfg.ext[i]
        Sn = ((E + 3) // 4) * 512
        K0 = min(256, Sn)

        def pro():
            A("dve", lambda: V.tensor_reduce(out=bs[:, 0:1], in_=I_[:, 0:Sn], axis=AX.X, op=ALU.max), [t_I], [t_bs])
            A("dve", lambda: V.tensor_reduce(out=bs[:, 1:2], in_=I_[:, 0:K0], axis=AX.X, op=ALU.min), [t_I], [t_bs])
            A("dve", lambda: V.tensor_scalar(out=bs[:, 1:2], in0=bs[:, 1:2], scalar1=-1e29, scalar2=None, op0=ALU.max), [t_bs], [t_bs])
            A("dve", lambda: V.tensor_tensor(out=bs[:, 2:3], in0=bs[:, 0:1], in1=bs[:, 1:2], op=ALU.subtract), [t_bs], [t_bs])

        def mk_it(it):
            f = 2.0 ** (-(it + 1))

            def run():
                A("dve", lambda: V.scalar_tensor_tensor(out=bs[:, 3:4], in0=bs[:, 2:3], scalar=f, in1=bs[:, 1:2], op0=ALU.mult, op1=ALU.add),
                  [t_bs], [t_bs])
                A("dve", lambda: V.tensor_scalar(out=junk8[:, 0:Sn], in0=I_[:, 0:Sn], scalar1=bs[:, 3:4], scalar2=None, op0=ALU.is_ge,
                                                 op1=ALU.add, accum_out=bs[:, 4:5]), [t_I, t_bs], [t_junk8, t_bs])
                A("dve", lambda: V.tensor_scalar(out=bs[:, 5:6], in0=bs[:, 4:5], scalar1=cfg.topk - 0.5, scalar2=f, op0=ALU.is_gt, op1=ALU.mult),
                  [t_bs], [t_bs])
                A("dve", lambda: V.scalar_tensor_tensor(out=bs[:, 1:2], in0=bs[:, 5:6], scalar=bs[:, 2:3], in1=bs[:, 1:2], op0=ALU.mult, op1=ALU.add),
                  [t_bs], [t_bs])
            return run

        def epi():
            A("dve", lambda: V.tensor_scalar(out=mask01[:, 0:E * 128], in0=I_[:, 0:E * 128], scalar1=bs[:, 1:2], scalar2=None, op0=ALU.is_ge),
              [t_I, t_bs], [t_mask])
        return pro, [mk_it(it) for it in range(NIT)], epi

    pbc = [0]

    def attention_parts(i):
        b = i % 2
        E = cfg.ext[i]
        steps = [(kb, g) for kb in range(E) for g in range(4)]
        DL = int(os.environ.get('DL', '5'))
        assert NPM > DL
        pmof = {}

        def front(n):
            kb, g = steps[n]
            mt = kb % 4
            if g == 0:
                bk = tbank()
                A("pe", (lambda bk=bk, kb=kb: PE.transpose(out=pbf(bk)[:, 0:128], in_=mask01[:, kb * 128:(kb + 1) * 128], identity=identB)),
                  [t_mask, t_identB], [tpb[bk]])
                A("act", (lambda bk=bk, mt=mt: S.copy(out=maskT[mt], in_=pbf(bk)[:, 0:128])), [tpb[bk]], [t_maskT[mt]])
            hp, gi = g // 2, g % 2
            bk = tbank()
            A("pe", (lambda bk=bk, hp=hp, gi=gi, kb=kb, b=b: PE.matmul(
                pbank[bk][:], lhsT=kT_all[:, gi, kb * 128:(kb + 1) * 128],
                rhs=qTb[b][hp][:, gi * 512:(gi + 1) * 512], start=True, stop=True)),
              [t_kT[kb], t_qTb[b]], [tpb[bk]])
            pe_ = pbc[0] % NPB
            pm_ = pbc[0] % NPM
            pbc[0] += 1
            pmof[n] = pm_
            A("act", (lambda bk=bk, pe_=pe_: S.activation(out=Pexp[pe_], in_=pbank[bk][:], func=AF.Exp, scale=SCALE)),
              [tpb[bk]], [t_Pexp[pe_]])
            A("dve", (lambda pe_=pe_, pm_=pm_, mt=mt: V.tensor_tensor(
                out=Pm[pm_].rearrange("p (r t) -> p r t", r=4), in0=Pexp[pe_].rearrange("p (r t) -> p r t", r=4),
                in1=maskT[mt].unsqueeze(1).to_broadcast([128, 4, 128]), op=ALU.mult)),
              [t_Pexp[pe_], t_maskT[mt]], [t_Pm[pm_]])

        def back(n):
            kb, g = steps[n]
            pm_ = pmof[n]
            A("pe", (lambda g=g, kb=kb, pm_=pm_, E=E: PE.matmul(pbank[4 + g][0:65, :], lhsT=Vaug[:, kb, g, 0:65], rhs=Pm[pm_],
                                                               start=(kb == 0), stop=(kb == E - 1))),
              [t_V[kb], t_Vones, t_Pm[pm_]], [tpb[4 + g]])

        def mk_step(n):
            def run():
                if n < len(steps):
                    front(n)
                if n - DL >= 0:
                    back(n - DL)
            return run

        return [mk_step(n) for n in range(len(steps) + DL)], (lambda: att_tail(i))

    def att_tail(i):
        b = i % 2
        E = cfg.ext[i]
        if ATT_PARTS < 2:
            return
        for g in range(4):
            A("act", (lambda g=g: S.activation(out=rs[64:65, :], in_=pbank[4 + g][64:65, :], func=AF.Ln)), [tpb[4 + g]], [t_rs])
            A("act", lambda: S.activation(out=rs[64:65, :], in_=rs[64:65, :], func=AF.Exp, scale=-1.0), [t_rs], [t_rs])
            bk = tbank()
            A("pe", (lambda bk=bk: PE.matmul(pbank[bk][0:64, :], lhsT=onesF[64:65, 0:64], rhs=rs[64:65, :], start=True, stop=True)),
              [t_onesF, t_rs], [tpb[bk]])
            A("act", (lambda bk=bk: S.copy(out=bcS[0:64, :], in_=pbank[bk][0:64, :])), [tpb[bk]], [t_bcS])
            A("act", (lambda g=g: S.copy(out=numS[0:64, :], in_=pbank[4 + g][0:64, :])), [tpb[4 + g]], [t_numS])
            if g < 2:
                A("pool", (lambda g=g: G.tensor_tensor(out=oT_all[0:64, g, :], in0=numS[0:64, :], in1=bcS[0:64, :], op=ALU.mult)),
                  [t_numS, t_bcS], [t_oTlo])
            else:
                A("pool", (lambda g=g: G.tensor_tensor(out=oT_tmp[0:64, g - 2, :], in0=numS[0:64, :], in1=bcS[0:64, :], op=ALU.mult)),
                  [t_numS, t_bcS], [t_oTtmp])
        if ATT_PARTS < 3:
            return
        DMA("sp", oT_all[64:128, :, :], oT_tmp[0:64, :, :], R=[t_oTtmp], W=[t_oThi], key="oThi")
        if ATT_PARTS < 4:
            return
        for ch in range(2):
            DMA("sp", xa, x_own[i * 128:(i + 1) * 128, ch * 512:(ch + 1) * 512], W=[t_xa], key="xa")
            bk = tbank()
            for gi in range(2):
                for r in range(4):
                    j = gi * 4 + r
                    A("pe", (lambda bk=bk, gi=gi, r=r, j=j, ch=ch: PE.matmul(
                        pbank[bk][:], lhsT=oT_all[:, gi, r * 128:(r + 1) * 128], rhs=Wo[:, j, ch * 512:(ch + 1) * 512],
                        start=(j == 0), stop=(j == 7))), [t_oTlo, t_oThi, t_Wo], [tpb[bk]])
            A("act", (lambda bk=bk: S.copy(out=ys, in_=pbank[bk][:])), [tpb[bk]], [t_ys])
            A("pool", (lambda ch=ch: G.tensor_tensor(out=ta, in0=ys, in1=G0[:, ch * 512:(ch + 1) * 512], op=ALU.mult)), [t_ys, t_G0], [t_ta])
            A("pool", lambda: G.tensor_tensor(out=ta, in0=ta, in1=xa, op=ALU.add), [t_ta, t_xa], [t_ta])
            DMA("sp", x1s[i * 128:(i + 1) * 128, ch * 512:(ch + 1) * 512], ta, R=[t_ta], W=[t_x1s[i]], key="x1s")

    indexer(0)
    if stop in ("a2i", "a2t", "a2a"):
        pass
        dump(I_[:, 0:1024], [t_I], 1024)
        dump(mask01[:, 0:1024], [t_mask], 1024, bf=True)
        dump(bs, [t_bs], 16)
        dump(ta, [t_ta], 512)
        P.emit()
        return nc, P, AR
    pro, its, epi = topk_parts(0)
    pro()
    for f_ in its:
        f_()
    epi()
    for i in range(NS):
        steps_, tail_ = attention_parts(i)
        if i + 1 < NS:
            indexer(i + 1)
            pro, its, epi = topk_parts(i + 1)
            pro()
        else:
            its, epi = [], None
        per = max(1, len(steps_) // (len(its) + 1)) if its else len(steps_)
        k_it = 0
        for k_, st_ in enumerate(steps_):
            st_()
            if its and (k_ + 1) % per == 0 and k_it < len(its):
                its[k_it]()
                k_it += 1
        while k_it < len(its):
            its[k_it]()
            k_it += 1
        tail_()
        if epi is not None:
            epi()
    P.barrier()
    if stop in ("a2", "a2x"):
        dump(I_[:, 0:1024], [t_I], 1024)
        dump(mask01[:, 0:1024], [t_mask], 1024, bf=True)
        dump(bs, [t_bs], 16)
        dump(ta, [t_ta], 512)
        P.emit()
        return nc, P, AR
    AR.release(mA)

    Gt = [AR.alloc(1024) for _ in range(3)]; t_Gt = [Tok() for _ in range(3)]
    dgb = [AR.alloc(128), AR.alloc(128)]
    make_gate(mod_cols(0, 5), Gt[0], t_Gt[0], dgb, t_dgb, [0, 1])
    make_gate(mod_cols(1, 2), Gt[1], t_Gt[1], dgb, t_dgb, [2, 3])
    make_gate(mod_cols(1, 5), Gt[2], t_Gt[2], dgb, t_dgb, [0, 1])
    MS = 4
    xm = AR.alloc([MS, 1024]); t_xm = [Tok() for _ in range(MS)]
    hTm = AR.alloc([8, MS * 128], BF16); t_hTm = Tok()
    hTs = AR.alloc([8, 128], BF16); t_hTs = Tok()
    scrB = (AR.alloc(1024, BF16), Tok(), AR.alloc(4), Tok(), AR.alloc(1024, BF16), Tok())
    aT = AR.alloc([NFC, MS * 128], BF16); t_aT = Tok()
    Wd = AR.alloc([NFC, 1024], BF16); t_Wd = [Tok() for _ in range(4)]
    NWB = 2
    WA = [AR.alloc([8, 512], BF16) for _ in range(NWB)]; t_WA = [Tok() for _ in range(NWB)]
    WB = [AR.alloc([8, 512], BF16) for _ in range(NWB)]; t_WB = [Tok() for _ in range(NWB)]
    WC = [AR.alloc([8, 512], BF16) for _ in range(NWB)]; t_WC = [Tok() for _ in range(NWB)]
    sg = [AR.alloc(MS * 128), AR.alloc(MS * 128)]; t_sg = [Tok(), Tok()]
    zb = AR.alloc(MS * 128 + 2); t_zb = Tok()
    zc = AR.alloc(MS * 128); t_zc = Tok()
    carry = AR.alloc([8, 2]); t_carry = Tok()
    tb = AR.alloc(512); t_tb = Tok()
    A("dve", lambda: V.memset(carry, 0.0), [], [t_carry])
    wbc = [0]

    def norm_macro(ns, gs, shc):
        for s in range(ns):
            norm_T(xm[:, s, :], t_xm[s], gs, shc, hTs, t_hTs, scrB, 7)
            A("pool", (lambda s=s: G.tensor_copy(out=hTm[:, :, s * 128:(s + 1) * 128], in_=hTs)), [t_hTs], [t_hTm])

    def down(ns, nchunks, Gate, t_Gate):
        N = ns * 128
        for s in range(ns):
            for ch in range(2):
                bk = 4 + (s * 2 + ch) % 2
                for j in range(nchunks):
                    A("pe", (lambda bk=bk, j=j, s=s, ch=ch: PE.matmul(pbank[bk][:], lhsT=aT[:, j, s * 128:(s + 1) * 128],
                                                                     rhs=Wd[:, j, ch * 512:(ch + 1) * 512],
                                                                     start=(j == 0), stop=(j == nchunks - 1))),
                      [t_aT, t_Wd[j // 6]], [tpb[bk]])
                A("dve", (lambda bk=bk, ch=ch: V.tensor_tensor(out=tb, in0=pbank[bk][:], in1=Gate[:, ch * 512:(ch + 1) * 512], op=ALU.mult)),
                  [tpb[bk], t_Gate], [t_tb])
                A("pool", (lambda s=s, ch=ch: G.tensor_tensor(out=xm[:, s, ch * 512:(ch + 1) * 512], in0=xm[:, s, ch * 512:(ch + 1) * 512],
                                                             in1=tb, op=ALU.add)), [t_tb, t_xm[s]], [t_xm[s]])

    def ffn(L, ns, Gate, t_Gate):
        N = ns * 128
        norm_macro(ns, gscT[:, (2 * L + 1) * 8:(2 * L + 1) * 8 + 8], mod_cols(L, 3))
        gsrc = wg_d[L].rearrange("(k p) f -> p k f", p=128)
        usrc = wu_d[L].rearrange("(k p) f -> p k f", p=128)
        for fg in range(6):
            f0 = fg * 512
            fw = min(512, FF - f0)
            wb = wbc[0] % NWB
            wbc[0] += 1
            DMA("pool", WA[wb][:, :, 0:fw], gsrc[:, :, f0:f0 + fw], W=[t_WA[wb]], key="WA%d" % wb)
            DMA("pool", WB[wb][:, :, 0:fw], usrc[:, :, f0:f0 + fw], W=[t_WB[wb]], key="WB%d" % wb)
            for fc in range(fw // 128):
                j = fg * 4 + fc
                bg, bu = (0, 1) if j % 2 == 0 else (2, 3)
                for k in range(8):
                    A("pe", (lambda bg=bg, k=k, fc=fc, wb=wb: PE.matmul(pbank[bg][:, 0:N], lhsT=WA[wb][:, k, fc * 128:(fc + 1) * 128],
                                                                       rhs=hTm[:, k, 0:N], start=(k == 0), stop=(k == 7))),
                      [t_WA[wb], t_hTm], [tpb[bg]])
                for k in range(8):
                    A("pe", (lambda bu=bu, k=k, fc=fc, wb=wb: PE.matmul(pbank[bu][:, 0:N], lhsT=WB[wb][:, k, fc * 128:(fc + 1) * 128],
                                                                       rhs=hTm[:, k, 0:N], start=(k == 0), stop=(k == 7))),
                      [t_WB[wb], t_hTm], [tpb[bu]])
                sb_ = j % 2
                A("act", (lambda bg=bg, sb_=sb_: S.activation(out=sg[sb_][:, 0:N], in_=pbank[bg][:, 0:N], func=AF.Silu)),
                  [tpb[bg]], [t_sg[sb_]])
                A("dve", (lambda bu=bu, sb_=sb_, j=j: V.tensor_tensor(out=aT[:, j, 0:N], in0=pbank[bu][:, 0:N], in1=sg[sb_][:, 0:N], op=ALU.mult)),
                  [tpb[bu], t_sg[sb_]], [t_aT])
        dsrc = wd_d[L].rearrange("(j p) c -> p j c", p=128)
        for q4 in range(0, NFC, 6):
            q5 = min(NFC, q4 + 6)
            DMA("pool", Wd[:, q4:q5, :], dsrc[:, q4:q5, :], W=[t_Wd[q4 // 6]], key="Wd%d" % (q4 // 6), max_dma_last_dim=4096)
        down(ns, NFC, Gate, t_Gate)

    def convmix(ns):
        N = ns * 128
        norm_macro(ns, gscT[:, 16:24], mod_cols(1, 0))
        src = cw_in.rearrange("(k p) f -> p k f", p=128)
        for cg in range(2):
            wb = wbc[0] % NWB
            wbc[0] += 1
            DMA("pool", WA[wb], src[:, :, cg * 512:(cg + 1) * 512], W=[t_WA[wb]], key="WA%d" % wb)
            DMA("pool", WB[wb], src[:, :, 1024 + cg * 512:1024 + (cg + 1) * 512], W=[t_WB[wb]], key="WB%d" % wb)
            DMA("pool", WC[wb], src[:, :, 2048 + cg * 512:2048 + (cg + 1) * 512], W=[t_WC[wb]], key="WC%d" % wb)
            for cc in range(4):
                cj = cg * 4 + cc
                for (bk, Wt, tW) in ((0, WA, t_WA), (1, WB, t_WB), (2, WC, t_WC)):
                    for k in range(8):
                        A("pe", (lambda bk=bk, Wt=Wt, k=k, cc=cc, wb=wb: PE.matmul(
                            pbank[bk][:, 0:N], lhsT=Wt[wb][:, k, cc * 128:(cc + 1) * 128], rhs=hTm[:, k, 0:N],
                            start=(k == 0), stop=(k == 7))), [tW[wb], t_hTm], [tpb[bk]])
                A("act", lambda: S.copy(out=sg[0][:, 0:N], in_=pbank[1][:, 0:N]), [tpb[1]], [t_sg[0]])
                A("dve", (lambda cj=cj: V.tensor_copy(out=zb[:, 0:2], in_=carry[:, cj, :])), [t_carry], [t_zb])
                A("dve", lambda: V.tensor_tensor(out=zb[:, 2:2 + N], in0=pbank[2][:, 0:N], in1=sg[0][:, 0:N], op=ALU.mult),
                  [tpb[2], t_sg[0]], [t_zb])
                A("dve", (lambda cj=cj: V.tensor_copy(out=carry[:, cj, :], in_=zb[:, N:N + 2])), [t_zb], [t_carry])
                A("dve", (lambda cj=cj: V.tensor_scalar(out=zc[:, 0:N], in0=zb[:, 2:2 + N], scalar1=cwT[:, cj * 3 + 2:cj * 3 + 3],
                                                       scalar2=None, op0=ALU.mult)), [t_zb, t_cwT], [t_zc])
                A("dve", (lambda cj=cj: V.scalar_tensor_tensor(out=zc[:, 0:N], in0=zb[:, 1:1 + N], scalar=cwT[:, cj * 3 + 1:cj * 3 + 2],
                                                              in1=zc[:, 0:N], op0=ALU.mult, op1=ALU.add)), [t_zb, t_cwT, t_zc], [t_zc])
                A("dve", (lambda cj=cj: V.scalar_tensor_tensor(out=zc[:, 0:N], in0=zb[:, 0:N], scalar=cwT[:, cj * 3:cj * 3 + 1],
                                                              in1=zc[:, 0:N], op0=ALU.mult, op1=ALU.add)), [t_zb, t_cwT, t_zc], [t_zc])
                A("dve", (lambda cj=cj: V.tensor_tensor(out=aT[:, cj, 0:N], in0=pbank[0][:, 0:N], in1=zc[:, 0:N], op=ALU.mult)),
                  [tpb[0], t_zc], [t_aT])
        osrc = cw_out.rearrange("(j p) c -> p j c", p=128)
        DMA("pool", Wd[:, 0:6, :], osrc[:, 0:6, :], W=[t_Wd[0]], key="Wd0", max_dma_last_dim=4096)
        DMA("pool", Wd[:, 6:8, :], osrc[:, 6:8, :], W=[t_Wd[1]], key="Wd1", max_dma_last_dim=4096)
        down(ns, 8, Gt[1], t_Gt[1])

    nmac = (NS + MS - 1) // MS
    for m in range(nmac):
        s0 = m * MS
        ns = min(MS, NS - s0)
        for s in range(ns):
            DMA("sp", xm[:, s, :], x1s[(s0 + s) * 128:(s0 + s + 1) * 128, :], R=[t_x1s[s0 + s]], W=[t_xm[s]], key="xm%d" % s)
        ffn(0, ns, Gt[0], t_Gt[0])
        convmix(ns)
        ffn(1, ns, Gt[2], t_Gt[2])
        for s in range(ns):
            DMA("sp", out_d[(s0 + s) * 128:(s0 + s + 1) * 128, :], xm[:, s, :], R=[t_xm[s]], W=[t_out[s0 + s]], key="out%d" % s)

    P.emit()
    return nc, P, AR


import os
ATT_PARTS = int(os.environ.get('ATT_PARTS', '9'))
_CACHE = {}
STOP = None
LAST = None


def _host_inputs(cfg, r, x, c, positions, ada_w, ada_b, norm1_g, norm2_g, attn_w_in, attn_q_norm_g, attn_k_norm_g,
                 idx_k_ln_g, idx_k_ln_b, attn_w_out, conv_w_in, conv_w, conv_w_out, ffn_w_gate, ffn_w_up, ffn_w_down,
                 shared):
    b, role = r // 2, r % 2
    tiles = cfg.tilesA if role == 0 else cfg.tilesB
    T, NT, NS = cfg.T, cfg.NT, cfg.NS
    xs = np.ascontiguousarray(x[b])
    xo = np.ascontiguousarray(xs.reshape(NT, 128, D)[tiles].reshape(NS * 128, D))
    ps = np.ascontiguousarray(positions[b].reshape(NT, 128).T)
    po = np.ascontiguousarray(positions[b].reshape(NT, 128)[tiles].T)
    tok = np.arange(T, dtype=np.float32).reshape(NT, 128)
    qp = np.ascontiguousarray(tok[tiles].T)
    cT = np.ascontiguousarray(c[b].reshape(8, 128).T)
    d = dict(shared)
    d.update({"x_seq": xs, "x_own": xo, "pos_seq": ps.astype(np.int32), "pos_own": po.astype(np.int32), "qpos": qp, "cT": cT})
    return d


def _shared_inputs(ada_w, ada_b, norm1_g, norm2_g, attn_w_in, attn_q_norm_g, attn_k_norm_g, idx_k_ln_g, idx_k_ln_b,
                   attn_w_out, conv_w_in, conv_w, conv_w_out, ffn_w_gate, ffn_w_up, ffn_w_down):
    w = attn_w_in[0]
    qc = w[:, 0:1024].reshape(D, 16, 64)
    qperm = np.stack([qc[:, [j, 8 + j], :] for j in range(8)], axis=1).reshape(D, 1024)
    kc = w[:, 1024:1280].reshape(D, 4, 64)
    kperm = np.concatenate([kc[:, 0], kc[:, 2], kc[:, 1], kc[:, 3]], axis=1)
    vcol = w[:, 1280:1536]
    qic = w[:, 1536:2048]
    kic = w[:, 2048:2112]
    wic = w[:, 2112:2120]
    w_in = np.ascontiguousarray(np.concatenate([qperm, kperm, vcol, qic, kic, kic, wic], axis=1), dtype=np.float32)
    assert w_in.shape[1] == WCOLS
    vecs = np.concatenate([ada_b[0].reshape(48, 128), ada_b[1].reshape(48, 128), norm1_g.reshape(16, 128),
                           norm2_g.reshape(16, 128)], axis=0).astype(np.float32)
    hv = np.tile(np.concatenate([attn_q_norm_g[0], attn_k_norm_g[0], idx_k_ln_g[0], idx_k_ln_b[0]])[None, :], (128, 1)).astype(np.float32)
    cwT = np.ascontiguousarray(conv_w[0].T.reshape(8, 128, 3).transpose(1, 0, 2).reshape(128, 24)).astype(np.float32)
    invf = np.float32(10000.0) ** (-(np.arange(32, dtype=np.float32) * np.float32(2.0) / np.float32(64)))
    return {
        "w_in": w_in, "w_out": np.ascontiguousarray(attn_w_out[0]), "ada_w": np.ascontiguousarray(ada_w),
        "vecs": np.ascontiguousarray(vecs), "hv": np.ascontiguousarray(hv),
        "cw_in": np.ascontiguousarray(conv_w_in[0]), "cwT": cwT, "cw_out": np.ascontiguousarray(conv_w_out[0]),
        "wg": np.ascontiguousarray(ffn_w_gate), "wu": np.ascontiguousarray(ffn_w_up), "wd": np.ascontiguousarray(ffn_w_down),
        "ident": np.eye(128, dtype=np.float32), "invf": np.tile(invf.astype(np.float32)[None, :], (128, 1)),
        "iota": np.tile(np.arange(512, dtype=np.float32)[None, :], (128, 1)),
    }


def kernel(x, c, positions, ada_w, ada_b, norm1_g, norm2_g, attn_w_in, attn_q_norm_g, attn_k_norm_g, idx_k_ln_g,
           idx_k_ln_b, attn_w_out, conv_w_in, conv_w, conv_w_out, ffn_w_gate, ffn_w_up, ffn_w_down):
    args = [np.asarray(a) for a in (x, c, positions, ada_w, ada_b, norm1_g, norm2_g, attn_w_in, attn_q_norm_g,
                                     attn_k_norm_g, idx_k_ln_g, idx_k_ln_b, attn_w_out, conv_w_in, conv_w, conv_w_out,
                                     ffn_w_gate, ffn_w_up, ffn_w_down)]
    x = args[0]
    B, T, _ = x.shape
    cfg = Cfg(T)
    if (T, STOP) not in _CACHE:
        _CACHE[(T, STOP)] = build(cfg, STOP)[0]
    nc = _CACHE[(T, STOP)]
    shared = _shared_inputs(*args[3:])
    ncores = 2 * B
    in_maps = [_host_inputs(cfg, r, *args, shared) for r in range(ncores)]
    res = run_bass_kernel_spmd(nc, in_maps, core_ids=list(range(ncores)))
    global LAST
    LAST = res
    out = np.empty((B, T, D), dtype=np.float32)
    for r in range(ncores):
        b, role = r // 2, r % 2
        tiles = cfg.tilesA if role == 0 else cfg.tilesB
        halo = cfg.haloA if role == 0 else cfg.haloB
        o = np.asarray(res.results[r]["out"]).reshape(cfg.NS, 128, D)
        for s, t in enumerate(tiles):
            if s == halo:
                continue
            out[b, t * 128:(t + 1) * 128, :] = o[s]
    return out
```

```python
import math
import numpy as np
import ml_dtypes
import concourse.bass as bass
import concourse.mybir as mybir
from concourse.bass_utils import run_bass_kernel_spmd

F32 = mybir.dt.float32
BF16 = mybir.dt.bfloat16
I32 = mybir.dt.int32
U8 = mybir.dt.uint8
ALU = mybir.AluOpType
AF = mybir.ActivationFunctionType
AX = mybir.AxisListType

D = 1024
FF = 2816
NFC = FF // 128
WCOLS = 2184
EPS = 1e-6
NIT = 25
TWO_PI = 2.0 * math.pi
C1 = 6.28125
C2 = TWO_PI - C1


class Tok:
    __slots__ = ("w", "r", "rd")

    def __init__(self):
        self.w = None
        self.r = {}
        self.rd = []


class Op:
    __slots__ = ("eng", "fn", "deps", "dma", "sig", "cnt", "dsem")

    def __init__(self, eng, fn, dma):
        self.eng = eng
        self.fn = fn
        self.deps = set()
        self.dma = dma
        self.sig = False
        self.cnt = 0
        self.dsem = None


class Prog:
    def __init__(self, nc):
        self.nc = nc
        self.ops = []
        self.engs = {"pe": nc.tensor, "act": nc.scalar, "dve": nc.vector,
                     "pool": nc.gpsimd, "sp": nc.sync}
        self.last = {}
        self.dmas_since_bar = []
        self.bar = {}

    def add(self, eng, fn, R=(), W=(), dma=None):
        idx = len(self.ops)
        op = Op(eng, fn, dma)
        deps = op.deps
        if eng in self.bar:
            deps.update(self.bar.pop(eng))
        for t in R:
            if t.w is not None:
                deps.add(t.w)
        for t in W:
            if t.w is not None:
                deps.add(t.w)
            deps.update(t.r.values())
            deps.update(t.rd)
        deps.discard(idx)
        for t in W:
            t.w = idx
            t.r = {}
            t.rd = []
        for t in R:
            if t.w == idx:
                continue
            if dma is not None:
                t.rd.append(idx)
            else:
                t.r[eng] = idx
        self.ops.append(op)
        if dma is None:
            self.last[eng] = idx
        else:
            self.dmas_since_bar.append(idx)
        return idx

    def barrier(self):
        s = set(self.last.values()) | set(self.dmas_since_bar)
        self.dmas_since_bar = []
        for e in self.engs:
            self.bar[e] = set(s) | self.bar.get(e, set())

    def emit(self, final_wait_eng="sp"):
        nc = self.nc
        ops = self.ops
        for op in ops:
            nd = set()
            for d in op.deps:
                dop = ops[d]
                if dop.dma is None and op.dma is None and dop.eng == op.eng == "pe":
                    continue
                nd.add(d)
                dop.sig = True
            op.deps = nd
        esem = {e: nc.semaphore("se_" + e).__enter__() for e in self.engs}
        dsem, dcnt = {}, {}
        ecnt = {e: 0 for e in self.engs}
        for op in ops:
            if op.dma is not None:
                if op.dma not in dsem:
                    dsem[op.dma] = nc.semaphore("sd_%d" % len(dsem)).__enter__()
                    dcnt[op.dma] = 0
                dcnt[op.dma] += 16
                op.cnt = dcnt[op.dma]
                op.dsem = dsem[op.dma]
                op.sig = True
            elif op.sig:
                ecnt[op.eng] += 1
                op.cnt = ecnt[op.eng]
        waited = {e: {} for e in self.engs}
        for op in ops:
            E = self.engs[op.eng]
            wd = waited[op.eng]
            need = {}
            for d in op.deps:
                dop = ops[d]
                if dop.dma is not None:
                    key, sem = ("d", dop.dma), dop.dsem
                else:
                    key, sem = ("e", dop.eng), esem[dop.eng]
                if need.get(key, (None, 0))[1] < dop.cnt:
                    need[key] = (sem, dop.cnt)
            for key, (sem, cnt) in need.items():
                if wd.get(key, 0) < cnt:
                    E.wait_ge(sem, cnt)
                    wd[key] = cnt
            ins = op.fn()
            if op.sig:
                ins.then_inc(op.dsem if op.dma is not None else esem[op.eng], 16 if op.dma is not None else 1)
        E = self.engs[final_wait_eng]
        for k, sem in dsem.items():
            E.wait_ge(sem, dcnt[k])
        for e in self.engs:
            if ecnt[e] > 0 and e != final_wait_eng:
                E.wait_ge(esem[e], ecnt[e])
        self.stats = (len(ops), ecnt, len(dsem))


def _dsize(dt):
    if dt in (F32, I32):
        return 4
    if dt == BF16:
        return 2
    return 1


class Arena:
    def __init__(self, nc, nbytes):
        self.t = nc.sbuf_tensor("arena", [128, nbytes // 4], F32).__enter__()
        self.cap = nbytes
        self.off = 0
        self.peak = 0

    def mark(self):
        return self.off

    def release(self, m):
        self.off = m

    def alloc(self, free, dt=F32, parts=128):
        if isinstance(free, int):
            free = [free]
        n = 1
        for f in free:
            n *= f
        sz = (n * _dsize(dt) + 63) // 64 * 64
        assert self.off + sz <= self.cap, "SBUF arena overflow: need %d have %d" % (self.off + sz, self.cap)
        ap = self.t[0:parts, self.off // 4:(self.off + sz) // 4]
        self.off += sz
        self.peak = max(self.peak, self.off)
        if dt != F32:
            ap = ap.bitcast(dt)
        ap = ap[:, 0:n]
        if len(free) == 2:
            ap = ap.rearrange("p (a b) -> p a b", a=free[0])
        elif len(free) == 3:
            ap = ap.rearrange("p (a b c) -> p a b c", a=free[0], b=free[1])
        return ap


class Cfg:
    def __init__(self, T):
        self.T = T
        self.NT = T // 128
        assert self.NT % 4 == 0
        CH = self.NT // 4
        self.CH = CH
        self.NS = 2 * CH + 1
        self.tilesA = list(range(CH)) + [3 * CH - 1] + list(range(3 * CH, 4 * CH))
        self.tilesB = [CH - 1] + list(range(CH, 3 * CH))
        self.ext = [max(a, b) + 1 for a, b in zip(self.tilesA, self.tilesB)]
        self.dchunk = [min(a, b) // 4 for a, b in zip(self.tilesA, self.tilesB)]
        self.haloA = CH
        self.haloB = 0
        self.topk = min(256, T // 4)


def build(cfg, stop=None):
    T, NT, NS = cfg.T, cfg.NT, cfg.NS
    nc = bass.Bass("TRN2", target_bir_lowering=False)
    P = Prog(nc)

    def din(name, shape, dt=F32):
        return nc.dram_tensor(name, list(shape), dt, kind="ExternalInput").ap()

    x_seq = din("x_seq", [T, D])
    x_own = din("x_own", [NS * 128, D])
    pos_seq = din("pos_seq", [128, NT], I32)
    pos_own = din("pos_own", [128, NS], I32)
    qpos_d = din("qpos", [128, NS])
    cT_d = din("cT", [128, 8])
    w_in = din("w_in", [D, WCOLS])
    w_out = din("w_out", [D, D])
    ada_w = din("ada_w", [2, D, 6 * D])
    vecs_d = din("vecs", [128, 128])
    hv_d = din("hv", [128, 256])
    cw_in = din("cw_in", [D, 3 * D])
    cwT_d = din("cwT", [128, 24])
    cw_out = din("cw_out", [D, D])
    wg_d = din("wg", [2, D, FF])
    wu_d = din("wu", [2, D, FF])
    wd_d = din("wd", [2, FF, D])
    ident_d = din("ident", [128, 128])
    invf_d = din("invf", [128, 32])
    iota_d = din("iota", [128, 512])
    out_d = nc.dram_tensor("out", [NS * 128, D], F32, kind="ExternalOutput").ap()
    dbg_d = nc.dram_tensor("dbg", [128, 8192], F32, kind="ExternalOutput").ap() if stop else None
    qTs = nc.dram_tensor("qTs", [NS, 128, 1024], BF16, kind="Internal").ap()
    qiTs = nc.dram_tensor("qiTs", [NS, 128, 512], BF16, kind="Internal").ap()
    x1s = nc.dram_tensor("x1s", [NS * 128, D], F32, kind="Internal").ap()
    wg_b = nc.dram_tensor("wg_b", [2, D, FF], BF16, kind="Internal").ap()
    wu_b = nc.dram_tensor("wu_b", [2, D, FF], BF16, kind="Internal").ap()
    wd_b = nc.dram_tensor("wd_b", [2, FF, D], BF16, kind="Internal").ap()
    cwin_b = nc.dram_tensor("cwin_b", [D, 3 * D], BF16, kind="Internal").ap()
    cwout_b = nc.dram_tensor("cwout_b", [D, D], BF16, kind="Internal").ap()
    t_wgb = [Tok(), Tok()]; t_wub = [Tok(), Tok()]; t_wdb = [Tok(), Tok()]; t_cwinb = Tok(); t_cwoutb = Tok()
    t_qTs = [Tok() for _ in range(NS)]
    t_qiTs = [Tok() for _ in range(NS)]
    t_x1s = [Tok() for _ in range(NS)]
    t_out = [Tok() for _ in range(NS)]

    AR = Arena(nc, 206 * 1024)
    pbank = [nc.psum_tensor("pb%d" % k, [128, 512], F32).__enter__() for k in range(8)]
    tpb = [Tok() for _ in range(8)]

    def pbf(k):
        return pbank[k][:].bitcast(BF16)

    V, S, G, PE = nc.vector, nc.scalar, nc.gpsimd, nc.tensor

    def A(eng, fn, R=(), W=()):
        P.add(eng, fn, R, W)

    dma_ctr = [0]

    def DMA(q, out, in_, R=(), W=(), key=None, **kw):
        if key is None:
            dma_ctr[0] += 1
            key = "k%d" % dma_ctr[0]
        e = {"sp": nc.sync, "pool": nc.gpsimd, "act": nc.scalar}[q]
        P.add(q, lambda: e.dma_start(out=out, in_=in_, **kw), R, W, dma=key)

    dbg_off = [0]

    def dump(ap2d, toks, n, bf=False):
        if stop.endswith("x"):
            return
        o = dbg_off[0]
        if bf:
            tmp = AR.alloc(n)
            tt = Tok()
            A("dve", lambda: V.tensor_copy(out=tmp, in_=ap2d), toks, [tt])
            DMA("sp", dbg_d[:, o:o + n], tmp, R=[tt], W=[Tok()])
        else:
            DMA("sp", dbg_d[:, o:o + n], ap2d, R=toks, W=[Tok()])
        dbg_off[0] += n

    identF = AR.alloc(128); t_identF = Tok()
    identB = AR.alloc(128, BF16); t_identB = Tok()
    onesF = AR.alloc(128); t_onesF = Tok()
    iota = AR.alloc(512); t_iota = Tok()
    vecT = AR.alloc(128); t_vecT = Tok()
    modT = AR.alloc(96); t_modT = Tok()
    gscT = AR.alloc(32); t_gscT = Tok()
    wsign = AR.alloc([NS, 8]); t_wsign = Tok()
    cwT = AR.alloc(24); t_cwT = Tok()
    qpos = AR.alloc(NS); t_qpos = Tok()
    hv = AR.alloc(256); t_hv = Tok()
    G0 = AR.alloc(1024); t_G0 = Tok()

    DMA("sp", identF, ident_d, W=[t_identF])
    DMA("sp", iota, iota_d, W=[t_iota])
    DMA("sp", cwT, cwT_d, W=[t_cwT])
    DMA("sp", qpos, qpos_d, W=[t_qpos])
    DMA("sp", hv, hv_d, W=[t_hv])
    A("dve", lambda: V.tensor_copy(out=identB, in_=identF), [t_identF], [t_identB])
    A("dve", lambda: V.memset(onesF, 1.0), [], [t_onesF])

    if stop == "pre":
        dump(identF, [t_identF], 128)
        dump(identB, [t_identB], 128, bf=True)
        dump(hv, [t_hv], 256)
        P.emit()
        return nc, P, AR
    m0 = AR.mark()
    vecs_sb = AR.alloc(128); t_vecs = Tok()
    cT_sb = AR.alloc(8); t_cT = Tok()
    cact2 = AR.alloc([8, 2]); t_cact = Tok()
    adaw = [AR.alloc([8, 512]) for _ in range(2)]
    t_adaw = [Tok(), Tok()]
    DMA("sp", vecs_sb, vecs_d, W=[t_vecs])
    DMA("sp", cT_sb, cT_d, W=[t_cT])
    A("pe", lambda: PE.transpose(out=pbank[0][:, 0:128], in_=vecs_sb, identity=identF), [t_vecs, t_identF], [tpb[0]])
    A("act", lambda: S.copy(out=vecT, in_=pbank[0][:, 0:128]), [tpb[0]], [t_vecT])
    A("act", lambda: S.activation(out=cact2[:, :, 0], in_=cT_sb, func=AF.Silu), [t_cT], [t_cact])
    A("act", lambda: S.activation(out=cact2[:, :, 1], in_=cT_sb, func=AF.Silu), [t_cT], [t_cact])
    n = 0
    for L in range(2):
        src = ada_w[L].rearrange("(k p) n -> p k n", p=128)
        for cg in range(12):
            b = n % 2
            n += 1
            DMA("sp", adaw[b], src[:, :, cg * 512:(cg + 1) * 512], W=[t_adaw[b]], key="adaw%d" % b)
            for m4 in range(4):
                m = L * 48 + cg * 4 + m4
                for k in range(8):
                    A("pe", (lambda b=b, m=m, m4=m4, k=k: PE.matmul(
                        pbank[1][:, 2 * m:2 * m + 2], lhsT=adaw[b][:, k, m4 * 128:(m4 + 1) * 128],
                        rhs=cact2[:, k, :], start=(k == 0), stop=(k == 7))),
                      [t_adaw[b], t_cact], [tpb[1]])
    if stop == "p0a":
        A("act", lambda: S.copy(out=modT, in_=pbank[1][:, 0:96]), [tpb[1]], [t_modT])
        dump(modT, [t_modT], 96)
        dump(vecT, [t_vecT], 128)
        dump(cact2.rearrange("p a b -> p (a b)"), [t_cact], 16)
        P.emit()
        return nc, P, AR
    A("dve", lambda: V.tensor_tensor(out=modT, in0=pbank[1][:, 0:192].rearrange("p (m t) -> p m t", t=2)[:, :, 0],
                                     in1=vecT[:, 0:96], op=ALU.add), [tpb[1], t_vecT], [t_modT])
    for L in range(2):
        for s in range(2):
            o = (2 * L + s) * 8
            sc = modT[:, L * 48 + (8 if s == 0 else 32): L * 48 + (16 if s == 0 else 40)]
            ng = vecT[:, 96 + s * 16 + L * 8: 96 + s * 16 + L * 8 + 8]
            A("dve", (lambda o=o, sc=sc, ng=ng: V.scalar_tensor_tensor(
                out=gscT[:, o:o + 8], in0=sc, scalar=1.0, in1=ng, op0=ALU.add, op1=ALU.mult)),
              [t_modT, t_vecT], [t_gscT])

    def mod_cols(L, which):
        return modT[:, L * 48 + which * 8: L * 48 + which * 8 + 8]

    def make_gate(gcols, dst, t_dst, dg_bufs, t_dg, banks):
        for j in range(8):
            b = j % 2
            A("dve", (lambda j=j, b=b: V.tensor_scalar(out=dg_bufs[b], in0=identF, scalar1=gcols[:, j:j + 1],
                                                      scalar2=None, op0=ALU.mult)),
              [t_identF, t_modT], [t_dg[b]])
            bk = banks[j // 4]
            A("pe", (lambda j=j, b=b, bk=bk: PE.matmul(pbank[bk][:, (j % 4) * 128:(j % 4 + 1) * 128], lhsT=onesF,
                                                      rhs=dg_bufs[b], start=True, stop=True)),
              [t_onesF, t_dg[b]], [tpb[bk]])
        for h in range(2):
            A("act", (lambda h=h: S.copy(out=dst[:, h * 512:(h + 1) * 512], in_=pbank[banks[h]][:])),
              [tpb[banks[h]]], [t_dst])

    if stop == "p0b":
        dump(modT, [t_modT], 96)
        dump(gscT, [t_gscT], 32)
        P.emit()
        return nc, P, AR
    dgb = [AR.alloc(128), AR.alloc(128)]
    t_dgb = [Tok(), Tok()]
    make_gate(mod_cols(0, 2), G0, t_G0, dgb, t_dgb, [2, 3])
    if stop == "p0c":
        dump(G0, [t_G0], 1024)
        P.emit()
        return nc, P, AR
    P.barrier()
    if stop == "p0":
        dump(modT, [t_modT], 96)
        dump(gscT, [t_gscT], 32)
        dump(G0, [t_G0], 1024)
        dump(vecT, [t_vecT], 128)
        P.emit()
        return nc, P, AR
    AR.release(m0)

    def norm_T(xt, t_xt, gs, shc, hT_dst, t_hT, scr, bank):
        junkb, t_junkb, ss, t_ss, xn, t_xn = scr
        A("act", lambda: S.activation(out=junkb, in_=xt, func=AF.Square, accum_out=ss[:, 0:1]), [t_xt], [t_junkb, t_ss])
        A("act", lambda: S.activation(out=ss[:, 1:2], in_=ss[:, 0:1], func=AF.Sqrt, scale=1.0 / D, bias=eps_t[:, 0:1]),
          [t_ss, t_eps], [t_ss])
        A("dve", lambda: V.reciprocal(out=ss[:, 2:3], in_=ss[:, 1:2]), [t_ss], [t_ss])
        A("act", lambda: S.activation(out=xn, in_=xt, func=AF.Identity, scale=ss[:, 2:3]), [t_xt, t_ss], [t_xn])
        pv = pbf(bank)
        for j in range(8):
            A("pe", (lambda j=j: PE.transpose(out=pv[:, j * 128:(j + 1) * 128], in_=xn[:, j * 128:(j + 1) * 128],
                                              identity=identB)), [t_xn, t_identB], [tpb[bank]])
        for j in range(8):
            A("act", (lambda j=j: S.activation(out=hT_dst[:, j, :], in_=pv[:, j * 128:(j + 1) * 128], func=AF.Identity,
                                               scale=gs[:, j:j + 1], bias=shc[:, j:j + 1])),
              [tpb[bank], t_gscT, t_modT], [t_hT])

    eps_t = AR.alloc(4); t_eps = Tok()
    A("dve", lambda: V.memset(eps_t, EPS), [], [t_eps])

    mA = AR.mark()
    kT_all = AR.alloc([2, T], BF16); t_kT = [Tok() for _ in range(NT)]
    Vaug = AR.alloc([NT, 4, 66], BF16); t_V = [Tok() for _ in range(NT)]
    kiT = AR.alloc(T, BF16); t_kiT = [Tok() for _ in range(NT)]
    t_Vones = Tok()
    A("pool", lambda: G.memset(Vaug[:, :, :, 64:65], 1.0), [], [t_Vones] + t_V)

    mA1 = AR.mark()
    Win = AR.alloc([8, WCOLS], BF16); t_Win = [Tok() for _ in range(8)]
    wsrc = w_in.rearrange("(k p) n -> p k n", p=128)
    for k in range(8):
        DMA("pool", Win[:, k, :], wsrc[:, k, :], W=[t_Win[k]], key="win%d" % k, max_dma_last_dim=4096)
    for L in range(2):
        DMA("pool", wg_b[L], wg_d[L], W=[t_wgb[L]], key="cwg%d" % L, max_dma_last_dim=4096)
        DMA("pool", wu_b[L], wu_d[L], W=[t_wub[L]], key="cwu%d" % L, max_dma_last_dim=4096)
        DMA("pool", wd_b[L], wd_d[L], W=[t_wdb[L]], key="cwd%d" % L, max_dma_last_dim=4096)
        if L == 0:
            DMA("pool", cwin_b, cw_in, W=[t_cwinb], key="ccwin", max_dma_last_dim=4096)
            DMA("pool", cwout_b, cw_out, W=[t_cwoutb], key="ccwout", max_dma_last_dim=4096)

    def rope_tables(pos_d, n, cos_t, sin_t, t_tab):
        m = AR.mark()
        pi_ = AR.alloc(n, I32); t_pi = Tok()
        pf = AR.alloc(n); ang = AR.alloc([n, 32]); u = AR.alloc([n, 32]); ki_ = AR.alloc([n, 32], I32)
        kf = AR.alloc([n, 32]); r = AR.alloc([n, 32]); r2 = AR.alloc([n, 32]); tmp = AR.alloc([n, 32])
        invt = AR.alloc(32)
        tk = Tok()
        DMA("sp", pi_, pos_d, W=[t_pi])
        DMA("sp", invt, invf_d, W=[tk])
        A("dve", lambda: V.tensor_copy(out=pf, in_=pi_), [t_pi], [tk])
        A("dve", lambda: V.tensor_tensor(out=ang, in0=pf.unsqueeze(2).to_broadcast([128, n, 32]),
                                         in1=invt.unsqueeze(1).to_broadcast([128, n, 32]), op=ALU.mult), [tk], [tk])
        A("dve", lambda: V.tensor_scalar(out=u, in0=ang, scalar1=1.0 / TWO_PI, scalar2=None, op0=ALU.mult), [tk], [tk])
        A("dve", lambda: V.tensor_copy(out=ki_, in_=u), [tk], [tk])
        A("dve", lambda: V.tensor_copy(out=kf, in_=ki_), [tk], [tk])
        A("dve", lambda: V.scalar_tensor_tensor(out=r, in0=kf, scalar=-C1, in1=ang, op0=ALU.mult, op1=ALU.add), [tk], [tk])
        A("dve", lambda: V.scalar_tensor_tensor(out=r, in0=kf, scalar=-C2, in1=r, op0=ALU.mult, op1=ALU.add), [tk], [tk])
        A("dve", lambda: V.tensor_scalar(out=r, in0=r, scalar1=-3.1415925, scalar2=3.1415925, op0=ALU.max, op1=ALU.min), [tk], [tk])
        A("dve", lambda: V.tensor_scalar(out=r2, in0=r, scalar1=math.pi / 2, scalar2=None, op0=ALU.add), [tk], [tk])
        A("dve", lambda: V.tensor_scalar(out=tmp, in0=r2, scalar1=math.pi, scalar2=-TWO_PI, op0=ALU.is_gt, op1=ALU.mult), [tk], [tk])
        A("dve", lambda: V.tensor_tensor(out=r2, in0=r2, in1=tmp, op=ALU.add), [tk], [tk])
        A("dve", lambda: V.tensor_scalar(out=r2, in0=r2, scalar1=-3.1415925, scalar2=3.1415925, op0=ALU.max, op1=ALU.min), [tk], [tk])
        A("act", lambda: S.activation(out=sin_t, in_=r, func=AF.Sin), [tk], [t_tab])
        A("act", lambda: S.activation(out=cos_t, in_=r2, func=AF.Sin), [tk], [t_tab])
        return m

    cos_s = AR.alloc([NT, 32]); sin_s = AR.alloc([NT, 32]); t_tabs = Tok()
    cos_o = AR.alloc([NS, 32]); sin_o = AR.alloc([NS, 32]); t_tabo = Tok()
    hn = NT // 2
    for hf in range(2):
        mm_ = rope_tables(pos_seq[:, hf * hn:(hf + 1) * hn], hn, cos_s[:, hf * hn:(hf + 1) * hn, :], sin_s[:, hf * hn:(hf + 1) * hn, :], t_tabs)
        P.barrier()
        AR.release(mm_)
    mm_ = rope_tables(pos_own, NS, cos_o, sin_o, t_tabo)
    P.barrier()
    AR.release(mm_)

    xt = [AR.alloc(1024), AR.alloc(1024)]; t_xt = [Tok(), Tok()]
    hT = [AR.alloc([8, 128], BF16), AR.alloc([8, 128], BF16)]; t_hT = [Tok(), Tok()]
    scr = (AR.alloc(1024, BF16), Tok(), AR.alloc(4), Tok(), AR.alloc(1024, BF16), Tok())
    sq = AR.alloc(1024); t_sq = Tok()
    qn = AR.alloc(1024); t_qn = Tok()
    r1 = AR.alloc(1024); t_r1 = Tok()
    r2_ = AR.alloc(1024); t_r2 = Tok()
    qb = AR.alloc(1024, BF16); t_qb = Tok()
    sm = AR.alloc(64); t_sm = Tok()
    qTt = [AR.alloc(1024, BF16), AR.alloc(1024, BF16)]; t_qTt = [Tok(), Tok()]
    qiTt = [AR.alloc(512, BF16), AR.alloc(512, BF16)]; t_qiTt = [Tok(), Tok()]
    kib = AR.alloc(128, BF16); t_kib = Tok()
    qr = AR.alloc(512); t_qr = Tok()
    wab = AR.alloc(8); t_wab = Tok()

    def headnorm_rope(src_ps, t_src, H, gcol, cosv, sinv, t_tab, outb, t_outb):
        W_ = H * 64
        s3 = src_ps.rearrange("p (h d) -> p h d", h=H)
        A("act", lambda: S.activation(out=sq[:, 0:W_], in_=src_ps, func=AF.Square), [t_src], [t_sq])
        A("dve", lambda: V.tensor_reduce(out=sm[:, 0:H], in_=sq[:, 0:W_].rearrange("p (h d) -> p h d", h=H), axis=AX.X, op=ALU.add),
          [t_sq], [t_sm])
        A("act", lambda: S.activation(out=sm[:, 16:16 + H], in_=sm[:, 0:H], func=AF.Sqrt, scale=1.0 / 64, bias=eps_t[:, 0:1]),
          [t_sm, t_eps], [t_sm])
        A("dve", lambda: V.reciprocal(out=sm[:, 32:32 + H], in_=sm[:, 16:16 + H]), [t_sm], [t_sm])
        q3 = qn[:, 0:W_].rearrange("p (h d) -> p h d", h=H)
        A("dve", lambda: V.tensor_tensor(out=q3, in0=s3, in1=sm[:, 32:32 + H].unsqueeze(2).to_broadcast([128, H, 64]), op=ALU.mult),
          [t_src, t_sm], [t_qn])
        A("pool", lambda: G.tensor_tensor(out=q3, in0=q3, in1=hv[:, gcol:gcol + 64].unsqueeze(1).to_broadcast([128, H, 64]), op=ALU.mult),
          [t_qn, t_hv], [t_qn])
        rope(q3, t_qn, H, cosv, sinv, t_tab, outb, t_outb)

    def rope(q3, t_q3, H, cosv, sinv, t_tab, outb, t_outb):
        W_ = H * 64
        a3 = r1[:, 0:W_].rearrange("p (h d) -> p h d", h=H)
        b3 = r2_[:, 0:W_].rearrange("p (h d) -> p h d", h=H)
        o3 = outb[:, 0:W_].rearrange("p (h d) -> p h d", h=H)
        cb = cosv.unsqueeze(1).to_broadcast([128, H, 32])
        sb_ = sinv.unsqueeze(1).to_broadcast([128, H, 32])
        A("pool", lambda: G.tensor_tensor(out=a3[:, :, 0:32], in0=q3[:, :, 0:32], in1=cb, op=ALU.mult), [t_q3, t_tab], [t_r1])
        A("pool", lambda: G.tensor_tensor(out=a3[:, :, 32:64], in0=q3[:, :, 32:64], in1=cb, op=ALU.mult), [t_q3, t_tab], [t_r1])
        A("dve", lambda: V.tensor_tensor(out=b3[:, :, 0:32], in0=q3[:, :, 32:64], in1=sb_, op=ALU.mult), [t_q3, t_tab], [t_r2])
        A("dve", lambda: V.tensor_tensor(out=b3[:, :, 32:64], in0=q3[:, :, 0:32], in1=sb_, op=ALU.mult), [t_q3, t_tab], [t_r2])
        A("pool", lambda: G.tensor_tensor(out=o3[:, :, 0:32], in0=a3[:, :, 0:32], in1=b3[:, :, 0:32], op=ALU.subtract), [t_r1, t_r2], [t_outb])
        A("pool", lambda: G.tensor_tensor(out=o3[:, :, 32:64], in0=a3[:, :, 32:64], in1=b3[:, :, 32:64], op=ALU.add), [t_r1, t_r2], [t_outb])

    def proj(dst_bank, ncols, c0, hTb, t_hTb):
        for k in range(8):
            A("pe", (lambda k=k: PE.matmul(pbank[dst_bank][:, 0:ncols], lhsT=hTb[:, k, :], rhs=Win[:, k, c0:c0 + ncols],
                                           start=(k == 0), stop=(k == 7))), [t_hTb, t_Win[k]], [tpb[dst_bank]])

    gs1_0 = gscT[:, 0:8]
    sh1_0 = mod_cols(0, 0)
    for j in range(NT):
        b = j % 2
        DMA("sp", xt[b], x_seq[j * 128:(j + 1) * 128, :], W=[t_xt[b]], key="xt%d" % b)
        norm_T(xt[b], t_xt[b], gs1_0, sh1_0, hT[b], t_hT[b], scr, 0)
        proj(1, 512, 1024, hT[b], t_hT[b])
        proj(2, 128, 2048, hT[b], t_hT[b])
        cj, sj = cos_s[:, j, :], sin_s[:, j, :]
        headnorm_rope(pbank[1][:, 0:256], tpb[1], 4, 64, cj, sj, t_tabs, qb, t_qb)
        A("act", (lambda j=j: S.copy(out=Vaug[:, j, :, 0:64], in_=pbank[1][:, 256:512].rearrange("p (g d) -> p g d", g=4))),
          [tpb[1]], [t_V[j]])
        pv3 = pbf(3)
        for i in range(2):
            A("pe", (lambda i=i: PE.transpose(out=pv3[:, i * 128:(i + 1) * 128], in_=qb[:, i * 128:(i + 1) * 128], identity=identB)),
              [t_qb, t_identB], [tpb[3]])
        A("act", (lambda j=j: S.copy(out=kT_all[:, :, j * 128:(j + 1) * 128], in_=pv3[:, 0:256].rearrange("p (i t) -> p i t", i=2))),
          [tpb[3]], [t_kT[j]])
        A("dve", lambda: V.bn_stats(out=sm[:, 48:54], in_=pbank[2][:, 0:64]), [tpb[2]], [t_sm])
        A("dve", lambda: V.bn_aggr(out=sm[:, 54:56], in_=sm[:, 48:54]), [t_sm], [t_sm])
        A("act", lambda: S.activation(out=sm[:, 56:57], in_=sm[:, 55:56], func=AF.Sqrt, scale=1.0, bias=eps_t[:, 0:1]), [t_sm, t_eps], [t_sm])
        A("dve", lambda: V.reciprocal(out=sm[:, 57:58], in_=sm[:, 56:57]), [t_sm], [t_sm])
        A("dve", lambda: V.tensor_scalar(out=qn[:, 0:64], in0=pbank[2][:, 0:64], scalar1=sm[:, 54:55], scalar2=sm[:, 57:58],
                                         op0=ALU.subtract, op1=ALU.mult), [tpb[2], t_sm], [t_qn])
        A("pool", lambda: G.tensor_tensor(out=qn[:, 0:64], in0=qn[:, 0:64], in1=hv[:, 128:192], op=ALU.mult), [t_qn, t_hv], [t_qn])
        A("pool", lambda: G.tensor_tensor(out=qn[:, 0:64], in0=qn[:, 0:64], in1=hv[:, 192:256], op=ALU.add), [t_qn, t_hv], [t_qn])
        rope(qn[:, 0:64].rearrange("p (h d) -> p h d", h=1), t_qn, 1, cj, sj, t_tabs, kib, t_kib)
        A("pool", lambda: G.tensor_copy(out=kib[:, 64:128], in_=kib[:, 0:64]), [t_kib], [t_kib])
        A("pe", lambda: PE.transpose(out=pv3[:, 256:384], in_=kib, identity=identB), [t_kib, t_identB], [tpb[3]])
        A("act", (lambda j=j: S.copy(out=kiT[:, j * 128:(j + 1) * 128], in_=pv3[:, 256:384])), [tpb[3]], [t_kiT[j]])

    WSC = (8 ** -0.5) * (64 ** -0.5)
    for i in range(NS):
        b = i % 2
        DMA("sp", xt[b], x_own[i * 128:(i + 1) * 128, :], W=[t_xt[b]], key="xt%d" % b)
        norm_T(xt[b], t_xt[b], gs1_0, sh1_0, hT[b], t_hT[b], scr, 0)
        proj(1, 512, 0, hT[b], t_hT[b])
        proj(2, 512, 512, hT[b], t_hT[b])
        proj(4, 512, 1536, hT[b], t_hT[b])
        proj(5, 8, 2176, hT[b], t_hT[b])
        ci, si = cos_o[:, i, :], sin_o[:, i, :]
        for hh in range(2):
            headnorm_rope(pbank[1 + hh][:], tpb[1 + hh], 8, 0, ci, si, t_tabo, qb, t_qb)
            pv3 = pbf(3)
            for jj in range(4):
                A("pe", (lambda jj=jj, hh=hh: PE.transpose(out=pv3[:, (hh * 4 + jj) * 128:(hh * 4 + jj + 1) * 128],
                                                          in_=qb[:, jj * 128:(jj + 1) * 128], identity=identB)),
                  [t_qb, t_identB], [tpb[3]])
        A("act", (lambda b=b: S.copy(out=qTt[b], in_=pbf(3))), [tpb[3]], [t_qTt[b]])
        DMA("sp", qTs[i], qTt[b], R=[t_qTt[b]], W=[t_qTs[i]], key="qTs")
        A("act", (lambda i=i: S.activation(out=wsign[:, i, :], in_=pbank[5][:, 0:8], func=AF.Sign)), [tpb[5]], [t_wsign])
        A("act", lambda: S.activation(out=wab, in_=pbank[5][:, 0:8], func=AF.Abs, scale=WSC), [tpb[5]], [t_wab])
        A("act", lambda: S.copy(out=qn[:, 0:512], in_=pbank[4][:]), [tpb[4]], [t_qn])
        rope(qn[:, 0:512].rearrange("p (h d) -> p h d", h=8), t_qn, 8, ci, si, t_tabo, qr, t_qr)
        A("dve", lambda: V.tensor_tensor(out=qb[:, 0:512].rearrange("p (h d) -> p h d", h=8),
                                         in0=qr[:, 0:512].rearrange("p (h d) -> p h d", h=8),
                                         in1=wab.unsqueeze(2).to_broadcast([128, 8, 64]), op=ALU.mult),
          [t_qr, t_wab], [t_qb])
        pv6 = pbf(6)
        for jj in range(4):
            A("pe", (lambda jj=jj: PE.transpose(out=pv6[:, jj * 128:(jj + 1) * 128], in_=qb[:, jj * 128:(jj + 1) * 128], identity=identB)),
              [t_qb, t_identB], [tpb[6]])
        A("act", (lambda b=b: S.copy(out=qiTt[b], in_=pv6[:, 0:512])), [tpb[6]], [t_qiTt[b]])
        DMA("sp", qiTs[i], qiTt[b], R=[t_qiTt[b]], W=[t_qiTs[i]], key="qiTs")
    P.barrier()
    if stop in ("a1", "a1x"):
        dump(kT_all[:, 0, 0:512], t_kT, 512, bf=True)
        dump(kT_all[:, 1, 0:512], t_kT, 512, bf=True)
        dump(kiT[:, 0:512], t_kiT, 512, bf=True)
        dump(Vaug[:, 0:4, :, :].rearrange("p a g d -> p (a g d)"), t_V, 1056, bf=True)
        dump(qTt[(NS - 1) % 2], t_qTt, 1024, bf=True)
        dump(qiTt[(NS - 1) % 2], t_qiTt, 512, bf=True)
        dump(wsign.rearrange("p a h -> p (a h)"), [t_wsign], NS * 8)
        dump(cos_s.rearrange("p a h -> p (a h)"), [t_tabs], NT * 32)
        dump(sin_s.rearrange("p a h -> p (a h)"), [t_tabs], NT * 32)
        P.emit()
        return nc, P, AR
    AR.release(mA1)

    Wo = AR.alloc([8, 1024], BF16); t_Wo = Tok()
    for hh in range(2):
        DMA("pool", Wo[hh * 64:(hh + 1) * 64, :, :], w_out[hh * 512:(hh + 1) * 512, :].rearrange("(j d) c -> d j c", d=64),
            W=[t_Wo], key="wo", max_dma_last_dim=4096)
    I_ = AR.alloc(T); t_I = Tok()
    mask01 = AR.alloc(T, BF16); t_mask = Tok()
    junk8 = AR.alloc(T, U8); t_junk8 = Tok()
    qTb = [AR.alloc(1024, BF16), AR.alloc(1024, BF16)]; t_qTb = [Tok(), Tok()]
    qiTb = [AR.alloc(512, BF16), AR.alloc(512, BF16)]; t_qiTb = [Tok(), Tok()]
    NR = 4
    Rb = [AR.alloc(512, BF16) for _ in range(NR)]; t_Rb = [Tok() for _ in range(NR)]
    Dh = AR.alloc([8, 128], BF16); t_Dh = Tok()
    biasb = [AR.alloc(512, BF16), AR.alloc(512, BF16)]; t_biasb = [Tok(), Tok()]
    NPB = 3
    Pexp = [AR.alloc(512, BF16) for _ in range(NPB)]; t_Pexp = [Tok() for _ in range(NPB)]
    NPM = 4
    Pm = [AR.alloc(512, BF16) for _ in range(NPM)]; t_Pm = [Tok() for _ in range(NPM)]
    maskT = [AR.alloc(128, BF16) for _ in range(3)]; t_maskT = [Tok() for _ in range(3)]
    rs = AR.alloc(512); t_rs = Tok()
    bcS = AR.alloc(512); t_bcS = Tok()
    numS = AR.alloc(512); t_numS = Tok()
    ys = AR.alloc(512); t_ys = Tok()
    oT_all = AR.alloc([2, 512], BF16); t_oTlo = Tok(); t_oThi = Tok()
    oT_tmp = AR.alloc([2, 512], BF16); t_oTtmp = Tok()
    xa = AR.alloc(1024); t_xa = Tok()
    ta = AR.alloc(1024); t_ta = Tok()
    bs = AR.alloc(16); t_bs = Tok()
    tr_banks = [0, 1, 2]
    trc = [0]

    def tbank():
        k = tr_banks[trc[0] % 3]
        trc[0] += 1
        return k

    rbc = [0]
    SCALE = 64 ** -0.5

    def indexer(i):
        b = i % 2
        E = cfg.ext[i]
        nch = (E + 3) // 4
        DMA("sp", qiTb[b], qiTs[i], R=[t_qiTs[i]], W=[t_qiTb[b]], key="qiTb%d" % b)
        DMA("sp", qTb[b], qTs[i], R=[t_qTs[i]], W=[t_qTb[b]], key="qTb%d" % b)
        for h in range(8):
            A("pool", (lambda h=h, i=i: G.tensor_scalar(out=Dh[:, h, :], in0=identB, scalar1=wsign[:, i, h:h + 1], scalar2=None, op0=ALU.mult)),
              [t_identB, t_wsign], [t_Dh])
        for c in range(nch):
            kts = [t_kiT[jj] for jj in range(c * 4, c * 4 + 4)]
            need_bias = c >= cfg.dchunk[i]
            if need_bias:
                bb = c % 2
                A("dve", (lambda c=c, i=i: V.tensor_scalar(out=bs[:, 6:7], in0=qpos[:, i:i + 1], scalar1=float(-512 * c), scalar2=None, op0=ALU.add)),
                  [t_qpos], [t_bs])
                A("dve", (lambda bb=bb: V.tensor_scalar(out=biasb[bb], in0=iota, scalar1=bs[:, 6:7], scalar2=-1e30, op0=ALU.is_gt, op1=ALU.mult)),
                  [t_iota, t_bs], [t_biasb[bb]])
            banks = []
            LA = 2

            def acc(hh, nb=need_bias):
                A("pe", (lambda hh=hh, rbb=banks[hh], nb=nb: PE.matmul(pbank[3][:], lhsT=Dh[:, hh, :], rhs=Rb[rbb], start=(hh == 0),
                                                                     stop=(hh == 7 and not nb))),
                  [t_Dh, t_Rb[banks[hh]]], [tpb[3]])

            for h in range(8):
                bk = tbank()
                hp, pr = h % 2, h // 2
                A("pe", (lambda bk=bk, hp=hp, pr=pr, c=c, b=b: PE.matmul(
                    pbank[bk][:], lhsT=qiTb[b][hp * 64:(hp + 1) * 64, pr * 128:(pr + 1) * 128],
                    rhs=kiT[hp * 64:(hp + 1) * 64, c * 512:(c + 1) * 512], start=True, stop=True)),
                  [t_qiTb[b]] + kts, [tpb[bk]])
                rb = rbc[0] % NR
                rbc[0] += 1
                A("act", (lambda bk=bk, rb=rb: S.activation(out=Rb[rb], in_=pbank[bk][:], func=AF.Relu)), [tpb[bk]], [t_Rb[rb]])
                banks.append(rb)
                if h >= LA:
                    acc(h - LA)
            for hh in range(8 - LA, 8):
                acc(hh)
            if need_bias:
                A("pe", (lambda bb=bb: PE.matmul(pbank[3][:], lhsT=identB, rhs=biasb[bb], start=False, stop=True)),
                  [t_identB, t_biasb[bb]], [tpb[3]])
            A("act", (lambda c=c: S.copy(out=I_[:, c * 512:(c + 1) * 512], in_=pbank[3][:])), [tpb[3]], [t_I])

    def topk(i):
        E = cfg.ext[i]
        Sn = ((E + 3) // 4) * 512
        K0 = min(256, Sn)
        A("dve", lambda: V.tensor_reduce(out=bs[:, 0:1], in_=I_[:, 0:Sn], axis=AX.X, op=ALU.max), [t_I], [t_bs])
        A("dve", lambda: V.tensor_reduce(out=bs[:, 1:2], in_=I_[:, 0:K0], axis=AX.X, op=ALU.min), [t_I], [t_bs])
        A("dve", lambda: V.tensor_scalar(out=bs[:, 1:2], in0=bs[:, 1:2], scalar1=-1e29, scalar2=None, op0=ALU.max), [t_bs], [t_bs])
        A("dve", lambda: V.tensor_tensor(out=bs[:, 2:3], in0=bs[:, 0:1], in1=bs[:, 1:2], op=ALU.subtract), [t_bs], [t_bs])
        for it in range(NIT):
            f = 2.0 ** (-(it + 1))
            A("dve", (lambda f=f: V.scalar_tensor_tensor(out=bs[:, 3:4], in0=bs[:, 2:3], scalar=f, in1=bs[:, 1:2], op0=ALU.mult, op1=ALU.add)),
              [t_bs], [t_bs])
            A("dve", lambda: V.tensor_scalar(out=junk8[:, 0:Sn], in0=I_[:, 0:Sn], scalar1=bs[:, 3:4], scalar2=None, op0=ALU.is_ge,
                                             op1=ALU.add, accum_out=bs[:, 4:5]), [t_I, t_bs], [t_junk8, t_bs])
            A("dve", (lambda f=f: V.tensor_scalar(out=bs[:, 5:6], in0=bs[:, 4:5], scalar1=cfg.topk - 0.5, scalar2=f, op0=ALU.is_gt, op1=ALU.mult)),
              [t_bs], [t_bs])
            A("dve", lambda: V.scalar_tensor_tensor(out=bs[:, 1:2], in0=bs[:, 5:6], scalar=bs[:, 2:3], in1=bs[:, 1:2], op0=ALU.mult, op1=ALU.add),
              [t_bs], [t_bs])
        A("dve", lambda: V.tensor_scalar(out=mask01[:, 0:E * 128], in0=I_[:, 0:E * 128], scalar1=bs[:, 1:2], scalar2=None, op0=ALU.is_ge),
          [t_I, t_bs], [t_mask])

    pbc = [0]

    def attention(i):
        b = i % 2
        E = cfg.ext[i]
        steps = [(kb, g) for kb in range(E) for g in range(4)]
        DL = 2
        pmof = {}

        def front(n):
            kb, g = steps[n]
            mt = kb % 3
            if g == 0:
                bk = tbank()
                A("pe", (lambda bk=bk, kb=kb: PE.transpose(out=pbf(bk)[:, 0:128], in_=mask01[:, kb * 128:(kb + 1) * 128], identity=identB)),
                  [t_mask, t_identB], [tpb[bk]])
                A("act", (lambda bk=bk, mt=mt: S.copy(out=maskT[mt], in_=pbf(bk)[:, 0:128])), [tpb[bk]], [t_maskT[mt]])
            hp, gi = g // 2, g % 2
            bk = tbank()
            A("pe", (lambda bk=bk, hp=hp, gi=gi, kb=kb, b=b: PE.matmul(
                pbank[bk][:], lhsT=kT_all[hp * 64:(hp + 1) * 64, gi, kb * 128:(kb + 1) * 128],
                rhs=qTb[b][hp * 64:(hp + 1) * 64, gi * 512:(gi + 1) * 512], start=True, stop=True)),
              [t_kT[kb], t_qTb[b]], [tpb[bk]])
            pe_ = pbc[0] % NPB
            pm_ = pbc[0] % NPM
            pbc[0] += 1
            pmof[n] = pm_
            A("act", (lambda bk=bk, pe_=pe_: S.activation(out=Pexp[pe_], in_=pbank[bk][:], func=AF.Exp, scale=SCALE)),
              [tpb[bk]], [t_Pexp[pe_]])
            A("pool", (lambda pe_=pe_, pm_=pm_, mt=mt: G.tensor_tensor(
                out=Pm[pm_].rearrange("p (r t) -> p r t", r=4), in0=Pexp[pe_].rearrange("p (r t) -> p r t", r=4),
                in1=maskT[mt].unsqueeze(1).to_broadcast([128, 4, 128]), op=ALU.mult)),
              [t_Pexp[pe_], t_maskT[mt]], [t_Pm[pm_]])

        def back(n):
            kb, g = steps[n]
            pm_ = pmof[n]
            A("pe", (lambda g=g, kb=kb, pm_=pm_, E=E: PE.matmul(pbank[4 + g][0:65, :], lhsT=Vaug[:, kb, g, 0:65], rhs=Pm[pm_],
                                                               start=(kb == 0), stop=(kb == E - 1))),
              [t_V[kb], t_Vones, t_Pm[pm_]], [tpb[4 + g]])

        for n in range(len(steps) + DL):
            if n < len(steps):
                front(n)
            if n - DL >= 0:
                back(n - DL)
        if ATT_PARTS < 2:
            return
        for g in range(4):
            A("act", (lambda g=g: S.activation(out=rs[64:65, :], in_=pbank[4 + g][64:65, :], func=AF.Ln)), [tpb[4 + g]], [t_rs])
            A("act", lambda: S.activation(out=rs[64:65, :], in_=rs[64:65, :], func=AF.Exp, scale=-1.0), [t_rs], [t_rs])
            bk = tbank()
            A("pe", (lambda bk=bk: PE.matmul(pbank[bk][0:64, :], lhsT=onesF[64:65, 0:64], rhs=rs[64:65, :], start=True, stop=True)),
              [t_onesF, t_rs], [tpb[bk]])
            A("act", (lambda bk=bk: S.copy(out=bcS[0:64, :], in_=pbank[bk][0:64, :])), [tpb[bk]], [t_bcS])
            A("act", (lambda g=g: S.copy(out=numS[0:64, :], in_=pbank[4 + g][0:64, :])), [tpb[4 + g]], [t_numS])
            if g < 2:
                A("pool", (lambda g=g: G.tensor_tensor(out=oT_all[0:64, g, :], in0=numS[0:64, :], in1=bcS[0:64, :], op=ALU.mult)),
                  [t_numS, t_bcS], [t_oTlo])
            else:
                A("pool", (lambda g=g: G.tensor_tensor(out=oT_tmp[0:64, g - 2, :], in0=numS[0:64, :], in1=bcS[0:64, :], op=ALU.mult)),
                  [t_numS, t_bcS], [t_oTtmp])
        if ATT_PARTS < 3:
            return
        DMA("sp", oT_all[64:128, :, :], oT_tmp[0:64, :, :], R=[t_oTtmp], W=[t_oThi], key="oThi")
        if ATT_PARTS < 4:
            return
        DMA("sp", xa, x_own[i * 128:(i + 1) * 128, :], W=[t_xa], key="xa")
        for ch in range(2):
            bk = tbank()
            for gi in range(2):
                for r in range(4):
                    j = gi * 4 + r
                    A("pe", (lambda bk=bk, gi=gi, r=r, j=j, ch=ch: PE.matmul(
                        pbank[bk][:], lhsT=oT_all[:, gi, r * 128:(r + 1) * 128], rhs=Wo[:, j, ch * 512:(ch + 1) * 512],
                        start=(j == 0), stop=(j == 7))), [t_oTlo, t_oThi, t_Wo], [tpb[bk]])
            A("act", (lambda bk=bk: S.copy(out=ys, in_=pbank[bk][:])), [tpb[bk]], [t_ys])
            A("pool", (lambda ch=ch: G.tensor_tensor(out=ta[:, ch * 512:(ch + 1) * 512], in0=ys,
                                                     in1=G0[:, ch * 512:(ch + 1) * 512], op=ALU.mult)), [t_ys, t_G0], [t_ta])
        A("pool", lambda: G.tensor_tensor(out=ta, in0=ta, in1=xa, op=ALU.add), [t_ta, t_xa], [t_ta])
        DMA("sp", x1s[i * 128:(i + 1) * 128, :], ta, R=[t_ta], W=[t_x1s[i]], key="x1s")

    indexer(0)
    if stop in ("a2i", "a2t", "a2a"):
        if stop in ("a2t", "a2a"):
            topk(0)
        if stop == "a2a":
            attention(0)
        dump(I_[:, 0:1024], [t_I], 1024)
        dump(mask01[:, 0:1024], [t_mask], 1024, bf=True)
        dump(bs, [t_bs], 16)
        dump(ta, [t_ta], 1024)
        P.emit()
        return nc, P, AR
    for i in range(NS):
        topk(i)
        if i + 1 < NS:
            indexer(i + 1)
        attention(i)
    P.barrier()
    if stop in ("a2", "a2x"):
        dump(I_[:, 0:1024], [t_I], 1024)
        dump(mask01[:, 0:1024], [t_mask], 1024, bf=True)
        dump(bs, [t_bs], 16)
        dump(ta, [t_ta], 1024)
        P.emit()
        return nc, P, AR
    AR.release(mA)

    Gt = [AR.alloc(1024) for _ in range(3)]; t_Gt = [Tok() for _ in range(3)]
    dgb = [AR.alloc(128), AR.alloc(128)]
    make_gate(mod_cols(0, 5), Gt[0], t_Gt[0], dgb, t_dgb, [0, 1])
    make_gate(mod_cols(1, 2), Gt[1], t_Gt[1], dgb, t_dgb, [2, 3])
    make_gate(mod_cols(1, 5), Gt[2], t_Gt[2], dgb, t_dgb, [0, 1])
    MS = 4
    xm = AR.alloc([MS, 1024]); t_xm = [Tok() for _ in range(MS)]
    hTm = AR.alloc([8, MS * 128], BF16); t_hTm = Tok()
    hTs = AR.alloc([8, 128], BF16); t_hTs = Tok()
    scrB = (AR.alloc(1024, BF16), Tok(), AR.alloc(4), Tok(), AR.alloc(1024, BF16), Tok())
    aT = AR.alloc([NFC, MS * 128], BF16); t_aT = Tok()
    Wd = AR.alloc([NFC, 1024], BF16); t_Wd = [Tok() for _ in range(4)]
    NWB = 2
    WA = [AR.alloc([8, 512], BF16) for _ in range(NWB)]; t_WA = [Tok() for _ in range(NWB)]
    WB = [AR.alloc([8, 512], BF16) for _ in range(NWB)]; t_WB = [Tok() for _ in range(NWB)]
    WC = [AR.alloc([8, 512], BF16) for _ in range(NWB)]; t_WC = [Tok() for _ in range(NWB)]
    sg = [AR.alloc(MS * 128), AR.alloc(MS * 128)]; t_sg = [Tok(), Tok()]
    zb = AR.alloc(MS * 128 + 2); t_zb = Tok()
    zc = AR.alloc(MS * 128); t_zc = Tok()
    carry = AR.alloc([8, 2]); t_carry = Tok()
    tb = AR.alloc(512); t_tb = Tok()
    A("dve", lambda: V.memset(carry, 0.0), [], [t_carry])
    wbc = [0]

    def norm_macro(ns, gs, shc):
        for s in range(ns):
            norm_T(xm[:, s, :], t_xm[s], gs, shc, hTs, t_hTs, scrB, 7)
            A("pool", (lambda s=s: G.tensor_copy(out=hTm[:, :, s * 128:(s + 1) * 128], in_=hTs)), [t_hTs], [t_hTm])

    def down(ns, nchunks, Gate, t_Gate):
        N = ns * 128
        for s in range(ns):
            for ch in range(2):
                bk = 4 + (s * 2 + ch) % 2
                for j in range(nchunks):
                    A("pe", (lambda bk=bk, j=j, s=s, ch=ch: PE.matmul(pbank[bk][:], lhsT=aT[:, j, s * 128:(s + 1) * 128],
                                                                     rhs=Wd[:, j, ch * 512:(ch + 1) * 512],
                                                                     start=(j == 0), stop=(j == nchunks - 1))),
                      [t_aT, t_Wd[j // 6]], [tpb[bk]])
                A("dve", (lambda bk=bk, ch=ch: V.tensor_tensor(out=tb, in0=pbank[bk][:], in1=Gate[:, ch * 512:(ch + 1) * 512], op=ALU.mult)),
                  [tpb[bk], t_Gate], [t_tb])
                A("pool", (lambda s=s, ch=ch: G.tensor_tensor(out=xm[:, s, ch * 512:(ch + 1) * 512], in0=xm[:, s, ch * 512:(ch + 1) * 512],
                                                             in1=tb, op=ALU.add)), [t_tb, t_xm[s]], [t_xm[s]])

    def ffn(L, ns, Gate, t_Gate):
        N = ns * 128
        norm_macro(ns, gscT[:, (2 * L + 1) * 8:(2 * L + 1) * 8 + 8], mod_cols(L, 3))
        gsrc = wg_b[L].rearrange("(k p) f -> p k f", p=128)
        usrc = wu_b[L].rearrange("(k p) f -> p k f", p=128)
        for fg in range(6):
            f0 = fg * 512
            fw = min(512, FF - f0)
            wb = wbc[0] % NWB
            wbc[0] += 1
            DMA("sp", WA[wb][:, :, 0:fw], gsrc[:, :, f0:f0 + fw], R=[t_wgb[L]], W=[t_WA[wb]], key="WA%d" % wb)
            DMA("sp", WB[wb][:, :, 0:fw], usrc[:, :, f0:f0 + fw], R=[t_wub[L]], W=[t_WB[wb]], key="WB%d" % wb)
            for fc in range(fw // 128):
                j = fg * 4 + fc
                bg, bu = (0, 1) if j % 2 == 0 else (2, 3)
                for k in range(8):
                    A("pe", (lambda bg=bg, k=k, fc=fc, wb=wb: PE.matmul(pbank[bg][:, 0:N], lhsT=WA[wb][:, k, fc * 128:(fc + 1) * 128],
                                                                       rhs=hTm[:, k, 0:N], start=(k == 0), stop=(k == 7))),
                      [t_WA[wb], t_hTm], [tpb[bg]])
                for k in range(8):
                    A("pe", (lambda bu=bu, k=k, fc=fc, wb=wb: PE.matmul(pbank[bu][:, 0:N], lhsT=WB[wb][:, k, fc * 128:(fc + 1) * 128],
                                                                       rhs=hTm[:, k, 0:N], start=(k == 0), stop=(k == 7))),
                      [t_WB[wb], t_hTm], [tpb[bu]])
                sb_ = j % 2
                A("act", (lambda bg=bg, sb_=sb_: S.activation(out=sg[sb_][:, 0:N], in_=pbank[bg][:, 0:N], func=AF.Silu)),
                  [tpb[bg]], [t_sg[sb_]])
                A("dve", (lambda bu=bu, sb_=sb_, j=j: V.tensor_tensor(out=aT[:, j, 0:N], in0=pbank[bu][:, 0:N], in1=sg[sb_][:, 0:N], op=ALU.mult)),
                  [tpb[bu], t_sg[sb_]], [t_aT])
        dsrc = wd_b[L].rearrange("(j p) c -> p j c", p=128)
        for q4 in range(0, NFC, 6):
            q5 = min(NFC, q4 + 6)
            DMA("sp", Wd[:, q4:q5, :], dsrc[:, q4:q5, :], R=[t_wdb[L]], W=[t_Wd[q4 // 6]], key="Wd%d" % (q4 // 6))
        down(ns, NFC, Gate, t_Gate)

    def convmix(ns):
        N = ns * 128
        norm_macro(ns, gscT[:, 16:24], mod_cols(1, 0))
        src = cwin_b.rearrange("(k p) f -> p k f", p=128)
        for cg in range(2):
            wb = wbc[0] % NWB
            wbc[0] += 1
            DMA("sp", WA[wb], src[:, :, cg * 512:(cg + 1) * 512], R=[t_cwinb], W=[t_WA[wb]], key="WA%d" % wb)
            DMA("sp", WB[wb], src[:, :, 1024 + cg * 512:1024 + (cg + 1) * 512], R=[t_cwinb], W=[t_WB[wb]], key="WB%d" % wb)
            DMA("sp", WC[wb], src[:, :, 2048 + cg * 512:2048 + (cg + 1) * 512], R=[t_cwinb], W=[t_WC[wb]], key="WC%d" % wb)
            for cc in range(4):
                cj = cg * 4 + cc
                for (bk, Wt, tW) in ((0, WA, t_WA), (1, WB, t_WB), (2, WC, t_WC)):
                    for k in range(8):
                        A("pe", (lambda bk=bk, Wt=Wt, k=k, cc=cc, wb=wb: PE.matmul(
                            pbank[bk][:, 0:N], lhsT=Wt[wb][:, k, cc * 128:(cc + 1) * 128], rhs=hTm[:, k, 0:N],
                            start=(k == 0), stop=(k == 7))), [tW[wb], t_hTm], [tpb[bk]])
                A("act", lambda: S.copy(out=sg[0][:, 0:N], in_=pbank[1][:, 0:N]), [tpb[1]], [t_sg[0]])
                A("dve", (lambda cj=cj: V.tensor_copy(out=zb[:, 0:2], in_=carry[:, cj, :])), [t_carry], [t_zb])
                A("dve", lambda: V.tensor_tensor(out=zb[:, 2:2 + N], in0=pbank[2][:, 0:N], in1=sg[0][:, 0:N], op=ALU.mult),
                  [tpb[2], t_sg[0]], [t_zb])
                A("dve", (lambda cj=cj: V.tensor_copy(out=carry[:, cj, :], in_=zb[:, N:N + 2])), [t_zb], [t_carry])
                A("dve", (lambda cj=cj: V.tensor_scalar(out=zc[:, 0:N], in0=zb[:, 2:2 + N], scalar1=cwT[:, cj * 3 + 2:cj * 3 + 3],
                                                       scalar2=None, op0=ALU.mult)), [t_zb, t_cwT], [t_zc])
                A("dve", (lambda cj=cj: V.scalar_tensor_tensor(out=zc[:, 0:N], in0=zb[:, 1:1 + N], scalar=cwT[:, cj * 3 + 1:cj * 3 + 2],
                                                              in1=zc[:, 0:N], op0=ALU.mult, op1=ALU.add)), [t_zb, t_cwT, t_zc], [t_zc])
                A("dve", (lambda cj=cj: V.scalar_tensor_tensor(out=zc[:, 0:N], in0=zb[:, 0:N], scalar=cwT[:, cj * 3:cj * 3 + 1],
                                                              in1=zc[:, 0:N], op0=ALU.mult, op1=ALU.add)), [t_zb, t_cwT, t_zc], [t_zc])
                A("dve", (lambda cj=cj: V.tensor_tensor(out=aT[:, cj, 0:N], in0=pbank[0][:, 0:N], in1=zc[:, 0:N], op=ALU.mult)),
                  [tpb[0], t_zc], [t_aT])
        osrc = cwout_b.rearrange("(j p) c -> p j c", p=128)
        DMA("sp", Wd[:, 0:6, :], osrc[:, 0:6, :], R=[t_cwoutb], W=[t_Wd[0]], key="Wd0")
        DMA("sp", Wd[:, 6:8, :], osrc[:, 6:8, :], R=[t_cwoutb], W=[t_Wd[1]], key="Wd1")
        down(ns, 8, Gt[1], t_Gt[1])

    nmac = (NS + MS - 1) // MS
    for m in range(nmac):
        s0 = m * MS
        ns = min(MS, NS - s0)
        for s in range(ns):
            DMA("sp", xm[:, s, :], x1s[(s0 + s) * 128:(s0 + s + 1) * 128, :], R=[t_x1s[s0 + s]], W=[t_xm[s]], key="xm%d" % s)
        ffn(0, ns, Gt[0], t_Gt[0])
        convmix(ns)
        ffn(1, ns, Gt[2], t_Gt[2])
        for s in range(ns):
            DMA("sp", out_d[(s0 + s) * 128:(s0 + s + 1) * 128, :], xm[:, s, :], R=[t_xm[s]], W=[t_out[s0 + s]], key="out%d" % s)

    P.emit()
    return nc, P, AR


import os
ATT_PARTS = int(os.environ.get('ATT_PARTS', '9'))
_CACHE = {}
STOP = None
LAST = None


def _host_inputs(cfg, r, x, c, positions, ada_w, ada_b, norm1_g, norm2_g, attn_w_in, attn_q_norm_g, attn_k_norm_g,
                 idx_k_ln_g, idx_k_ln_b, attn_w_out, conv_w_in, conv_w, conv_w_out, ffn_w_gate, ffn_w_up, ffn_w_down,
                 shared):
    b, role = r // 2, r % 2
    tiles = cfg.tilesA if role == 0 else cfg.tilesB
    T, NT, NS = cfg.T, cfg.NT, cfg.NS
    xs = np.ascontiguousarray(x[b])
    xo = np.ascontiguousarray(xs.reshape(NT, 128, D)[tiles].reshape(NS * 128, D))
    ps = np.ascontiguousarray(positions[b].reshape(NT, 128).T)
    po = np.ascontiguousarray(positions[b].reshape(NT, 128)[tiles].T)
    tok = np.arange(T, dtype=np.float32).reshape(NT, 128)
    qp = np.ascontiguousarray(tok[tiles].T)
    cT = np.ascontiguousarray(c[b].reshape(8, 128).T)
    d = dict(shared)
    d.update({"x_seq": xs, "x_own": xo, "pos_seq": ps.astype(np.int32), "pos_own": po.astype(np.int32), "qpos": qp, "cT": cT})
    return d


def _shared_inputs(ada_w, ada_b, norm1_g, norm2_g, attn_w_in, attn_q_norm_g, attn_k_norm_g, idx_k_ln_g, idx_k_ln_b,
                   attn_w_out, conv_w_in, conv_w, conv_w_out, ffn_w_gate, ffn_w_up, ffn_w_down):
    w = attn_w_in[0]
    qc = w[:, 0:1024].reshape(D, 16, 64)
    qperm = np.stack([qc[:, [j, 8 + j], :] for j in range(8)], axis=1).reshape(D, 1024)
    kc = w[:, 1024:1280].reshape(D, 4, 64)
    kperm = np.concatenate([kc[:, 0], kc[:, 2], kc[:, 1], kc[:, 3]], axis=1)
    vcol = w[:, 1280:1536]
    qic = w[:, 1536:2048]
    kic = w[:, 2048:2112]
    wic = w[:, 2112:2120]
    w_in = np.ascontiguousarray(np.concatenate([qperm, kperm, vcol, qic, kic, kic, wic], axis=1), dtype=np.float32)
    assert w_in.shape[1] == WCOLS
    vecs = np.concatenate([ada_b[0].reshape(48, 128), ada_b[1].reshape(48, 128), norm1_g.reshape(16, 128),
                           norm2_g.reshape(16, 128)], axis=0).astype(np.float32)
    hv = np.tile(np.concatenate([attn_q_norm_g[0], attn_k_norm_g[0], idx_k_ln_g[0], idx_k_ln_b[0]])[None, :], (128, 1)).astype(np.float32)
    cwT = np.ascontiguousarray(conv_w[0].T.reshape(8, 128, 3).transpose(1, 0, 2).reshape(128, 24)).astype(np.float32)
    invf = np.float32(10000.0) ** (-(np.arange(32, dtype=np.float32) * np.float32(2.0) / np.float32(64)))
    return {
        "w_in": w_in, "w_out": np.ascontiguousarray(attn_w_out[0]), "ada_w": np.ascontiguousarray(ada_w),
        "vecs": np.ascontiguousarray(vecs), "hv": np.ascontiguousarray(hv),
        "cw_in": np.ascontiguousarray(conv_w_in[0]), "cwT": cwT, "cw_out": np.ascontiguousarray(conv_w_out[0]),
        "wg": np.ascontiguousarray(ffn_w_gate), "wu": np.ascontiguousarray(ffn_w_up), "wd": np.ascontiguousarray(ffn_w_down),
        "ident": np.eye(128, dtype=np.float32), "invf": np.tile(invf.astype(np.float32)[None, :], (128, 1)),
        "iota": np.tile(np.arange(512, dtype=np.float32)[None, :], (128, 1)),
    }


def kernel(x, c, positions, ada_w, ada_b, norm1_g, norm2_g, attn_w_in, attn_q_norm_g, attn_k_norm_g, idx_k_ln_g,
           idx_k_ln_b, attn_w_out, conv_w_in, conv_w, conv_w_out, ffn_w_gate, ffn_w_up, ffn_w_down):
    args = [np.asarray(a) for a in (x, c, positions, ada_w, ada_b, norm1_g, norm2_g, attn_w_in, attn_q_norm_g,
                                     attn_k_norm_g, idx_k_ln_g, idx_k_ln_b, attn_w_out, conv_w_in, conv_w, conv_w_out,
                                     ffn_w_gate, ffn_w_up, ffn_w_down)]
    x = args[0]
    B, T, _ = x.shape
    cfg = Cfg(T)
    if (T, STOP) not in _CACHE:
        _CACHE[(T, STOP)] = build(cfg, STOP)[0]
    nc = _CACHE[(T, STOP)]
    shared = _shared_inputs(*args[3:])
    ncores = 2 * B
    in_maps = [_host_inputs(cfg, r, *args, shared) for r in range(ncores)]
    res = run_bass_kernel_spmd(nc, in_maps, core_ids=list(range(ncores)))
    global LAST
    LAST = res
    out = np.empty((B, T, D), dtype=np.float32)
    for r in range(ncores):
        b, role = r // 2, r % 2
        tiles = cfg.tilesA if role == 0 else cfg.tilesB
        halo = cfg.haloA if role == 0 else cfg.haloB
        o = np.asarray(res.results[r]["out"]).reshape(cfg.NS, 128, D)
        for s, t in enumerate(tiles):
            if s == halo:
                continue
            out[b, t * 128:(t + 1) * 128, :] = o[s]
    return out
```

```python
import math
import numpy as np
import ml_dtypes
import concourse.bass as bass
import concourse.mybir as mybir
from concourse.bass_utils import run_bass_kernel_spmd

F32 = mybir.dt.float32
BF16 = mybir.dt.bfloat16
I32 = mybir.dt.int32
U8 = mybir.dt.uint8
ALU = mybir.AluOpType
AF = mybir.ActivationFunctionType
AX = mybir.AxisListType

D = 1024
FF = 2816
NFC = FF // 128
WCOLS = 2184
EPS = 1e-6
NIT = 25
TWO_PI = 2.0 * math.pi
C1 = 6.28125
C2 = TWO_PI - C1


class Tok:
    __slots__ = ("w", "r", "rd")

    def __init__(self):
        self.w = None
        self.r = {}
        self.rd = []


class Op:
    __slots__ = ("eng", "fn", "deps", "dma", "sig", "cnt", "dsem")

    def __init__(self, eng, fn, dma):
        self.eng = eng
        self.fn = fn
        self.deps = set()
        self.dma = dma
        self.sig = False
        self.cnt = 0
        self.dsem = None


class Prog:
    def __init__(self, nc):
        self.nc = nc
        self.ops = []
        self.engs = {"pe": nc.tensor, "act": nc.scalar, "dve": nc.vector,
                     "pool": nc.gpsimd, "sp": nc.sync}
        self.last = {}
        self.dmas_since_bar = []
        self.bar = {}

    def add(self, eng, fn, R=(), W=(), dma=None):
        idx = len(self.ops)
        op = Op(eng, fn, dma)
        deps = op.deps
        if eng in self.bar:
            deps.update(self.bar.pop(eng))
        for t in R:
            if t.w is not None:
                deps.add(t.w)
        for t in W:
            if t.w is not None:
                deps.add(t.w)
            deps.update(t.r.values())
            deps.update(t.rd)
        deps.discard(idx)
        for t in W:
            t.w = idx
            t.r = {}
            t.rd = []
        for t in R:
            if t.w == idx:
                continue
            if dma is not None:
                t.rd.append(idx)
            else:
                t.r[eng] = idx
        self.ops.append(op)
        if dma is None:
            self.last[eng] = idx
        else:
            self.dmas_since_bar.append(idx)
        return idx

    def barrier(self):
        s = set(self.last.values()) | set(self.dmas_since_bar)
        self.dmas_since_bar = []
        for e in self.engs:
            self.bar[e] = set(s) | self.bar.get(e, set())

    def emit(self, final_wait_eng="sp"):
        nc = self.nc
        ops = self.ops
        for op in ops:
            nd = set()
            for d in op.deps:
                dop = ops[d]
                if dop.dma is None and op.dma is None and dop.eng == op.eng == "pe":
                    continue
                nd.add(d)
                dop.sig = True
            op.deps = nd
        esem = {e: nc.semaphore("se_" + e).__enter__() for e in self.engs}
        dsem, dcnt = {}, {}
        ecnt = {e: 0 for e in self.engs}
        for op in ops:
            if op.dma is not None:
                if op.dma not in dsem:
                    dsem[op.dma] = nc.semaphore("sd_%d" % len(dsem)).__enter__()
                    dcnt[op.dma] = 0
                dcnt[op.dma] += 16
                op.cnt = dcnt[op.dma]
                op.dsem = dsem[op.dma]
                op.sig = True
            elif op.sig:
                ecnt[op.eng] += 1
                op.cnt = ecnt[op.eng]
        waited = {e: {} for e in self.engs}
        for op in ops:
            E = self.engs[op.eng]
            wd = waited[op.eng]
            need = {}
            for d in op.deps:
                dop = ops[d]
                if dop.dma is not None:
                    key, sem = ("d", dop.dma), dop.dsem
                else:
                    key, sem = ("e", dop.eng), esem[dop.eng]
                if need.get(key, (None, 0))[1] < dop.cnt:
                    need[key] = (sem, dop.cnt)
            for key, (sem, cnt) in need.items():
                if wd.get(key, 0) < cnt:
                    E.wait_ge(sem, cnt)
                    wd[key] = cnt
            ins = op.fn()
            if op.sig:
                ins.then_inc(op.dsem if op.dma is not None else esem[op.eng], 16 if op.dma is not None else 1)
        E = self.engs[final_wait_eng]
        for k, sem in dsem.items():
            E.wait_ge(sem, dcnt[k])
        for e in self.engs:
            if ecnt[e] > 0 and e != final_wait_eng:
                E.wait_ge(esem[e], ecnt[e])
        self.stats = (len(ops), ecnt, len(dsem))


def _dsize(dt):
    if dt in (F32, I32):
        return 4
    if dt == BF16:
        return 2
    return 1


class Arena:
    def __init__(self, nc, nbytes):
        self.t = nc.sbuf_tensor("arena", [128, nbytes // 4], F32).__enter__()
        self.cap = nbytes
        self.off = 0
        self.peak = 0

    def mark(self):
        return self.off

    def release(self, m):
        self.off = m

    def alloc(self, free, dt=F32, parts=128):
        if isinstance(free, int):
            free = [free]
        n = 1
        for f in free:
            n *= f
        sz = (n * _dsize(dt) + 63) // 64 * 64
        assert self.off + sz <= self.cap, "SBUF arena overflow: need %d have %d" % (self.off + sz, self.cap)
        ap = self.t[0:parts, self.off // 4:(self.off + sz) // 4]
        self.off += sz
        self.peak = max(self.peak, self.off)
        if dt != F32:
            ap = ap.bitcast(dt)
        ap = ap[:, 0:n]
        if len(free) == 2:
            ap = ap.rearrange("p (a b) -> p a b", a=free[0])
        elif len(free) == 3:
            ap = ap.rearrange("p (a b c) -> p a b c", a=free[0], b=free[1])
        return ap


class Cfg:
    def __init__(self, T):
        self.T = T
        self.NT = T // 128
        assert self.NT % 4 == 0
        CH = self.NT // 4
        self.CH = CH
        self.NS = 2 * CH + 1
        self.tilesA = list(range(CH)) + [3 * CH - 1] + list(range(3 * CH, 4 * CH))
        self.tilesB = [CH - 1] + list(range(CH, 3 * CH))
        self.ext = [max(a, b) + 1 for a, b in zip(self.tilesA, self.tilesB)]
        self.dchunk = [min(a, b) // 4 for a, b in zip(self.tilesA, self.tilesB)]
        self.haloA = CH
        self.haloB = 0
        self.topk = min(256, T // 4)


def build(cfg, stop=None):
    T, NT, NS = cfg.T, cfg.NT, cfg.NS
    nc = bass.Bass("TRN2", target_bir_lowering=False)
    P = Prog(nc)

    def din(name, shape, dt=F32):
        return nc.dram_tensor(name, list(shape), dt, kind="ExternalInput").ap()

    x_seq = din("x_seq", [T, D])
    x_own = din("x_own", [NS * 128, D])
    pos_seq = din("pos_seq", [128, NT], I32)
    pos_own = din("pos_own", [128, NS], I32)
    qpos_d = din("qpos", [128, NS])
    cT_d = din("cT", [128, 8])
    w_in = din("w_in", [D, WCOLS])
    w_out = din("w_out", [D, D])
    ada_w = din("ada_w", [2, D, 6 * D])
    vecs_d = din("vecs", [128, 128])
    hv_d = din("hv", [128, 256])
    cw_in = din("cw_in", [D, 3 * D])
    cwT_d = din("cwT", [128, 24])
    cw_out = din("cw_out", [D, D])
    wg_d = din("wg", [2, D, FF])
    wu_d = din("wu", [2, D, FF])
    wd_d = din("wd", [2, FF, D])
    ident_d = din("ident", [128, 128])
    invf_d = din("invf", [128, 32])
    iota_d = din("iota", [128, 512])
    out_d = nc.dram_tensor("out", [NS * 128, D], F32, kind="ExternalOutput").ap()
    dbg_d = nc.dram_tensor("dbg", [128, 8192], F32, kind="ExternalOutput").ap() if stop else None
    qTs = nc.dram_tensor("qTs", [NS, 128, 1024], BF16, kind="Internal").ap()
    qiTs = nc.dram_tensor("qiTs", [NS, 128, 512], BF16, kind="Internal").ap()
    x1s = nc.dram_tensor("x1s", [NS * 128, D], F32, kind="Internal").ap()
    wg_b = nc.dram_tensor("wg_b", [2, D, FF], BF16, kind="Internal").ap()
    wu_b = nc.dram_tensor("wu_b", [2, D, FF], BF16, kind="Internal").ap()
    wd_b = nc.dram_tensor("wd_b", [2, FF, D], BF16, kind="Internal").ap()
    cwin_b = nc.dram_tensor("cwin_b", [D, 3 * D], BF16, kind="Internal").ap()
    cwout_b = nc.dram_tensor("cwout_b", [D, D], BF16, kind="Internal").ap()
    t_wgb = [Tok(), Tok()]; t_wub = [Tok(), Tok()]; t_wdb = [Tok(), Tok()]; t_cwinb = Tok(); t_cwoutb = Tok()
    t_qTs = [Tok() for _ in range(NS)]
    t_qiTs = [Tok() for _ in range(NS)]
    t_x1s = [Tok() for _ in range(NS)]
    t_out = [Tok() for _ in range(NS)]

    AR = Arena(nc, 206 * 1024)
    pbank = [nc.psum_tensor("pb%d" % k, [128, 512], F32).__enter__() for k in range(8)]
    tpb = [Tok() for _ in range(8)]

    def pbf(k):
        return pbank[k][:].bitcast(BF16)

    V, S, G, PE = nc.vector, nc.scalar, nc.gpsimd, nc.tensor

    def A(eng, fn, R=(), W=()):
        P.add(eng, fn, R, W)

    dma_ctr = [0]

    def DMA(q, out, in_, R=(), W=(), key=None, **kw):
        if key is None:
            dma_ctr[0] += 1
            key = "k%d" % dma_ctr[0]
        e = {"sp": nc.sync, "pool": nc.gpsimd, "act": nc.scalar}[q]
        P.add(q, lambda: e.dma_start(out=out, in_=in_, **kw), R, W, dma=key)

    dbg_off = [0]

    def dump(ap2d, toks, n, bf=False):
        if stop.endswith("x"):
            return
        o = dbg_off[0]
        if bf:
            tmp = AR.alloc(n)
            tt = Tok()
            A("dve", lambda: V.tensor_copy(out=tmp, in_=ap2d), toks, [tt])
            DMA("sp", dbg_d[:, o:o + n], tmp, R=[tt], W=[Tok()])
        else:
            DMA("sp", dbg_d[:, o:o + n], ap2d, R=toks, W=[Tok()])
        dbg_off[0] += n

    identF = AR.alloc(128); t_identF = Tok()
    identB = AR.alloc(128, BF16); t_identB = Tok()
    onesF = AR.alloc(128); t_onesF = Tok()
    iota = AR.alloc(512); t_iota = Tok()
    vecT = AR.alloc(128); t_vecT = Tok()
    modT = AR.alloc(96); t_modT = Tok()
    gscT = AR.alloc(32); t_gscT = Tok()
    wsign = AR.alloc([NS, 8]); t_wsign = Tok()
    cwT = AR.alloc(24); t_cwT = Tok()
    qpos = AR.alloc(NS); t_qpos = Tok()
    hv = AR.alloc(256); t_hv = Tok()
    G0 = AR.alloc(1024); t_G0 = Tok()

    DMA("sp", identF, ident_d, W=[t_identF])
    DMA("sp", iota, iota_d, W=[t_iota])
    DMA("sp", cwT, cwT_d, W=[t_cwT])
    DMA("sp", qpos, qpos_d, W=[t_qpos])
    DMA("sp", hv, hv_d, W=[t_hv])
    A("dve", lambda: V.tensor_copy(out=identB, in_=identF), [t_identF], [t_identB])
    A("dve", lambda: V.memset(onesF, 1.0), [], [t_onesF])

    if stop == "pre":
        dump(identF, [t_identF], 128)
        dump(identB, [t_identB], 128, bf=True)
        dump(hv, [t_hv], 256)
        P.emit()
        return nc, P, AR
    m0 = AR.mark()
    vecs_sb = AR.alloc(128); t_vecs = Tok()
    cT_sb = AR.alloc(8); t_cT = Tok()
    cact2 = AR.alloc([8, 2]); t_cact = Tok()
    adaw = [AR.alloc([8, 512]) for _ in range(2)]
    t_adaw = [Tok(), Tok()]
    DMA("sp", vecs_sb, vecs_d, W=[t_vecs])
    DMA("sp", cT_sb, cT_d, W=[t_cT])
    A("pe", lambda: PE.transpose(out=pbank[0][:, 0:128], in_=vecs_sb, identity=identF), [t_vecs, t_identF], [tpb[0]])
    A("act", lambda: S.copy(out=vecT, in_=pbank[0][:, 0:128]), [tpb[0]], [t_vecT])
    A("act", lambda: S.activation(out=cact2[:, :, 0], in_=cT_sb, func=AF.Silu), [t_cT], [t_cact])
    A("act", lambda: S.activation(out=cact2[:, :, 1], in_=cT_sb, func=AF.Silu), [t_cT], [t_cact])
    n = 0
    for L in range(2):
        src = ada_w[L].rearrange("(k p) n -> p k n", p=128)
        for cg in range(12):
            b = n % 2
            n += 1
            DMA("sp", adaw[b], src[:, :, cg * 512:(cg + 1) * 512], W=[t_adaw[b]], key="adaw%d" % b)
            for m4 in range(4):
                m = L * 48 + cg * 4 + m4
                for k in range(8):
                    A("pe", (lambda b=b, m=m, m4=m4, k=k: PE.matmul(
                        pbank[1][:, 2 * m:2 * m + 2], lhsT=adaw[b][:, k, m4 * 128:(m4 + 1) * 128],
                        rhs=cact2[:, k, :], start=(k == 0), stop=(k == 7))),
                      [t_adaw[b], t_cact], [tpb[1]])
    if stop == "p0a":
        A("act", lambda: S.copy(out=modT, in_=pbank[1][:, 0:96]), [tpb[1]], [t_modT])
        dump(modT, [t_modT], 96)
        dump(vecT, [t_vecT], 128)
        dump(cact2.rearrange("p a b -> p (a b)"), [t_cact], 16)
        P.emit()
        return nc, P, AR
    A("dve", lambda: V.tensor_tensor(out=modT, in0=pbank[1][:, 0:192].rearrange("p (m t) -> p m t", t=2)[:, :, 0],
                                     in1=vecT[:, 0:96], op=ALU.add), [tpb[1], t_vecT], [t_modT])
    for L in range(2):
        for s in range(2):
            o = (2 * L + s) * 8
            sc = modT[:, L * 48 + (8 if s == 0 else 32): L * 48 + (16 if s == 0 else 40)]
            ng = vecT[:, 96 + s * 16 + L * 8: 96 + s * 16 + L * 8 + 8]
            A("dve", (lambda o=o, sc=sc, ng=ng: V.scalar_tensor_tensor(
                out=gscT[:, o:o + 8], in0=sc, scalar=1.0, in1=ng, op0=ALU.add, op1=ALU.mult)),
              [t_modT, t_vecT], [t_gscT])

    def mod_cols(L, which):
        return modT[:, L * 48 + which * 8: L * 48 + which * 8 + 8]

    def make_gate(gcols, dst, t_dst, dg_bufs, t_dg, banks):
        for j in range(8):
            b = j % 2
            A("dve", (lambda j=j, b=b: V.tensor_scalar(out=dg_bufs[b], in0=identF, scalar1=gcols[:, j:j + 1],
                                                      scalar2=None, op0=ALU.mult)),
              [t_identF, t_modT], [t_dg[b]])
            bk = banks[j // 4]
            A("pe", (lambda j=j, b=b, bk=bk: PE.matmul(pbank[bk][:, (j % 4) * 128:(j % 4 + 1) * 128], lhsT=onesF,
                                                      rhs=dg_bufs[b], start=True, stop=True)),
              [t_onesF, t_dg[b]], [tpb[bk]])
        for h in range(2):
            A("act", (lambda h=h: S.copy(out=dst[:, h * 512:(h + 1) * 512], in_=pbank[banks[h]][:])),
              [tpb[banks[h]]], [t_dst])

    if stop == "p0b":
        dump(modT, [t_modT], 96)
        dump(gscT, [t_gscT], 32)
        P.emit()
        return nc, P, AR
    dgb = [AR.alloc(128), AR.alloc(128)]
    t_dgb = [Tok(), Tok()]
    make_gate(mod_cols(0, 2), G0, t_G0, dgb, t_dgb, [2, 3])
    if stop == "p0c":
        dump(G0, [t_G0], 1024)
        P.emit()
        return nc, P, AR
    P.barrier()
    if stop == "p0":
        dump(modT, [t_modT], 96)
        dump(gscT, [t_gscT], 32)
        dump(G0, [t_G0], 1024)
        dump(vecT, [t_vecT], 128)
        P.emit()
        return nc, P, AR
    AR.release(m0)

    def norm_T(xt, t_xt, gs, shc, hT_dst, t_hT, scr, bank):
        junkb, t_junkb, ss, t_ss, xn, t_xn = scr
        A("act", lambda: S.activation(out=junkb, in_=xt, func=AF.Square, accum_out=ss[:, 0:1]), [t_xt], [t_junkb, t_ss])
        A("act", lambda: S.activation(out=ss[:, 1:2], in_=ss[:, 0:1], func=AF.Sqrt, scale=1.0 / D, bias=eps_t[:, 0:1]),
          [t_ss, t_eps], [t_ss])
        A("dve", lambda: V.reciprocal(out=ss[:, 2:3], in_=ss[:, 1:2]), [t_ss], [t_ss])
        A("act", lambda: S.activation(out=xn, in_=xt, func=AF.Identity, scale=ss[:, 2:3]), [t_xt, t_ss], [t_xn])
        pv = pbf(bank)
        for j in range(8):
            A("pe", (lambda j=j: PE.transpose(out=pv[:, j * 128:(j + 1) * 128], in_=xn[:, j * 128:(j + 1) * 128],
                                              identity=identB)), [t_xn, t_identB], [tpb[bank]])
        for j in range(8):
            A("act", (lambda j=j: S.activation(out=hT_dst[:, j, :], in_=pv[:, j * 128:(j + 1) * 128], func=AF.Identity,
                                               scale=gs[:, j:j + 1], bias=shc[:, j:j + 1])),
              [tpb[bank], t_gscT, t_modT], [t_hT])

    eps_t = AR.alloc(4); t_eps = Tok()
    A("dve", lambda: V.memset(eps_t, EPS), [], [t_eps])

    mA = AR.mark()
    kT_all = AR.alloc([2, T], BF16); t_kT = [Tok() for _ in range(NT)]
    Vaug = AR.alloc([NT, 4, 66], BF16); t_V = [Tok() for _ in range(NT)]
    kiT = AR.alloc(T, BF16); t_kiT = [Tok() for _ in range(NT)]
    t_Vones = Tok()
    A("pool", lambda: G.memset(Vaug[:, :, :, 64:65], 1.0), [], [t_Vones] + t_V)

    mA1 = AR.mark()
    Win = AR.alloc([8, WCOLS], BF16); t_Win = [Tok() for _ in range(8)]
    wsrc = w_in.rearrange("(k p) n -> p k n", p=128)
    for k in range(8):
        DMA("pool", Win[:, k, :], wsrc[:, k, :], W=[t_Win[k]], key="win%d" % k, max_dma_last_dim=4096)
    for L in range(2):
        DMA("pool", wg_b[L], wg_d[L], W=[t_wgb[L]], key="cwg%d" % L, max_dma_last_dim=4096)
        DMA("pool", wu_b[L], wu_d[L], W=[t_wub[L]], key="cwu%d" % L, max_dma_last_dim=4096)
        DMA("pool", wd_b[L], wd_d[L], W=[t_wdb[L]], key="cwd%d" % L, max_dma_last_dim=4096)
        if L == 0:
            DMA("pool", cwin_b, cw_in, W=[t_cwinb], key="ccwin", max_dma_last_dim=4096)
            DMA("pool", cwout_b, cw_out, W=[t_cwoutb], key="ccwout", max_dma_last_dim=4096)

    def rope_tables(pos_d, n, cos_t, sin_t, t_tab):
        m = AR.mark()
        pi_ = AR.alloc(n, I32); t_pi = Tok()
        pf = AR.alloc(n); ang = AR.alloc([n, 32]); u = AR.alloc([n, 32]); ki_ = AR.alloc([n, 32], I32)
        kf = AR.alloc([n, 32]); r = AR.alloc([n, 32]); r2 = AR.alloc([n, 32]); tmp = AR.alloc([n, 32])
        invt = AR.alloc(32)
        tk = Tok()
        DMA("sp", pi_, pos_d, W=[t_pi])
        DMA("sp", invt, invf_d, W=[tk])
        A("dve", lambda: V.tensor_copy(out=pf, in_=pi_), [t_pi], [tk])
        A("dve", lambda: V.tensor_tensor(out=ang, in0=pf.unsqueeze(2).to_broadcast([128, n, 32]),
                                         in1=invt.unsqueeze(1).to_broadcast([128, n, 32]), op=ALU.mult), [tk], [tk])
        A("dve", lambda: V.tensor_scalar(out=u, in0=ang, scalar1=1.0 / TWO_PI, scalar2=None, op0=ALU.mult), [tk], [tk])
        A("dve", lambda: V.tensor_copy(out=ki_, in_=u), [tk], [tk])
        A("dve", lambda: V.tensor_copy(out=kf, in_=ki_), [tk], [tk])
        A("dve", lambda: V.scalar_tensor_tensor(out=r, in0=kf, scalar=-C1, in1=ang, op0=ALU.mult, op1=ALU.add), [tk], [tk])
        A("dve", lambda: V.scalar_tensor_tensor(out=r, in0=kf, scalar=-C2, in1=r, op0=ALU.mult, op1=ALU.add), [tk], [tk])
        A("dve", lambda: V.tensor_scalar(out=r, in0=r, scalar1=-3.1415925, scalar2=3.1415925, op0=ALU.max, op1=ALU.min), [tk], [tk])
        A("dve", lambda: V.tensor_scalar(out=r2, in0=r, scalar1=math.pi / 2, scalar2=None, op0=ALU.add), [tk], [tk])
        A("dve", lambda: V.tensor_scalar(out=tmp, in0=r2, scalar1=math.pi, scalar2=-TWO_PI, op0=ALU.is_gt, op1=ALU.mult), [tk], [tk])
        A("dve", lambda: V.tensor_tensor(out=r2, in0=r2, in1=tmp, op=ALU.add), [tk], [tk])
        A("dve", lambda: V.tensor_scalar(out=r2, in0=r2, scalar1=-3.1415925, scalar2=3.1415925, op0=ALU.max, op1=ALU.min), [tk], [tk])
        A("act", lambda: S.activation(out=sin_t, in_=r, func=AF.Sin), [tk], [t_tab])
        A("act", lambda: S.activation(out=cos_t, in_=r2, func=AF.Sin), [tk], [t_tab])
        return m

    cos_s = AR.alloc([NT, 32]); sin_s = AR.alloc([NT, 32]); t_tabs = Tok()
    cos_o = AR.alloc([NS, 32]); sin_o = AR.alloc([NS, 32]); t_tabo = Tok()
    hn = NT // 2
    for hf in range(2):
        mm_ = rope_tables(pos_seq[:, hf * hn:(hf + 1) * hn], hn, cos_s[:, hf * hn:(hf + 1) * hn, :], sin_s[:, hf * hn:(hf + 1) * hn, :], t_tabs)
        P.barrier()
        AR.release(mm_)
    mm_ = rope_tables(pos_own, NS, cos_o, sin_o, t_tabo)
    P.barrier()
    AR.release(mm_)

    xt = [AR.alloc(1024), AR.alloc(1024)]; t_xt = [Tok(), Tok()]
    hT = [AR.alloc([8, 128], BF16), AR.alloc([8, 128], BF16)]; t_hT = [Tok(), Tok()]
    scr = (AR.alloc(1024, BF16), Tok(), AR.alloc(4), Tok(), AR.alloc(1024, BF16), Tok())
    sq = AR.alloc(1024); t_sq = Tok()
    qn = AR.alloc(1024); t_qn = Tok()
    r1 = AR.alloc(1024); t_r1 = Tok()
    r2_ = AR.alloc(1024); t_r2 = Tok()
    qb = AR.alloc(1024, BF16); t_qb = Tok()
    sm = AR.alloc(64); t_sm = Tok()
    qTt = [AR.alloc(1024, BF16), AR.alloc(1024, BF16)]; t_qTt = [Tok(), Tok()]
    qiTt = [AR.alloc(512, BF16), AR.alloc(512, BF16)]; t_qiTt = [Tok(), Tok()]
    kib = AR.alloc(128, BF16); t_kib = Tok()
    qr = AR.alloc(512); t_qr = Tok()
    wab = AR.alloc(8); t_wab = Tok()

    def headnorm_rope(src_ps, t_src, H, gcol, cosv, sinv, t_tab, outb, t_outb):
        W_ = H * 64
        s3 = src_ps.rearrange("p (h d) -> p h d", h=H)
        A("act", lambda: S.activation(out=sq[:, 0:W_], in_=src_ps, func=AF.Square), [t_src], [t_sq])
        A("dve", lambda: V.tensor_reduce(out=sm[:, 0:H], in_=sq[:, 0:W_].rearrange("p (h d) -> p h d", h=H), axis=AX.X, op=ALU.add),
          [t_sq], [t_sm])
        A("act", lambda: S.activation(out=sm[:, 16:16 + H], in_=sm[:, 0:H], func=AF.Sqrt, scale=1.0 / 64, bias=eps_t[:, 0:1]),
          [t_sm, t_eps], [t_sm])
        A("dve", lambda: V.reciprocal(out=sm[:, 32:32 + H], in_=sm[:, 16:16 + H]), [t_sm], [t_sm])
        q3 = qn[:, 0:W_].rearrange("p (h d) -> p h d", h=H)
        A("dve", lambda: V.tensor_tensor(out=q3, in0=s3, in1=sm[:, 32:32 + H].unsqueeze(2).to_broadcast([128, H, 64]), op=ALU.mult),
          [t_src, t_sm], [t_qn])
        A("pool", lambda: G.tensor_tensor(out=q3, in0=q3, in1=hv[:, gcol:gcol + 64].unsqueeze(1).to_broadcast([128, H, 64]), op=ALU.mult),
          [t_qn, t_hv], [t_qn])
        rope(q3, t_qn, H, cosv, sinv, t_tab, outb, t_outb)

    def rope(q3, t_q3, H, cosv, sinv, t_tab, outb, t_outb):
        W_ = H * 64
        a3 = r1[:, 0:W_].rearrange("p (h d) -> p h d", h=H)
        b3 = r2_[:, 0:W_].rearrange("p (h d) -> p h d", h=H)
        o3 = outb[:, 0:W_].rearrange("p (h d) -> p h d", h=H)
        cb = cosv.unsqueeze(1).to_broadcast([128, H, 32])
        sb_ = sinv.unsqueeze(1).to_broadcast([128, H, 32])
        A("pool", lambda: G.tensor_tensor(out=a3[:, :, 0:32], in0=q3[:, :, 0:32], in1=cb, op=ALU.mult), [t_q3, t_tab], [t_r1])
        A("pool", lambda: G.tensor_tensor(out=a3[:, :, 32:64], in0=q3[:, :, 32:64], in1=cb, op=ALU.mult), [t_q3, t_tab], [t_r1])
        A("dve", lambda: V.tensor_tensor(out=b3[:, :, 0:32], in0=q3[:, :, 32:64], in1=sb_, op=ALU.mult), [t_q3, t_tab], [t_r2])
        A("dve", lambda: V.tensor_tensor(out=b3[:, :, 32:64], in0=q3[:, :, 0:32], in1=sb_, op=ALU.mult), [t_q3, t_tab], [t_r2])
        A("pool", lambda: G.tensor_tensor(out=o3[:, :, 0:32], in0=a3[:, :, 0:32], in1=b3[:, :, 0:32], op=ALU.subtract), [t_r1, t_r2], [t_outb])
        A("pool", lambda: G.tensor_tensor(out=o3[:, :, 32:64], in0=a3[:, :, 32:64], in1=b3[:, :, 32:64], op=ALU.add), [t_r1, t_r2], [t_outb])

    def proj(dst_bank, ncols, c0, hTb, t_hTb):
        for k in range(8):
            A("pe", (lambda k=k: PE.matmul(pbank[dst_bank][:, 0:ncols], lhsT=hTb[:, k, :], rhs=Win[:, k, c0:c0 + ncols],
                                           start=(k == 0), stop=(k == 7))), [t_hTb, t_Win[k]], [tpb[dst_bank]])

    gs1_0 = gscT[:, 0:8]
    sh1_0 = mod_cols(0, 0)
    for j in range(NT):
        b = j % 2
        DMA("sp", xt[b], x_seq[j * 128:(j + 1) * 128, :], W=[t_xt[b]], key="xt%d" % b)
        norm_T(xt[b], t_xt[b], gs1_0, sh1_0, hT[b], t_hT[b], scr, 0)
        proj(1, 512, 1024, hT[b], t_hT[b])
        proj(2, 128, 2048, hT[b], t_hT[b])
        cj, sj = cos_s[:, j, :], sin_s[:, j, :]
        headnorm_rope(pbank[1][:, 0:256], tpb[1], 4, 64, cj, sj, t_tabs, qb, t_qb)
        A("act", (lambda j=j: S.copy(out=Vaug[:, j, :, 0:64], in_=pbank[1][:, 256:512].rearrange("p (g d) -> p g d", g=4))),
          [tpb[1]], [t_V[j]])
        pv3 = pbf(3)
        for i in range(2):
            A("pe", (lambda i=i: PE.transpose(out=pv3[:, i * 128:(i + 1) * 128], in_=qb[:, i * 128:(i + 1) * 128], identity=identB)),
              [t_qb, t_identB], [tpb[3]])
        A("act", (lambda j=j: S.copy(out=kT_all[:, :, j * 128:(j + 1) * 128], in_=pv3[:, 0:256].rearrange("p (i t) -> p i t", i=2))),
          [tpb[3]], [t_kT[j]])
        A("dve", lambda: V.bn_stats(out=sm[:, 48:54], in_=pbank[2][:, 0:64]), [tpb[2]], [t_sm])
        A("dve", lambda: V.bn_aggr(out=sm[:, 54:56], in_=sm[:, 48:54]), [t_sm], [t_sm])
        A("act", lambda: S.activation(out=sm[:, 56:57], in_=sm[:, 55:56], func=AF.Sqrt, scale=1.0, bias=eps_t[:, 0:1]), [t_sm, t_eps], [t_sm])
        A("dve", lambda: V.reciprocal(out=sm[:, 57:58], in_=sm[:, 56:57]), [t_sm], [t_sm])
        A("dve", lambda: V.tensor_scalar(out=qn[:, 0:64], in0=pbank[2][:, 0:64], scalar1=sm[:, 54:55], scalar2=sm[:, 57:58],
                                         op0=ALU.subtract, op1=ALU.mult), [tpb[2], t_sm], [t_qn])
        A("pool", lambda: G.tensor_tensor(out=qn[:, 0:64], in0=qn[:, 0:64], in1=hv[:, 128:192], op=ALU.mult), [t_qn, t_hv], [t_qn])
        A("pool", lambda: G.tensor_tensor(out=qn[:, 0:64], in0=qn[:, 0:64], in1=hv[:, 192:256], op=ALU.add), [t_qn, t_hv], [t_qn])
        rope(qn[:, 0:64].rearrange("p (h d) -> p h d", h=1), t_qn, 1, cj, sj, t_tabs, kib, t_kib)
        A("pool", lambda: G.tensor_copy(out=kib[:, 64:128], in_=kib[:, 0:64]), [t_kib], [t_kib])
        A("pe", lambda: PE.transpose(out=pv3[:, 256:384], in_=kib, identity=identB), [t_kib, t_identB], [tpb[3]])
        A("act", (lambda j=j: S.copy(out=kiT[:, j * 128:(j + 1) * 128], in_=pv3[:, 256:384])), [tpb[3]], [t_kiT[j]])

    WSC = (8 ** -0.5) * (64 ** -0.5)
    for i in range(NS):
        b = i % 2
        DMA("sp", xt[b], x_own[i * 128:(i + 1) * 128, :], W=[t_xt[b]], key="xt%d" % b)
        norm_T(xt[b], t_xt[b], gs1_0, sh1_0, hT[b], t_hT[b], scr, 0)
        proj(1, 512, 0, hT[b], t_hT[b])
        proj(2, 512, 512, hT[b], t_hT[b])
        proj(4, 512, 1536, hT[b], t_hT[b])
        proj(5, 8, 2176, hT[b], t_hT[b])
        ci, si = cos_o[:, i, :], sin_o[:, i, :]
        for hh in range(2):
            headnorm_rope(pbank[1 + hh][:], tpb[1 + hh], 8, 0, ci, si, t_tabo, qb, t_qb)
            pv3 = pbf(3)
            for jj in range(4):
                A("pe", (lambda jj=jj, hh=hh: PE.transpose(out=pv3[:, (hh * 4 + jj) * 128:(hh * 4 + jj + 1) * 128],
                                                          in_=qb[:, jj * 128:(jj + 1) * 128], identity=identB)),
                  [t_qb, t_identB], [tpb[3]])
        A("act", (lambda b=b: S.copy(out=qTt[b], in_=pbf(3))), [tpb[3]], [t_qTt[b]])
        DMA("sp", qTs[i], qTt[b], R=[t_qTt[b]], W=[t_qTs[i]], key="qTs")
        A("act", (lambda i=i: S.activation(out=wsign[:, i, :], in_=pbank[5][:, 0:8], func=AF.Sign)), [tpb[5]], [t_wsign])
        A("act", lambda: S.activation(out=wab, in_=pbank[5][:, 0:8], func=AF.Abs, scale=WSC), [tpb[5]], [t_wab])
        A("act", lambda: S.copy(out=qn[:, 0:512], in_=pbank[4][:]), [tpb[4]], [t_qn])
        rope(qn[:, 0:512].rearrange("p (h d) -> p h d", h=8), t_qn, 8, ci, si, t_tabo, qr, t_qr)
        A("dve", lambda: V.tensor_tensor(out=qb[:, 0:512].rearrange("p (h d) -> p h d", h=8),
                                         in0=qr[:, 0:512].rearrange("p (h d) -> p h d", h=8),
                                         in1=wab.unsqueeze(2).to_broadcast([128, 8, 64]), op=ALU.mult),
          [t_qr, t_wab], [t_qb])
        pv6 = pbf(6)
        for jj in range(4):
            A("pe", (lambda jj=jj: PE.transpose(out=pv6[:, jj * 128:(jj + 1) * 128], in_=qb[:, jj * 128:(jj + 1) * 128], identity=identB)),
              [t_qb, t_identB], [tpb[6]])
        A("act", (lambda b=b: S.copy(out=qiTt[b], in_=pv6[:, 0:512])), [tpb[6]], [t_qiTt[b]])
        DMA("sp", qiTs[i], qiTt[b], R=[t_qiTt[b]], W=[t_qiTs[i]], key="qiTs")
    P.barrier()
    if stop in ("a1", "a1x"):
        dump(kT_all[:, 0, 0:512], t_kT, 512, bf=True)
        dump(kT_all[:, 1, 0:512], t_kT, 512, bf=True)
        dump(kiT[:, 0:512], t_kiT, 512, bf=True)
        dump(Vaug[:, 0:4, :, :].rearrange("p a g d -> p (a g d)"), t_V, 1056, bf=True)
        dump(qTt[(NS - 1) % 2], t_qTt, 1024, bf=True)
        dump(qiTt[(NS - 1) % 2], t_qiTt, 512, bf=True)
        dump(wsign.rearrange("p a h -> p (a h)"), [t_wsign], NS * 8)
        dump(cos_s.rearrange("p a h -> p (a h)"), [t_tabs], NT * 32)
        dump(sin_s.rearrange("p a h -> p (a h)"), [t_tabs], NT * 32)
        P.emit()
        return nc, P, AR
    AR.release(mA1)

    Wo = AR.alloc([8, 1024], BF16); t_Wo = Tok()
    for hh in range(2):
        DMA("pool", Wo[hh * 64:(hh + 1) * 64, :, :], w_out[hh * 512:(hh + 1) * 512, :].rearrange("(j d) c -> d j c", d=64),
            W=[t_Wo], key="wo", max_dma_last_dim=4096)
    I_ = AR.alloc(T); t_I = Tok()
    mask01 = AR.alloc(T, BF16); t_mask = Tok()
    junk8 = AR.alloc(T, U8); t_junk8 = Tok()
    qTb = [[AR.alloc(1024, BF16), AR.alloc(1024, BF16)] for _ in range(2)]; t_qTb = [Tok(), Tok()]
    qiTb = [[AR.alloc(512, BF16), AR.alloc(512, BF16)] for _ in range(2)]; t_qiTb = [Tok(), Tok()]
    for b_ in range(2):
        for hf_ in range(2):
            A("pool", (lambda b_=b_, hf_=hf_: G.memset(qTb[b_][hf_], 0.0)), [], [t_qTb[b_]])
            A("pool", (lambda b_=b_, hf_=hf_: G.memset(qiTb[b_][hf_], 0.0)), [], [t_qiTb[b_]])
    NR = 4
    Rb = [AR.alloc(512, BF16) for _ in range(NR)]; t_Rb = [Tok() for _ in range(NR)]
    Dh = AR.alloc([8, 128], BF16); t_Dh = Tok()
    biasb = [AR.alloc(512, BF16), AR.alloc(512, BF16)]; t_biasb = [Tok(), Tok()]
    NPB = 3
    Pexp = [AR.alloc(512, BF16) for _ in range(NPB)]; t_Pexp = [Tok() for _ in range(NPB)]
    NPM = 4
    Pm = [AR.alloc(512, BF16) for _ in range(NPM)]; t_Pm = [Tok() for _ in range(NPM)]
    maskT = [AR.alloc(128, BF16) for _ in range(3)]; t_maskT = [Tok() for _ in range(3)]
    rs = AR.alloc(512); t_rs = Tok()
    bcS = AR.alloc(512); t_bcS = Tok()
    ys = AR.alloc(512); t_ys = Tok()
    numS = ys; t_numS = t_ys
    oT_all = AR.alloc([2, 512], BF16); t_oTlo = Tok(); t_oThi = Tok()
    oT_tmp = AR.alloc([2, 512], BF16); t_oTtmp = Tok()
    xa = AR.alloc(512); t_xa = Tok()
    ta = AR.alloc(512); t_ta = Tok()
    bs = AR.alloc(16); t_bs = Tok()
    tr_banks = [0, 1, 2]
    trc = [0]

    def tbank():
        k = tr_banks[trc[0] % 3]
        trc[0] += 1
        return k

    rbc = [0]
    SCALE = 64 ** -0.5

    def indexer(i):
        b = i % 2
        E = cfg.ext[i]
        nch = (E + 3) // 4
        for hf_ in range(2):
            DMA("sp", qiTb[b][hf_][hf_ * 64:(hf_ + 1) * 64, :], qiTs[i][hf_ * 64:(hf_ + 1) * 64, :], R=[t_qiTs[i]], W=[t_qiTb[b]],
                key="qiTb%d" % b)
        for hf_ in range(2):
            DMA("sp", qTb[b][hf_][hf_ * 64:(hf_ + 1) * 64, :], qTs[i][hf_ * 64:(hf_ + 1) * 64, :], R=[t_qTs[i]], W=[t_qTb[b]],
                key="qTb%d" % b)
        for h in range(8):
            A("pool", (lambda h=h, i=i: G.tensor_scalar(out=Dh[:, h, :], in0=identB, scalar1=wsign[:, i, h:h + 1], scalar2=None, op0=ALU.mult)),
              [t_identB, t_wsign], [t_Dh])
        for c in range(nch):
            kts = [t_kiT[jj] for jj in range(c * 4, c * 4 + 4)]
            need_bias = c >= cfg.dchunk[i]
            if need_bias:
                bb = c % 2
                A("dve", (lambda c=c, i=i: V.tensor_scalar(out=bs[:, 6:7], in0=qpos[:, i:i + 1], scalar1=float(-512 * c), scalar2=None, op0=ALU.add)),
                  [t_qpos], [t_bs])
                A("dve", (lambda bb=bb: V.tensor_scalar(out=biasb[bb], in0=iota, scalar1=bs[:, 6:7], scalar2=-1e30, op0=ALU.is_gt, op1=ALU.mult)),
                  [t_iota, t_bs], [t_biasb[bb]])
            banks = []
            LA = 2

            def acc(hh, nb=need_bias):
                A("pe", (lambda hh=hh, rbb=banks[hh], nb=nb: PE.matmul(pbank[3][:], lhsT=Dh[:, hh, :], rhs=Rb[rbb], start=(hh == 0),
                                                                     stop=(hh == 7 and not nb))),
                  [t_Dh, t_Rb[banks[hh]]], [tpb[3]])

            for h in range(8):
                bk = tbank()
                hp, pr = h % 2, h // 2
                A("pe", (lambda bk=bk, hp=hp, pr=pr, c=c, b=b: PE.matmul(
                    pbank[bk][:], lhsT=qiTb[b][hp][:, pr * 128:(pr + 1) * 128],
                    rhs=kiT[:, c * 512:(c + 1) * 512], start=True, stop=True)),
                  [t_qiTb[b]] + kts, [tpb[bk]])
                rb = rbc[0] % NR
                rbc[0] += 1
                A("act", (lambda bk=bk, rb=rb: S.activation(out=Rb[rb], in_=pbank[bk][:], func=AF.Relu)), [tpb[bk]], [t_Rb[rb]])
                banks.append(rb)
                if h >= LA:
                    acc(h - LA)
            for hh in range(8 - LA, 8):
                acc(hh)
            if need_bias:
                A("pe", (lambda bb=bb: PE.matmul(pbank[3][:], lhsT=identB, rhs=biasb[bb], start=False, stop=True)),
                  [t_identB, t_biasb[bb]], [tpb[3]])
            A("act", (lambda c=c: S.copy(out=I_[:, c * 512:(c + 1) * 512], in_=pbank[3][:])), [tpb[3]], [t_I])

    def topk(i):
        E = cfg.ext[i]
        Sn = ((E + 3) // 4) * 512
        K0 = min(256, Sn)
        A("dve", lambda: V.tensor_reduce(out=bs[:, 0:1], in_=I_[:, 0:Sn], axis=AX.X, op=ALU.max), [t_I], [t_bs])
        A("dve", lambda: V.tensor_reduce(out=bs[:, 1:2], in_=I_[:, 0:K0], axis=AX.X, op=ALU.min), [t_I], [t_bs])
        A("dve", lambda: V.tensor_scalar(out=bs[:, 1:2], in0=bs[:, 1:2], scalar1=-1e29, scalar2=None, op0=ALU.max), [t_bs], [t_bs])
        A("dve", lambda: V.tensor_tensor(out=bs[:, 2:3], in0=bs[:, 0:1], in1=bs[:, 1:2], op=ALU.subtract), [t_bs], [t_bs])
        for it in range(NIT):
            f = 2.0 ** (-(it + 1))
            A("dve", (lambda f=f: V.tensor_scalar(out=bs[:, 3:4], in0=bs[:, 2:3], scalar1=f, scalar2=bs[:, 1:2], op0=ALU.mult, op1=ALU.add)),
              [t_bs], [t_bs])
            A("dve", lambda: V.tensor_scalar(out=junk8[:, 0:Sn], in0=I_[:, 0:Sn], scalar1=bs[:, 3:4], scalar2=None, op0=ALU.is_ge,
                                             op1=ALU.add, accum_out=bs[:, 4:5]), [t_I, t_bs], [t_junk8, t_bs])
            A("dve", (lambda f=f: V.tensor_scalar(out=bs[:, 5:6], in0=bs[:, 4:5], scalar1=cfg.topk - 0.5, scalar2=f, op0=ALU.is_gt, op1=ALU.mult)),
              [t_bs], [t_bs])
            A("dve", lambda: V.tensor_scalar(out=bs[:, 1:2], in0=bs[:, 5:6], scalar1=bs[:, 2:3], scalar2=bs[:, 1:2], op0=ALU.mult, op1=ALU.add),
              [t_bs], [t_bs])
        A("dve", lambda: V.tensor_scalar(out=mask01[:, 0:E * 128], in0=I_[:, 0:E * 128], scalar1=bs[:, 1:2], scalar2=None, op0=ALU.is_ge),
          [t_I, t_bs], [t_mask])

    pbc = [0]

    def attention(i):
        b = i % 2
        E = cfg.ext[i]
        steps = [(kb, g) for kb in range(E) for g in range(4)]
        DL = 2
        pmof = {}

        def front(n):
            kb, g = steps[n]
            mt = kb % 3
            if g == 0:
                bk = tbank()
                A("pe", (lambda bk=bk, kb=kb: PE.transpose(out=pbf(bk)[:, 0:128], in_=mask01[:, kb * 128:(kb + 1) * 128], identity=identB)),
                  [t_mask, t_identB], [tpb[bk]])
                A("act", (lambda bk=bk, mt=mt: S.copy(out=maskT[mt], in_=pbf(bk)[:, 0:128])), [tpb[bk]], [t_maskT[mt]])
            hp, gi = g // 2, g % 2
            bk = tbank()
            A("pe", (lambda bk=bk, hp=hp, gi=gi, kb=kb, b=b: PE.matmul(
                pbank[bk][:], lhsT=kT_all[:, gi, kb * 128:(kb + 1) * 128],
                rhs=qTb[b][hp][:, gi * 512:(gi + 1) * 512], start=True, stop=True)),
              [t_kT[kb], t_qTb[b]], [tpb[bk]])
            pe_ = pbc[0] % NPB
            pm_ = pbc[0] % NPM
            pbc[0] += 1
            pmof[n] = pm_
            A("act", (lambda bk=bk, pe_=pe_: S.activation(out=Pexp[pe_], in_=pbank[bk][:], func=AF.Exp, scale=SCALE)),
              [tpb[bk]], [t_Pexp[pe_]])
            A("pool", (lambda pe_=pe_, pm_=pm_, mt=mt: G.tensor_tensor(
                out=Pm[pm_].rearrange("p (r t) -> p r t", r=4), in0=Pexp[pe_].rearrange("p (r t) -> p r t", r=4),
                in1=maskT[mt].unsqueeze(1).to_broadcast([128, 4, 128]), op=ALU.mult)),
              [t_Pexp[pe_], t_maskT[mt]], [t_Pm[pm_]])

        def back(n):
            kb, g = steps[n]
            pm_ = pmof[n]
            A("pe", (lambda g=g, kb=kb, pm_=pm_, E=E: PE.matmul(pbank[4 + g][0:65, :], lhsT=Vaug[:, kb, g, 0:65], rhs=Pm[pm_],
                                                               start=(kb == 0), stop=(kb == E - 1))),
              [t_V[kb], t_Vones, t_Pm[pm_]], [tpb[4 + g]])

        for n in range(len(steps) + DL):
            if n < len(steps):
                front(n)
            if n - DL >= 0:
                back(n - DL)
        if ATT_PARTS < 2:
            return
        for g in range(4):
            A("act", (lambda g=g: S.activation(out=rs[64:65, :], in_=pbank[4 + g][64:65, :], func=AF.Ln)), [tpb[4 + g]], [t_rs])
            A("act", lambda: S.activation(out=rs[64:65, :], in_=rs[64:65, :], func=AF.Exp, scale=-1.0), [t_rs], [t_rs])
            bk = tbank()
            A("pe", (lambda bk=bk: PE.matmul(pbank[bk][0:64, :], lhsT=onesF[64:65, 0:64], rhs=rs[64:65, :], start=True, stop=True)),
              [t_onesF, t_rs], [tpb[bk]])
            A("act", (lambda bk=bk: S.copy(out=bcS[0:64, :], in_=pbank[bk][0:64, :])), [tpb[bk]], [t_bcS])
            A("act", (lambda g=g: S.copy(out=numS[0:64, :], in_=pbank[4 + g][0:64, :])), [tpb[4 + g]], [t_numS])
            if g < 2:
                A("pool", (lambda g=g: G.tensor_tensor(out=oT_all[0:64, g, :], in0=numS[0:64, :], in1=bcS[0:64, :], op=ALU.mult)),
                  [t_numS, t_bcS], [t_oTlo])
            else:
                A("pool", (lambda g=g: G.tensor_tensor(out=oT_tmp[0:64, g - 2, :], in0=numS[0:64, :], in1=bcS[0:64, :], op=ALU.mult)),
                  [t_numS, t_bcS], [t_oTtmp])
        if ATT_PARTS < 3:
            return
        DMA("sp", oT_all[64:128, :, :], oT_tmp[0:64, :, :], R=[t_oTtmp], W=[t_oThi], key="oThi")
        if ATT_PARTS < 4:
            return
        for ch in range(2):
            DMA("sp", xa, x_own[i * 128:(i + 1) * 128, ch * 512:(ch + 1) * 512], W=[t_xa], key="xa")
            bk = tbank()
            for gi in range(2):
                for r in range(4):
                    j = gi * 4 + r
                    A("pe", (lambda bk=bk, gi=gi, r=r, j=j, ch=ch: PE.matmul(
                        pbank[bk][:], lhsT=oT_all[:, gi, r * 128:(r + 1) * 128], rhs=Wo[:, j, ch * 512:(ch + 1) * 512],
                        start=(j == 0), stop=(j == 7))), [t_oTlo, t_oThi, t_Wo], [tpb[bk]])
            A("act", (lambda bk=bk: S.copy(out=ys, in_=pbank[bk][:])), [tpb[bk]], [t_ys])
            A("pool", (lambda ch=ch: G.tensor_tensor(out=ta, in0=ys, in1=G0[:, ch * 512:(ch + 1) * 512], op=ALU.mult)), [t_ys, t_G0], [t_ta])
            A("pool", lambda: G.tensor_tensor(out=ta, in0=ta, in1=xa, op=ALU.add), [t_ta, t_xa], [t_ta])
            DMA("sp", x1s[i * 128:(i + 1) * 128, ch * 512:(ch + 1) * 512], ta, R=[t_ta], W=[t_x1s[i]], key="x1s")

    indexer(0)
    if stop in ("a2i", "a2t", "a2a"):
        if stop in ("a2t", "a2a"):
            topk(0)
        if stop == "a2a":
            attention(0)
        dump(I_[:, 0:1024], [t_I], 1024)
        dump(mask01[:, 0:1024], [t_mask], 1024, bf=True)
        dump(bs, [t_bs], 16)
        dump(ta, [t_ta], 512)
        P.emit()
        return nc, P, AR
    for i in range(NS):
        topk(i)
        if i + 1 < NS:
            indexer(i + 1)
        attention(i)
    P.barrier()
    if stop in ("a2", "a2x"):
        dump(I_[:, 0:1024], [t_I], 1024)
        dump(mask01[:, 0:1024], [t_mask], 1024, bf=True)
        dump(bs, [t_bs], 16)
        dump(ta, [t_ta], 512)
        P.emit()
        return nc, P, AR
    AR.release(mA)

    Gt = [AR.alloc(1024) for _ in range(3)]; t_Gt = [Tok() for _ in range(3)]
    dgb = [AR.alloc(128), AR.alloc(128)]
    make_gate(mod_cols(0, 5), Gt[0], t_Gt[0], dgb, t_dgb, [0, 1])
    make_gate(mod_cols(1, 2), Gt[1], t_Gt[1], dgb, t_dgb, [2, 3])
    make_gate(mod_cols(1, 5), Gt[2], t_Gt[2], dgb, t_dgb, [0, 1])
    MS = 4
    xm = AR.alloc([MS, 1024]); t_xm = [Tok() for _ in range(MS)]
    hTm = AR.alloc([8, MS * 128], BF16); t_hTm = Tok()
    hTs = AR.alloc([8, 128], BF16); t_hTs = Tok()
    scrB = (AR.alloc(1024, BF16), Tok(), AR.alloc(4), Tok(), AR.alloc(1024, BF16), Tok())
    aT = AR.alloc([NFC, MS * 128], BF16); t_aT = Tok()
    Wd = AR.alloc([NFC, 1024], BF16); t_Wd = [Tok() for _ in range(4)]
    NWB = 2
    WA = [AR.alloc([8, 512], BF16) for _ in range(NWB)]; t_WA = [Tok() for _ in range(NWB)]
    WB = [AR.alloc([8, 512], BF16) for _ in range(NWB)]; t_WB = [Tok() for _ in range(NWB)]
    WC = [AR.alloc([8, 512], BF16) for _ in range(NWB)]; t_WC = [Tok() for _ in range(NWB)]
    sg = [AR.alloc(MS * 128), AR.alloc(MS * 128)]; t_sg = [Tok(), Tok()]
    zb = AR.alloc(MS * 128 + 2); t_zb = Tok()
    zc = AR.alloc(MS * 128); t_zc = Tok()
    carry = AR.alloc([8, 2]); t_carry = Tok()
    tb = AR.alloc(512); t_tb = Tok()
    A("dve", lambda: V.memset(carry, 0.0), [], [t_carry])
    wbc = [0]

    def norm_macro(ns, gs, shc):
        for s in range(ns):
            norm_T(xm[:, s, :], t_xm[s], gs, shc, hTs, t_hTs, scrB, 7)
            A("pool", (lambda s=s: G.tensor_copy(out=hTm[:, :, s * 128:(s + 1) * 128], in_=hTs)), [t_hTs], [t_hTm])

    def down(ns, nchunks, Gate, t_Gate):
        N = ns * 128
        for s in range(ns):
            for ch in range(2):
                bk = 4 + (s * 2 + ch) % 2
                for j in range(nchunks):
                    A("pe", (lambda bk=bk, j=j, s=s, ch=ch: PE.matmul(pbank[bk][:], lhsT=aT[:, j, s * 128:(s + 1) * 128],
                                                                     rhs=Wd[:, j, ch * 512:(ch + 1) * 512],
                                                                     start=(j == 0), stop=(j == nchunks - 1))),
                      [t_aT, t_Wd[j // 6]], [tpb[bk]])
                A("dve", (lambda bk=bk, ch=ch: V.tensor_tensor(out=tb, in0=pbank[bk][:], in1=Gate[:, ch * 512:(ch + 1) * 512], op=ALU.mult)),
                  [tpb[bk], t_Gate], [t_tb])
                A("pool", (lambda s=s, ch=ch: G.tensor_tensor(out=xm[:, s, ch * 512:(ch + 1) * 512], in0=xm[:, s, ch * 512:(ch + 1) * 512],
                                                             in1=tb, op=ALU.add)), [t_tb, t_xm[s]], [t_xm[s]])

    def ffn(L, ns, Gate, t_Gate):
        N = ns * 128
        norm_macro(ns, gscT[:, (2 * L + 1) * 8:(2 * L + 1) * 8 + 8], mod_cols(L, 3))
        gsrc = wg_b[L].rearrange("(k p) f -> p k f", p=128)
        usrc = wu_b[L].rearrange("(k p) f -> p k f", p=128)
        for fg in range(6):
            f0 = fg * 512
            fw = min(512, FF - f0)
            wb = wbc[0] % NWB
            wbc[0] += 1
            DMA("sp", WA[wb][:, :, 0:fw], gsrc[:, :, f0:f0 + fw], R=[t_wgb[L]], W=[t_WA[wb]], key="WA%d" % wb)
            DMA("sp", WB[wb][:, :, 0:fw], usrc[:, :, f0:f0 + fw], R=[t_wub[L]], W=[t_WB[wb]], key="WB%d" % wb)
            for fc in range(fw // 128):
                j = fg * 4 + fc
                bg, bu = (0, 1) if j % 2 == 0 else (2, 3)
                for k in range(8):
                    A("pe", (lambda bg=bg, k=k, fc=fc, wb=wb: PE.matmul(pbank[bg][:, 0:N], lhsT=WA[wb][:, k, fc * 128:(fc + 1) * 128],
                                                                       rhs=hTm[:, k, 0:N], start=(k == 0), stop=(k == 7))),
                      [t_WA[wb], t_hTm], [tpb[bg]])
                for k in range(8):
                    A("pe", (lambda bu=bu, k=k, fc=fc, wb=wb: PE.matmul(pbank[bu][:, 0:N], lhsT=WB[wb][:, k, fc * 128:(fc + 1) * 128],
                                                                       rhs=hTm[:, k, 0:N], start=(k == 0), stop=(k == 7))),
                      [t_WB[wb], t_hTm], [tpb[bu]])
                sb_ = j % 2
                A("act", (lambda bg=bg, sb_=sb_: S.activation(out=sg[sb_][:, 0:N], in_=pbank[bg][:, 0:N], func=AF.Silu)),
                  [tpb[bg]], [t_sg[sb_]])
                A("dve", (lambda bu=bu, sb_=sb_, j=j: V.tensor_tensor(out=aT[:, j, 0:N], in0=pbank[bu][:, 0:N], in1=sg[sb_][:, 0:N], op=ALU.mult)),
                  [tpb[bu], t_sg[sb_]], [t_aT])
        dsrc = wd_b[L].rearrange("(j p) c -> p j c", p=128)
        for q4 in range(0, NFC, 6):
            q5 = min(NFC, q4 + 6)
            DMA("sp", Wd[:, q4:q5, :], dsrc[:, q4:q5, :], R=[t_wdb[L]], W=[t_Wd[q4 // 6]], key="Wd%d" % (q4 // 6))
        down(ns, NFC, Gate, t_Gate)

    def convmix(ns):
        N = ns * 128
        norm_macro(ns, gscT[:, 16:24], mod_cols(1, 0))
        src = cwin_b.rearrange("(k p) f -> p k f", p=128)
        for cg in range(2):
            wb = wbc[0] % NWB
            wbc[0] += 1
            DMA("sp", WA[wb], src[:, :, cg * 512:(cg + 1) * 512], R=[t_cwinb], W=[t_WA[wb]], key="WA%d" % wb)
            DMA("sp", WB[wb], src[:, :, 1024 + cg * 512:1024 + (cg + 1) * 512], R=[t_cwinb], W=[t_WB[wb]], key="WB%d" % wb)
            DMA("sp", WC[wb], src[:, :, 2048 + cg * 512:2048 + (cg + 1) * 512], R=[t_cwinb], W=[t_WC[wb]], key="WC%d" % wb)
            for cc in range(4):
                cj = cg * 4 + cc
                for (bk, Wt, tW) in ((0, WA, t_WA), (1, WB, t_WB), (2, WC, t_WC)):
                    for k in range(8):
                        A("pe", (lambda bk=bk, Wt=Wt, k=k, cc=cc, wb=wb: PE.matmul(
                            pbank[bk][:, 0:N], lhsT=Wt[wb][:, k, cc * 128:(cc + 1) * 128], rhs=hTm[:, k, 0:N],
                            start=(k == 0), stop=(k == 7))), [tW[wb], t_hTm], [tpb[bk]])
                A("act", lambda: S.copy(out=sg[0][:, 0:N], in_=pbank[1][:, 0:N]), [tpb[1]], [t_sg[0]])
                A("dve", (lambda cj=cj: V.tensor_copy(out=zb[:, 0:2], in_=carry[:, cj, :])), [t_carry], [t_zb])
                A("dve", lambda: V.tensor_tensor(out=zb[:, 2:2 + N], in0=pbank[2][:, 0:N], in1=sg[0][:, 0:N], op=ALU.mult),
                  [tpb[2], t_sg[0]], [t_zb])
                A("dve", (lambda cj=cj: V.tensor_copy(out=carry[:, cj, :], in_=zb[:, N:N + 2])), [t_zb], [t_carry])
                A("dve", (lambda cj=cj: V.tensor_scalar(out=zc[:, 0:N], in0=zb[:, 2:2 + N], scalar1=cwT[:, cj * 3 + 2:cj * 3 + 3],
                                                       scalar2=None, op0=ALU.mult)), [t_zb, t_cwT], [t_zc])
                A("dve", (lambda cj=cj: V.scalar_tensor_tensor(out=zc[:, 0:N], in0=zb[:, 1:1 + N], scalar=cwT[:, cj * 3 + 1:cj * 3 + 2],
                                                              in1=zc[:, 0:N], op0=ALU.mult, op1=ALU.add)), [t_zb, t_cwT, t_zc], [t_zc])
                A("dve", (lambda cj=cj: V.scalar_tensor_tensor(out=zc[:, 0:N], in0=zb[:, 0:N], scalar=cwT[:, cj * 3:cj * 3 + 1],
                                                              in1=zc[:, 0:N], op0=ALU.mult, op1=ALU.add)), [t_zb, t_cwT, t_zc], [t_zc])
                A("dve", (lambda cj=cj: V.tensor_tensor(out=aT[:, cj, 0:N], in0=pbank[0][:, 0:N], in1=zc[:, 0:N], op=ALU.mult)),
                  [tpb[0], t_zc], [t_aT])
        osrc = cwout_b.rearrange("(j p) c -> p j c", p=128)
        DMA("sp", Wd[:, 0:6, :], osrc[:, 0:6, :], R=[t_cwoutb], W=[t_Wd[0]], key="Wd0")
        DMA("sp", Wd[:, 6:8, :], osrc[:, 6:8, :], R=[t_cwoutb], W=[t_Wd[1]], key="Wd1")
        down(ns, 8, Gt[1], t_Gt[1])

    nmac = (NS + MS - 1) // MS
    for m in range(nmac):
        s0 = m * MS
        ns = min(MS, NS - s0)
        for s in range(ns):
            DMA("sp", xm[:, s, :], x1s[(s0 + s) * 128:(s0 + s + 1) * 128, :], R=[t_x1s[s0 + s]], W=[t_xm[s]], key="xm%d" % s)
        ffn(0, ns, Gt[0], t_Gt[0])
        convmix(ns)
        ffn(1, ns, Gt[2], t_Gt[2])
        for s in range(ns):
            DMA("sp", out_d[(s0 + s) * 128:(s0 + s + 1) * 128, :], xm[:, s, :], R=[t_xm[s]], W=[t_out[s0 + s]], key="out%d" % s)

    P.emit()
    return nc, P, AR


import os
ATT_PARTS = int(os.environ.get('ATT_PARTS', '9'))
_CACHE = {}
STOP = None
LAST = None


def _host_inputs(cfg, r, x, c, positions, ada_w, ada_b, norm1_g, norm2_g, attn_w_in, attn_q_norm_g, attn_k_norm_g,
                 idx_k_ln_g, idx_k_ln_b, attn_w_out, conv_w_in, conv_w, conv_w_out, ffn_w_gate, ffn_w_up, ffn_w_down,
                 shared):
    b, role = r // 2, r % 2
    tiles = cfg.tilesA if role == 0 else cfg.tilesB
    T, NT, NS = cfg.T, cfg.NT, cfg.NS
    xs = np.ascontiguousarray(x[b])
    xo = np.ascontiguousarray(xs.reshape(NT, 128, D)[tiles].reshape(NS * 128, D))
    ps = np.ascontiguousarray(positions[b].reshape(NT, 128).T)
    po = np.ascontiguousarray(positions[b].reshape(NT, 128)[tiles].T)
    tok = np.arange(T, dtype=np.float32).reshape(NT, 128)
    qp = np.ascontiguousarray(tok[tiles].T)
    cT = np.ascontiguousarray(c[b].reshape(8, 128).T)
    d = dict(shared)
    d.update({"x_seq": xs, "x_own": xo, "pos_seq": ps.astype(np.int32), "pos_own": po.astype(np.int32), "qpos": qp, "cT": cT})
    return d


def _shared_inputs(ada_w, ada_b, norm1_g, norm2_g, attn_w_in, attn_q_norm_g, attn_k_norm_g, idx_k_ln_g, idx_k_ln_b,
                   attn_w_out, conv_w_in, conv_w, conv_w_out, ffn_w_gate, ffn_w_up, ffn_w_down):
    w = attn_w_in[0]
    qc = w[:, 0:1024].reshape(D, 16, 64)
    qperm = np.stack([qc[:, [j, 8 + j], :] for j in range(8)], axis=1).reshape(D, 1024)
    kc = w[:, 1024:1280].reshape(D, 4, 64)
    kperm = np.concatenate([kc[:, 0], kc[:, 2], kc[:, 1], kc[:, 3]], axis=1)
    vcol = w[:, 1280:1536]
    qic = w[:, 1536:2048]
    kic = w[:, 2048:2112]
    wic = w[:, 2112:2120]
    w_in = np.ascontiguousarray(np.concatenate([qperm, kperm, vcol, qic, kic, kic, wic], axis=1), dtype=np.float32)
    assert w_in.shape[1] == WCOLS
    vecs = np.concatenate([ada_b[0].reshape(48, 128), ada_b[1].reshape(48, 128), norm1_g.reshape(16, 128),
                           norm2_g.reshape(16, 128)], axis=0).astype(np.float32)
    hv = np.tile(np.concatenate([attn_q_norm_g[0], attn_k_norm_g[0], idx_k_ln_g[0], idx_k_ln_b[0]])[None, :], (128, 1)).astype(np.float32)
    cwT = np.ascontiguousarray(conv_w[0].T.reshape(8, 128, 3).transpose(1, 0, 2).reshape(128, 24)).astype(np.float32)
    invf = np.float32(10000.0) ** (-(np.arange(32, dtype=np.float32) * np.float32(2.0) / np.float32(64)))
    return {
        "w_in": w_in, "w_out": np.ascontiguousarray(attn_w_out[0]), "ada_w": np.ascontiguousarray(ada_w),
        "vecs": np.ascontiguousarray(vecs), "hv": np.ascontiguousarray(hv),
        "cw_in": np.ascontiguousarray(conv_w_in[0]), "cwT": cwT, "cw_out": np.ascontiguousarray(conv_w_out[0]),
        "wg": np.ascontiguousarray(ffn_w_gate), "wu": np.ascontiguousarray(ffn_w_up), "wd": np.ascontiguousarray(ffn_w_down),
        "ident": np.eye(128, dtype=np.float32), "invf": np.tile(invf.astype(np.float32)[None, :], (128, 1)),
        "iota": np.tile(np.arange(512, dtype=np.float32)[None, :], (128, 1)),
    }


def kernel(x, c, positions, ada_w, ada_b, norm1_g, norm2_g, attn_w_in, attn_q_norm_g, attn_k_norm_g, idx_k_ln_g,
           idx_k_ln_b, attn_w_out, conv_w_in, conv_w, conv_w_out, ffn_w_gate, ffn_w_up, ffn_w_down):
    args = [np.asarray(a) for a in (x, c, positions, ada_w, ada_b, norm1_g, norm2_g, attn_w_in, attn_q_norm_g,
                                     attn_k_norm_g, idx_k_ln_g, idx_k_ln_b, attn_w_out, conv_w_in, conv_w, conv_w_out,
                                     ffn_w_gate, ffn_w_up, ffn_w_down)]
    x = args[0]
    B, T, _ = x.shape
    cfg = Cfg(T)
    if (T, STOP) not in _CACHE:
        _CACHE[(T, STOP)] = build(cfg, STOP)[0]
    nc = _CACHE[(T, STOP)]
    shared = _shared_inputs(*args[3:])
    ncores = 2 * B
    in_maps = [_host_inputs(cfg, r, *args, shared) for r in range(ncores)]
    res = run_bass_kernel_spmd(nc, in_maps, core_ids=list(range(ncores)))
    global LAST
    LAST = res
    out = np.empty((B, T, D), dtype=np.float32)
    for r in range(ncores):
        b, role = r // 2, r % 2
        tiles = cfg.tilesA if role == 0 else cfg.tilesB
        halo = cfg.haloA if role == 0 else cfg.haloB
        o = np.asarray(res.results[r]["out"]).reshape(cfg.NS, 128, D)
        for s, t in enumerate(tiles):
            if s == halo:
                continue
            out[b, t * 128:(t + 1) * 128, :] = o[s]
    return out
```

```python
import math
import numpy as np
import ml_dtypes
import concourse.bass as bass
import concourse.mybir as mybir
from concourse.bass_utils import run_bass_kernel_spmd

F32 = mybir.dt.float32
BF16 = mybir.dt.bfloat16
I32 = mybir.dt.int32
U8 = mybir.dt.uint8
ALU = mybir.AluOpType
AF = mybir.ActivationFunctionType
AX = mybir.AxisListType

D = 1024
FF = 2816
NFC = FF // 128
WCOLS = 2184
EPS = 1e-6
NIT = 25
TWO_PI = 2.0 * math.pi
C1 = 6.28125
C2 = TWO_PI - C1


class Tok:
    __slots__ = ("w", "r", "rd")

    def __init__(self):
        self.w = None
        self.r = {}
        self.rd = []


class Op:
    __slots__ = ("eng", "fn", "deps", "dma", "sig", "cnt", "dsem")

    def __init__(self, eng, fn, dma):
        self.eng = eng
        self.fn = fn
        self.deps = set()
        self.dma = dma
        self.sig = False
        self.cnt = 0
        self.dsem = None


class Prog:
    def __init__(self, nc):
        self.nc = nc
        self.ops = []
        self.engs = {"pe": nc.tensor, "act": nc.scalar, "dve": nc.vector,
                     "pool": nc.gpsimd, "sp": nc.sync}
        self.last = {}
        self.dmas_since_bar = []
        self.bar = {}

    def add(self, eng, fn, R=(), W=(), dma=None):
        idx = len(self.ops)
        op = Op(eng, fn, dma)
        deps = op.deps
        if eng in self.bar:
            deps.update(self.bar.pop(eng))
        for t in R:
            if t.w is not None:
                deps.add(t.w)
        for t in W:
            if t.w is not None:
                deps.add(t.w)
            deps.update(t.r.values())
            deps.update(t.rd)
        deps.discard(idx)
        for t in W:
            t.w = idx
            t.r = {}
            t.rd = []
        for t in R:
            if t.w == idx:
                continue
            if dma is not None:
                t.rd.append(idx)
            else:
                t.r[eng] = idx
        self.ops.append(op)
        if dma is None:
            self.last[eng] = idx
        else:
            self.dmas_since_bar.append(idx)
        return idx

    def barrier(self):
        s = set(self.last.values()) | set(self.dmas_since_bar)
        self.dmas_since_bar = []
        for e in self.engs:
            self.bar[e] = set(s) | self.bar.get(e, set())

    def emit(self, final_wait_eng="sp"):
        nc = self.nc
        ops = self.ops
        for op in ops:
            nd = set()
            for d in op.deps:
                dop = ops[d]
                if dop.dma is None and op.dma is None and dop.eng == op.eng == "pe":
                    continue
                nd.add(d)
                dop.sig = True
            op.deps = nd
        esem = {e: nc.semaphore("se_" + e).__enter__() for e in self.engs}
        dsem, dcnt = {}, {}
        ecnt = {e: 0 for e in self.engs}
        for op in ops:
            if op.dma is not None:
                if op.dma not in dsem:
                    dsem[op.dma] = nc.semaphore("sd_%d" % len(dsem)).__enter__()
                    dcnt[op.dma] = 0
                dcnt[op.dma] += 16
                op.cnt = dcnt[op.dma]
                op.dsem = dsem[op.dma]
                op.sig = True
            elif op.sig:
                ecnt[op.eng] += 1
                op.cnt = ecnt[op.eng]
        waited = {e: {} for e in self.engs}
        for op in ops:
            E = self.engs[op.eng]
            wd = waited[op.eng]
            need = {}
            for d in op.deps:
                dop = ops[d]
                if dop.dma is not None:
                    key, sem = ("d", dop.dma), dop.dsem
                else:
                    key, sem = ("e", dop.eng), esem[dop.eng]
                if need.get(key, (None, 0))[1] < dop.cnt:
                    need[key] = (sem, dop.cnt)
            for key, (sem, cnt) in need.items():
                if wd.get(key, 0) < cnt:
                    E.wait_ge(sem, cnt)
                    wd[key] = cnt
            ins = op.fn()
            if op.sig:
                ins.then_inc(op.dsem if op.dma is not None else esem[op.eng], 16 if op.dma is not None else 1)
        E = self.engs[final_wait_eng]
        for k, sem in dsem.items():
            E.wait_ge(sem, dcnt[k])
        for e in self.engs:
            if ecnt[e] > 0 and e != final_wait_eng:
                E.wait_ge(esem[e], ecnt[e])
        self.stats = (len(ops), ecnt, len(dsem))


def _dsize(dt):
    if dt in (F32, I32):
        return 4
    if dt == BF16:
        return 2
    return 1


class Arena:
    def __init__(self, nc, nbytes):
        self.t = nc.sbuf_tensor("arena", [128, nbytes // 4], F32).__enter__()
        self.cap = nbytes
        self.off = 0
        self.peak = 0

    def mark(self):
        return self.off

    def release(self, m):
        self.off = m

    def alloc(self, free, dt=F32, parts=128):
        if isinstance(free, int):
            free = [free]
        n = 1
        for f in free:
            n *= f
        sz = (n * _dsize(dt) + 63) // 64 * 64
        assert self.off + sz <= self.cap, "SBUF arena overflow: need %d have %d" % (self.off + sz, self.cap)
        ap = self.t[0:parts, self.off // 4:(self.off + sz) // 4]
        self.off += sz
        self.peak = max(self.peak, self.off)
        if dt != F32:
            ap = ap.bitcast(dt)
        ap = ap[:, 0:n]
        if len(free) == 2:
            ap = ap.rearrange("p (a b) -> p a b", a=free[0])
        elif len(free) == 3:
            ap = ap.rearrange("p (a b c) -> p a b c", a=free[0], b=free[1])
        return ap


class Cfg:
    def __init__(self, T):
        self.T = T
        self.NT = T // 128
        assert self.NT % 4 == 0
        CH = self.NT // 4
        self.CH = CH
        self.NS = 2 * CH + 1
        self.tilesA = list(range(CH)) + [3 * CH - 1] + list(range(3 * CH, 4 * CH))
        self.tilesB = [CH - 1] + list(range(CH, 3 * CH))
        self.ext = [max(a, b) + 1 for a, b in zip(self.tilesA, self.tilesB)]
        self.dchunk = [min(a, b) // 4 for a, b in zip(self.tilesA, self.tilesB)]
        self.haloA = CH
        self.haloB = 0
        self.topk = min(256, T // 4)


def build(cfg, stop=None):
    T, NT, NS = cfg.T, cfg.NT, cfg.NS
    nc = bass.Bass("TRN2", target_bir_lowering=False)
    P = Prog(nc)

    def din(name, shape, dt=F32):
        return nc.dram_tensor(name, list(shape), dt, kind="ExternalInput").ap()

    x_seq = din("x_seq", [T, D])
    x_own = din("x_own", [NS * 128, D])
    pos_seq = din("pos_seq", [128, NT], I32)
    pos_own = din("pos_own", [128, NS], I32)
    qpos_d = din("qpos", [128, NS])
    cT_d = din("cT", [128, 8])
    w_in = din("w_in", [D, WCOLS])
    w_out = din("w_out", [D, D])
    ada_w = din("ada_w", [2, D, 6 * D])
    vecs_d = din("vecs", [128, 128])
    hv_d = din("hv", [128, 256])
    cw_in = din("cw_in", [D, 3 * D])
    cwT_d = din("cwT", [128, 24])
    cw_out = din("cw_out", [D, D])
    wg_d = din("wg", [2, D, FF])
    wu_d = din("wu", [2, D, FF])
    wd_d = din("wd", [2, FF, D])
    ident_d = din("ident", [128, 128])
    invf_d = din("invf", [128, 32])
    iota_d = din("iota", [128, 512])
    out_d = nc.dram_tensor("out", [NS * 128, D], F32, kind="ExternalOutput").ap()
    dbg_d = nc.dram_tensor("dbg", [128, 8192], F32, kind="ExternalOutput").ap() if stop else None
    qTs = nc.dram_tensor("qTs", [NS, 128, 1024], BF16, kind="Internal").ap()
    qiTs = nc.dram_tensor("qiTs", [NS, 128, 512], BF16, kind="Internal").ap()
    x1s = nc.dram_tensor("x1s", [NS * 128, D], F32, kind="Internal").ap()
    wg_b = nc.dram_tensor("wg_b", [2, D, FF], BF16, kind="Internal").ap()
    wu_b = nc.dram_tensor("wu_b", [2, D, FF], BF16, kind="Internal").ap()
    wd_b = nc.dram_tensor("wd_b", [2, FF, D], BF16, kind="Internal").ap()
    cwin_b = nc.dram_tensor("cwin_b", [D, 3 * D], BF16, kind="Internal").ap()
    cwout_b = nc.dram_tensor("cwout_b", [D, D], BF16, kind="Internal").ap()
    t_wgb = [Tok(), Tok()]; t_wub = [Tok(), Tok()]; t_wdb = [Tok(), Tok()]; t_cwinb = Tok(); t_cwoutb = Tok()
    t_qTs = [Tok() for _ in range(NS)]
    t_qiTs = [Tok() for _ in range(NS)]
    t_x1s = [Tok() for _ in range(NS)]
    t_out = [Tok() for _ in range(NS)]

    AR = Arena(nc, 206 * 1024)
    pbank = [nc.psum_tensor("pb%d" % k, [128, 512], F32).__enter__() for k in range(8)]
    tpb = [Tok() for _ in range(8)]

    def pbf(k):
        return pbank[k][:].bitcast(BF16)

    V, S, G, PE = nc.vector, nc.scalar, nc.gpsimd, nc.tensor

    def A(eng, fn, R=(), W=()):
        P.add(eng, fn, R, W)

    dma_ctr = [0]

    def DMA(q, out, in_, R=(), W=(), key=None, **kw):
        if key is None:
            dma_ctr[0] += 1
            key = "k%d" % dma_ctr[0]
        e = {"sp": nc.sync, "pool": nc.gpsimd, "act": nc.scalar}[q]
        P.add(q, lambda: e.dma_start(out=out, in_=in_, **kw), R, W, dma=key)

    dbg_off = [0]

    def dump(ap2d, toks, n, bf=False):
        if stop.endswith("x"):
            return
        o = dbg_off[0]
        if bf:
            tmp = AR.alloc(n)
            tt = Tok()
            A("dve", lambda: V.tensor_copy(out=tmp, in_=ap2d), toks, [tt])
            DMA("sp", dbg_d[:, o:o + n], tmp, R=[tt], W=[Tok()])
        else:
            DMA("sp", dbg_d[:, o:o + n], ap2d, R=toks, W=[Tok()])
        dbg_off[0] += n

    identF = AR.alloc(128); t_identF = Tok()
    identB = AR.alloc(128, BF16); t_identB = Tok()
    onesF = AR.alloc(128); t_onesF = Tok()
    iota = AR.alloc(512); t_iota = Tok()
    vecT = AR.alloc(128); t_vecT = Tok()
    modT = AR.alloc(96); t_modT = Tok()
    gscT = AR.alloc(32); t_gscT = Tok()
    wsign = AR.alloc([NS, 8]); t_wsign = Tok()
    cwT = AR.alloc(24); t_cwT = Tok()
    qpos = AR.alloc(NS); t_qpos = Tok()
    hv = AR.alloc(256); t_hv = Tok()
    G0 = AR.alloc(1024); t_G0 = Tok()

    DMA("sp", identF, ident_d, W=[t_identF])
    DMA("sp", iota, iota_d, W=[t_iota])
    DMA("sp", cwT, cwT_d, W=[t_cwT])
    DMA("sp", qpos, qpos_d, W=[t_qpos])
    DMA("sp", hv, hv_d, W=[t_hv])
    A("dve", lambda: V.tensor_copy(out=identB, in_=identF), [t_identF], [t_identB])
    A("dve", lambda: V.memset(onesF, 1.0), [], [t_onesF])

    if stop == "pre":
        dump(identF, [t_identF], 128)
        dump(identB, [t_identB], 128, bf=True)
        dump(hv, [t_hv], 256)
        P.emit()
        return nc, P, AR
    m0 = AR.mark()
    vecs_sb = AR.alloc(128); t_vecs = Tok()
    cT_sb = AR.alloc(8); t_cT = Tok()
    cact2 = AR.alloc([8, 2]); t_cact = Tok()
    adaw = [AR.alloc([8, 512]) for _ in range(2)]
    t_adaw = [Tok(), Tok()]
    DMA("sp", vecs_sb, vecs_d, W=[t_vecs])
    DMA("sp", cT_sb, cT_d, W=[t_cT])
    A("pe", lambda: PE.transpose(out=pbank[0][:, 0:128], in_=vecs_sb, identity=identF), [t_vecs, t_identF], [tpb[0]])
    A("act", lambda: S.copy(out=vecT, in_=pbank[0][:, 0:128]), [tpb[0]], [t_vecT])
    A("act", lambda: S.activation(out=cact2[:, :, 0], in_=cT_sb, func=AF.Silu), [t_cT], [t_cact])
    A("act", lambda: S.activation(out=cact2[:, :, 1], in_=cT_sb, func=AF.Silu), [t_cT], [t_cact])
    n = 0
    for L in range(2):
        src = ada_w[L].rearrange("(k p) n -> p k n", p=128)
        for cg in range(12):
            b = n % 2
            n += 1
            DMA("sp", adaw[b], src[:, :, cg * 512:(cg + 1) * 512], W=[t_adaw[b]], key="adaw%d" % b)
            for m4 in range(4):
                m = L * 48 + cg * 4 + m4
                for k in range(8):
                    A("pe", (lambda b=b, m=m, m4=m4, k=k: PE.matmul(
                        pbank[1][:, 2 * m:2 * m + 2], lhsT=adaw[b][:, k, m4 * 128:(m4 + 1) * 128],
                        rhs=cact2[:, k, :], start=(k == 0), stop=(k == 7))),
                      [t_adaw[b], t_cact], [tpb[1]])
    if stop == "p0a":
        A("act", lambda: S.copy(out=modT, in_=pbank[1][:, 0:96]), [tpb[1]], [t_modT])
        dump(modT, [t_modT], 96)
        dump(vecT, [t_vecT], 128)
        dump(cact2.rearrange("p a b -> p (a b)"), [t_cact], 16)
        P.emit()
        return nc, P, AR
    A("dve", lambda: V.tensor_tensor(out=modT, in0=pbank[1][:, 0:192].rearrange("p (m t) -> p m t", t=2)[:, :, 0],
                                     in1=vecT[:, 0:96], op=ALU.add), [tpb[1], t_vecT], [t_modT])
    for L in range(2):
        for s in range(2):
            o = (2 * L + s) * 8
            sc = modT[:, L * 48 + (8 if s == 0 else 32): L * 48 + (16 if s == 0 else 40)]
            ng = vecT[:, 96 + s * 16 + L * 8: 96 + s * 16 + L * 8 + 8]
            A("dve", (lambda o=o, sc=sc, ng=ng: V.scalar_tensor_tensor(
                out=gscT[:, o:o + 8], in0=sc, scalar=1.0, in1=ng, op0=ALU.add, op1=ALU.mult)),
              [t_modT, t_vecT], [t_gscT])

    def mod_cols(L, which):
        return modT[:, L * 48 + which * 8: L * 48 + which * 8 + 8]

    def make_gate(gcols, dst, t_dst, dg_bufs, t_dg, banks):
        for j in range(8):
            b = j % 2
            A("dve", (lambda j=j, b=b: V.tensor_scalar(out=dg_bufs[b], in0=identF, scalar1=gcols[:, j:j + 1],
                                                      scalar2=None, op0=ALU.mult)),
              [t_identF, t_modT], [t_dg[b]])
            bk = banks[j // 4]
            A("pe", (lambda j=j, b=b, bk=bk: PE.matmul(pbank[bk][:, (j % 4) * 128:(j % 4 + 1) * 128], lhsT=onesF,
                                                      rhs=dg_bufs[b], start=True, stop=True)),
              [t_onesF, t_dg[b]], [tpb[bk]])
        for h in range(2):
            A("act", (lambda h=h: S.copy(out=dst[:, h * 512:(h + 1) * 512], in_=pbank[banks[h]][:])),
              [tpb[banks[h]]], [t_dst])

    if stop == "p0b":
        dump(modT, [t_modT], 96)
        dump(gscT, [t_gscT], 32)
        P.emit()
        return nc, P, AR
    dgb = [AR.alloc(128), AR.alloc(128)]
    t_dgb = [Tok(), Tok()]
    make_gate(mod_cols(0, 2), G0, t_G0, dgb, t_dgb, [2, 3])
    if stop == "p0c":
        dump(G0, [t_G0], 1024)
        P.emit()
        return nc, P, AR
    P.barrier()
    if stop == "p0":
        dump(modT, [t_modT], 96)
        dump(gscT, [t_gscT], 32)
        dump(G0, [t_G0], 1024)
        dump(vecT, [t_vecT], 128)
        P.emit()
        return nc, P, AR
    AR.release(m0)

    def norm_T(xt, t_xt, gs, shc, hT_dst, t_hT, scr, bank):
        junkb, t_junkb, ss, t_ss, xn, t_xn = scr
        A("act", lambda: S.activation(out=junkb, in_=xt, func=AF.Square, accum_out=ss[:, 0:1]), [t_xt], [t_junkb, t_ss])
        A("act", lambda: S.activation(out=ss[:, 1:2], in_=ss[:, 0:1], func=AF.Sqrt, scale=1.0 / D, bias=eps_t[:, 0:1]),
          [t_ss, t_eps], [t_ss])
        A("dve", lambda: V.reciprocal(out=ss[:, 2:3], in_=ss[:, 1:2]), [t_ss], [t_ss])
        A("act", lambda: S.activation(out=xn, in_=xt, func=AF.Identity, scale=ss[:, 2:3]), [t_xt, t_ss], [t_xn])
        pv = pbf(bank)
        for j in range(8):
            A("pe", (lambda j=j: PE.transpose(out=pv[:, j * 128:(j + 1) * 128], in_=xn[:, j * 128:(j + 1) * 128],
                                              identity=identB)), [t_xn, t_identB], [tpb[bank]])
        for j in range(8):
            A("act", (lambda j=j: S.activation(out=hT_dst[:, j, :], in_=pv[:, j * 128:(j + 1) * 128], func=AF.Identity,
                                               scale=gs[:, j:j + 1], bias=shc[:, j:j + 1])),
              [tpb[bank], t_gscT, t_modT], [t_hT])

    eps_t = AR.alloc(4); t_eps = Tok()
    A("dve", lambda: V.memset(eps_t, EPS), [], [t_eps])

    mA = AR.mark()
    kT_all = AR.alloc([2, T], BF16); t_kT = [Tok() for _ in range(NT)]
    Vaug = AR.alloc([NT, 4, 66], BF16); t_V = [Tok() for _ in range(NT)]
    kiT = AR.alloc(T, BF16); t_kiT = [Tok() for _ in range(NT)]
    t_Vones = Tok()
    A("pool", lambda: G.memset(Vaug[:, :, :, 64:65], 1.0), [], [t_Vones] + t_V)

    mA1 = AR.mark()
    Win = AR.alloc([8, WCOLS], BF16); t_Win = [Tok() for _ in range(8)]
    wsrc = w_in.rearrange("(k p) n -> p k n", p=128)
    for k in range(8):
        DMA("pool", Win[:, k, :], wsrc[:, k, :], W=[t_Win[k]], key="win%d" % k, max_dma_last_dim=4096)
    for L in range(2):
        DMA("pool", wg_b[L], wg_d[L], W=[t_wgb[L]], key="cwg%d" % L, max_dma_last_dim=4096)
        DMA("pool", wu_b[L], wu_d[L], W=[t_wub[L]], key="cwu%d" % L, max_dma_last_dim=4096)
        DMA("pool", wd_b[L], wd_d[L], W=[t_wdb[L]], key="cwd%d" % L, max_dma_last_dim=4096)
        if L == 0:
            DMA("pool", cwin_b, cw_in, W=[t_cwinb], key="ccwin", max_dma_last_dim=4096)
            DMA("pool", cwout_b, cw_out, W=[t_cwoutb], key="ccwout", max_dma_last_dim=4096)

    def rope_tables(pos_d, n, cos_t, sin_t, t_tab):
        m = AR.mark()
        pi_ = AR.alloc(n, I32); t_pi = Tok()
        pf = AR.alloc(n); ang = AR.alloc([n, 32]); u = AR.alloc([n, 32]); ki_ = AR.alloc([n, 32], I32)
        kf = AR.alloc([n, 32]); r = AR.alloc([n, 32]); r2 = AR.alloc([n, 32]); tmp = AR.alloc([n, 32])
        invt = AR.alloc(32)
        tk = Tok()
        DMA("sp", pi_, pos_d, W=[t_pi])
        DMA("sp", invt, invf_d, W=[tk])
        A("dve", lambda: V.tensor_copy(out=pf, in_=pi_), [t_pi], [tk])
        A("dve", lambda: V.tensor_tensor(out=ang, in0=pf.unsqueeze(2).to_broadcast([128, n, 32]),
                                         in1=invt.unsqueeze(1).to_broadcast([128, n, 32]), op=ALU.mult), [tk], [tk])
        A("dve", lambda: V.tensor_scalar(out=u, in0=ang, scalar1=1.0 / TWO_PI, scalar2=None, op0=ALU.mult), [tk], [tk])
        A("dve", lambda: V.tensor_copy(out=ki_, in_=u), [tk], [tk])
        A("dve", lambda: V.tensor_copy(out=kf, in_=ki_), [tk], [tk])
        A("dve", lambda: V.scalar_tensor_tensor(out=r, in0=kf, scalar=-C1, in1=ang, op0=ALU.mult, op1=ALU.add), [tk], [tk])
        A("dve", lambda: V.scalar_tensor_tensor(out=r, in0=kf, scalar=-C2, in1=r, op0=ALU.mult, op1=ALU.add), [tk], [tk])
        A("dve", lambda: V.tensor_scalar(out=r, in0=r, scalar1=-3.1415925, scalar2=3.1415925, op0=ALU.max, op1=ALU.min), [tk], [tk])
        A("dve", lambda: V.tensor_scalar(out=r2, in0=r, scalar1=math.pi / 2, scalar2=None, op0=ALU.add), [tk], [tk])
        A("dve", lambda: V.tensor_scalar(out=tmp, in0=r2, scalar1=math.pi, scalar2=-TWO_PI, op0=ALU.is_gt, op1=ALU.mult), [tk], [tk])
        A("dve", lambda: V.tensor_tensor(out=r2, in0=r2, in1=tmp, op=ALU.add), [tk], [tk])
        A("dve", lambda: V.tensor_scalar(out=r2, in0=r2, scalar1=-3.1415925, scalar2=3.1415925, op0=ALU.max, op1=ALU.min), [tk], [tk])
        A("act", lambda: S.activation(out=sin_t, in_=r, func=AF.Sin), [tk], [t_tab])
        A("act", lambda: S.activation(out=cos_t, in_=r2, func=AF.Sin), [tk], [t_tab])
        return m

    cos_s = AR.alloc([NT, 32]); sin_s = AR.alloc([NT, 32]); t_tabs = Tok()
    cos_o = AR.alloc([NS, 32]); sin_o = AR.alloc([NS, 32]); t_tabo = Tok()
    hn = NT // 2
    for hf in range(2):
        mm_ = rope_tables(pos_seq[:, hf * hn:(hf + 1) * hn], hn, cos_s[:, hf * hn:(hf + 1) * hn, :], sin_s[:, hf * hn:(hf + 1) * hn, :], t_tabs)
        P.barrier()
        AR.release(mm_)
    mm_ = rope_tables(pos_own, NS, cos_o, sin_o, t_tabo)
    P.barrier()
    AR.release(mm_)

    xt = [AR.alloc(1024), AR.alloc(1024)]; t_xt = [Tok(), Tok()]
    hT = [AR.alloc([8, 128], BF16), AR.alloc([8, 128], BF16)]; t_hT = [Tok(), Tok()]
    scr = (AR.alloc(1024, BF16), Tok(), AR.alloc(4), Tok(), AR.alloc(1024, BF16), Tok())
    sq = AR.alloc(1024); t_sq = Tok()
    qn = AR.alloc(1024); t_qn = Tok()
    r1 = AR.alloc(1024); t_r1 = Tok()
    r2_ = AR.alloc(1024); t_r2 = Tok()
    qb = AR.alloc(1024, BF16); t_qb = Tok()
    sm = AR.alloc(64); t_sm = Tok()
    qTt = [AR.alloc(1024, BF16), AR.alloc(1024, BF16)]; t_qTt = [Tok(), Tok()]
    qiTt = [AR.alloc(512, BF16), AR.alloc(512, BF16)]; t_qiTt = [Tok(), Tok()]
    kib = AR.alloc(128, BF16); t_kib = Tok()
    qr = AR.alloc(512); t_qr = Tok()
    wab = AR.alloc(8); t_wab = Tok()

    def headnorm_rope(src_ps, t_src, H, gcol, cosv, sinv, t_tab, outb, t_outb):
        W_ = H * 64
        s3 = src_ps.rearrange("p (h d) -> p h d", h=H)
        A("act", lambda: S.activation(out=sq[:, 0:W_], in_=src_ps, func=AF.Square), [t_src], [t_sq])
        A("dve", lambda: V.tensor_reduce(out=sm[:, 0:H], in_=sq[:, 0:W_].rearrange("p (h d) -> p h d", h=H), axis=AX.X, op=ALU.add),
          [t_sq], [t_sm])
        A("act", lambda: S.activation(out=sm[:, 16:16 + H], in_=sm[:, 0:H], func=AF.Sqrt, scale=1.0 / 64, bias=eps_t[:, 0:1]),
          [t_sm, t_eps], [t_sm])
        A("dve", lambda: V.reciprocal(out=sm[:, 32:32 + H], in_=sm[:, 16:16 + H]), [t_sm], [t_sm])
        q3 = qn[:, 0:W_].rearrange("p (h d) -> p h d", h=H)
        A("dve", lambda: V.tensor_tensor(out=q3, in0=s3, in1=sm[:, 32:32 + H].unsqueeze(2).to_broadcast([128, H, 64]), op=ALU.mult),
          [t_src, t_sm], [t_qn])
        A("pool", lambda: G.tensor_tensor(out=q3, in0=q3, in1=hv[:, gcol:gcol + 64].unsqueeze(1).to_broadcast([128, H, 64]), op=ALU.mult),
          [t_qn, t_hv], [t_qn])
        rope(q3, t_qn, H, cosv, sinv, t_tab, outb, t_outb)

    def rope(q3, t_q3, H, cosv, sinv, t_tab, outb, t_outb):
        W_ = H * 64
        a3 = r1[:, 0:W_].rearrange("p (h d) -> p h d", h=H)
        b3 = r2_[:, 0:W_].rearrange("p (h d) -> p h d", h=H)
        o3 = outb[:, 0:W_].rearrange("p (h d) -> p h d", h=H)
        cb = cosv.unsqueeze(1).to_broadcast([128, H, 32])
        sb_ = sinv.unsqueeze(1).to_broadcast([128, H, 32])
        A("pool", lambda: G.tensor_tensor(out=a3[:, :, 0:32], in0=q3[:, :, 0:32], in1=cb, op=ALU.mult), [t_q3, t_tab], [t_r1])
        A("pool", lambda: G.tensor_tensor(out=a3[:, :, 32:64], in0=q3[:, :, 32:64], in1=cb, op=ALU.mult), [t_q3, t_tab], [t_r1])
        A("dve", lambda: V.tensor_tensor(out=b3[:, :, 0:32], in0=q3[:, :, 32:64], in1=sb_, op=ALU.mult), [t_q3, t_tab], [t_r2])
        A("dve", lambda: V.tensor_tensor(out=b3[:, :, 32:64], in0=q3[:, :, 0:32], in1=sb_, op=ALU.mult), [t_q3, t_tab], [t_r2])
        A("pool", lambda: G.tensor_tensor(out=o3[:, :, 0:32], in0=a3[:, :, 0:32], in1=b3[:, :, 0:32], op=ALU.subtract), [t_r1, t_r2], [t_outb])
        A("pool", lambda: G.tensor_tensor(out=o3[:, :, 32:64], in0=a3[:, :, 32:64], in1=b3[:, :, 32:64], op=ALU.add), [t_r1, t_r2], [t_outb])

    def proj(dst_bank, ncols, c0, hTb, t_hTb):
        for k in range(8):
            A("pe", (lambda k=k: PE.matmul(pbank[dst_bank][:, 0:ncols], lhsT=hTb[:, k, :], rhs=Win[:, k, c0:c0 + ncols],
                                           start=(k == 0), stop=(k == 7))), [t_hTb, t_Win[k]], [tpb[dst_bank]])

    gs1_0 = gscT[:, 0:8]
    sh1_0 = mod_cols(0, 0)
    for j in range(NT):
        b = j % 2
        DMA("sp", xt[b], x_seq[j * 128:(j + 1) * 128, :], W=[t_xt[b]], key="xt%d" % b)
        norm_T(xt[b], t_xt[b], gs1_0, sh1_0, hT[b], t_hT[b], scr, 0)
        proj(1, 512, 1024, hT[b], t_hT[b])
        proj(2, 128, 2048, hT[b], t_hT[b])
        cj, sj = cos_s[:, j, :], sin_s[:, j, :]
        headnorm_rope(pbank[1][:, 0:256], tpb[1], 4, 64, cj, sj, t_tabs, qb, t_qb)
        A("act", (lambda j=j: S.copy(out=Vaug[:, j, :, 0:64], in_=pbank[1][:, 256:512].rearrange("p (g d) -> p g d", g=4))),
          [tpb[1]], [t_V[j]])
        pv3 = pbf(3)
        for i in range(2):
            A("pe", (lambda i=i: PE.transpose(out=pv3[:, i * 128:(i + 1) * 128], in_=qb[:, i * 128:(i + 1) * 128], identity=identB)),
              [t_qb, t_identB], [tpb[3]])
        A("act", (lambda j=j: S.copy(out=kT_all[:, :, j * 128:(j + 1) * 128], in_=pv3[:, 0:256].rearrange("p (i t) -> p i t", i=2))),
          [tpb[3]], [t_kT[j]])
        A("dve", lambda: V.bn_stats(out=sm[:, 48:54], in_=pbank[2][:, 0:64]), [tpb[2]], [t_sm])
        A("dve", lambda: V.bn_aggr(out=sm[:, 54:56], in_=sm[:, 48:54]), [t_sm], [t_sm])
        A("act", lambda: S.activation(out=sm[:, 56:57], in_=sm[:, 55:56], func=AF.Sqrt, scale=1.0, bias=eps_t[:, 0:1]), [t_sm, t_eps], [t_sm])
        A("dve", lambda: V.reciprocal(out=sm[:, 57:58], in_=sm[:, 56:57]), [t_sm], [t_sm])
        A("dve", lambda: V.tensor_scalar(out=qn[:, 0:64], in0=pbank[2][:, 0:64], scalar1=sm[:, 54:55], scalar2=sm[:, 57:58],
                                         op0=ALU.subtract, op1=ALU.mult), [tpb[2], t_sm], [t_qn])
        A("pool", lambda: G.tensor_tensor(out=qn[:, 0:64], in0=qn[:, 0:64], in1=hv[:, 128:192], op=ALU.mult), [t_qn, t_hv], [t_qn])
        A("pool", lambda: G.tensor_tensor(out=qn[:, 0:64], in0=qn[:, 0:64], in1=hv[:, 192:256], op=ALU.add), [t_qn, t_hv], [t_qn])
        rope(qn[:, 0:64].rearrange("p (h d) -> p h d", h=1), t_qn, 1, cj, sj, t_tabs, kib, t_kib)
        A("pool", lambda: G.tensor_copy(out=kib[:, 64:128], in_=kib[:, 0:64]), [t_kib], [t_kib])
        A("pe", lambda: PE.transpose(out=pv3[:, 256:384], in_=kib, identity=identB), [t_kib, t_identB], [tpb[3]])
        A("act", (lambda j=j: S.copy(out=kiT[:, j * 128:(j + 1) * 128], in_=pv3[:, 256:384])), [tpb[3]], [t_kiT[j]])

    WSC = (8 ** -0.5) * (64 ** -0.5)
    for i in range(NS):
        b = i % 2
        DMA("sp", xt[b], x_own[i * 128:(i + 1) * 128, :], W=[t_xt[b]], key="xt%d" % b)
        norm_T(xt[b], t_xt[b], gs1_0, sh1_0, hT[b], t_hT[b], scr, 0)
        proj(1, 512, 0, hT[b], t_hT[b])
        proj(2, 512, 512, hT[b], t_hT[b])
        proj(4, 512, 1536, hT[b], t_hT[b])
        proj(5, 8, 2176, hT[b], t_hT[b])
        ci, si = cos_o[:, i, :], sin_o[:, i, :]
        for hh in range(2):
            headnorm_rope(pbank[1 + hh][:], tpb[1 + hh], 8, 0, ci, si, t_tabo, qb, t_qb)
            pv3 = pbf(3)
            for jj in range(4):
                A("pe", (lambda jj=jj, hh=hh: PE.transpose(out=pv3[:, (hh * 4 + jj) * 128:(hh * 4 + jj + 1) * 128],
                                                          in_=qb[:, jj * 128:(jj + 1) * 128], identity=identB)),
                  [t_qb, t_identB], [tpb[3]])
        A("act", (lambda b=b: S.copy(out=qTt[b], in_=pbf(3))), [tpb[3]], [t_qTt[b]])
        DMA("sp", qTs[i], qTt[b], R=[t_qTt[b]], W=[t_qTs[i]], key="qTs")
        A("act", (lambda i=i: S.activation(out=wsign[:, i, :], in_=pbank[5][:, 0:8], func=AF.Sign)), [tpb[5]], [t_wsign])
        A("act", lambda: S.activation(out=wab, in_=pbank[5][:, 0:8], func=AF.Abs, scale=WSC), [tpb[5]], [t_wab])
        A("act", lambda: S.copy(out=qn[:, 0:512], in_=pbank[4][:]), [tpb[4]], [t_qn])
        rope(qn[:, 0:512].rearrange("p (h d) -> p h d", h=8), t_qn, 8, ci, si, t_tabo, qr, t_qr)
        A("dve", lambda: V.tensor_tensor(out=qb[:, 0:512].rearrange("p (h d) -> p h d", h=8),
                                         in0=qr[:, 0:512].rearrange("p (h d) -> p h d", h=8),
                                         in1=wab.unsqueeze(2).to_broadcast([128, 8, 64]), op=ALU.mult),
          [t_qr, t_wab], [t_qb])
        pv6 = pbf(6)
        for jj in range(4):
            A("pe", (lambda jj=jj: PE.transpose(out=pv6[:, jj * 128:(jj + 1) * 128], in_=qb[:, jj * 128:(jj + 1) * 128], identity=identB)),
              [t_qb, t_identB], [tpb[6]])
        A("act", (lambda b=b: S.copy(out=qiTt[b], in_=pv6[:, 0:512])), [tpb[6]], [t_qiTt[b]])
        DMA("sp", qiTs[i], qiTt[b], R=[t_qiTt[b]], W=[t_qiTs[i]], key="qiTs")
    P.barrier()
    if stop in ("a1", "a1x"):
        dump(kT_all[:, 0, 0:512], t_kT, 512, bf=True)
        dump(kT_all[:, 1, 0:512], t_kT, 512, bf=True)
        dump(kiT[:, 0:512], t_kiT, 512, bf=True)
        dump(Vaug[:, 0:4, :, :].rearrange("p a g d -> p (a g d)"), t_V, 1056, bf=True)
        dump(qTt[(NS - 1) % 2], t_qTt, 1024, bf=True)
        dump(qiTt[(NS - 1) % 2], t_qiTt, 512, bf=True)
        dump(wsign.rearrange("p a h -> p (a h)"), [t_wsign], NS * 8)
        dump(cos_s.rearrange("p a h -> p (a h)"), [t_tabs], NT * 32)
        dump(sin_s.rearrange("p a h -> p (a h)"), [t_tabs], NT * 32)
        P.emit()
        return nc, P, AR
    AR.release(mA1)

    Wo = AR.alloc([8, 1024], BF16); t_Wo = Tok()
    for hh in range(2):
        DMA("pool", Wo[hh * 64:(hh + 1) * 64, :, :], w_out[hh * 512:(hh + 1) * 512, :].rearrange("(j d) c -> d j c", d=64),
            W=[t_Wo], key="wo", max_dma_last_dim=4096)
    I_ = AR.alloc(T); t_I = Tok()
    mask01 = AR.alloc(T, BF16); t_mask = Tok()
    junk8 = AR.alloc(T, U8); t_junk8 = Tok()
    qTb = [[AR.alloc(1024, BF16), AR.alloc(1024, BF16)] for _ in range(2)]; t_qTb = [Tok(), Tok()]
    qiTb = [[AR.alloc(512, BF16), AR.alloc(512, BF16)] for _ in range(2)]; t_qiTb = [Tok(), Tok()]
    for b_ in range(2):
        for hf_ in range(2):
            A("pool", (lambda b_=b_, hf_=hf_: G.memset(qTb[b_][hf_], 0.0)), [], [t_qTb[b_]])
            A("pool", (lambda b_=b_, hf_=hf_: G.memset(qiTb[b_][hf_], 0.0)), [], [t_qiTb[b_]])
    NR = 4
    Rb = [AR.alloc(512, BF16) for _ in range(NR)]; t_Rb = [Tok() for _ in range(NR)]
    Dh = AR.alloc([8, 128], BF16); t_Dh = Tok()
    biasb = [AR.alloc(512, BF16), AR.alloc(512, BF16)]; t_biasb = [Tok(), Tok()]
    NPB = 6
    Pexp = [AR.alloc(512, BF16) for _ in range(NPB)]; t_Pexp = [Tok() for _ in range(NPB)]
    Sel4 = AR.alloc(512, BF16); t_Sel4 = Tok()
    for r_ in range(4):
        A("pool", (lambda r_=r_: G.tensor_copy(out=Sel4[:, r_ * 128:(r_ + 1) * 128], in_=identB)), [t_identB], [t_Sel4])
    rs = AR.alloc(512); t_rs = Tok()
    bcS = AR.alloc(512); t_bcS = Tok()
    ys = AR.alloc(512); t_ys = Tok()
    numS = ys; t_numS = t_ys
    oT_all = AR.alloc([2, 512], BF16); t_oTlo = Tok(); t_oThi = Tok()
    oT_tmp = AR.alloc([2, 512], BF16); t_oTtmp = Tok()
    xa = AR.alloc(512); t_xa = Tok()
    ta = AR.alloc(512); t_ta = Tok()
    bs = AR.alloc(16); t_bs = Tok()
    tr_banks = [0, 1, 2]
    trc = [0]

    def tbank():
        k = tr_banks[trc[0] % 3]
        trc[0] += 1
        return k

    rbc = [0]
    SCALE = 64 ** -0.5

    def indexer(i):
        b = i % 2
        E = cfg.ext[i]
        nch = (E + 3) // 4
        for hf_ in range(2):
            DMA("sp", qiTb[b][hf_][hf_ * 64:(hf_ + 1) * 64, :], qiTs[i][hf_ * 64:(hf_ + 1) * 64, :], R=[t_qiTs[i]], W=[t_qiTb[b]],
                key="qiTb%d" % b)
        for hf_ in range(2):
            DMA("sp", qTb[b][hf_][hf_ * 64:(hf_ + 1) * 64, :], qTs[i][hf_ * 64:(hf_ + 1) * 64, :], R=[t_qTs[i]], W=[t_qTb[b]],
                key="qTb%d" % b)
        for h in range(8):
            A("pool", (lambda h=h, i=i: G.tensor_scalar(out=Dh[:, h, :], in0=identB, scalar1=wsign[:, i, h:h + 1], scalar2=None, op0=ALU.mult)),
              [t_identB, t_wsign], [t_Dh])
        for c in range(nch):
            kts = [t_kiT[jj] for jj in range(c * 4, c * 4 + 4)]
            need_bias = c >= cfg.dchunk[i]
            if need_bias:
                bb = c % 2
                A("dve", (lambda c=c, i=i: V.tensor_scalar(out=bs[:, 6:7], in0=qpos[:, i:i + 1], scalar1=float(-512 * c), scalar2=None, op0=ALU.add)),
                  [t_qpos], [t_bs])
                A("dve", (lambda bb=bb: V.tensor_scalar(out=biasb[bb], in0=iota, scalar1=bs[:, 6:7], scalar2=-1e30, op0=ALU.is_gt, op1=ALU.mult)),
                  [t_iota, t_bs], [t_biasb[bb]])
            banks = []
            LA = 2

            def acc(hh, nb=need_bias):
                A("pe", (lambda hh=hh, rbb=banks[hh], nb=nb: PE.matmul(pbank[3][:], lhsT=Dh[:, hh, :], rhs=Rb[rbb], start=(hh == 0),
                                                                     stop=(hh == 7 and not nb))),
                  [t_Dh, t_Rb[banks[hh]]], [tpb[3]])

            for h in range(8):
                bk = tbank()
                hp, pr = h % 2, h // 2
                A("pe", (lambda bk=bk, hp=hp, pr=pr, c=c, b=b: PE.matmul(
                    pbank[bk][:], lhsT=qiTb[b][hp][:, pr * 128:(pr + 1) * 128],
                    rhs=kiT[:, c * 512:(c + 1) * 512], start=True, stop=True)),
                  [t_qiTb[b]] + kts, [tpb[bk]])
                rb = rbc[0] % NR
                rbc[0] += 1
                A("act", (lambda bk=bk, rb=rb: S.activation(out=Rb[rb], in_=pbank[bk][:], func=AF.Relu)), [tpb[bk]], [t_Rb[rb]])
                banks.append(rb)
                if h >= LA:
                    acc(h - LA)
            for hh in range(8 - LA, 8):
                acc(hh)
            if need_bias:
                A("pe", (lambda bb=bb: PE.matmul(pbank[3][:], lhsT=identB, rhs=biasb[bb], start=False, stop=True)),
                  [t_identB, t_biasb[bb]], [tpb[3]])
            A("act", (lambda c=c: S.copy(out=I_[:, c * 512:(c + 1) * 512], in_=pbank[3][:])), [tpb[3]], [t_I])

    def topk(i):
        E = cfg.ext[i]
        Sn = ((E + 3) // 4) * 512
        K0 = min(256, Sn)
        A("dve", lambda: V.tensor_reduce(out=bs[:, 0:1], in_=I_[:, 0:Sn], axis=AX.X, op=ALU.max), [t_I], [t_bs])
        A("dve", lambda: V.tensor_reduce(out=bs[:, 1:2], in_=I_[:, 0:K0], axis=AX.X, op=ALU.min), [t_I], [t_bs])
        A("dve", lambda: V.tensor_scalar(out=bs[:, 1:2], in0=bs[:, 1:2], scalar1=-1e29, scalar2=None, op0=ALU.max), [t_bs], [t_bs])
        A("dve", lambda: V.tensor_tensor(out=bs[:, 2:3], in0=bs[:, 0:1], in1=bs[:, 1:2], op=ALU.subtract), [t_bs], [t_bs])
        for it in range(NIT):
            f = 2.0 ** (-(it + 1))
            A("dve", (lambda f=f: V.tensor_scalar(out=bs[:, 3:4], in0=bs[:, 2:3], scalar1=f, scalar2=bs[:, 1:2], op0=ALU.mult, op1=ALU.add)),
              [t_bs], [t_bs])
            A("dve", lambda: V.tensor_scalar(out=junk8[:, 0:Sn], in0=I_[:, 0:Sn], scalar1=bs[:, 3:4], scalar2=None, op0=ALU.is_ge,
                                             op1=ALU.add, accum_out=bs[:, 4:5]), [t_I, t_bs], [t_junk8, t_bs])
            A("dve", (lambda f=f: V.tensor_scalar(out=bs[:, 5:6], in0=bs[:, 4:5], scalar1=cfg.topk - 0.5, scalar2=f, op0=ALU.is_gt, op1=ALU.mult)),
              [t_bs], [t_bs])
            A("dve", lambda: V.tensor_scalar(out=bs[:, 1:2], in0=bs[:, 5:6], scalar1=bs[:, 2:3], scalar2=bs[:, 1:2], op0=ALU.mult, op1=ALU.add),
              [t_bs], [t_bs])
        A("dve", lambda: V.tensor_scalar(out=mask01[:, 0:E * 128], in0=I_[:, 0:E * 128], scalar1=bs[:, 1:2], scalar2=-30000.0,
                                         op0=ALU.is_lt, op1=ALU.mult), [t_I, t_bs], [t_mask])

    pbc = [0]

    def attention(i):
        b = i % 2
        E = cfg.ext[i]
        steps = [(kb, g) for kb in range(E) for g in range(4)]
        DL = 3
        assert NPB > DL
        pmof = {}

        def front(n):
            kb, g = steps[n]
            hp, gi = g // 2, g % 2
            bk = tbank()
            A("pe", (lambda bk=bk, hp=hp, gi=gi, kb=kb, b=b: PE.matmul(
                pbank[bk][:], lhsT=kT_all[:, gi, kb * 128:(kb + 1) * 128],
                rhs=qTb[b][hp][:, gi * 512:(gi + 1) * 512], start=True, stop=False)),
              [t_kT[kb], t_qTb[b]], [tpb[bk]])
            A("pe", (lambda bk=bk, kb=kb: PE.matmul(pbank[bk][:], lhsT=mask01[:, kb * 128:(kb + 1) * 128], rhs=Sel4, start=False, stop=True)),
              [t_mask, t_Sel4], [tpb[bk]])
            pe_ = pbc[0] % NPB
            pbc[0] += 1
            pmof[n] = pe_
            A("act", (lambda bk=bk, pe_=pe_: S.activation(out=Pexp[pe_], in_=pbank[bk][:], func=AF.Exp, scale=SCALE)),
              [tpb[bk]], [t_Pexp[pe_]])

        def back(n):
            kb, g = steps[n]
            pe_ = pmof[n]
            A("pe", (lambda g=g, kb=kb, pe_=pe_, E=E: PE.matmul(pbank[4 + g][0:65, :], lhsT=Vaug[:, kb, g, 0:65], rhs=Pexp[pe_],
                                                               start=(kb == 0), stop=(kb == E - 1))),
              [t_V[kb], t_Vones, t_Pexp[pe_]], [tpb[4 + g]])

        for n in range(len(steps) + DL):
            if n < len(steps):
                front(n)
            if n - DL >= 0:
                back(n - DL)
        if ATT_PARTS < 2:
            return
        for g in range(4):
            A("act", (lambda g=g: S.activation(out=rs[64:65, :], in_=pbank[4 + g][64:65, :], func=AF.Ln)), [tpb[4 + g]], [t_rs])
            A("act", lambda: S.activation(out=rs[64:65, :], in_=rs[64:65, :], func=AF.Exp, scale=-1.0), [t_rs], [t_rs])
            bk = tbank()
            A("pe", (lambda bk=bk: PE.matmul(pbank[bk][0:64, :], lhsT=onesF[64:65, 0:64], rhs=rs[64:65, :], start=True, stop=True)),
              [t_onesF, t_rs], [tpb[bk]])
            A("act", (lambda bk=bk: S.copy(out=bcS[0:64, :], in_=pbank[bk][0:64, :])), [tpb[bk]], [t_bcS])
            A("act", (lambda g=g: S.copy(out=numS[0:64, :], in_=pbank[4 + g][0:64, :])), [tpb[4 + g]], [t_numS])
            if g < 2:
                A("pool", (lambda g=g: G.tensor_tensor(out=oT_all[0:64, g, :], in0=numS[0:64, :], in1=bcS[0:64, :], op=ALU.mult)),
                  [t_numS, t_bcS], [t_oTlo])
            else:
                A("pool", (lambda g=g: G.tensor_tensor(out=oT_tmp[0:64, g - 2, :], in0=numS[0:64, :], in1=bcS[0:64, :], op=ALU.mult)),
                  [t_numS, t_bcS], [t_oTtmp])
        if ATT_PARTS < 3:
            return
        DMA("sp", oT_all[64:128, :, :], oT_tmp[0:64, :, :], R=[t_oTtmp], W=[t_oThi], key="oThi")
        if ATT_PARTS < 4:
            return
        for ch in range(2):
            DMA("sp", xa, x_own[i * 128:(i + 1) * 128, ch * 512:(ch + 1) * 512], W=[t_xa], key="xa")
            bk = tbank()
            for gi in range(2):
                for r in range(4):
                    j = gi * 4 + r
                    A("pe", (lambda bk=bk, gi=gi, r=r, j=j, ch=ch: PE.matmul(
                        pbank[bk][:], lhsT=oT_all[:, gi, r * 128:(r + 1) * 128], rhs=Wo[:, j, ch * 512:(ch + 1) * 512],
                        start=(j == 0), stop=(j == 7))), [t_oTlo, t_oThi, t_Wo], [tpb[bk]])
            A("act", (lambda bk=bk: S.copy(out=ys, in_=pbank[bk][:])), [tpb[bk]], [t_ys])
            A("pool", (lambda ch=ch: G.tensor_tensor(out=ta, in0=ys, in1=G0[:, ch * 512:(ch + 1) * 512], op=ALU.mult)), [t_ys, t_G0], [t_ta])
            A("pool", lambda: G.tensor_tensor(out=ta, in0=ta, in1=xa, op=ALU.add), [t_ta, t_xa], [t_ta])
            DMA("sp", x1s[i * 128:(i + 1) * 128, ch * 512:(ch + 1) * 512], ta, R=[t_ta], W=[t_x1s[i]], key="x1s")

    indexer(0)
    if stop in ("a2i", "a2t", "a2a"):
        if stop in ("a2t", "a2a"):
            topk(0)
        if stop == "a2a":
            attention(0)
        dump(I_[:, 0:1024], [t_I], 1024)
        dump(mask01[:, 0:1024], [t_mask], 1024, bf=True)
        dump(bs, [t_bs], 16)
        dump(ta, [t_ta], 512)
        P.emit()
        return nc, P, AR
    for i in range(NS):
        topk(i)
        if i + 1 < NS:
            indexer(i + 1)
        attention(i)
    P.barrier()
    if stop in ("a2", "a2x"):
        dump(I_[:, 0:1024], [t_I], 1024)
        dump(mask01[:, 0:1024], [t_mask], 1024, bf=True)
        dump(bs, [t_bs], 16)
        dump(ta, [t_ta], 512)
        P.emit()
        return nc, P, AR
    AR.release(mA)

    Gt = [AR.alloc(1024) for _ in range(3)]; t_Gt = [Tok() for _ in range(3)]
    dgb = [AR.alloc(128), AR.alloc(128)]
    make_gate(mod_cols(0, 5), Gt[0], t_Gt[0], dgb, t_dgb, [0, 1])
    make_gate(mod_cols(1, 2), Gt[1], t_Gt[1], dgb, t_dgb, [2, 3])
    make_gate(mod_cols(1, 5), Gt[2], t_Gt[2], dgb, t_dgb, [0, 1])
    MS = 4
    xm = AR.alloc([MS, 1024]); t_xm = [Tok() for _ in range(MS)]
    hTm = AR.alloc([8, MS * 128], BF16); t_hTm = Tok()
    hTs = AR.alloc([8, 128], BF16); t_hTs = Tok()
    scrB = (AR.alloc(1024, BF16), Tok(), AR.alloc(4), Tok(), AR.alloc(1024, BF16), Tok())
    aT = AR.alloc([NFC, MS * 128], BF16); t_aT = Tok()
    Wd = AR.alloc([NFC, 1024], BF16); t_Wd = [Tok() for _ in range(4)]
    NWB = 2
    WA = [AR.alloc([8, 512], BF16) for _ in range(NWB)]; t_WA = [Tok() for _ in range(NWB)]
    WB = [AR.alloc([8, 512], BF16) for _ in range(NWB)]; t_WB = [Tok() for _ in range(NWB)]
    WC = [AR.alloc([8, 512], BF16) for _ in range(NWB)]; t_WC = [Tok() for _ in range(NWB)]
    sg = [AR.alloc(MS * 128), AR.alloc(MS * 128)]; t_sg = [Tok(), Tok()]
    zb = AR.alloc(MS * 128 + 2); t_zb = Tok()
    zc = AR.alloc(MS * 128); t_zc = Tok()
    carry = AR.alloc([8, 2]); t_carry = Tok()
    tb = AR.alloc(512); t_tb = Tok()
    A("dve", lambda: V.memset(carry, 0.0), [], [t_carry])
    wbc = [0]

    def norm_macro(ns, gs, shc):
        for s in range(ns):
            norm_T(xm[:, s, :], t_xm[s], gs, shc, hTs, t_hTs, scrB, 7)
            A("pool", (lambda s=s: G.tensor_copy(out=hTm[:, :, s * 128:(s + 1) * 128], in_=hTs)), [t_hTs], [t_hTm])

    def down(ns, nchunks, Gate, t_Gate):
        N = ns * 128
        for s in range(ns):
            for ch in range(2):
                bk = 4 + (s * 2 + ch) % 2
                for j in range(nchunks):
                    A("pe", (lambda bk=bk, j=j, s=s, ch=ch: PE.matmul(pbank[bk][:], lhsT=aT[:, j, s * 128:(s + 1) * 128],
                                                                     rhs=Wd[:, j, ch * 512:(ch + 1) * 512],
                                                                     start=(j == 0), stop=(j == nchunks - 1))),
                      [t_aT, t_Wd[j // 6]], [tpb[bk]])
                A("dve", (lambda bk=bk, ch=ch: V.tensor_tensor(out=tb, in0=pbank[bk][:], in1=Gate[:, ch * 512:(ch + 1) * 512], op=ALU.mult)),
                  [tpb[bk], t_Gate], [t_tb])
                A("pool", (lambda s=s, ch=ch: G.tensor_tensor(out=xm[:, s, ch * 512:(ch + 1) * 512], in0=xm[:, s, ch * 512:(ch + 1) * 512],
                                                             in1=tb, op=ALU.add)), [t_tb, t_xm[s]], [t_xm[s]])

    def ffn(L, ns, Gate, t_Gate):
        N = ns * 128
        norm_macro(ns, gscT[:, (2 * L + 1) * 8:(2 * L + 1) * 8 + 8], mod_cols(L, 3))
        gsrc = wg_b[L].rearrange("(k p) f -> p k f", p=128)
        usrc = wu_b[L].rearrange("(k p) f -> p k f", p=128)
        for fg in range(6):
            f0 = fg * 512
            fw = min(512, FF - f0)
            wb = wbc[0] % NWB
            wbc[0] += 1
            DMA("sp", WA[wb][:, :, 0:fw], gsrc[:, :, f0:f0 + fw], R=[t_wgb[L]], W=[t_WA[wb]], key="WA%d" % wb)
            DMA("sp", WB[wb][:, :, 0:fw], usrc[:, :, f0:f0 + fw], R=[t_wub[L]], W=[t_WB[wb]], key="WB%d" % wb)
            for fc in range(fw // 128):
                j = fg * 4 + fc
                bg, bu = (0, 1) if j % 2 == 0 else (2, 3)
                for k in range(8):
                    A("pe", (lambda bg=bg, k=k, fc=fc, wb=wb: PE.matmul(pbank[bg][:, 0:N], lhsT=WA[wb][:, k, fc * 128:(fc + 1) * 128],
                                                                       rhs=hTm[:, k, 0:N], start=(k == 0), stop=(k == 7))),
                      [t_WA[wb], t_hTm], [tpb[bg]])
                for k in range(8):
                    A("pe", (lambda bu=bu, k=k, fc=fc, wb=wb: PE.matmul(pbank[bu][:, 0:N], lhsT=WB[wb][:, k, fc * 128:(fc + 1) * 128],
                                                                       rhs=hTm[:, k, 0:N], start=(k == 0), stop=(k == 7))),
                      [t_WB[wb], t_hTm], [tpb[bu]])
                sb_ = j % 2
                A("act", (lambda bg=bg, sb_=sb_: S.activation(out=sg[sb_][:, 0:N], in_=pbank[bg][:, 0:N], func=AF.Silu)),
                  [tpb[bg]], [t_sg[sb_]])
                A("dve", (lambda bu=bu, sb_=sb_, j=j: V.tensor_tensor(out=aT[:, j, 0:N], in0=pbank[bu][:, 0:N], in1=sg[sb_][:, 0:N], op=ALU.mult)),
                  [tpb[bu], t_sg[sb_]], [t_aT])
        dsrc = wd_b[L].rearrange("(j p) c -> p j c", p=128)
        for q4 in range(0, NFC, 6):
            q5 = min(NFC, q4 + 6)
            DMA("sp", Wd[:, q4:q5, :], dsrc[:, q4:q5, :], R=[t_wdb[L]], W=[t_Wd[q4 // 6]], key="Wd%d" % (q4 // 6))
        down(ns, NFC, Gate, t_Gate)

    def convmix(ns):
        N = ns * 128
        norm_macro(ns, gscT[:, 16:24], mod_cols(1, 0))
        src = cwin_b.rearrange("(k p) f -> p k f", p=128)
        for cg in range(2):
            wb = wbc[0] % NWB
            wbc[0] += 1
            DMA("sp", WA[wb], src[:, :, cg * 512:(cg + 1) * 512], R=[t_cwinb], W=[t_WA[wb]], key="WA%d" % wb)
            DMA("sp", WB[wb], src[:, :, 1024 + cg * 512:1024 + (cg + 1) * 512], R=[t_cwinb], W=[t_WB[wb]], key="WB%d" % wb)
            DMA("sp", WC[wb], src[:, :, 2048 + cg * 512:2048 + (cg + 1) * 512], R=[t_cwinb], W=[t_WC[wb]], key="WC%d" % wb)
            for cc in range(4):
                cj = cg * 4 + cc
                for (bk, Wt, tW) in ((0, WA, t_WA), (1, WB, t_WB), (2, WC, t_WC)):
                    for k in range(8):
                        A("pe", (lambda bk=bk, Wt=Wt, k=k, cc=cc, wb=wb: PE.matmul(
                            pbank[bk][:, 0:N], lhsT=Wt[wb][:, k, cc * 128:(cc + 1) * 128], rhs=hTm[:, k, 0:N],
                            start=(k == 0), stop=(k == 7))), [tW[wb], t_hTm], [tpb[bk]])
                A("act", lambda: S.copy(out=sg[0][:, 0:N], in_=pbank[1][:, 0:N]), [tpb[1]], [t_sg[0]])
                A("dve", (lambda cj=cj: V.tensor_copy(out=zb[:, 0:2], in_=carry[:, cj, :])), [t_carry], [t_zb])
                A("dve", lambda: V.tensor_tensor(out=zb[:, 2:2 + N], in0=pbank[2][:, 0:N], in1=sg[0][:, 0:N], op=ALU.mult),
                  [tpb[2], t_sg[0]], [t_zb])
                A("dve", (lambda cj=cj: V.tensor_copy(out=carry[:, cj, :], in_=zb[:, N:N + 2])), [t_zb], [t_carry])
                A("dve", (lambda cj=cj: V.tensor_scalar(out=zc[:, 0:N], in0=zb[:, 2:2 + N], scalar1=cwT[:, cj * 3 + 2:cj * 3 + 3],
                                                       scalar2=None, op0=ALU.mult)), [t_zb, t_cwT], [t_zc])
                A("dve", (lambda cj=cj: V.scalar_tensor_tensor(out=zc[:, 0:N], in0=zb[:, 1:1 + N], scalar=cwT[:, cj * 3 + 1:cj * 3 + 2],
                                                              in1=zc[:, 0:N], op0=ALU.mult, op1=ALU.add)), [t_zb, t_cwT, t_zc], [t_zc])
                A("dve", (lambda cj=cj: V.scalar_tensor_tensor(out=zc[:, 0:N], in0=zb[:, 0:N], scalar=cwT[:, cj * 3:cj * 3 + 1],
                                                              in1=zc[:, 0:N], op0=ALU.mult, op1=ALU.add)), [t_zb, t_cwT, t_zc], [t_zc])
                A("dve", (lambda cj=cj: V.tensor_tensor(out=aT[:, cj, 0:N], in0=pbank[0][:, 0:N], in1=zc[:, 0:N], op=ALU.mult)),
                  [tpb[0], t_zc], [t_aT])
        osrc = cwout_b.rearrange("(j p) c -> p j c", p=128)
        DMA("sp", Wd[:, 0:6, :], osrc[:, 0:6, :], R=[t_cwoutb], W=[t_Wd[0]], key="Wd0")
        DMA("sp", Wd[:, 6:8, :], osrc[:, 6:8, :], R=[t_cwoutb], W=[t_Wd[1]], key="Wd1")
        down(ns, 8, Gt[1], t_Gt[1])

    nmac = (NS + MS - 1) // MS
    for m in range(nmac):
        s0 = m * MS
        ns = min(MS, NS - s0)
        for s in range(ns):
            DMA("sp", xm[:, s, :], x1s[(s0 + s) * 128:(s0 + s + 1) * 128, :], R=[t_x1s[s0 + s]], W=[t_xm[s]], key="xm%d" % s)
        ffn(0, ns, Gt[0], t_Gt[0])
        convmix(ns)
        ffn(1, ns, Gt[2], t_Gt[2])
        for s in range(ns):
            DMA("sp", out_d[(s0 + s) * 128:(s0 + s + 1) * 128, :], xm[:, s, :], R=[t_xm[s]], W=[t_out[s0 + s]], key="out%d" % s)

    P.emit()
    return nc, P, AR


import os
ATT_PARTS = int(os.environ.get('ATT_PARTS', '9'))
_CACHE = {}
STOP = None
LAST = None


def _host_inputs(cfg, r, x, c, positions, ada_w, ada_b, norm1_g, norm2_g, attn_w_in, attn_q_norm_g, attn_k_norm_g,
                 idx_k_ln_g, idx_k_ln_b, attn_w_out, conv_w_in, conv_w, conv_w_out, ffn_w_gate, ffn_w_up, ffn_w_down,
                 shared):
    b, role = r // 2, r % 2
    tiles = cfg.tilesA if role == 0 else cfg.tilesB
    T, NT, NS = cfg.T, cfg.NT, cfg.NS
    xs = np.ascontiguousarray(x[b])
    xo = np.ascontiguousarray(xs.reshape(NT, 128, D)[tiles].reshape(NS * 128, D))
    ps = np.ascontiguousarray(positions[b].reshape(NT, 128).T)
    po = np.ascontiguousarray(positions[b].reshape(NT, 128)[tiles].T)
    tok = np.arange(T, dtype=np.float32).reshape(NT, 128)
    qp = np.ascontiguousarray(tok[tiles].T)
    cT = np.ascontiguousarray(c[b].reshape(8, 128).T)
    d = dict(shared)
    d.update({"x_seq": xs, "x_own": xo, "pos_seq": ps.astype(np.int32), "pos_own": po.astype(np.int32), "qpos": qp, "cT": cT})
    return d


def _shared_inputs(ada_w, ada_b, norm1_g, norm2_g, attn_w_in, attn_q_norm_g, attn_k_norm_g, idx_k_ln_g, idx_k_ln_b,
                   attn_w_out, conv_w_in, conv_w, conv_w_out, ffn_w_gate, ffn_w_up, ffn_w_down):
    w = attn_w_in[0]
    qc = w[:, 0:1024].reshape(D, 16, 64)
    qperm = np.stack([qc[:, [j, 8 + j], :] for j in range(8)], axis=1).reshape(D, 1024)
    kc = w[:, 1024:1280].reshape(D, 4, 64)
    kperm = np.concatenate([kc[:, 0], kc[:, 2], kc[:, 1], kc[:, 3]], axis=1)
    vcol = w[:, 1280:1536]
    qic = w[:, 1536:2048]
    kic = w[:, 2048:2112]
    wic = w[:, 2112:2120]
    w_in = np.ascontiguousarray(np.concatenate([qperm, kperm, vcol, qic, kic, kic, wic], axis=1), dtype=np.float32)
    assert w_in.shape[1] == WCOLS
    vecs = np.concatenate([ada_b[0].reshape(48, 128), ada_b[1].reshape(48, 128), norm1_g.reshape(16, 128),
                           norm2_g.reshape(16, 128)], axis=0).astype(np.float32)
    hv = np.tile(np.concatenate([attn_q_norm_g[0], attn_k_norm_g[0], idx_k_ln_g[0], idx_k_ln_b[0]])[None, :], (128, 1)).astype(np.float32)
    cwT = np.ascontiguousarray(conv_w[0].T.reshape(8, 128, 3).transpose(1, 0, 2).reshape(128, 24)).astype(np.float32)
    invf = np.float32(10000.0) ** (-(np.arange(32, dtype=np.float32) * np.float32(2.0) / np.float32(64)))
    return {
        "w_in": w_in, "w_out": np.ascontiguousarray(attn_w_out[0]), "ada_w": np.ascontiguousarray(ada_w),
        "vecs": np.ascontiguousarray(vecs), "hv": np.ascontiguousarray(hv),
        "cw_in": np.ascontiguousarray(conv_w_in[0]), "cwT": cwT, "cw_out": np.ascontiguousarray(conv_w_out[0]),
        "wg": np.ascontiguousarray(ffn_w_gate), "wu": np.ascontiguousarray(ffn_w_up), "wd": np.ascontiguousarray(ffn_w_down),
        "ident": np.eye(128, dtype=np.float32), "invf": np.tile(invf.astype(np.float32)[None, :], (128, 1)),
        "iota": np.tile(np.arange(512, dtype=np.float32)[None, :], (128, 1)),
    }


def kernel(x, c, positions, ada_w, ada_b, norm1_g, norm2_g, attn_w_in, attn_q_norm_g, attn_k_norm_g, idx_k_ln_g,
           idx_k_ln_b, attn_w_out, conv_w_in, conv_w, conv_w_out, ffn_w_gate, ffn_w_up, ffn_w_down):
    args = [np.asarray(a) for a in (x, c, positions, ada_w, ada_b, norm1_g, norm2_g, attn_w_in, attn_q_norm_g,
                                     attn_k_norm_g, idx_k_ln_g, idx_k_ln_b, attn_w_out, conv_w_in, conv_w, conv_w_out,
                                     ffn_w_gate, ffn_w_up, ffn_w_down)]
    x = args[0]
    B, T, _ = x.shape
    cfg = Cfg(T)
    if (T, STOP) not in _CACHE:
        _CACHE[(T, STOP)] = build(cfg, STOP)[0]
    nc = _CACHE[(T, STOP)]
    shared = _shared_inputs(*args[3:])
    ncores = 2 * B
    in_maps = [_host_inputs(cfg, r, *args, shared) for r in range(ncores)]
    res = run_bass_kernel_spmd(nc, in_maps, core_ids=list(range(ncores)))
    global LAST
    LAST = res
    out = np.empty((B, T, D), dtype=np.float32)
    for r in range(ncores):
        b, role = r // 2, r % 2
        tiles = cfg.tilesA if role == 0 else cfg.tilesB
        halo = cfg.haloA if role == 0 else cfg.haloB
        o = np.asarray(res.results[r]["out"]).reshape(cfg.NS, 128, D)
        for s, t in enumerate(tiles):
            if s == halo:
                continue
            out[b, t * 128:(t + 1) * 128, :] = o[s]
    return out
```

```python
import math
import numpy as np
import ml_dtypes
import concourse.bass as bass
import concourse.mybir as mybir
from concourse.bass_utils import run_bass_kernel_spmd

F32 = mybir.dt.float32
BF16 = mybir.dt.bfloat16
I32 = mybir.dt.int32
U8 = mybir.dt.uint8
ALU = mybir.AluOpType
AF = mybir.ActivationFunctionType
AX = mybir.AxisListType

D = 1024
FF = 2816
NFC = FF // 128
WCOLS = 2184
EPS = 1e-6
NIT = 25
TWO_PI = 2.0 * math.pi
C1 = 6.28125
C2 = TWO_PI - C1


class Tok:
    __slots__ = ("w", "r", "rd")

    def __init__(self):
        self.w = None
        self.r = {}
        self.rd = []


class Op:
    __slots__ = ("eng", "fn", "deps", "dma", "sig", "cnt", "dsem")

    def __init__(self, eng, fn, dma):
        self.eng = eng
        self.fn = fn
        self.deps = set()
        self.dma = dma
        self.sig = False
        self.cnt = 0
        self.dsem = None


class Prog:
    def __init__(self, nc):
        self.nc = nc
        self.ops = []
        self.engs = {"pe": nc.tensor, "act": nc.scalar, "dve": nc.vector,
                     "pool": nc.gpsimd, "sp": nc.sync}
        self.last = {}
        self.dmas_since_bar = []
        self.bar = {}

    def add(self, eng, fn, R=(), W=(), dma=None):
        idx = len(self.ops)
        op = Op(eng, fn, dma)
        deps = op.deps
        if eng in self.bar:
            deps.update(self.bar.pop(eng))
        for t in R:
            if t.w is not None:
                deps.add(t.w)
        for t in W:
            if t.w is not None:
                deps.add(t.w)
            deps.update(t.r.values())
            deps.update(t.rd)
        deps.discard(idx)
        for t in W:
            t.w = idx
            t.r = {}
            t.rd = []
        for t in R:
            if t.w == idx:
                continue
            if dma is not None:
                t.rd.append(idx)
            else:
                t.r[eng] = idx
        self.ops.append(op)
        if dma is None:
            self.last[eng] = idx
        else:
            self.dmas_since_bar.append(idx)
        return idx

    def barrier(self):
        s = set(self.last.values()) | set(self.dmas_since_bar)
        self.dmas_since_bar = []
        for e in self.engs:
            self.bar[e] = set(s) | self.bar.get(e, set())

    def emit(self, final_wait_eng="sp"):
        nc = self.nc
        ops = self.ops
        for op in ops:
            nd = set()
            for d in op.deps:
                dop = ops[d]
                if dop.dma is None and op.dma is None and dop.eng == op.eng == "pe":
                    continue
                nd.add(d)
                dop.sig = True
            op.deps = nd
        esem = {e: nc.semaphore("se_" + e).__enter__() for e in self.engs}
        dsem, dcnt = {}, {}
        ecnt = {e: 0 for e in self.engs}
        for op in ops:
            if op.dma is not None:
                if op.dma not in dsem:
                    dsem[op.dma] = nc.semaphore("sd_%d" % len(dsem)).__enter__()
                    dcnt[op.dma] = 0
                dcnt[op.dma] += 16
                op.cnt = dcnt[op.dma]
                op.dsem = dsem[op.dma]
                op.sig = True
            elif op.sig:
                ecnt[op.eng] += 1
                op.cnt = ecnt[op.eng]
        waited = {e: {} for e in self.engs}
        for op in ops:
            E = self.engs[op.eng]
            wd = waited[op.eng]
            need = {}
            for d in op.deps:
                dop = ops[d]
                if dop.dma is not None:
                    key, sem = ("d", dop.dma), dop.dsem
                else:
                    key, sem = ("e", dop.eng), esem[dop.eng]
                if need.get(key, (None, 0))[1] < dop.cnt:
                    need[key] = (sem, dop.cnt)
            for key, (sem, cnt) in need.items():
                if wd.get(key, 0) < cnt:
                    E.wait_ge(sem, cnt)
                    wd[key] = cnt
            ins = op.fn()
            if op.sig:
                ins.then_inc(op.dsem if op.dma is not None else esem[op.eng], 16 if op.dma is not None else 1)
        E = self.engs[final_wait_eng]
        for k, sem in dsem.items():
            E.wait_ge(sem, dcnt[k])
        for e in self.engs:
            if ecnt[e] > 0 and e != final_wait_eng:
                E.wait_ge(esem[e], ecnt[e])
        self.stats = (len(ops), ecnt, len(dsem))


def _dsize(dt):
    if dt in (F32, I32):
        return 4
    if dt == BF16:
        return 2
    return 1


class Arena:
    def __init__(self, nc, nbytes):
        self.t = nc.sbuf_tensor("arena", [128, nbytes // 4], F32).__enter__()
        self.cap = nbytes
        self.off = 0
        self.peak = 0

    def mark(self):
        return self.off

    def release(self, m):
        self.off = m

    def alloc(self, free, dt=F32, parts=128):
        if isinstance(free, int):
            free = [free]
        n = 1
        for f in free:
            n *= f
        sz = (n * _dsize(dt) + 63) // 64 * 64
        assert self.off + sz <= self.cap, "SBUF arena overflow: need %d have %d" % (self.off + sz, self.cap)
        ap = self.t[0:parts, self.off // 4:(self.off + sz) // 4]
        self.off += sz
        self.peak = max(self.peak, self.off)
        if dt != F32:
            ap = ap.bitcast(dt)
        ap = ap[:, 0:n]
        if len(free) == 2:
            ap = ap.rearrange("p (a b) -> p a b", a=free[0])
        elif len(free) == 3:
            ap = ap.rearrange("p (a b c) -> p a b c", a=free[0], b=free[1])
        return ap


class Cfg:
    def __init__(self, T):
        self.T = T
        self.NT = T // 128
        assert self.NT % 4 == 0
        CH = self.NT // 4
        self.CH = CH
        self.NS = 2 * CH + 1
        self.tilesA = list(range(CH)) + [3 * CH - 1] + list(range(3 * CH, 4 * CH))
        self.tilesB = [CH - 1] + list(range(CH, 3 * CH))
        self.ext = [max(a, b) + 1 for a, b in zip(self.tilesA, self.tilesB)]
        self.dchunk = [min(a, b) // 4 for a, b in zip(self.tilesA, self.tilesB)]
        self.haloA = CH
        self.haloB = 0
        self.topk = min(256, T // 4)


def build(cfg, stop=None):
    T, NT, NS = cfg.T, cfg.NT, cfg.NS
    nc = bass.Bass("TRN2", target_bir_lowering=False)
    P = Prog(nc)

    def din(name, shape, dt=F32):
        return nc.dram_tensor(name, list(shape), dt, kind="ExternalInput").ap()

    x_seq = din("x_seq", [T, D])
    x_own = din("x_own", [NS * 128, D])
    pos_seq = din("pos_seq", [128, NT], I32)
    pos_own = din("pos_own", [128, NS], I32)
    qpos_d = din("qpos", [128, NS])
    cT_d = din("cT", [128, 8])
    w_in = din("w_in", [D, WCOLS])
    w_out = din("w_out", [D, D])
    ada_w = din("ada_w", [2, D, 6 * D])
    vecs_d = din("vecs", [128, 128])
    hv_d = din("hv", [128, 256])
    cw_in = din("cw_in", [D, 3 * D])
    cwT_d = din("cwT", [128, 24])
    cw_out = din("cw_out", [D, D])
    wg_d = din("wg", [2, D, FF])
    wu_d = din("wu", [2, D, FF])
    wd_d = din("wd", [2, FF, D])
    ident_d = din("ident", [128, 128])
    invf_d = din("invf", [128, 32])
    iota_d = din("iota", [128, 512])
    out_d = nc.dram_tensor("out", [NS * 128, D], F32, kind="ExternalOutput").ap()
    dbg_d = nc.dram_tensor("dbg", [128, 8192], F32, kind="ExternalOutput").ap() if stop else None
    qTs = nc.dram_tensor("qTs", [NS, 128, 1024], BF16, kind="Internal").ap()
    qiTs = nc.dram_tensor("qiTs", [NS, 128, 512], BF16, kind="Internal").ap()
    x1s = nc.dram_tensor("x1s", [NS * 128, D], F32, kind="Internal").ap()
    wg_b = nc.dram_tensor("wg_b", [2, D, FF], BF16, kind="Internal").ap()
    wu_b = nc.dram_tensor("wu_b", [2, D, FF], BF16, kind="Internal").ap()
    wd_b = nc.dram_tensor("wd_b", [2, FF, D], BF16, kind="Internal").ap()
    cwin_b = nc.dram_tensor("cwin_b", [D, 3 * D], BF16, kind="Internal").ap()
    cwout_b = nc.dram_tensor("cwout_b", [D, D], BF16, kind="Internal").ap()
    t_wgb = [Tok(), Tok()]; t_wub = [Tok(), Tok()]; t_wdb = [Tok(), Tok()]; t_cwinb = Tok(); t_cwoutb = Tok()
    t_qTs = [Tok() for _ in range(NS)]
    t_qiTs = [Tok() for _ in range(NS)]
    t_x1s = [Tok() for _ in range(NS)]
    t_out = [Tok() for _ in range(NS)]

    AR = Arena(nc, 206 * 1024)
    pbank = [nc.psum_tensor("pb%d" % k, [128, 512], F32).__enter__() for k in range(8)]
    tpb = [Tok() for _ in range(8)]

    def pbf(k):
        return pbank[k][:].bitcast(BF16)

    V, S, G, PE = nc.vector, nc.scalar, nc.gpsimd, nc.tensor

    def A(eng, fn, R=(), W=()):
        P.add(eng, fn, R, W)

    dma_ctr = [0]

    def DMA(q, out, in_, R=(), W=(), key=None, **kw):
        if key is None:
            dma_ctr[0] += 1
            key = "k%d" % dma_ctr[0]
        e = {"sp": nc.sync, "pool": nc.gpsimd, "act": nc.scalar}[q]
        P.add(q, lambda: e.dma_start(out=out, in_=in_, **kw), R, W, dma=key)

    dbg_off = [0]

    def dump(ap2d, toks, n, bf=False):
        if stop.endswith("x"):
            return
        o = dbg_off[0]
        if bf:
            tmp = AR.alloc(n)
            tt = Tok()
            A("dve", lambda: V.tensor_copy(out=tmp, in_=ap2d), toks, [tt])
            DMA("sp", dbg_d[:, o:o + n], tmp, R=[tt], W=[Tok()])
        else:
            DMA("sp", dbg_d[:, o:o + n], ap2d, R=toks, W=[Tok()])
        dbg_off[0] += n

    identF = AR.alloc(128); t_identF = Tok()
    identB = AR.alloc(128, BF16); t_identB = Tok()
    onesF = AR.alloc(128); t_onesF = Tok()
    iota = AR.alloc(512); t_iota = Tok()
    vecT = AR.alloc(128); t_vecT = Tok()
    modT = AR.alloc(96); t_modT = Tok()
    gscT = AR.alloc(32); t_gscT = Tok()
    wsign = AR.alloc([NS, 8]); t_wsign = Tok()
    cwT = AR.alloc(24); t_cwT = Tok()
    qpos = AR.alloc(NS); t_qpos = Tok()
    hv = AR.alloc(256); t_hv = Tok()
    G0 = AR.alloc(1024); t_G0 = Tok()

    DMA("sp", identF, ident_d, W=[t_identF])
    DMA("sp", iota, iota_d, W=[t_iota])
    DMA("sp", cwT, cwT_d, W=[t_cwT])
    DMA("sp", qpos, qpos_d, W=[t_qpos])
    DMA("sp", hv, hv_d, W=[t_hv])
    A("dve", lambda: V.tensor_copy(out=identB, in_=identF), [t_identF], [t_identB])
    A("dve", lambda: V.memset(onesF, 1.0), [], [t_onesF])

    if stop == "pre":
        dump(identF, [t_identF], 128)
        dump(identB, [t_identB], 128, bf=True)
        dump(hv, [t_hv], 256)
        P.emit()
        return nc, P, AR
    m0 = AR.mark()
    vecs_sb = AR.alloc(128); t_vecs = Tok()
    cT_sb = AR.alloc(8); t_cT = Tok()
    cact2 = AR.alloc([8, 2]); t_cact = Tok()
    adaw = [AR.alloc([8, 512]) for _ in range(2)]
    t_adaw = [Tok(), Tok()]
    DMA("sp", vecs_sb, vecs_d, W=[t_vecs])
    DMA("sp", cT_sb, cT_d, W=[t_cT])
    A("pe", lambda: PE.transpose(out=pbank[0][:, 0:128], in_=vecs_sb, identity=identF), [t_vecs, t_identF], [tpb[0]])
    A("act", lambda: S.copy(out=vecT, in_=pbank[0][:, 0:128]), [tpb[0]], [t_vecT])
    A("act", lambda: S.activation(out=cact2[:, :, 0], in_=cT_sb, func=AF.Silu), [t_cT], [t_cact])
    A("act", lambda: S.activation(out=cact2[:, :, 1], in_=cT_sb, func=AF.Silu), [t_cT], [t_cact])
    n = 0
    for L in range(2):
        src = ada_w[L].rearrange("(k p) n -> p k n", p=128)
        for cg in range(12):
            b = n % 2
            n += 1
            DMA("sp", adaw[b], src[:, :, cg * 512:(cg + 1) * 512], W=[t_adaw[b]], key="adaw%d" % b)
            for m4 in range(4):
                m = L * 48 + cg * 4 + m4
                for k in range(8):
                    A("pe", (lambda b=b, m=m, m4=m4, k=k: PE.matmul(
                        pbank[1][:, 2 * m:2 * m + 2], lhsT=adaw[b][:, k, m4 * 128:(m4 + 1) * 128],
                        rhs=cact2[:, k, :], start=(k == 0), stop=(k == 7))),
                      [t_adaw[b], t_cact], [tpb[1]])
    if stop == "p0a":
        A("act", lambda: S.copy(out=modT, in_=pbank[1][:, 0:96]), [tpb[1]], [t_modT])
        dump(modT, [t_modT], 96)
        dump(vecT, [t_vecT], 128)
        dump(cact2.rearrange("p a b -> p (a b)"), [t_cact], 16)
        P.emit()
        return nc, P, AR
    A("dve", lambda: V.tensor_tensor(out=modT, in0=pbank[1][:, 0:192].rearrange("p (m t) -> p m t", t=2)[:, :, 0],
                                     in1=vecT[:, 0:96], op=ALU.add), [tpb[1], t_vecT], [t_modT])
    for L in range(2):
        for s in range(2):
            o = (2 * L + s) * 8
            sc = modT[:, L * 48 + (8 if s == 0 else 32): L * 48 + (16 if s == 0 else 40)]
            ng = vecT[:, 96 + s * 16 + L * 8: 96 + s * 16 + L * 8 + 8]
            A("dve", (lambda o=o, sc=sc, ng=ng: V.scalar_tensor_tensor(
                out=gscT[:, o:o + 8], in0=sc, scalar=1.0, in1=ng, op0=ALU.add, op1=ALU.mult)),
              [t_modT, t_vecT], [t_gscT])

    def mod_cols(L, which):
        return modT[:, L * 48 + which * 8: L * 48 + which * 8 + 8]

    def make_gate(gcols, dst, t_dst, dg_bufs, t_dg, banks):
        for j in range(8):
            b = j % 2
            A("dve", (lambda j=j, b=b: V.tensor_scalar(out=dg_bufs[b], in0=identF, scalar1=gcols[:, j:j + 1],
                                                      scalar2=None, op0=ALU.mult)),
              [t_identF, t_modT], [t_dg[b]])
            bk = banks[j // 4]
            A("pe", (lambda j=j, b=b, bk=bk: PE.matmul(pbank[bk][:, (j % 4) * 128:(j % 4 + 1) * 128], lhsT=onesF,
                                                      rhs=dg_bufs[b], start=True, stop=True)),
              [t_onesF, t_dg[b]], [tpb[bk]])
        for h in range(2):
            A("act", (lambda h=h: S.copy(out=dst[:, h * 512:(h + 1) * 512], in_=pbank[banks[h]][:])),
              [tpb[banks[h]]], [t_dst])

    if stop == "p0b":
        dump(modT, [t_modT], 96)
        dump(gscT, [t_gscT], 32)
        P.emit()
        return nc, P, AR
    dgb = [AR.alloc(128), AR.alloc(128)]
    t_dgb = [Tok(), Tok()]
    make_gate(mod_cols(0, 2), G0, t_G0, dgb, t_dgb, [2, 3])
    if stop == "p0c":
        dump(G0, [t_G0], 1024)
        P.emit()
        return nc, P, AR
    P.barrier()
    if stop == "p0":
        dump(modT, [t_modT], 96)
        dump(gscT, [t_gscT], 32)
        dump(G0, [t_G0], 1024)
        dump(vecT, [t_vecT], 128)
        P.emit()
        return nc, P, AR
    AR.release(m0)

    def norm_T(xt, t_xt, gs, shc, hT_dst, t_hT, scr, bank):
        junkb, t_junkb, ss, t_ss, xn, t_xn = scr
        A("act", lambda: S.activation(out=junkb, in_=xt, func=AF.Square, accum_out=ss[:, 0:1]), [t_xt], [t_junkb, t_ss])
        A("act", lambda: S.activation(out=ss[:, 1:2], in_=ss[:, 0:1], func=AF.Sqrt, scale=1.0 / D, bias=eps_t[:, 0:1]),
          [t_ss, t_eps], [t_ss])
        A("dve", lambda: V.reciprocal(out=ss[:, 2:3], in_=ss[:, 1:2]), [t_ss], [t_ss])
        A("act", lambda: S.activation(out=xn, in_=xt, func=AF.Identity, scale=ss[:, 2:3]), [t_xt, t_ss], [t_xn])
        pv = pbf(bank)
        for j in range(8):
            A("pe", (lambda j=j: PE.transpose(out=pv[:, j * 128:(j + 1) * 128], in_=xn[:, j * 128:(j + 1) * 128],
                                              identity=identB)), [t_xn, t_identB], [tpb[bank]])
        for j in range(8):
            A("act", (lambda j=j: S.activation(out=hT_dst[:, j, :], in_=pv[:, j * 128:(j + 1) * 128], func=AF.Identity,
                                               scale=gs[:, j:j + 1], bias=shc[:, j:j + 1])),
              [tpb[bank], t_gscT, t_modT], [t_hT])

    eps_t = AR.alloc(4); t_eps = Tok()
    A("dve", lambda: V.memset(eps_t, EPS), [], [t_eps])

    mA = AR.mark()
    kT_all = AR.alloc([2, T], BF16); t_kT = [Tok() for _ in range(NT)]
    Vaug = AR.alloc([NT, 4, 66], BF16); t_V = [Tok() for _ in range(NT)]
    kiT = AR.alloc(T, BF16); t_kiT = [Tok() for _ in range(NT)]
    t_Vones = Tok()
    A("pool", lambda: G.memset(Vaug[:, :, :, 64:65], 1.0), [], [t_Vones] + t_V)

    mA1 = AR.mark()
    Win = AR.alloc([8, WCOLS], BF16); t_Win = [Tok() for _ in range(8)]
    wsrc = w_in.rearrange("(k p) n -> p k n", p=128)
    for k in range(8):
        DMA("pool", Win[:, k, :], wsrc[:, k, :], W=[t_Win[k]], key="win%d" % k, max_dma_last_dim=4096)
    for L in range(2):
        DMA("pool", wg_b[L], wg_d[L], W=[t_wgb[L]], key="cwg%d" % L, max_dma_last_dim=4096)
        DMA("pool", wu_b[L], wu_d[L], W=[t_wub[L]], key="cwu%d" % L, max_dma_last_dim=4096)
        DMA("pool", wd_b[L], wd_d[L], W=[t_wdb[L]], key="cwd%d" % L, max_dma_last_dim=4096)
        if L == 0:
            DMA("pool", cwin_b, cw_in, W=[t_cwinb], key="ccwin", max_dma_last_dim=4096)
            DMA("pool", cwout_b, cw_out, W=[t_cwoutb], key="ccwout", max_dma_last_dim=4096)

    def rope_tables(pos_d, n, cos_t, sin_t, t_tab):
        m = AR.mark()
        pi_ = AR.alloc(n, I32); t_pi = Tok()
        pf = AR.alloc(n); ang = AR.alloc([n, 32]); u = AR.alloc([n, 32]); ki_ = AR.alloc([n, 32], I32)
        kf = AR.alloc([n, 32]); r = AR.alloc([n, 32]); r2 = AR.alloc([n, 32]); tmp = AR.alloc([n, 32])
        invt = AR.alloc(32)
        tk = Tok()
        DMA("sp", pi_, pos_d, W=[t_pi])
        DMA("sp", invt, invf_d, W=[tk])
        A("dve", lambda: V.tensor_copy(out=pf, in_=pi_), [t_pi], [tk])
        A("dve", lambda: V.tensor_tensor(out=ang, in0=pf.unsqueeze(2).to_broadcast([128, n, 32]),
                                         in1=invt.unsqueeze(1).to_broadcast([128, n, 32]), op=ALU.mult), [tk], [tk])
        A("dve", lambda: V.tensor_scalar(out=u, in0=ang, scalar1=1.0 / TWO_PI, scalar2=None, op0=ALU.mult), [tk], [tk])
        A("dve", lambda: V.tensor_copy(out=ki_, in_=u), [tk], [tk])
        A("dve", lambda: V.tensor_copy(out=kf, in_=ki_), [tk], [tk])
        A("dve", lambda: V.scalar_tensor_tensor(out=r, in0=kf, scalar=-C1, in1=ang, op0=ALU.mult, op1=ALU.add), [tk], [tk])
        A("dve", lambda: V.scalar_tensor_tensor(out=r, in0=kf, scalar=-C2, in1=r, op0=ALU.mult, op1=ALU.add), [tk], [tk])
        A("dve", lambda: V.tensor_scalar(out=r, in0=r, scalar1=-3.1415925, scalar2=3.1415925, op0=ALU.max, op1=ALU.min), [tk], [tk])
        A("dve", lambda: V.tensor_scalar(out=r2, in0=r, scalar1=math.pi / 2, scalar2=None, op0=ALU.add), [tk], [tk])
        A("dve", lambda: V.tensor_scalar(out=tmp, in0=r2, scalar1=math.pi, scalar2=-TWO_PI, op0=ALU.is_gt, op1=ALU.mult), [tk], [tk])
        A("dve", lambda: V.tensor_tensor(out=r2, in0=r2, in1=tmp, op=ALU.add), [tk], [tk])
        A("dve", lambda: V.tensor_scalar(out=r2, in0=r2, scalar1=-3.1415925, scalar2=3.1415925, op0=ALU.max, op1=ALU.min), [tk], [tk])
        A("act", lambda: S.activation(out=sin_t, in_=r, func=AF.Sin), [tk], [t_tab])
        A("act", lambda: S.activation(out=cos_t, in_=r2, func=AF.Sin), [tk], [t_tab])
        return m

    cos_s = AR.alloc([NT, 32]); sin_s = AR.alloc([NT, 32]); t_tabs = Tok()
    cos_o = AR.alloc([NS, 32]); sin_o = AR.alloc([NS, 32]); t_tabo = Tok()
    hn = NT // 2
    for hf in range(2):
        mm_ = rope_tables(pos_seq[:, hf * hn:(hf + 1) * hn], hn, cos_s[:, hf * hn:(hf + 1) * hn, :], sin_s[:, hf * hn:(hf + 1) * hn, :], t_tabs)
        P.barrier()
        AR.release(mm_)
    mm_ = rope_tables(pos_own, NS, cos_o, sin_o, t_tabo)
    P.barrier()
    AR.release(mm_)

    xt = [AR.alloc(1024), AR.alloc(1024)]; t_xt = [Tok(), Tok()]
    hT = [AR.alloc([8, 128], BF16), AR.alloc([8, 128], BF16)]; t_hT = [Tok(), Tok()]
    scr = (AR.alloc(1024, BF16), Tok(), AR.alloc(4), Tok(), AR.alloc(1024, BF16), Tok())
    sq = AR.alloc(1024); t_sq = Tok()
    qn = AR.alloc(1024); t_qn = Tok()
    r1 = AR.alloc(1024); t_r1 = Tok()
    r2_ = AR.alloc(1024); t_r2 = Tok()
    qb = AR.alloc(1024, BF16); t_qb = Tok()
    sm = AR.alloc(64); t_sm = Tok()
    qTt = [AR.alloc(1024, BF16), AR.alloc(1024, BF16)]; t_qTt = [Tok(), Tok()]
    qiTt = [AR.alloc(512, BF16), AR.alloc(512, BF16)]; t_qiTt = [Tok(), Tok()]
    kib = AR.alloc(128, BF16); t_kib = Tok()
    qr = AR.alloc(512); t_qr = Tok()
    wab = AR.alloc(8); t_wab = Tok()

    def headnorm_rope(src_ps, t_src, H, gcol, cosv, sinv, t_tab, outb, t_outb):
        W_ = H * 64
        s3 = src_ps.rearrange("p (h d) -> p h d", h=H)
        A("act", lambda: S.activation(out=sq[:, 0:W_], in_=src_ps, func=AF.Square), [t_src], [t_sq])
        A("dve", lambda: V.tensor_reduce(out=sm[:, 0:H], in_=sq[:, 0:W_].rearrange("p (h d) -> p h d", h=H), axis=AX.X, op=ALU.add),
          [t_sq], [t_sm])
        A("act", lambda: S.activation(out=sm[:, 16:16 + H], in_=sm[:, 0:H], func=AF.Sqrt, scale=1.0 / 64, bias=eps_t[:, 0:1]),
          [t_sm, t_eps], [t_sm])
        A("dve", lambda: V.reciprocal(out=sm[:, 32:32 + H], in_=sm[:, 16:16 + H]), [t_sm], [t_sm])
        q3 = qn[:, 0:W_].rearrange("p (h d) -> p h d", h=H)
        A("dve", lambda: V.tensor_tensor(out=q3, in0=s3, in1=sm[:, 32:32 + H].unsqueeze(2).to_broadcast([128, H, 64]), op=ALU.mult),
          [t_src, t_sm], [t_qn])
        A("pool", lambda: G.tensor_tensor(out=q3, in0=q3, in1=hv[:, gcol:gcol + 64].unsqueeze(1).to_broadcast([128, H, 64]), op=ALU.mult),
          [t_qn, t_hv], [t_qn])
        rope(q3, t_qn, H, cosv, sinv, t_tab, outb, t_outb)

    def rope(q3, t_q3, H, cosv, sinv, t_tab, outb, t_outb):
        W_ = H * 64
        a3 = r1[:, 0:W_].rearrange("p (h d) -> p h d", h=H)
        b3 = r2_[:, 0:W_].rearrange("p (h d) -> p h d", h=H)
        o3 = outb[:, 0:W_].rearrange("p (h d) -> p h d", h=H)
        cb = cosv.unsqueeze(1).to_broadcast([128, H, 32])
        sb_ = sinv.unsqueeze(1).to_broadcast([128, H, 32])
        A("pool", lambda: G.tensor_tensor(out=a3[:, :, 0:32], in0=q3[:, :, 0:32], in1=cb, op=ALU.mult), [t_q3, t_tab], [t_r1])
        A("pool", lambda: G.tensor_tensor(out=a3[:, :, 32:64], in0=q3[:, :, 32:64], in1=cb, op=ALU.mult), [t_q3, t_tab], [t_r1])
        A("dve", lambda: V.tensor_tensor(out=b3[:, :, 0:32], in0=q3[:, :, 32:64], in1=sb_, op=ALU.mult), [t_q3, t_tab], [t_r2])
        A("dve", lambda: V.tensor_tensor(out=b3[:, :, 32:64], in0=q3[:, :, 0:32], in1=sb_, op=ALU.mult), [t_q3, t_tab], [t_r2])
        A("pool", lambda: G.tensor_tensor(out=o3[:, :, 0:32], in0=a3[:, :, 0:32], in1=b3[:, :, 0:32], op=ALU.subtract), [t_r1, t_r2], [t_outb])
        A("pool", lambda: G.tensor_tensor(out=o3[:, :, 32:64], in0=a3[:, :, 32:64], in1=b3[:, :, 32:64], op=ALU.add), [t_r1, t_r2], [t_outb])

    def proj(dst_bank, ncols, c0, hTb, t_hTb):
        for k in range(8):
            A("pe", (lambda k=k: PE.matmul(pbank[dst_bank][:, 0:ncols], lhsT=hTb[:, k, :], rhs=Win[:, k, c0:c0 + ncols],
                                           start=(k == 0), stop=(k == 7))), [t_hTb, t_Win[k]], [tpb[dst_bank]])

    gs1_0 = gscT[:, 0:8]
    sh1_0 = mod_cols(0, 0)
    for j in range(NT):
        b = j % 2
        DMA("sp", xt[b], x_seq[j * 128:(j + 1) * 128, :], W=[t_xt[b]], key="xt%d" % b)
        norm_T(xt[b], t_xt[b], gs1_0, sh1_0, hT[b], t_hT[b], scr, 0)
        proj(1, 512, 1024, hT[b], t_hT[b])
        proj(2, 128, 2048, hT[b], t_hT[b])
        cj, sj = cos_s[:, j, :], sin_s[:, j, :]
        headnorm_rope(pbank[1][:, 0:256], tpb[1], 4, 64, cj, sj, t_tabs, qb, t_qb)
        A("act", (lambda j=j: S.copy(out=Vaug[:, j, :, 0:64], in_=pbank[1][:, 256:512].rearrange("p (g d) -> p g d", g=4))),
          [tpb[1]], [t_V[j]])
        pv3 = pbf(3)
        for i in range(2):
            A("pe", (lambda i=i: PE.transpose(out=pv3[:, i * 128:(i + 1) * 128], in_=qb[:, i * 128:(i + 1) * 128], identity=identB)),
              [t_qb, t_identB], [tpb[3]])
        A("act", (lambda j=j: S.copy(out=kT_all[:, :, j * 128:(j + 1) * 128], in_=pv3[:, 0:256].rearrange("p (i t) -> p i t", i=2))),
          [tpb[3]], [t_kT[j]])
        A("dve", lambda: V.bn_stats(out=sm[:, 48:54], in_=pbank[2][:, 0:64]), [tpb[2]], [t_sm])
        A("dve", lambda: V.bn_aggr(out=sm[:, 54:56], in_=sm[:, 48:54]), [t_sm], [t_sm])
        A("act", lambda: S.activation(out=sm[:, 56:57], in_=sm[:, 55:56], func=AF.Sqrt, scale=1.0, bias=eps_t[:, 0:1]), [t_sm, t_eps], [t_sm])
        A("dve", lambda: V.reciprocal(out=sm[:, 57:58], in_=sm[:, 56:57]), [t_sm], [t_sm])
        A("dve", lambda: V.tensor_scalar(out=qn[:, 0:64], in0=pbank[2][:, 0:64], scalar1=sm[:, 54:55], scalar2=sm[:, 57:58],
                                         op0=ALU.subtract, op1=ALU.mult), [tpb[2], t_sm], [t_qn])
        A("pool", lambda: G.tensor_tensor(out=qn[:, 0:64], in0=qn[:, 0:64], in1=hv[:, 128:192], op=ALU.mult), [t_qn, t_hv], [t_qn])
        A("pool", lambda: G.tensor_tensor(out=qn[:, 0:64], in0=qn[:, 0:64], in1=hv[:, 192:256], op=ALU.add), [t_qn, t_hv], [t_qn])
        rope(qn[:, 0:64].rearrange("p (h d) -> p h d", h=1), t_qn, 1, cj, sj, t_tabs, kib, t_kib)
        A("pool", lambda: G.tensor_copy(out=kib[:, 64:128], in_=kib[:, 0:64]), [t_kib], [t_kib])
        A("pe", lambda: PE.transpose(out=pv3[:, 256:384], in_=kib, identity=identB), [t_kib, t_identB], [tpb[3]])
        A("act", (lambda j=j: S.copy(out=kiT[:, j * 128:(j + 1) * 128], in_=pv3[:, 256:384])), [tpb[3]], [t_kiT[j]])

    WSC = (8 ** -0.5) * (64 ** -0.5)
    for i in range(NS):
        b = i % 2
        DMA("sp", xt[b], x_own[i * 128:(i + 1) * 128, :], W=[t_xt[b]], key="xt%d" % b)
        norm_T(xt[b], t_xt[b], gs1_0, sh1_0, hT[b], t_hT[b], scr, 0)
        proj(1, 512, 0, hT[b], t_hT[b])
        proj(2, 512, 512, hT[b], t_hT[b])
        proj(4, 512, 1536, hT[b], t_hT[b])
        proj(5, 8, 2176, hT[b], t_hT[b])
        ci, si = cos_o[:, i, :], sin_o[:, i, :]
        for hh in range(2):
            headnorm_rope(pbank[1 + hh][:], tpb[1 + hh], 8, 0, ci, si, t_tabo, qb, t_qb)
            pv3 = pbf(3)
            for jj in range(4):
                A("pe", (lambda jj=jj, hh=hh: PE.transpose(out=pv3[:, (hh * 4 + jj) * 128:(hh * 4 + jj + 1) * 128],
                                                          in_=qb[:, jj * 128:(jj + 1) * 128], identity=identB)),
                  [t_qb, t_identB], [tpb[3]])
        A("act", (lambda b=b: S.copy(out=qTt[b], in_=pbf(3))), [tpb[3]], [t_qTt[b]])
        DMA("sp", qTs[i], qTt[b], R=[t_qTt[b]], W=[t_qTs[i]], key="qTs")
        A("act", (lambda i=i: S.activation(out=wsign[:, i, :], in_=pbank[5][:, 0:8], func=AF.Sign)), [tpb[5]], [t_wsign])
        A("act", lambda: S.activation(out=wab, in_=pbank[5][:, 0:8], func=AF.Abs, scale=WSC), [tpb[5]], [t_wab])
        A("act", lambda: S.copy(out=qn[:, 0:512], in_=pbank[4][:]), [tpb[4]], [t_qn])
        rope(qn[:, 0:512].rearrange("p (h d) -> p h d", h=8), t_qn, 8, ci, si, t_tabo, qr, t_qr)
        A("dve", lambda: V.tensor_tensor(out=qb[:, 0:512].rearrange("p (h d) -> p h d", h=8),
                                         in0=qr[:, 0:512].rearrange("p (h d) -> p h d", h=8),
                                         in1=wab.unsqueeze(2).to_broadcast([128, 8, 64]), op=ALU.mult),
          [t_qr, t_wab], [t_qb])
        pv6 = pbf(6)
        for jj in range(4):
            A("pe", (lambda jj=jj: PE.transpose(out=pv6[:, jj * 128:(jj + 1) * 128], in_=qb[:, jj * 128:(jj + 1) * 128], identity=identB)),
              [t_qb, t_identB], [tpb[6]])
        A("act", (lambda b=b: S.copy(out=qiTt[b], in_=pv6[:, 0:512])), [tpb[6]], [t_qiTt[b]])
        DMA("sp", qiTs[i], qiTt[b], R=[t_qiTt[b]], W=[t_qiTs[i]], key="qiTs")
    P.barrier()
    if stop in ("a1", "a1x"):
        dump(kT_all[:, 0, 0:512], t_kT, 512, bf=True)
        dump(kT_all[:, 1, 0:512], t_kT, 512, bf=True)
        dump(kiT[:, 0:512], t_kiT, 512, bf=True)
        dump(Vaug[:, 0:4, :, :].rearrange("p a g d -> p (a g d)"), t_V, 1056, bf=True)
        dump(qTt[(NS - 1) % 2], t_qTt, 1024, bf=True)
        dump(qiTt[(NS - 1) % 2], t_qiTt, 512, bf=True)
        dump(wsign.rearrange("p a h -> p (a h)"), [t_wsign], NS * 8)
        dump(cos_s.rearrange("p a h -> p (a h)"), [t_tabs], NT * 32)
        dump(sin_s.rearrange("p a h -> p (a h)"), [t_tabs], NT * 32)
        P.emit()
        return nc, P, AR
    AR.release(mA1)

    Wo = AR.alloc([8, 1024], BF16); t_Wo = Tok()
    for hh in range(2):
        DMA("pool", Wo[hh * 64:(hh + 1) * 64, :, :], w_out[hh * 512:(hh + 1) * 512, :].rearrange("(j d) c -> d j c", d=64),
            W=[t_Wo], key="wo", max_dma_last_dim=4096)
    I_ = AR.alloc(T); t_I = Tok()
    mask01 = AR.alloc(T, BF16); t_mask = Tok()
    junk8 = AR.alloc(T, U8); t_junk8 = Tok()
    qTb = [[AR.alloc(1024, BF16), AR.alloc(1024, BF16)] for _ in range(2)]; t_qTb = [Tok(), Tok()]
    qiTb = [[AR.alloc(512, BF16), AR.alloc(512, BF16)] for _ in range(2)]; t_qiTb = [Tok(), Tok()]
    for b_ in range(2):
        for hf_ in range(2):
            A("pool", (lambda b_=b_, hf_=hf_: G.memset(qTb[b_][hf_], 0.0)), [], [t_qTb[b_]])
            A("pool", (lambda b_=b_, hf_=hf_: G.memset(qiTb[b_][hf_], 0.0)), [], [t_qiTb[b_]])
    NR = 4
    Rb = [AR.alloc(512, BF16) for _ in range(NR)]; t_Rb = [Tok() for _ in range(NR)]
    Dh = AR.alloc([8, 128], BF16); t_Dh = Tok()
    biasb = [AR.alloc(512, BF16), AR.alloc(512, BF16)]; t_biasb = [Tok(), Tok()]
    NPB = 6
    Pexp = [AR.alloc(512, BF16) for _ in range(NPB)]; t_Pexp = [Tok() for _ in range(NPB)]
    Sel4 = AR.alloc(512, BF16); t_Sel4 = Tok()
    for r_ in range(4):
        A("pool", (lambda r_=r_: G.tensor_copy(out=Sel4[:, r_ * 128:(r_ + 1) * 128], in_=identB)), [t_identB], [t_Sel4])
    rs = AR.alloc(512); t_rs = Tok()
    bcS = AR.alloc(512); t_bcS = Tok()
    ys = AR.alloc(512); t_ys = Tok()
    numS = ys; t_numS = t_ys
    oT_all = AR.alloc([2, 512], BF16); t_oTlo = Tok(); t_oThi = Tok()
    oT_tmp = AR.alloc([2, 512], BF16); t_oTtmp = Tok()
    xa = AR.alloc(512); t_xa = Tok()
    ta = AR.alloc(512); t_ta = Tok()
    bs = AR.alloc(16); t_bs = Tok()
    tr_banks = [0, 1, 2]
    trc = [0]

    def tbank():
        k = tr_banks[trc[0] % 3]
        trc[0] += 1
        return k

    rbc = [0]
    SCALE = 64 ** -0.5

    def indexer(i):
        b = i % 2
        E = cfg.ext[i]
        nch = (E + 3) // 4
        for hf_ in range(2):
            DMA("sp", qiTb[b][hf_][hf_ * 64:(hf_ + 1) * 64, :], qiTs[i][hf_ * 64:(hf_ + 1) * 64, :], R=[t_qiTs[i]], W=[t_qiTb[b]],
                key="qiTb%d" % b)
        for hf_ in range(2):
            DMA("sp", qTb[b][hf_][hf_ * 64:(hf_ + 1) * 64, :], qTs[i][hf_ * 64:(hf_ + 1) * 64, :], R=[t_qTs[i]], W=[t_qTb[b]],
                key="qTb%d" % b)
        for h in range(8):
            A("pool", (lambda h=h, i=i: G.tensor_scalar(out=Dh[:, h, :], in0=identB, scalar1=wsign[:, i, h:h + 1], scalar2=None, op0=ALU.mult)),
              [t_identB, t_wsign], [t_Dh])
        for c in range(nch):
            kts = [t_kiT[jj] for jj in range(c * 4, c * 4 + 4)]
            need_bias = c >= cfg.dchunk[i]
            if need_bias:
                bb = c % 2
                A("dve", (lambda c=c, i=i: V.tensor_scalar(out=bs[:, 6:7], in0=qpos[:, i:i + 1], scalar1=float(-512 * c), scalar2=None, op0=ALU.add)),
                  [t_qpos], [t_bs])
                A("dve", (lambda bb=bb: V.tensor_scalar(out=biasb[bb], in0=iota, scalar1=bs[:, 6:7], scalar2=-1e30, op0=ALU.is_gt, op1=ALU.mult)),
                  [t_iota, t_bs], [t_biasb[bb]])
            banks = []
            LA = 2

            def acc(hh, nb=need_bias):
                A("pe", (lambda hh=hh, rbb=banks[hh], nb=nb: PE.matmul(pbank[3][:], lhsT=Dh[:, hh, :], rhs=Rb[rbb], start=(hh == 0),
                                                                     stop=(hh == 7 and not nb))),
                  [t_Dh, t_Rb[banks[hh]]], [tpb[3]])

            for h in range(8):
                bk = tbank()
                hp, pr = h % 2, h // 2
                A("pe", (lambda bk=bk, hp=hp, pr=pr, c=c, b=b: PE.matmul(
                    pbank[bk][:], lhsT=qiTb[b][hp][:, pr * 128:(pr + 1) * 128],
                    rhs=kiT[:, c * 512:(c + 1) * 512], start=True, stop=True)),
                  [t_qiTb[b]] + kts, [tpb[bk]])
                rb = rbc[0] % NR
                rbc[0] += 1
                A("act", (lambda bk=bk, rb=rb: S.activation(out=Rb[rb], in_=pbank[bk][:], func=AF.Relu)), [tpb[bk]], [t_Rb[rb]])
                banks.append(rb)
                if h >= LA:
                    acc(h - LA)
            for hh in range(8 - LA, 8):
                acc(hh)
            if need_bias:
                A("pe", (lambda bb=bb: PE.matmul(pbank[3][:], lhsT=identB, rhs=biasb[bb], start=False, stop=True)),
                  [t_identB, t_biasb[bb]], [tpb[3]])
            A("act", (lambda c=c: S.copy(out=I_[:, c * 512:(c + 1) * 512], in_=pbank[3][:])), [tpb[3]], [t_I])

    def topk(i):
        E = cfg.ext[i]
        Sn = ((E + 3) // 4) * 512
        K0 = min(256, Sn)
        A("dve", lambda: V.tensor_reduce(out=bs[:, 0:1], in_=I_[:, 0:Sn], axis=AX.X, op=ALU.max), [t_I], [t_bs])
        A("dve", lambda: V.tensor_reduce(out=bs[:, 1:2], in_=I_[:, 0:K0], axis=AX.X, op=ALU.min), [t_I], [t_bs])
        A("dve", lambda: V.tensor_scalar(out=bs[:, 1:2], in0=bs[:, 1:2], scalar1=-1e29, scalar2=None, op0=ALU.max), [t_bs], [t_bs])
        A("dve", lambda: V.tensor_tensor(out=bs[:, 2:3], in0=bs[:, 0:1], in1=bs[:, 1:2], op=ALU.subtract), [t_bs], [t_bs])
        for it in range(NIT):
            f = 2.0 ** (-(it + 1))
            A("dve", (lambda f=f: V.tensor_scalar(out=bs[:, 3:4], in0=bs[:, 2:3], scalar1=f, scalar2=bs[:, 1:2], op0=ALU.mult, op1=ALU.add)),
              [t_bs], [t_bs])
            A("dve", lambda: V.tensor_scalar(out=junk8[:, 0:Sn], in0=I_[:, 0:Sn], scalar1=bs[:, 3:4], scalar2=None, op0=ALU.is_ge,
                                             op1=ALU.add, accum_out=bs[:, 4:5]), [t_I, t_bs], [t_junk8, t_bs])
            A("dve", (lambda f=f: V.tensor_scalar(out=bs[:, 5:6], in0=bs[:, 4:5], scalar1=cfg.topk - 0.5, scalar2=f, op0=ALU.is_gt, op1=ALU.mult)),
              [t_bs], [t_bs])
            A("dve", lambda: V.tensor_scalar(out=bs[:, 1:2], in0=bs[:, 5:6], scalar1=bs[:, 2:3], scalar2=bs[:, 1:2], op0=ALU.mult, op1=ALU.add),
              [t_bs], [t_bs])
        A("dve", lambda: V.tensor_scalar(out=mask01[:, 0:E * 128], in0=I_[:, 0:E * 128], scalar1=bs[:, 1:2], scalar2=-30000.0,
                                         op0=ALU.is_lt, op1=ALU.mult), [t_I, t_bs], [t_mask])

    pbc = [0]

    def attention(i):
        b = i % 2
        E = cfg.ext[i]
        steps = [(kb, g) for kb in range(E) for g in range(4)]
        DL = 3
        assert NPB > DL
        pmof = {}

        def front(n):
            kb, g = steps[n]
            hp, gi = g // 2, g % 2
            bk = tbank()
            A("pe", (lambda bk=bk, hp=hp, gi=gi, kb=kb, b=b: PE.matmul(
                pbank[bk][:], lhsT=kT_all[:, gi, kb * 128:(kb + 1) * 128],
                rhs=qTb[b][hp][:, gi * 512:(gi + 1) * 512], start=True, stop=False)),
              [t_kT[kb], t_qTb[b]], [tpb[bk]])
            A("pe", (lambda bk=bk, kb=kb: PE.matmul(pbank[bk][:], lhsT=mask01[:, kb * 128:(kb + 1) * 128], rhs=Sel4, start=False, stop=True)),
              [t_mask, t_Sel4], [tpb[bk]])
            pe_ = pbc[0] % NPB
            pbc[0] += 1
            pmof[n] = pe_
            A("act", (lambda bk=bk, pe_=pe_: S.activation(out=Pexp[pe_], in_=pbank[bk][:], func=AF.Exp, scale=SCALE)),
              [tpb[bk]], [t_Pexp[pe_]])

        def back(n):
            kb, g = steps[n]
            pe_ = pmof[n]
            A("pe", (lambda g=g, kb=kb, pe_=pe_, E=E: PE.matmul(pbank[4 + g][0:65, :], lhsT=Vaug[:, kb, g, 0:65], rhs=Pexp[pe_],
                                                               start=(kb == 0), stop=(kb == E - 1))),
              [t_V[kb], t_Vones, t_Pexp[pe_]], [tpb[4 + g]])

        for n in range(len(steps) + DL):
            if n < len(steps):
                front(n)
            if n - DL >= 0:
                back(n - DL)
        if ATT_PARTS < 2:
            return
        for g in range(4):
            A("act", (lambda g=g: S.activation(out=rs[64:65, :], in_=pbank[4 + g][64:65, :], func=AF.Ln)), [tpb[4 + g]], [t_rs])
            A("act", lambda: S.activation(out=rs[64:65, :], in_=rs[64:65, :], func=AF.Exp, scale=-1.0), [t_rs], [t_rs])
            bk = tbank()
            A("pe", (lambda bk=bk: PE.matmul(pbank[bk][0:64, :], lhsT=onesF[64:65, 0:64], rhs=rs[64:65, :], start=True, stop=True)),
              [t_onesF, t_rs], [tpb[bk]])
            A("act", (lambda bk=bk: S.copy(out=bcS[0:64, :], in_=pbank[bk][0:64, :])), [tpb[bk]], [t_bcS])
            A("act", (lambda g=g: S.copy(out=numS[0:64, :], in_=pbank[4 + g][0:64, :])), [tpb[4 + g]], [t_numS])
            if g < 2:
                A("pool", (lambda g=g: G.tensor_tensor(out=oT_all[0:64, g, :], in0=numS[0:64, :], in1=bcS[0:64, :], op=ALU.mult)),
                  [t_numS, t_bcS], [t_oTlo])
            else:
                A("pool", (lambda g=g: G.tensor_tensor(out=oT_tmp[0:64, g - 2, :], in0=numS[0:64, :], in1=bcS[0:64, :], op=ALU.mult)),
                  [t_numS, t_bcS], [t_oTtmp])
        if ATT_PARTS < 3:
            return
        DMA("sp", oT_all[64:128, :, :], oT_tmp[0:64, :, :], R=[t_oTtmp], W=[t_oThi], key="oThi")
        if ATT_PARTS < 4:
            return
        for ch in range(2):
            DMA("sp", xa, x_own[i * 128:(i + 1) * 128, ch * 512:(ch + 1) * 512], W=[t_xa], key="xa")
            bk = tbank()
            for gi in range(2):
                for r in range(4):
                    j = gi * 4 + r
                    A("pe", (lambda bk=bk, gi=gi, r=r, j=j, ch=ch: PE.matmul(
                        pbank[bk][:], lhsT=oT_all[:, gi, r * 128:(r + 1) * 128], rhs=Wo[:, j, ch * 512:(ch + 1) * 512],
                        start=(j == 0), stop=(j == 7))), [t_oTlo, t_oThi, t_Wo], [tpb[bk]])
            A("act", (lambda bk=bk: S.copy(out=ys, in_=pbank[bk][:])), [tpb[bk]], [t_ys])
            A("pool", (lambda ch=ch: G.tensor_tensor(out=ta, in0=ys, in1=G0[:, ch * 512:(ch + 1) * 512], op=ALU.mult)), [t_ys, t_G0], [t_ta])
            A("pool", lambda: G.tensor_tensor(out=ta, in0=ta, in1=xa, op=ALU.add), [t_ta, t_xa], [t_ta])
            DMA("sp", x1s[i * 128:(i + 1) * 128, ch * 512:(ch + 1) * 512], ta, R=[t_ta], W=[t_x1s[i]], key="x1s")

    indexer(0)
    if stop in ("a2i", "a2t", "a2a"):
        if stop in ("a2t", "a2a"):
            topk(0)
        if stop == "a2a":
            attention(0)
        dump(I_[:, 0:1024], [t_I], 1024)
        dump(mask01[:, 0:1024], [t_mask], 1024, bf=True)
        dump(bs, [t_bs], 16)
        dump(ta, [t_ta], 512)
        P.emit()
        return nc, P, AR
    for i in range(NS):
        topk(i)
        if i + 1 < NS:
            indexer(i + 1)
        attention(i)
    P.barrier()
    if stop in ("a2", "a2x"):
        dump(I_[:, 0:1024], [t_I], 1024)
        dump(mask01[:, 0:1024], [t_mask], 1024, bf=True)
        dump(bs, [t_bs], 16)
        dump(ta, [t_ta], 512)
        P.emit()
        return nc, P, AR
    AR.release(mA)

    Gt = [AR.alloc(1024) for _ in range(3)]; t_Gt = [Tok() for _ in range(3)]
    dgb = [AR.alloc(128), AR.alloc(128)]
    make_gate(mod_cols(0, 5), Gt[0], t_Gt[0], dgb, t_dgb, [0, 1])
    make_gate(mod_cols(1, 2), Gt[1], t_Gt[1], dgb, t_dgb, [2, 3])
    make_gate(mod_cols(1, 5), Gt[2], t_Gt[2], dgb, t_dgb, [0, 1])
    MS = 4
    xm = AR.alloc([MS, 1024]); t_xm = [Tok() for _ in range(MS)]
    hTm = AR.alloc([8, MS * 128], BF16); t_hTm = Tok()
    scrB2 = [(AR.alloc(1024, BF16), Tok(), AR.alloc(4), Tok(), AR.alloc(1024, BF16), Tok()) for _ in range(2)]
    t_hTm_s = [Tok() for _ in range(MS)]
    aT = AR.alloc([NFC, MS * 128], BF16); t_aT = Tok()
    Wd = AR.alloc([NFC, 1024], BF16); t_Wd = [Tok() for _ in range(4)]
    NWB = 2
    WA = [AR.alloc([8, 512], BF16) for _ in range(NWB)]; t_WA = [Tok() for _ in range(NWB)]
    WB = [AR.alloc([8, 512], BF16) for _ in range(NWB)]; t_WB = [Tok() for _ in range(NWB)]
    WC = [AR.alloc([8, 512], BF16) for _ in range(NWB)]; t_WC = [Tok() for _ in range(NWB)]
    sg = [AR.alloc(MS * 128), AR.alloc(MS * 128)]; t_sg = [Tok(), Tok()]
    zb = AR.alloc(MS * 128 + 2); t_zb = Tok()
    zc = AR.alloc(MS * 128); t_zc = Tok()
    carry = AR.alloc([8, 2]); t_carry = Tok()
    tb = AR.alloc(512); t_tb = Tok()
    A("dve", lambda: V.memset(carry, 0.0), [], [t_carry])
    wbc = [0]

    def norm_macro(ns, gs, shc):
        for s in range(ns):
            norm_T(xm[:, s, :], t_xm[s], gs, shc, hTm[:, :, s * 128:(s + 1) * 128], t_hTm_s[s], scrB2[s % 2], 6 + s % 2)

    def down(ns, nchunks, Gate, t_Gate):
        N = ns * 128
        for s in range(ns):
            for ch in range(2):
                bk = 4 + (s * 2 + ch) % 2
                for j in range(nchunks):
                    A("pe", (lambda bk=bk, j=j, s=s, ch=ch: PE.matmul(pbank[bk][:], lhsT=aT[:, j, s * 128:(s + 1) * 128],
                                                                     rhs=Wd[:, j, ch * 512:(ch + 1) * 512],
                                                                     start=(j == 0), stop=(j == nchunks - 1))),
                      [t_aT, t_Wd[j // 6]], [tpb[bk]])
                A("dve", (lambda bk=bk, ch=ch: V.tensor_tensor(out=tb, in0=pbank[bk][:], in1=Gate[:, ch * 512:(ch + 1) * 512], op=ALU.mult)),
                  [tpb[bk], t_Gate], [t_tb])
                A("pool", (lambda s=s, ch=ch: G.tensor_tensor(out=xm[:, s, ch * 512:(ch + 1) * 512], in0=xm[:, s, ch * 512:(ch + 1) * 512],
                                                             in1=tb, op=ALU.add)), [t_tb, t_xm[s]], [t_xm[s]])

    def ffn(L, ns, Gate, t_Gate):
        N = ns * 128
        dsrc = wd_b[L].rearrange("(j p) c -> p j c", p=128)
        for q4 in range(0, NFC, 6):
            q5 = min(NFC, q4 + 6)
            DMA("act", Wd[:, q4:q5, :], dsrc[:, q4:q5, :], R=[t_wdb[L]], W=[t_Wd[q4 // 6]], key="Wd%d" % (q4 // 6))
        norm_macro(ns, gscT[:, (2 * L + 1) * 8:(2 * L + 1) * 8 + 8], mod_cols(L, 3))
        gsrc = wg_b[L].rearrange("(k p) f -> p k f", p=128)
        usrc = wu_b[L].rearrange("(k p) f -> p k f", p=128)
        for fg in range(6):
            f0 = fg * 512
            fw = min(512, FF - f0)
            wb = wbc[0] % NWB
            wbc[0] += 1
            DMA("sp", WA[wb][:, :, 0:fw], gsrc[:, :, f0:f0 + fw], R=[t_wgb[L]], W=[t_WA[wb]], key="WA%d" % wb)
            DMA("sp", WB[wb][:, :, 0:fw], usrc[:, :, f0:f0 + fw], R=[t_wub[L]], W=[t_WB[wb]], key="WB%d" % wb)
            for fc in range(fw // 128):
                j = fg * 4 + fc
                bg, bu = (0, 1) if j % 2 == 0 else (2, 3)
                for k in range(8):
                    A("pe", (lambda bg=bg, k=k, fc=fc, wb=wb: PE.matmul(pbank[bg][:, 0:N], lhsT=WA[wb][:, k, fc * 128:(fc + 1) * 128],
                                                                       rhs=hTm[:, k, 0:N], start=(k == 0), stop=(k == 7))),
                      [t_WA[wb]] + t_hTm_s[0:ns], [tpb[bg]])
                for k in range(8):
                    A("pe", (lambda bu=bu, k=k, fc=fc, wb=wb: PE.matmul(pbank[bu][:, 0:N], lhsT=WB[wb][:, k, fc * 128:(fc + 1) * 128],
                                                                       rhs=hTm[:, k, 0:N], start=(k == 0), stop=(k == 7))),
                      [t_WB[wb]] + t_hTm_s[0:ns], [tpb[bu]])
                sb_ = j % 2
                A("act", (lambda bg=bg, sb_=sb_: S.activation(out=sg[sb_][:, 0:N], in_=pbank[bg][:, 0:N], func=AF.Silu)),
                  [tpb[bg]], [t_sg[sb_]])
                A("dve", (lambda bu=bu, sb_=sb_, j=j: V.tensor_tensor(out=aT[:, j, 0:N], in0=pbank[bu][:, 0:N], in1=sg[sb_][:, 0:N], op=ALU.mult)),
                  [tpb[bu], t_sg[sb_]], [t_aT])
        down(ns, NFC, Gate, t_Gate)

    def convmix(ns):
        N = ns * 128
        osrc = cwout_b.rearrange("(j p) c -> p j c", p=128)
        DMA("act", Wd[:, 0:6, :], osrc[:, 0:6, :], R=[t_cwoutb], W=[t_Wd[0]], key="Wd0")
        DMA("act", Wd[:, 6:8, :], osrc[:, 6:8, :], R=[t_cwoutb], W=[t_Wd[1]], key="Wd1")
        norm_macro(ns, gscT[:, 16:24], mod_cols(1, 0))
        src = cwin_b.rearrange("(k p) f -> p k f", p=128)
        for cg in range(2):
            wb = wbc[0] % NWB
            wbc[0] += 1
            DMA("sp", WA[wb], src[:, :, cg * 512:(cg + 1) * 512], R=[t_cwinb], W=[t_WA[wb]], key="WA%d" % wb)
            DMA("sp", WB[wb], src[:, :, 1024 + cg * 512:1024 + (cg + 1) * 512], R=[t_cwinb], W=[t_WB[wb]], key="WB%d" % wb)
            DMA("sp", WC[wb], src[:, :, 2048 + cg * 512:2048 + (cg + 1) * 512], R=[t_cwinb], W=[t_WC[wb]], key="WC%d" % wb)
            for cc in range(4):
                cj = cg * 4 + cc
                for (bk, Wt, tW) in ((0, WA, t_WA), (1, WB, t_WB), (2, WC, t_WC)):
                    for k in range(8):
                        A("pe", (lambda bk=bk, Wt=Wt, k=k, cc=cc, wb=wb: PE.matmul(
                            pbank[bk][:, 0:N], lhsT=Wt[wb][:, k, cc * 128:(cc + 1) * 128], rhs=hTm[:, k, 0:N],
                            start=(k == 0), stop=(k == 7))), [tW[wb]] + t_hTm_s[0:ns], [tpb[bk]])
                A("act", lambda: S.copy(out=sg[0][:, 0:N], in_=pbank[1][:, 0:N]), [tpb[1]], [t_sg[0]])
                A("dve", (lambda cj=cj: V.tensor_copy(out=zb[:, 0:2], in_=carry[:, cj, :])), [t_carry], [t_zb])
                A("dve", lambda: V.tensor_tensor(out=zb[:, 2:2 + N], in0=pbank[2][:, 0:N], in1=sg[0][:, 0:N], op=ALU.mult),
                  [tpb[2], t_sg[0]], [t_zb])
                A("dve", (lambda cj=cj: V.tensor_copy(out=carry[:, cj, :], in_=zb[:, N:N + 2])), [t_zb], [t_carry])
                A("dve", (lambda cj=cj: V.tensor_scalar(out=zc[:, 0:N], in0=zb[:, 2:2 + N], scalar1=cwT[:, cj * 3 + 2:cj * 3 + 3],
                                                       scalar2=None, op0=ALU.mult)), [t_zb, t_cwT], [t_zc])
                A("dve", (lambda cj=cj: V.scalar_tensor_tensor(out=zc[:, 0:N], in0=zb[:, 1:1 + N], scalar=cwT[:, cj * 3 + 1:cj * 3 + 2],
                                                              in1=zc[:, 0:N], op0=ALU.mult, op1=ALU.add)), [t_zb, t_cwT, t_zc], [t_zc])
                A("dve", (lambda cj=cj: V.scalar_tensor_tensor(out=zc[:, 0:N], in0=zb[:, 0:N], scalar=cwT[:, cj * 3:cj * 3 + 1],
                                                              in1=zc[:, 0:N], op0=ALU.mult, op1=ALU.add)), [t_zb, t_cwT, t_zc], [t_zc])
                A("dve", (lambda cj=cj: V.tensor_tensor(out=aT[:, cj, 0:N], in0=pbank[0][:, 0:N], in1=zc[:, 0:N], op=ALU.mult)),
                  [tpb[0], t_zc], [t_aT])
        down(ns, 8, Gt[1], t_Gt[1])

    nmac = (NS + MS - 1) // MS
    for m in range(nmac):
        s0 = m * MS
        ns = min(MS, NS - s0)
        for s in range(ns):
            DMA("sp", xm[:, s, :], x1s[(s0 + s) * 128:(s0 + s + 1) * 128, :], R=[t_x1s[s0 + s]], W=[t_xm[s]], key="xm%d" % s)
        ffn(0, ns, Gt[0], t_Gt[0])
        convmix(ns)
        ffn(1, ns, Gt[2], t_Gt[2])
        for s in range(ns):
            DMA("sp", out_d[(s0 + s) * 128:(s0 + s + 1) * 128, :], xm[:, s, :], R=[t_xm[s]], W=[t_out[s0 + s]], key="out%d" % s)

    P.emit()
    return nc, P, AR


import os
ATT_PARTS = int(os.environ.get('ATT_PARTS', '9'))
_CACHE = {}
STOP = None
LAST = None


def _host_inputs(cfg, r, x, c, positions, ada_w, ada_b, norm1_g, norm2_g, attn_w_in, attn_q_norm_g, attn_k_norm_g,
                 idx_k_ln_g, idx_k_ln_b, attn_w_out, conv_w_in, conv_w, conv_w_out, ffn_w_gate, ffn_w_up, ffn_w_down,
                 shared):
    b, role = r // 2, r % 2
    tiles = cfg.tilesA if role == 0 else cfg.tilesB
    T, NT, NS = cfg.T, cfg.NT, cfg.NS
    xs = np.ascontiguousarray(x[b])
    xo = np.ascontiguousarray(xs.reshape(NT, 128, D)[tiles].reshape(NS * 128, D))
    ps = np.ascontiguousarray(positions[b].reshape(NT, 128).T)
    po = np.ascontiguousarray(positions[b].reshape(NT, 128)[tiles].T)
    tok = np.arange(T, dtype=np.float32).reshape(NT, 128)
    qp = np.ascontiguousarray(tok[tiles].T)
    cT = np.ascontiguousarray(c[b].reshape(8, 128).T)
    d = dict(shared)
    d.update({"x_seq": xs, "x_own": xo, "pos_seq": ps.astype(np.int32), "pos_own": po.astype(np.int32), "qpos": qp, "cT": cT})
    return d


def _shared_inputs(ada_w, ada_b, norm1_g, norm2_g, attn_w_in, attn_q_norm_g, attn_k_norm_g, idx_k_ln_g, idx_k_ln_b,
                   attn_w_out, conv_w_in, conv_w, conv_w_out, ffn_w_gate, ffn_w_up, ffn_w_down):
    w = attn_w_in[0]
    qc = w[:, 0:1024].reshape(D, 16, 64)
    qperm = np.stack([qc[:, [j, 8 + j], :] for j in range(8)], axis=1).reshape(D, 1024)
    kc = w[:, 1024:1280].reshape(D, 4, 64)
    kperm = np.concatenate([kc[:, 0], kc[:, 2], kc[:, 1], kc[:, 3]], axis=1)
    vcol = w[:, 1280:1536]
    qic = w[:, 1536:2048]
    kic = w[:, 2048:2112]
    wic = w[:, 2112:2120]
    w_in = np.ascontiguousarray(np.concatenate([qperm, kperm, vcol, qic, kic, kic, wic], axis=1), dtype=np.float32)
    assert w_in.shape[1] == WCOLS
    vecs = np.concatenate([ada_b[0].reshape(48, 128), ada_b[1].reshape(48, 128), norm1_g.reshape(16, 128),
                           norm2_g.reshape(16, 128)], axis=0).astype(np.float32)
    hv = np.tile(np.concatenate([attn_q_norm_g[0], attn_k_norm_g[0], idx_k_ln_g[0], idx_k_ln_b[0]])[None, :], (128, 1)).astype(np.float32)
    cwT = np.ascontiguousarray(conv_w[0].T.reshape(8, 128, 3).transpose(1, 0, 2).reshape(128, 24)).astype(np.float32)
    invf = np.float32(10000.0) ** (-(np.arange(32, dtype=np.float32) * np.float32(2.0) / np.float32(64)))
    return {
        "w_in": w_in, "w_out": np.ascontiguousarray(attn_w_out[0]), "ada_w": np.ascontiguousarray(ada_w),
        "vecs": np.ascontiguousarray(vecs), "hv": np.ascontiguousarray(hv),
        "cw_in": np.ascontiguousarray(conv_w_in[0]), "cwT": cwT, "cw_out": np.ascontiguousarray(conv_w_out[0]),
        "wg": np.ascontiguousarray(ffn_w_gate), "wu": np.ascontiguousarray(ffn_w_up), "wd": np.ascontiguousarray(ffn_w_down),
        "ident": np.eye(128, dtype=np.float32), "invf": np.tile(invf.astype(np.float32)[None, :], (128, 1)),
        "iota": np.tile(np.arange(512, dtype=np.float32)[None, :], (128, 1)),
    }


def kernel(x, c, positions, ada_w, ada_b, norm1_g, norm2_g, attn_w_in, attn_q_norm_g, attn_k_norm_g, idx_k_ln_g,
           idx_k_ln_b, attn_w_out, conv_w_in, conv_w, conv_w_out, ffn_w_gate, ffn_w_up, ffn_w_down):
    args = [np.asarray(a) for a in (x, c, positions, ada_w, ada_b, norm1_g, norm2_g, attn_w_in, attn_q_norm_g,
                                     attn_k_norm_g, idx_k_ln_g, idx_k_ln_b, attn_w_out, conv_w_in, conv_w, conv_w_out,
                                     ffn_w_gate, ffn_w_up, ffn_w_down)]
    x = args[0]
    B, T, _ = x.shape
    cfg = Cfg(T)
    if (T, STOP) not in _CACHE:
        _CACHE[(T, STOP)] = build(cfg, STOP)[0]
    nc = _CACHE[(T, STOP)]
    shared = _shared_inputs(*args[3:])
    ncores = 2 * B
    in_maps = [_host_inputs(cfg, r, *args, shared) for r in range(ncores)]
    res = run_bass_kernel_spmd(nc, in_maps, core_ids=list(range(ncores)))
    global LAST
    LAST = res
    out = np.empty((B, T, D), dtype=np.float32)
    for r in range(ncores):
        b, role = r // 2, r % 2
        tiles = cfg.tilesA if role == 0 else cfg.tilesB
        halo = cfg.haloA if role == 0 else cfg.haloB
        o = np.asarray(res.results[r]["out"]).reshape(cfg.NS, 128, D)
        for s, t in enumerate(tiles):
            if s == halo:
                continue
            out[b, t * 128:(t + 1) * 128, :] = o[s]
    return out
```

```python
import math
import numpy as np
import ml_dtypes
import concourse.bass as bass
import concourse.mybir as mybir
from concourse.bass_utils import run_bass_kernel_spmd

F32 = mybir.dt.float32
BF16 = mybir.dt.bfloat16
I32 = mybir.dt.int32
U8 = mybir.dt.uint8
ALU = mybir.AluOpType
AF = mybir.ActivationFunctionType
AX = mybir.AxisListType

D = 1024
FF = 2816
NFC = FF // 128
WCOLS = 2184
EPS = 1e-6
NIT = 25
TWO_PI = 2.0 * math.pi
C1 = 6.28125
C2 = TWO_PI - C1


class Tok:
    __slots__ = ("w", "r", "rd")

    def __init__(self):
        self.w = None
        self.r = {}
        self.rd = []


class Op:
    __slots__ = ("eng", "fn", "deps", "dma", "sig", "cnt", "dsem")

    def __init__(self, eng, fn, dma):
        self.eng = eng
        self.fn = fn
        self.deps = set()
        self.dma = dma
        self.sig = False
        self.cnt = 0
        self.dsem = None


class Prog:
    def __init__(self, nc):
        self.nc = nc
        self.ops = []
        self.engs = {"pe": nc.tensor, "act": nc.scalar, "dve": nc.vector,
                     "pool": nc.gpsimd, "sp": nc.sync}
        self.last = {}
        self.dmas_since_bar = []
        self.bar = {}

    def add(self, eng, fn, R=(), W=(), dma=None):
        idx = len(self.ops)
        op = Op(eng, fn, dma)
        deps = op.deps
        if eng in self.bar:
            deps.update(self.bar.pop(eng))
        for t in R:
            if t.w is not None:
                deps.add(t.w)
        for t in W:
            if t.w is not None:
                deps.add(t.w)
            deps.update(t.r.values())
            deps.update(t.rd)
        deps.discard(idx)
        for t in W:
            t.w = idx
            t.r = {}
            t.rd = []
        for t in R:
            if t.w == idx:
                continue
            if dma is not None:
                t.rd.append(idx)
            else:
                t.r[eng] = idx
        self.ops.append(op)
        if dma is None:
            self.last[eng] = idx
        else:
            self.dmas_since_bar.append(idx)
        return idx

    def barrier(self):
        s = set(self.last.values()) | set(self.dmas_since_bar)
        self.dmas_since_bar = []
        for e in self.engs:
            self.bar[e] = set(s) | self.bar.get(e, set())

    def emit(self, final_wait_eng="sp"):
        nc = self.nc
        ops = self.ops
        for op in ops:
            nd = set()
            for d in op.deps:
                dop = ops[d]
                if dop.dma is None and op.dma is None and dop.eng == op.eng == "pe":
                    continue
                nd.add(d)
                dop.sig = True
            op.deps = nd
        esem = {e: nc.semaphore("se_" + e).__enter__() for e in self.engs}
        dsem, dcnt = {}, {}
        ecnt = {e: 0 for e in self.engs}
        for op in ops:
            if op.dma is not None:
                if op.dma not in dsem:
                    dsem[op.dma] = nc.semaphore("sd_%d" % len(dsem)).__enter__()
                    dcnt[op.dma] = 0
                dcnt[op.dma] += 16
                op.cnt = dcnt[op.dma]
                op.dsem = dsem[op.dma]
                op.sig = True
            elif op.sig:
                ecnt[op.eng] += 1
                op.cnt = ecnt[op.eng]
        waited = {e: {} for e in self.engs}
        for op in ops:
            E = self.engs[op.eng]
            wd = waited[op.eng]
            need = {}
            for d in op.deps:
                dop = ops[d]
                if dop.dma is not None:
                    key, sem = ("d", dop.dma), dop.dsem
                else:
                    key, sem = ("e", dop.eng), esem[dop.eng]
                if need.get(key, (None, 0))[1] < dop.cnt:
                    need[key] = (sem, dop.cnt)
            for key, (sem, cnt) in need.items():
                if wd.get(key, 0) < cnt:
                    E.wait_ge(sem, cnt)
                    wd[key] = cnt
            ins = op.fn()
            if op.sig:
                ins.then_inc(op.dsem if op.dma is not None else esem[op.eng], 16 if op.dma is not None else 1)
        E = self.engs[final_wait_eng]
        for k, sem in dsem.items():
            E.wait_ge(sem, dcnt[k])
        for e in self.engs:
            if ecnt[e] > 0 and e != final_wait_eng:
                E.wait_ge(esem[e], ecnt[e])
        self.stats = (len(ops), ecnt, len(dsem))


def _dsize(dt):
    if dt in (F32, I32):
        return 4
    if dt == BF16:
        return 2
    return 1


class Arena:
    def __init__(self, nc, nbytes):
        self.t = nc.sbuf_tensor("arena", [128, nbytes // 4], F32).__enter__()
        self.cap = nbytes
        self.off = 0
        self.peak = 0

    def mark(self):
        return self.off

    def release(self, m):
        self.off = m

    def alloc(self, free, dt=F32, parts=128):
        if isinstance(free, int):
            free = [free]
        n = 1
        for f in free:
            n *= f
        sz = (n * _dsize(dt) + 63) // 64 * 64
        assert self.off + sz <= self.cap, "SBUF arena overflow: need %d have %d" % (self.off + sz, self.cap)
        ap = self.t[0:parts, self.off // 4:(self.off + sz) // 4]
        self.off += sz
        self.peak = max(self.peak, self.off)
        if dt != F32:
            ap = ap.bitcast(dt)
        ap = ap[:, 0:n]
        if len(free) == 2:
            ap = ap.rearrange("p (a b) -> p a b", a=free[0])
        elif len(free) == 3:
            ap = ap.rearrange("p (a b c) -> p a b c", a=free[0], b=free[1])
        return ap


class Cfg:
    def __init__(self, T):
        self.T = T
        self.NT = T // 128
        assert self.NT % 4 == 0
        CH = self.NT // 4
        self.CH = CH
        self.NS = 2 * CH + 1
        self.tilesA = list(range(CH)) + [3 * CH - 1] + list(range(3 * CH, 4 * CH))
        self.tilesB = [CH - 1] + list(range(CH, 3 * CH))
        self.ext = [max(a, b) + 1 for a, b in zip(self.tilesA, self.tilesB)]
        self.dchunk = [min(a, b) // 4 for a, b in zip(self.tilesA, self.tilesB)]
        self.haloA = CH
        self.haloB = 0
        self.topk = min(256, T // 4)


def build(cfg, stop=None):
    T, NT, NS = cfg.T, cfg.NT, cfg.NS
    nc = bass.Bass("TRN2", target_bir_lowering=False)
    P = Prog(nc)

    def din(name, shape, dt=F32):
        return nc.dram_tensor(name, list(shape), dt, kind="ExternalInput").ap()

    x_seq = din("x_seq", [T, D])
    x_own = din("x_own", [NS * 128, D])
    pos_seq = din("pos_seq", [128, NT], I32)
    pos_own = din("pos_own", [128, NS], I32)
    qpos_d = din("qpos", [128, NS])
    cT_d = din("cT", [128, 8])
    w_in = din("w_in", [D, WCOLS])
    w_out = din("w_out", [D, D])
    ada_w = din("ada_w", [2, D, 6 * D])
    vecs_d = din("vecs", [128, 128])
    hv_d = din("hv", [128, 256])
    cw_in = din("cw_in", [D, 3 * D])
    cwT_d = din("cwT", [128, 24])
    cw_out = din("cw_out", [D, D])
    wg_d = din("wg", [2, D, FF])
    wu_d = din("wu", [2, D, FF])
    wd_d = din("wd", [2, FF, D])
    ident_d = din("ident", [128, 128])
    invf_d = din("invf", [128, 32])
    iota_d = din("iota", [128, 512])
    out_d = nc.dram_tensor("out", [NS * 128, D], F32, kind="ExternalOutput").ap()
    dbg_d = nc.dram_tensor("dbg", [128, 8192], F32, kind="ExternalOutput").ap() if stop else None
    qTs = nc.dram_tensor("qTs", [NS, 128, 1024], BF16, kind="Internal").ap()
    qiTs = nc.dram_tensor("qiTs", [NS, 128, 512], BF16, kind="Internal").ap()
    x1s = nc.dram_tensor("x1s", [NS * 128, D], F32, kind="Internal").ap()
    wg_b = nc.dram_tensor("wg_b", [2, D, FF], BF16, kind="Internal").ap()
    wu_b = nc.dram_tensor("wu_b", [2, D, FF], BF16, kind="Internal").ap()
    wd_b = nc.dram_tensor("wd_b", [2, FF, D], BF16, kind="Internal").ap()
    cwin_b = nc.dram_tensor("cwin_b", [D, 3 * D], BF16, kind="Internal").ap()
    cwout_b = nc.dram_tensor("cwout_b", [D, D], BF16, kind="Internal").ap()
    t_wgb = [Tok(), Tok()]; t_wub = [Tok(), Tok()]; t_wdb = [Tok(), Tok()]; t_cwinb = Tok(); t_cwoutb = Tok()
    t_qTs = [Tok() for _ in range(NS)]
    t_qiTs = [Tok() for _ in range(NS)]
    t_x1s = [Tok() for _ in range(NS)]
    t_out = [Tok() for _ in range(NS)]

    AR = Arena(nc, 206 * 1024)
    pbank = [nc.psum_tensor("pb%d" % k, [128, 512], F32).__enter__() for k in range(8)]
    tpb = [Tok() for _ in range(8)]

    def pbf(k):
        return pbank[k][:].bitcast(BF16)

    V, S, G, PE = nc.vector, nc.scalar, nc.gpsimd, nc.tensor

    def A(eng, fn, R=(), W=()):
        P.add(eng, fn, R, W)

    dma_ctr = [0]

    def DMA(q, out, in_, R=(), W=(), key=None, **kw):
        if key is None:
            dma_ctr[0] += 1
            key = "k%d" % dma_ctr[0]
        e = {"sp": nc.sync, "pool": nc.gpsimd, "act": nc.scalar}[q]
        P.add(q, lambda: e.dma_start(out=out, in_=in_, **kw), R, W, dma=key)

    dbg_off = [0]

    def dump(ap2d, toks, n, bf=False):
        if stop.endswith("x"):
            return
        o = dbg_off[0]
        if bf:
            tmp = AR.alloc(n)
            tt = Tok()
            A("dve", lambda: V.tensor_copy(out=tmp, in_=ap2d), toks, [tt])
            DMA("sp", dbg_d[:, o:o + n], tmp, R=[tt], W=[Tok()])
        else:
            DMA("sp", dbg_d[:, o:o + n], ap2d, R=toks, W=[Tok()])
        dbg_off[0] += n

    identF = AR.alloc(128); t_identF = Tok()
    identB = AR.alloc(128, BF16); t_identB = Tok()
    onesF = AR.alloc(128); t_onesF = Tok()
    iota = AR.alloc(512); t_iota = Tok()
    vecT = AR.alloc(128); t_vecT = Tok()
    modT = AR.alloc(96); t_modT = Tok()
    gscT = AR.alloc(32); t_gscT = Tok()
    wsign = AR.alloc([NS, 8]); t_wsign = Tok()
    cwT = AR.alloc(24); t_cwT = Tok()
    qpos = AR.alloc(NS); t_qpos = Tok()
    hv = AR.alloc(256); t_hv = Tok()
    G0 = AR.alloc(1024); t_G0 = Tok()

    DMA("sp", identF, ident_d, W=[t_identF])
    DMA("sp", iota, iota_d, W=[t_iota])
    DMA("sp", cwT, cwT_d, W=[t_cwT])
    DMA("sp", qpos, qpos_d, W=[t_qpos])
    DMA("sp", hv, hv_d, W=[t_hv])
    A("dve", lambda: V.tensor_copy(out=identB, in_=identF), [t_identF], [t_identB])
    A("dve", lambda: V.memset(onesF, 1.0), [], [t_onesF])

    if stop == "pre":
        dump(identF, [t_identF], 128)
        dump(identB, [t_identB], 128, bf=True)
        dump(hv, [t_hv], 256)
        P.emit()
        return nc, P, AR
    m0 = AR.mark()
    vecs_sb = AR.alloc(128); t_vecs = Tok()
    cT_sb = AR.alloc(8); t_cT = Tok()
    cact2 = AR.alloc([8, 2]); t_cact = Tok()
    adaw = [AR.alloc([8, 512]) for _ in range(2)]
    t_adaw = [Tok(), Tok()]
    DMA("sp", vecs_sb, vecs_d, W=[t_vecs])
    DMA("sp", cT_sb, cT_d, W=[t_cT])
    A("pe", lambda: PE.transpose(out=pbank[0][:, 0:128], in_=vecs_sb, identity=identF), [t_vecs, t_identF], [tpb[0]])
    A("act", lambda: S.copy(out=vecT, in_=pbank[0][:, 0:128]), [tpb[0]], [t_vecT])
    A("act", lambda: S.activation(out=cact2[:, :, 0], in_=cT_sb, func=AF.Silu), [t_cT], [t_cact])
    A("act", lambda: S.activation(out=cact2[:, :, 1], in_=cT_sb, func=AF.Silu), [t_cT], [t_cact])
    n = 0
    for L in range(2):
        src = ada_w[L].rearrange("(k p) n -> p k n", p=128)
        for cg in range(12):
            b = n % 2
            n += 1
            DMA("sp", adaw[b], src[:, :, cg * 512:(cg + 1) * 512], W=[t_adaw[b]], key="adaw%d" % b)
            for m4 in range(4):
                m = L * 48 + cg * 4 + m4
                for k in range(8):
                    A("pe", (lambda b=b, m=m, m4=m4, k=k: PE.matmul(
                        pbank[1][:, 2 * m:2 * m + 2], lhsT=adaw[b][:, k, m4 * 128:(m4 + 1) * 128],
                        rhs=cact2[:, k, :], start=(k == 0), stop=(k == 7))),
                      [t_adaw[b], t_cact], [tpb[1]])
    if stop == "p0a":
        A("act", lambda: S.copy(out=modT, in_=pbank[1][:, 0:96]), [tpb[1]], [t_modT])
        dump(modT, [t_modT], 96)
        dump(vecT, [t_vecT], 128)
        dump(cact2.rearrange("p a b -> p (a b)"), [t_cact], 16)
        P.emit()
        return nc, P, AR
    A("dve", lambda: V.tensor_tensor(out=modT, in0=pbank[1][:, 0:192].rearrange("p (m t) -> p m t", t=2)[:, :, 0],
                                     in1=vecT[:, 0:96], op=ALU.add), [tpb[1], t_vecT], [t_modT])
    for L in range(2):
        for s in range(2):
            o = (2 * L + s) * 8
            sc = modT[:, L * 48 + (8 if s == 0 else 32): L * 48 + (16 if s == 0 else 40)]
            ng = vecT[:, 96 + s * 16 + L * 8: 96 + s * 16 + L * 8 + 8]
            A("dve", (lambda o=o, sc=sc, ng=ng: V.scalar_tensor_tensor(
                out=gscT[:, o:o + 8], in0=sc, scalar=1.0, in1=ng, op0=ALU.add, op1=ALU.mult)),
              [t_modT, t_vecT], [t_gscT])

    def mod_cols(L, which):
        return modT[:, L * 48 + which * 8: L * 48 + which * 8 + 8]

    def make_gate(gcols, dst, t_dst, dg_bufs, t_dg, banks):
        for j in range(8):
            b = j % 2
            A("dve", (lambda j=j, b=b: V.tensor_scalar(out=dg_bufs[b], in0=identF, scalar1=gcols[:, j:j + 1],
                                                      scalar2=None, op0=ALU.mult)),
              [t_identF, t_modT], [t_dg[b]])
            bk = banks[j // 4]
            A("pe", (lambda j=j, b=b, bk=bk: PE.matmul(pbank[bk][:, (j % 4) * 128:(j % 4 + 1) * 128], lhsT=onesF,
                                                      rhs=dg_bufs[b], start=True, stop=True)),
              [t_onesF, t_dg[b]], [tpb[bk]])
        for h in range(2):
            A("act", (lambda h=h: S.copy(out=dst[:, h * 512:(h + 1) * 512], in_=pbank[banks[h]][:])),
              [tpb[banks[h]]], [t_dst])

    if stop == "p0b":
        dump(modT, [t_modT], 96)
        dump(gscT, [t_gscT], 32)
        P.emit()
        return nc, P, AR
    dgb = [AR.alloc(128), AR.alloc(128)]
    t_dgb = [Tok(), Tok()]
    make_gate(mod_cols(0, 2), G0, t_G0, dgb, t_dgb, [2, 3])
    if stop == "p0c":
        dump(G0, [t_G0], 1024)
        P.emit()
        return nc, P, AR
    P.barrier()
    if stop == "p0":
        dump(modT, [t_modT], 96)
        dump(gscT, [t_gscT], 32)
        dump(G0, [t_G0], 1024)
        dump(vecT, [t_vecT], 128)
        P.emit()
        return nc, P, AR
    AR.release(m0)

    def norm_T(xt, t_xt, gs, shc, hT_dst, t_hT, scr, bank):
        junkb, t_junkb, ss, t_ss, xn, t_xn = scr
        A("act", lambda: S.activation(out=junkb, in_=xt, func=AF.Square, accum_out=ss[:, 0:1]), [t_xt], [t_junkb, t_ss])
        A("act", lambda: S.activation(out=ss[:, 1:2], in_=ss[:, 0:1], func=AF.Sqrt, scale=1.0 / D, bias=eps_t[:, 0:1]),
          [t_ss, t_eps], [t_ss])
        A("dve", lambda: V.reciprocal(out=ss[:, 2:3], in_=ss[:, 1:2]), [t_ss], [t_ss])
        A("act", lambda: S.activation(out=xn, in_=xt, func=AF.Identity, scale=ss[:, 2:3]), [t_xt, t_ss], [t_xn])
        pv = pbf(bank)
        for j in range(8):
            A("pe", (lambda j=j: PE.transpose(out=pv[:, j * 128:(j + 1) * 128], in_=xn[:, j * 128:(j + 1) * 128],
                                              identity=identB)), [t_xn, t_identB], [tpb[bank]])
        for j in range(8):
            A("act", (lambda j=j: S.activation(out=hT_dst[:, j, :], in_=pv[:, j * 128:(j + 1) * 128], func=AF.Identity,
                                               scale=gs[:, j:j + 1], bias=shc[:, j:j + 1])),
              [tpb[bank], t_gscT, t_modT], [t_hT])

    eps_t = AR.alloc(4); t_eps = Tok()
    A("dve", lambda: V.memset(eps_t, EPS), [], [t_eps])

    mA = AR.mark()
    kT_all = AR.alloc([2, T], BF16); t_kT = [Tok() for _ in range(NT)]
    Vaug = AR.alloc([NT, 4, 66], BF16); t_V = [Tok() for _ in range(NT)]
    kiT = AR.alloc(T, BF16); t_kiT = [Tok() for _ in range(NT)]
    t_Vones = Tok()
    A("pool", lambda: G.memset(Vaug[:, :, :, 64:65], 1.0), [], [t_Vones] + t_V)

    mA1 = AR.mark()
    Win = AR.alloc([8, WCOLS], BF16); t_Win = [Tok() for _ in range(8)]
    wsrc = w_in.rearrange("(k p) n -> p k n", p=128)
    for k in range(8):
        DMA("pool", Win[:, k, :], wsrc[:, k, :], W=[t_Win[k]], key="win%d" % k, max_dma_last_dim=4096)
    for L in range(2):
        DMA("pool", wg_b[L], wg_d[L], W=[t_wgb[L]], key="cwg%d" % L, max_dma_last_dim=4096)
        DMA("pool", wu_b[L], wu_d[L], W=[t_wub[L]], key="cwu%d" % L, max_dma_last_dim=4096)
        DMA("pool", wd_b[L], wd_d[L], W=[t_wdb[L]], key="cwd%d" % L, max_dma_last_dim=4096)
        if L == 0:
            DMA("pool", cwin_b, cw_in, W=[t_cwinb], key="ccwin", max_dma_last_dim=4096)
            DMA("pool", cwout_b, cw_out, W=[t_cwoutb], key="ccwout", max_dma_last_dim=4096)

    def rope_tables(pos_d, n, cos_t, sin_t, t_tab):
        m = AR.mark()
        pi_ = AR.alloc(n, I32); t_pi = Tok()
        pf = AR.alloc(n); ang = AR.alloc([n, 32]); u = AR.alloc([n, 32]); ki_ = AR.alloc([n, 32], I32)
        kf = AR.alloc([n, 32]); r = AR.alloc([n, 32]); r2 = AR.alloc([n, 32]); tmp = AR.alloc([n, 32])
        invt = AR.alloc(32)
        tk = Tok()
        DMA("sp", pi_, pos_d, W=[t_pi])
        DMA("sp", invt, invf_d, W=[tk])
        A("dve", lambda: V.tensor_copy(out=pf, in_=pi_), [t_pi], [tk])
        A("dve", lambda: V.tensor_tensor(out=ang, in0=pf.unsqueeze(2).to_broadcast([128, n, 32]),
                                         in1=invt.unsqueeze(1).to_broadcast([128, n, 32]), op=ALU.mult), [tk], [tk])
        A("dve", lambda: V.tensor_scalar(out=u, in0=ang, scalar1=1.0 / TWO_PI, scalar2=None, op0=ALU.mult), [tk], [tk])
        A("dve", lambda: V.tensor_copy(out=ki_, in_=u), [tk], [tk])
        A("dve", lambda: V.tensor_copy(out=kf, in_=ki_), [tk], [tk])
        A("dve", lambda: V.scalar_tensor_tensor(out=r, in0=kf, scalar=-C1, in1=ang, op0=ALU.mult, op1=ALU.add), [tk], [tk])
        A("dve", lambda: V.scalar_tensor_tensor(out=r, in0=kf, scalar=-C2, in1=r, op0=ALU.mult, op1=ALU.add), [tk], [tk])
        A("dve", lambda: V.tensor_scalar(out=r, in0=r, scalar1=-3.1415925, scalar2=3.1415925, op0=ALU.max, op1=ALU.min), [tk], [tk])
        A("dve", lambda: V.tensor_scalar(out=r2, in0=r, scalar1=math.pi / 2, scalar2=None, op0=ALU.add), [tk], [tk])
        A("dve", lambda: V.tensor_scalar(out=tmp, in0=r2, scalar1=math.pi, scalar2=-TWO_PI, op0=ALU.is_gt, op1=ALU.mult), [tk], [tk])
        A("dve", lambda: V.tensor_tensor(out=r2, in0=r2, in1=tmp, op=ALU.add), [tk], [tk])
        A("dve", lambda: V.tensor_scalar(out=r2, in0=r2, scalar1=-3.1415925, scalar2=3.1415925, op0=ALU.max, op1=ALU.min), [tk], [tk])
        A("act", lambda: S.activation(out=sin_t, in_=r, func=AF.Sin), [tk], [t_tab])
        A("act", lambda: S.activation(out=cos_t, in_=r2, func=AF.Sin), [tk], [t_tab])
        return m

    cos_s = AR.alloc([NT, 32]); sin_s = AR.alloc([NT, 32]); t_tabs = Tok()
    cos_o = AR.alloc([NS, 32]); sin_o = AR.alloc([NS, 32]); t_tabo = Tok()
    hn = NT // 2
    for hf in range(2):
        mm_ = rope_tables(pos_seq[:, hf * hn:(hf + 1) * hn], hn, cos_s[:, hf * hn:(hf + 1) * hn, :], sin_s[:, hf * hn:(hf + 1) * hn, :], t_tabs)
        P.barrier()
        AR.release(mm_)
    mm_ = rope_tables(pos_own, NS, cos_o, sin_o, t_tabo)
    P.barrier()
    AR.release(mm_)

    xt = [AR.alloc(1024), AR.alloc(1024)]; t_xt = [Tok(), Tok()]
    hT = [AR.alloc([8, 128], BF16), AR.alloc([8, 128], BF16)]; t_hT = [Tok(), Tok()]
    scr = (AR.alloc(1024, BF16), Tok(), AR.alloc(4), Tok(), AR.alloc(1024, BF16), Tok())
    sq = AR.alloc(1024); t_sq = Tok()
    qn = AR.alloc(1024); t_qn = Tok()
    r1 = AR.alloc(1024); t_r1 = Tok()
    r2_ = AR.alloc(1024); t_r2 = Tok()
    qb = AR.alloc(1024, BF16); t_qb = Tok()
    sm = AR.alloc(64); t_sm = Tok()
    qTt = [AR.alloc(1024, BF16), AR.alloc(1024, BF16)]; t_qTt = [Tok(), Tok()]
    qiTt = [AR.alloc(512, BF16), AR.alloc(512, BF16)]; t_qiTt = [Tok(), Tok()]
    kib = AR.alloc(128, BF16); t_kib = Tok()
    qr = AR.alloc(512); t_qr = Tok()
    wab = AR.alloc(8); t_wab = Tok()

    def headnorm_rope(src_ps, t_src, H, gcol, cosv, sinv, t_tab, outb, t_outb):
        W_ = H * 64
        s3 = src_ps.rearrange("p (h d) -> p h d", h=H)
        A("act", lambda: S.activation(out=sq[:, 0:W_], in_=src_ps, func=AF.Square), [t_src], [t_sq])
        A("dve", lambda: V.tensor_reduce(out=sm[:, 0:H], in_=sq[:, 0:W_].rearrange("p (h d) -> p h d", h=H), axis=AX.X, op=ALU.add),
          [t_sq], [t_sm])
        A("act", lambda: S.activation(out=sm[:, 16:16 + H], in_=sm[:, 0:H], func=AF.Sqrt, scale=1.0 / 64, bias=eps_t[:, 0:1]),
          [t_sm, t_eps], [t_sm])
        A("dve", lambda: V.reciprocal(out=sm[:, 32:32 + H], in_=sm[:, 16:16 + H]), [t_sm], [t_sm])
        q3 = qn[:, 0:W_].rearrange("p (h d) -> p h d", h=H)
        A("dve", lambda: V.tensor_tensor(out=q3, in0=s3, in1=sm[:, 32:32 + H].unsqueeze(2).to_broadcast([128, H, 64]), op=ALU.mult),
          [t_src, t_sm], [t_qn])
        A("pool", lambda: G.tensor_tensor(out=q3, in0=q3, in1=hv[:, gcol:gcol + 64].unsqueeze(1).to_broadcast([128, H, 64]), op=ALU.mult),
          [t_qn, t_hv], [t_qn])
        rope(q3, t_qn, H, cosv, sinv, t_tab, outb, t_outb)

    def rope(q3, t_q3, H, cosv, sinv, t_tab, outb, t_outb):
        W_ = H * 64
        a3 = r1[:, 0:W_].rearrange("p (h d) -> p h d", h=H)
        b3 = r2_[:, 0:W_].rearrange("p (h d) -> p h d", h=H)
        o3 = outb[:, 0:W_].rearrange("p (h d) -> p h d", h=H)
        cb = cosv.unsqueeze(1).to_broadcast([128, H, 32])
        sb_ = sinv.unsqueeze(1).to_broadcast([128, H, 32])
        A("pool", lambda: G.tensor_tensor(out=a3[:, :, 0:32], in0=q3[:, :, 0:32], in1=cb, op=ALU.mult), [t_q3, t_tab], [t_r1])
        A("pool", lambda: G.tensor_tensor(out=a3[:, :, 32:64], in0=q3[:, :, 32:64], in1=cb, op=ALU.mult), [t_q3, t_tab], [t_r1])
        A("dve", lambda: V.tensor_tensor(out=b3[:, :, 0:32], in0=q3[:, :, 32:64], in1=sb_, op=ALU.mult), [t_q3, t_tab], [t_r2])
        A("dve", lambda: V.tensor_tensor(out=b3[:, :, 32:64], in0=q3[:, :, 0:32], in1=sb_, op=ALU.mult), [t_q3, t_tab], [t_r2])
        A("pool", lambda: G.tensor_tensor(out=o3[:, :, 0:32], in0=a3[:, :, 0:32], in1=b3[:, :, 0:32], op=ALU.subtract), [t_r1, t_r2], [t_outb])
        A("pool", lambda: G.tensor_tensor(out=o3[:, :, 32:64], in0=a3[:, :, 32:64], in1=b3[:, :, 32:64], op=ALU.add), [t_r1, t_r2], [t_outb])

    def proj(dst_bank, ncols, c0, hTb, t_hTb):
        for k in range(8):
            A("pe", (lambda k=k: PE.matmul(pbank[dst_bank][:, 0:ncols], lhsT=hTb[:, k, :], rhs=Win[:, k, c0:c0 + ncols],
                                           start=(k == 0), stop=(k == 7))), [t_hTb, t_Win[k]], [tpb[dst_bank]])

    gs1_0 = gscT[:, 0:8]
    sh1_0 = mod_cols(0, 0)
    for j in range(NT):
        b = j % 2
        DMA("sp", xt[b], x_seq[j * 128:(j + 1) * 128, :], W=[t_xt[b]], key="xt%d" % b)
        norm_T(xt[b], t_xt[b], gs1_0, sh1_0, hT[b], t_hT[b], scr, 0)
        proj(1, 512, 1024, hT[b], t_hT[b])
        proj(2, 128, 2048, hT[b], t_hT[b])
        cj, sj = cos_s[:, j, :], sin_s[:, j, :]
        headnorm_rope(pbank[1][:, 0:256], tpb[1], 4, 64, cj, sj, t_tabs, qb, t_qb)
        A("act", (lambda j=j: S.copy(out=Vaug[:, j, :, 0:64], in_=pbank[1][:, 256:512].rearrange("p (g d) -> p g d", g=4))),
          [tpb[1]], [t_V[j]])
        pv3 = pbf(3)
        for i in range(2):
            A("pe", (lambda i=i: PE.transpose(out=pv3[:, i * 128:(i + 1) * 128], in_=qb[:, i * 128:(i + 1) * 128], identity=identB)),
              [t_qb, t_identB], [tpb[3]])
        A("act", (lambda j=j: S.copy(out=kT_all[:, :, j * 128:(j + 1) * 128], in_=pv3[:, 0:256].rearrange("p (i t) -> p i t", i=2))),
          [tpb[3]], [t_kT[j]])
        A("dve", lambda: V.bn_stats(out=sm[:, 48:54], in_=pbank[2][:, 0:64]), [tpb[2]], [t_sm])
        A("dve", lambda: V.bn_aggr(out=sm[:, 54:56], in_=sm[:, 48:54]), [t_sm], [t_sm])
        A("act", lambda: S.activation(out=sm[:, 56:57], in_=sm[:, 55:56], func=AF.Sqrt, scale=1.0, bias=eps_t[:, 0:1]), [t_sm, t_eps], [t_sm])
        A("dve", lambda: V.reciprocal(out=sm[:, 57:58], in_=sm[:, 56:57]), [t_sm], [t_sm])
        A("dve", lambda: V.tensor_scalar(out=qn[:, 0:64], in0=pbank[2][:, 0:64], scalar1=sm[:, 54:55], scalar2=sm[:, 57:58],
                                         op0=ALU.subtract, op1=ALU.mult), [tpb[2], t_sm], [t_qn])
        A("pool", lambda: G.tensor_tensor(out=qn[:, 0:64], in0=qn[:, 0:64], in1=hv[:, 128:192], op=ALU.mult), [t_qn, t_hv], [t_qn])
        A("pool", lambda: G.tensor_tensor(out=qn[:, 0:64], in0=qn[:, 0:64], in1=hv[:, 192:256], op=ALU.add), [t_qn, t_hv], [t_qn])
        rope(qn[:, 0:64].rearrange("p (h d) -> p h d", h=1), t_qn, 1, cj, sj, t_tabs, kib, t_kib)
        A("pool", lambda: G.tensor_copy(out=kib[:, 64:128], in_=kib[:, 0:64]), [t_kib], [t_kib])
        A("pe", lambda: PE.transpose(out=pv3[:, 256:384], in_=kib, identity=identB), [t_kib, t_identB], [tpb[3]])
        A("act", (lambda j=j: S.copy(out=kiT[:, j * 128:(j + 1) * 128], in_=pv3[:, 256:384])), [tpb[3]], [t_kiT[j]])

    WSC = (8 ** -0.5) * (64 ** -0.5)
    for i in range(NS):
        b = i % 2
        DMA("sp", xt[b], x_own[i * 128:(i + 1) * 128, :], W=[t_xt[b]], key="xt%d" % b)
        norm_T(xt[b], t_xt[b], gs1_0, sh1_0, hT[b], t_hT[b], scr, 0)
        proj(1, 512, 0, hT[b], t_hT[b])
        proj(2, 512, 512, hT[b], t_hT[b])
        proj(4, 512, 1536, hT[b], t_hT[b])
        proj(5, 8, 2176, hT[b], t_hT[b])
        ci, si = cos_o[:, i, :], sin_o[:, i, :]
        for hh in range(2):
            headnorm_rope(pbank[1 + hh][:], tpb[1 + hh], 8, 0, ci, si, t_tabo, qb, t_qb)
            pv3 = pbf(3)
            for jj in range(4):
                A("pe", (lambda jj=jj, hh=hh: PE.transpose(out=pv3[:, (hh * 4 + jj) * 128:(hh * 4 + jj + 1) * 128],
                                                          in_=qb[:, jj * 128:(jj + 1) * 128], identity=identB)),
                  [t_qb, t_identB], [tpb[3]])
        A("act", (lambda b=b: S.copy(out=qTt[b], in_=pbf(3))), [tpb[3]], [t_qTt[b]])
        DMA("sp", qTs[i], qTt[b], R=[t_qTt[b]], W=[t_qTs[i]], key="qTs%d" % b)
        A("act", (lambda i=i: S.activation(out=wsign[:, i, :], in_=pbank[5][:, 0:8], func=AF.Sign)), [tpb[5]], [t_wsign])
        A("act", lambda: S.activation(out=wab, in_=pbank[5][:, 0:8], func=AF.Abs, scale=WSC), [tpb[5]], [t_wab])
        A("act", lambda: S.copy(out=qn[:, 0:512], in_=pbank[4][:]), [tpb[4]], [t_qn])
        rope(qn[:, 0:512].rearrange("p (h d) -> p h d", h=8), t_qn, 8, ci, si, t_tabo, qr, t_qr)
        A("dve", lambda: V.tensor_tensor(out=qb[:, 0:512].rearrange("p (h d) -> p h d", h=8),
                                         in0=qr[:, 0:512].rearrange("p (h d) -> p h d", h=8),
                                         in1=wab.unsqueeze(2).to_broadcast([128, 8, 64]), op=ALU.mult),
          [t_qr, t_wab], [t_qb])
        pv6 = pbf(6)
        for jj in range(4):
            A("pe", (lambda jj=jj: PE.transpose(out=pv6[:, jj * 128:(jj + 1) * 128], in_=qb[:, jj * 128:(jj + 1) * 128], identity=identB)),
              [t_qb, t_identB], [tpb[6]])
        A("act", (lambda b=b: S.copy(out=qiTt[b], in_=pv6[:, 0:512])), [tpb[6]], [t_qiTt[b]])
        DMA("sp", qiTs[i], qiTt[b], R=[t_qiTt[b]], W=[t_qiTs[i]], key="qiTs%d" % b)
    P.barrier()
    if stop in ("a1", "a1x"):
        dump(kT_all[:, 0, 0:512], t_kT, 512, bf=True)
        dump(kT_all[:, 1, 0:512], t_kT, 512, bf=True)
        dump(kiT[:, 0:512], t_kiT, 512, bf=True)
        dump(Vaug[:, 0:4, :, :].rearrange("p a g d -> p (a g d)"), t_V, 1056, bf=True)
        dump(qTt[(NS - 1) % 2], t_qTt, 1024, bf=True)
        dump(qiTt[(NS - 1) % 2], t_qiTt, 512, bf=True)
        dump(wsign.rearrange("p a h -> p (a h)"), [t_wsign], NS * 8)
        dump(cos_s.rearrange("p a h -> p (a h)"), [t_tabs], NT * 32)
        dump(sin_s.rearrange("p a h -> p (a h)"), [t_tabs], NT * 32)
        P.emit()
        return nc, P, AR
    AR.release(mA1)

    Wo = AR.alloc([8, 1024], BF16); t_Wo = Tok()
    for hh in range(2):
        DMA("pool", Wo[hh * 64:(hh + 1) * 64, :, :], w_out[hh * 512:(hh + 1) * 512, :].rearrange("(j d) c -> d j c", d=64),
            W=[t_Wo], key="wo", max_dma_last_dim=4096)
    I_ = AR.alloc(T); t_I = Tok()
    mask01 = AR.alloc(T, BF16); t_mask = Tok()
    junk8 = AR.alloc(T, U8); t_junk8 = Tok()
    qTb = [[AR.alloc(1024, BF16), AR.alloc(1024, BF16)] for _ in range(2)]; t_qTb = [Tok(), Tok()]
    qiTb = [[AR.alloc(512, BF16), AR.alloc(512, BF16)] for _ in range(2)]; t_qiTb = [Tok(), Tok()]
    for b_ in range(2):
        for hf_ in range(2):
            A("pool", (lambda b_=b_, hf_=hf_: G.memset(qTb[b_][hf_], 0.0)), [], [t_qTb[b_]])
            A("pool", (lambda b_=b_, hf_=hf_: G.memset(qiTb[b_][hf_], 0.0)), [], [t_qiTb[b_]])
    NR = 4
    Rb = [AR.alloc(512, BF16) for _ in range(NR)]; t_Rb = [Tok() for _ in range(NR)]
    Dh = AR.alloc([8, 128], BF16); t_Dh = Tok()
    biasb = [AR.alloc(512, BF16), AR.alloc(512, BF16)]; t_biasb = [Tok(), Tok()]
    NPB = 6
    Pexp = [AR.alloc(512, BF16) for _ in range(NPB)]; t_Pexp = [Tok() for _ in range(NPB)]
    Sel4 = AR.alloc(512, BF16); t_Sel4 = Tok()
    for r_ in range(4):
        A("pool", (lambda r_=r_: G.tensor_copy(out=Sel4[:, r_ * 128:(r_ + 1) * 128], in_=identB)), [t_identB], [t_Sel4])
    rs = AR.alloc(512); t_rs = Tok()
    bcS = AR.alloc(512); t_bcS = Tok()
    ys = AR.alloc(512); t_ys = Tok()
    numS = ys; t_numS = t_ys
    oT_all = AR.alloc([2, 512], BF16); t_oTlo = Tok(); t_oThi = Tok()
    oT_tmp = AR.alloc([2, 512], BF16); t_oTtmp = Tok()
    xa = AR.alloc(512); t_xa = Tok()
    ta = AR.alloc(512); t_ta = Tok()
    bs = AR.alloc(16); t_bs = Tok()
    tr_banks = [0, 1, 2]
    trc = [0]

    def tbank():
        k = tr_banks[trc[0] % 3]
        trc[0] += 1
        return k

    rbc = [0]
    SCALE = 64 ** -0.5

    def indexer(i):
        b = i % 2
        E = cfg.ext[i]
        nch = (E + 3) // 4
        for hf_ in range(2):
            DMA("sp", qiTb[b][hf_][hf_ * 64:(hf_ + 1) * 64, :], qiTs[i][hf_ * 64:(hf_ + 1) * 64, :], R=[t_qiTs[i]], W=[t_qiTb[b]],
                key="qiTb%d" % b)
        for hf_ in range(2):
            DMA("sp", qTb[b][hf_][hf_ * 64:(hf_ + 1) * 64, :], qTs[i][hf_ * 64:(hf_ + 1) * 64, :], R=[t_qTs[i]], W=[t_qTb[b]],
                key="qTb%d" % b)
        for h in range(8):
            A("pool", (lambda h=h, i=i: G.tensor_scalar(out=Dh[:, h, :], in0=identB, scalar1=wsign[:, i, h:h + 1], scalar2=None, op0=ALU.mult)),
              [t_identB, t_wsign], [t_Dh])
        for c in range(nch):
            kts = [t_kiT[jj] for jj in range(c * 4, c * 4 + 4)]
            need_bias = c >= cfg.dchunk[i]
            if need_bias:
                bb = c % 2
                A("dve", (lambda c=c, i=i: V.tensor_scalar(out=bs[:, 6:7], in0=qpos[:, i:i + 1], scalar1=float(-512 * c), scalar2=None, op0=ALU.add)),
                  [t_qpos], [t_bs])
                A("dve", (lambda bb=bb: V.tensor_scalar(out=biasb[bb], in0=iota, scalar1=bs[:, 6:7], scalar2=-1e30, op0=ALU.is_gt, op1=ALU.mult)),
                  [t_iota, t_bs], [t_biasb[bb]])
            banks = []
            LA = 2

            def acc(hh, nb=need_bias):
                A("pe", (lambda hh=hh, rbb=banks[hh], nb=nb: PE.matmul(pbank[3][:], lhsT=Dh[:, hh, :], rhs=Rb[rbb], start=(hh == 0),
                                                                     stop=(hh == 7 and not nb))),
                  [t_Dh, t_Rb[banks[hh]]], [tpb[3]])

            for h in range(8):
                bk = tbank()
                hp, pr = h % 2, h // 2
                A("pe", (lambda bk=bk, hp=hp, pr=pr, c=c, b=b: PE.matmul(
                    pbank[bk][:], lhsT=qiTb[b][hp][:, pr * 128:(pr + 1) * 128],
                    rhs=kiT[:, c * 512:(c + 1) * 512], start=True, stop=True)),
                  [t_qiTb[b]] + kts, [tpb[bk]])
                rb = rbc[0] % NR
                rbc[0] += 1
                A("act", (lambda bk=bk, rb=rb: S.activation(out=Rb[rb], in_=pbank[bk][:], func=AF.Relu)), [tpb[bk]], [t_Rb[rb]])
                banks.append(rb)
                if h >= LA:
                    acc(h - LA)
            for hh in range(8 - LA, 8):
                acc(hh)
            if need_bias:
                A("pe", (lambda bb=bb: PE.matmul(pbank[3][:], lhsT=identB, rhs=biasb[bb], start=False, stop=True)),
                  [t_identB, t_biasb[bb]], [tpb[3]])
            A("act", (lambda c=c: S.copy(out=I_[:, c * 512:(c + 1) * 512], in_=pbank[3][:])), [tpb[3]], [t_I])

    def topk(i):
        E = cfg.ext[i]
        Sn = ((E + 3) // 4) * 512
        K0 = min(256, Sn)
        A("dve", lambda: V.tensor_reduce(out=bs[:, 0:1], in_=I_[:, 0:Sn], axis=AX.X, op=ALU.max), [t_I], [t_bs])
        A("dve", lambda: V.tensor_reduce(out=bs[:, 1:2], in_=I_[:, 0:K0], axis=AX.X, op=ALU.min), [t_I], [t_bs])
        A("dve", lambda: V.tensor_scalar(out=bs[:, 1:2], in0=bs[:, 1:2], scalar1=-1e29, scalar2=None, op0=ALU.max), [t_bs], [t_bs])
        A("dve", lambda: V.tensor_tensor(out=bs[:, 2:3], in0=bs[:, 0:1], in1=bs[:, 1:2], op=ALU.subtract), [t_bs], [t_bs])
        for it in range(NIT):
            f = 2.0 ** (-(it + 1))
            A("dve", (lambda f=f: V.tensor_scalar(out=bs[:, 3:4], in0=bs[:, 2:3], scalar1=f, scalar2=bs[:, 1:2], op0=ALU.mult, op1=ALU.add)),
              [t_bs], [t_bs])
            A("dve", lambda: V.tensor_scalar(out=junk8[:, 0:Sn], in0=I_[:, 0:Sn], scalar1=bs[:, 3:4], scalar2=None, op0=ALU.is_ge,
                                             op1=ALU.add, accum_out=bs[:, 4:5]), [t_I, t_bs], [t_junk8, t_bs])
            A("dve", (lambda f=f: V.tensor_scalar(out=bs[:, 5:6], in0=bs[:, 4:5], scalar1=cfg.topk - 0.5, scalar2=f, op0=ALU.is_gt, op1=ALU.mult)),
              [t_bs], [t_bs])
            A("dve", lambda: V.tensor_scalar(out=bs[:, 1:2], in0=bs[:, 5:6], scalar1=bs[:, 2:3], scalar2=bs[:, 1:2], op0=ALU.mult, op1=ALU.add),
              [t_bs], [t_bs])
        A("dve", lambda: V.tensor_scalar(out=mask01[:, 0:E * 128], in0=I_[:, 0:E * 128], scalar1=bs[:, 1:2], scalar2=-30000.0,
                                         op0=ALU.is_lt, op1=ALU.mult), [t_I, t_bs], [t_mask])

    pbc = [0]

    def attention(i):
        b = i % 2
        E = cfg.ext[i]
        steps = [(kb, g) for kb in range(E) for g in range(4)]
        DL = 3
        assert NPB > DL
        pmof = {}

        def front(n):
            kb, g = steps[n]
            hp, gi = g // 2, g % 2
            bk = tbank()
            A("pe", (lambda bk=bk, hp=hp, gi=gi, kb=kb, b=b: PE.matmul(
                pbank[bk][:], lhsT=kT_all[:, gi, kb * 128:(kb + 1) * 128],
                rhs=qTb[b][hp][:, gi * 512:(gi + 1) * 512], start=True, stop=False)),
              [t_kT[kb], t_qTb[b]], [tpb[bk]])
            A("pe", (lambda bk=bk, kb=kb: PE.matmul(pbank[bk][:], lhsT=mask01[:, kb * 128:(kb + 1) * 128], rhs=Sel4, start=False, stop=True)),
              [t_mask, t_Sel4], [tpb[bk]])
            pe_ = pbc[0] % NPB
            pbc[0] += 1
            pmof[n] = pe_
            A("act", (lambda bk=bk, pe_=pe_: S.activation(out=Pexp[pe_], in_=pbank[bk][:], func=AF.Exp, scale=SCALE)),
              [tpb[bk]], [t_Pexp[pe_]])

        def back(n):
            kb, g = steps[n]
            pe_ = pmof[n]
            A("pe", (lambda g=g, kb=kb, pe_=pe_, E=E: PE.matmul(pbank[4 + g][0:65, :], lhsT=Vaug[:, kb, g, 0:65], rhs=Pexp[pe_],
                                                               start=(kb == 0), stop=(kb == E - 1))),
              [t_V[kb], t_Vones, t_Pexp[pe_]], [tpb[4 + g]])

        for n in range(len(steps) + DL):
            if n < len(steps):
                front(n)
            if n - DL >= 0:
                back(n - DL)
        if ATT_PARTS < 2:
            return
        for g in range(4):
            A("act", (lambda g=g: S.activation(out=rs[64:65, :], in_=pbank[4 + g][64:65, :], func=AF.Ln)), [tpb[4 + g]], [t_rs])
            A("act", lambda: S.activation(out=rs[64:65, :], in_=rs[64:65, :], func=AF.Exp, scale=-1.0), [t_rs], [t_rs])
            bk = tbank()
            A("pe", (lambda bk=bk: PE.matmul(pbank[bk][0:64, :], lhsT=onesF[64:65, 0:64], rhs=rs[64:65, :], start=True, stop=True)),
              [t_onesF, t_rs], [tpb[bk]])
            A("act", (lambda bk=bk: S.copy(out=bcS[0:64, :], in_=pbank[bk][0:64, :])), [tpb[bk]], [t_bcS])
            A("act", (lambda g=g: S.copy(out=numS[0:64, :], in_=pbank[4 + g][0:64, :])), [tpb[4 + g]], [t_numS])
            if g < 2:
                A("pool", (lambda g=g: G.tensor_tensor(out=oT_all[0:64, g, :], in0=numS[0:64, :], in1=bcS[0:64, :], op=ALU.mult)),
                  [t_numS, t_bcS], [t_oTlo])
            else:
                A("pool", (lambda g=g: G.tensor_tensor(out=oT_tmp[0:64, g - 2, :], in0=numS[0:64, :], in1=bcS[0:64, :], op=ALU.mult)),
                  [t_numS, t_bcS], [t_oTtmp])
        if ATT_PARTS < 3:
            return
        DMA("sp", oT_all[64:128, :, :], oT_tmp[0:64, :, :], R=[t_oTtmp], W=[t_oThi], key="oThi")
        if ATT_PARTS < 4:
            return
        for ch in range(2):
            DMA("sp", xa, x_own[i * 128:(i + 1) * 128, ch * 512:(ch + 1) * 512], W=[t_xa], key="xa")
            bk = tbank()
            for gi in range(2):
                for r in range(4):
                    j = gi * 4 + r
                    A("pe", (lambda bk=bk, gi=gi, r=r, j=j, ch=ch: PE.matmul(
                        pbank[bk][:], lhsT=oT_all[:, gi, r * 128:(r + 1) * 128], rhs=Wo[:, j, ch * 512:(ch + 1) * 512],
                        start=(j == 0), stop=(j == 7))), [t_oTlo, t_oThi, t_Wo], [tpb[bk]])
            A("act", (lambda bk=bk: S.copy(out=ys, in_=pbank[bk][:])), [tpb[bk]], [t_ys])
            A("pool", (lambda ch=ch: G.tensor_tensor(out=ta, in0=ys, in1=G0[:, ch * 512:(ch + 1) * 512], op=ALU.mult)), [t_ys, t_G0], [t_ta])
            A("pool", lambda: G.tensor_tensor(out=ta, in0=ta, in1=xa, op=ALU.add), [t_ta, t_xa], [t_ta])
            DMA("sp", x1s[i * 128:(i + 1) * 128, ch * 512:(ch + 1) * 512], ta, R=[t_ta], W=[t_x1s[i]], key="x1s")

    indexer(0)
    if stop in ("a2i", "a2t", "a2a"):
        if stop in ("a2t", "a2a"):
            topk(0)
        if stop == "a2a":
            attention(0)
        dump(I_[:, 0:1024], [t_I], 1024)
        dump(mask01[:, 0:1024], [t_mask], 1024, bf=True)
        dump(bs, [t_bs], 16)
        dump(ta, [t_ta], 512)
        P.emit()
        return nc, P, AR
    for i in range(NS):
        topk(i)
        if i + 1 < NS:
            indexer(i + 1)
        attention(i)
    P.barrier()
    if stop in ("a2", "a2x"):
        dump(I_[:, 0:1024], [t_I], 1024)
        dump(mask01[:, 0:1024], [t_mask], 1024, bf=True)
        dump(bs, [t_bs], 16)
        dump(ta, [t_ta], 512)
        P.emit()
        return nc, P, AR
    AR.release(mA)

    Gt = [AR.alloc(1024) for _ in range(3)]; t_Gt = [Tok() for _ in range(3)]
    dgb = [AR.alloc(128), AR.alloc(128)]
    make_gate(mod_cols(0, 5), Gt[0], t_Gt[0], dgb, t_dgb, [0, 1])
    make_gate(mod_cols(1, 2), Gt[1], t_Gt[1], dgb, t_dgb, [2, 3])
    make_gate(mod_cols(1, 5), Gt[2], t_Gt[2], dgb, t_dgb, [0, 1])
    MS = 4
    xm = AR.alloc([MS, 1024]); t_xm = [Tok() for _ in range(MS)]
    hTm = AR.alloc([8, MS * 128], BF16); t_hTm = Tok()
    scrB2 = [(AR.alloc(1024, BF16), Tok(), AR.alloc(4), Tok(), AR.alloc(1024, BF16), Tok()) for _ in range(2)]
    t_hTm_s = [Tok() for _ in range(MS)]
    aT = AR.alloc([NFC, MS * 128], BF16); t_aT = Tok()
    Wd = AR.alloc([NFC, 1024], BF16); t_Wd = [Tok() for _ in range(4)]
    NWB = 2
    WA = [AR.alloc([8, 512], BF16) for _ in range(NWB)]; t_WA = [Tok() for _ in range(NWB)]
    WB = [AR.alloc([8, 512], BF16) for _ in range(NWB)]; t_WB = [Tok() for _ in range(NWB)]
    WC = [AR.alloc([8, 512], BF16) for _ in range(NWB)]; t_WC = [Tok() for _ in range(NWB)]
    sg = [AR.alloc(MS * 128), AR.alloc(MS * 128)]; t_sg = [Tok(), Tok()]
    zb = AR.alloc(MS * 128 + 2); t_zb = Tok()
    zc = AR.alloc(MS * 128); t_zc = Tok()
    carry = AR.alloc([8, 2]); t_carry = Tok()
    tb = AR.alloc(512); t_tb = Tok()
    A("dve", lambda: V.memset(carry, 0.0), [], [t_carry])
    wbc = [0]

    def norm_macro(ns, gs, shc):
        for s in range(ns):
            norm_T(xm[:, s, :], t_xm[s], gs, shc, hTm[:, :, s * 128:(s + 1) * 128], t_hTm_s[s], scrB2[s % 2], 6 + s % 2)

    def down(ns, nchunks, Gate, t_Gate):
        N = ns * 128
        for s in range(ns):
            for ch in range(2):
                bk = 4 + (s * 2 + ch) % 2
                for j in range(nchunks):
                    A("pe", (lambda bk=bk, j=j, s=s, ch=ch: PE.matmul(pbank[bk][:], lhsT=aT[:, j, s * 128:(s + 1) * 128],
                                                                     rhs=Wd[:, j, ch * 512:(ch + 1) * 512],
                                                                     start=(j == 0), stop=(j == nchunks - 1))),
                      [t_aT, t_Wd[j // 6]], [tpb[bk]])
                A("dve", (lambda bk=bk, ch=ch: V.tensor_tensor(out=tb, in0=pbank[bk][:], in1=Gate[:, ch * 512:(ch + 1) * 512], op=ALU.mult)),
                  [tpb[bk], t_Gate], [t_tb])
                A("pool", (lambda s=s, ch=ch: G.tensor_tensor(out=xm[:, s, ch * 512:(ch + 1) * 512], in0=xm[:, s, ch * 512:(ch + 1) * 512],
                                                             in1=tb, op=ALU.add)), [t_tb, t_xm[s]], [t_xm[s]])

    def ffn(L, ns, Gate, t_Gate):
        N = ns * 128
        dsrc = wd_b[L].rearrange("(j p) c -> p j c", p=128)
        for q4 in range(0, NFC, 6):
            q5 = min(NFC, q4 + 6)
            DMA("act", Wd[:, q4:q5, :], dsrc[:, q4:q5, :], R=[t_wdb[L]], W=[t_Wd[q4 // 6]], key="Wd%d" % (q4 // 6))
        norm_macro(ns, gscT[:, (2 * L + 1) * 8:(2 * L + 1) * 8 + 8], mod_cols(L, 3))
        gsrc = wg_b[L].rearrange("(k p) f -> p k f", p=128)
        usrc = wu_b[L].rearrange("(k p) f -> p k f", p=128)
        for fg in range(6):
            f0 = fg * 512
            fw = min(512, FF - f0)
            wb = wbc[0] % NWB
            wbc[0] += 1
            DMA("sp", WA[wb][:, :, 0:fw], gsrc[:, :, f0:f0 + fw], R=[t_wgb[L]], W=[t_WA[wb]], key="WA%d" % wb)
            DMA("sp", WB[wb][:, :, 0:fw], usrc[:, :, f0:f0 + fw], R=[t_wub[L]], W=[t_WB[wb]], key="WB%d" % wb)
            for fc in range(fw // 128):
                j = fg * 4 + fc
                bg, bu = (0, 1) if j % 2 == 0 else (2, 3)
                for k in range(8):
                    A("pe", (lambda bg=bg, k=k, fc=fc, wb=wb: PE.matmul(pbank[bg][:, 0:N], lhsT=WA[wb][:, k, fc * 128:(fc + 1) * 128],
                                                                       rhs=hTm[:, k, 0:N], start=(k == 0), stop=(k == 7))),
                      [t_WA[wb]] + t_hTm_s[0:ns], [tpb[bg]])
                for k in range(8):
                    A("pe", (lambda bu=bu, k=k, fc=fc, wb=wb: PE.matmul(pbank[bu][:, 0:N], lhsT=WB[wb][:, k, fc * 128:(fc + 1) * 128],
                                                                       rhs=hTm[:, k, 0:N], start=(k == 0), stop=(k == 7))),
                      [t_WB[wb]] + t_hTm_s[0:ns], [tpb[bu]])
                sb_ = j % 2
                A("act", (lambda bg=bg, sb_=sb_: S.activation(out=sg[sb_][:, 0:N], in_=pbank[bg][:, 0:N], func=AF.Silu)),
                  [tpb[bg]], [t_sg[sb_]])
                A("dve", (lambda bu=bu, sb_=sb_, j=j: V.tensor_tensor(out=aT[:, j, 0:N], in0=pbank[bu][:, 0:N], in1=sg[sb_][:, 0:N], op=ALU.mult)),
                  [tpb[bu], t_sg[sb_]], [t_aT])
        down(ns, NFC, Gate, t_Gate)

    def convmix(ns):
        N = ns * 128
        osrc = cwout_b.rearrange("(j p) c -> p j c", p=128)
        DMA("act", Wd[:, 0:6, :], osrc[:, 0:6, :], R=[t_cwoutb], W=[t_Wd[0]], key="Wd0")
        DMA("act", Wd[:, 6:8, :], osrc[:, 6:8, :], R=[t_cwoutb], W=[t_Wd[1]], key="Wd1")
        norm_macro(ns, gscT[:, 16:24], mod_cols(1, 0))
        src = cwin_b.rearrange("(k p) f -> p k f", p=128)
        for cg in range(2):
            wb = wbc[0] % NWB
            wbc[0] += 1
            DMA("sp", WA[wb], src[:, :, cg * 512:(cg + 1) * 512], R=[t_cwinb], W=[t_WA[wb]], key="WA%d" % wb)
            DMA("sp", WB[wb], src[:, :, 1024 + cg * 512:1024 + (cg + 1) * 512], R=[t_cwinb], W=[t_WB[wb]], key="WB%d" % wb)
            DMA("sp", WC[wb], src[:, :, 2048 + cg * 512:2048 + (cg + 1) * 512], R=[t_cwinb], W=[t_WC[wb]], key="WC%d" % wb)
            for cc in range(4):
                cj = cg * 4 + cc
                for (bk, Wt, tW) in ((0, WA, t_WA), (1, WB, t_WB), (2, WC, t_WC)):
                    for k in range(8):
                        A("pe", (lambda bk=bk, Wt=Wt, k=k, cc=cc, wb=wb: PE.matmul(
                            pbank[bk][:, 0:N], lhsT=Wt[wb][:, k, cc * 128:(cc + 1) * 128], rhs=hTm[:, k, 0:N],
                            start=(k == 0), stop=(k == 7))), [tW[wb]] + t_hTm_s[0:ns], [tpb[bk]])
                A("act", lambda: S.copy(out=sg[0][:, 0:N], in_=pbank[1][:, 0:N]), [tpb[1]], [t_sg[0]])
                A("dve", (lambda cj=cj: V.tensor_copy(out=zb[:, 0:2], in_=carry[:, cj, :])), [t_carry], [t_zb])
                A("dve", lambda: V.tensor_tensor(out=zb[:, 2:2 + N], in0=pbank[2][:, 0:N], in1=sg[0][:, 0:N], op=ALU.mult),
                  [tpb[2], t_sg[0]], [t_zb])
                A("dve", (lambda cj=cj: V.tensor_copy(out=carry[:, cj, :], in_=zb[:, N:N + 2])), [t_zb], [t_carry])
                A("dve", (lambda cj=cj: V.tensor_scalar(out=zc[:, 0:N], in0=zb[:, 2:2 + N], scalar1=cwT[:, cj * 3 + 2:cj * 3 + 3],
                                                       scalar2=None, op0=ALU.mult)), [t_zb, t_cwT], [t_zc])
                A("dve", (lambda cj=cj: V.scalar_tensor_tensor(out=zc[:, 0:N], in0=zb[:, 1:1 + N], scalar=cwT[:, cj * 3 + 1:cj * 3 + 2],
                                                              in1=zc[:, 0:N], op0=ALU.mult, op1=ALU.add)), [t_zb, t_cwT, t_zc], [t_zc])
                A("dve", (lambda cj=cj: V.scalar_tensor_tensor(out=zc[:, 0:N], in0=zb[:, 0:N], scalar=cwT[:, cj * 3:cj * 3 + 1],
                                                              in1=zc[:, 0:N], op0=ALU.mult, op1=ALU.add)), [t_zb, t_cwT, t_zc], [t_zc])
                A("dve", (lambda cj=cj: V.tensor_tensor(out=aT[:, cj, 0:N], in0=pbank[0][:, 0:N], in1=zc[:, 0:N], op=ALU.mult)),
                  [tpb[0], t_zc], [t_aT])
        down(ns, 8, Gt[1], t_Gt[1])

    nmac = (NS + MS - 1) // MS
    for m in range(nmac):
        s0 = m * MS
        ns = min(MS, NS - s0)
        for s in range(ns):
            DMA("sp", xm[:, s, :], x1s[(s0 + s) * 128:(s0 + s + 1) * 128, :], R=[t_x1s[s0 + s]], W=[t_xm[s]], key="xm%d" % s)
        ffn(0, ns, Gt[0], t_Gt[0])
        convmix(ns)
        ffn(1, ns, Gt[2], t_Gt[2])
        for s in range(ns):
            DMA("sp", out_d[(s0 + s) * 128:(s0 + s + 1) * 128, :], xm[:, s, :], R=[t_xm[s]], W=[t_out[s0 + s]], key="out%d" % s)

    P.emit()
    return nc, P, AR


import os
ATT_PARTS = int(os.environ.get('ATT_PARTS', '9'))
_CACHE = {}
STOP = None
LAST = None


def _host_inputs(cfg, r, x, c, positions, ada_w, ada_b, norm1_g, norm2_g, attn_w_in, attn_q_norm_g, attn_k_norm_g,
                 idx_k_ln_g, idx_k_ln_b, attn_w_out, conv_w_in, conv_w, conv_w_out, ffn_w_gate, ffn_w_up, ffn_w_down,
                 shared):
    b, role = r // 2, r % 2
    tiles = cfg.tilesA if role == 0 else cfg.tilesB
    T, NT, NS = cfg.T, cfg.NT, cfg.NS
    xs = np.ascontiguousarray(x[b])
    xo = np.ascontiguousarray(xs.reshape(NT, 128, D)[tiles].reshape(NS * 128, D))
    ps = np.ascontiguousarray(positions[b].reshape(NT, 128).T)
    po = np.ascontiguousarray(positions[b].reshape(NT, 128)[tiles].T)
    tok = np.arange(T, dtype=np.float32).reshape(NT, 128)
    qp = np.ascontiguousarray(tok[tiles].T)
    cT = np.ascontiguousarray(c[b].reshape(8, 128).T)
    d = dict(shared)
    d.update({"x_seq": xs, "x_own": xo, "pos_seq": ps.astype(np.int32), "pos_own": po.astype(np.int32), "qpos": qp, "cT": cT})
    return d


def _shared_inputs(ada_w, ada_b, norm1_g, norm2_g, attn_w_in, attn_q_norm_g, attn_k_norm_g, idx_k_ln_g, idx_k_ln_b,
                   attn_w_out, conv_w_in, conv_w, conv_w_out, ffn_w_gate, ffn_w_up, ffn_w_down):
    w = attn_w_in[0]
    qc = w[:, 0:1024].reshape(D, 16, 64)
    qperm = np.stack([qc[:, [j, 8 + j], :] for j in range(8)], axis=1).reshape(D, 1024)
    kc = w[:, 1024:1280].reshape(D, 4, 64)
    kperm = np.concatenate([kc[:, 0], kc[:, 2], kc[:, 1], kc[:, 3]], axis=1)
    vcol = w[:, 1280:1536]
    qic = w[:, 1536:2048]
    kic = w[:, 2048:2112]
    wic = w[:, 2112:2120]
    w_in = np.ascontiguousarray(np.concatenate([qperm, kperm, vcol, qic, kic, kic, wic], axis=1), dtype=np.float32)
    assert w_in.shape[1] == WCOLS
    vecs = np.concatenate([ada_b[0].reshape(48, 128), ada_b[1].reshape(48, 128), norm1_g.reshape(16, 128),
                           norm2_g.reshape(16, 128)], axis=0).astype(np.float32)
    hv = np.tile(np.concatenate([attn_q_norm_g[0], attn_k_norm_g[0], idx_k_ln_g[0], idx_k_ln_b[0]])[None, :], (128, 1)).astype(np.float32)
    cwT = np.ascontiguousarray(conv_w[0].T.reshape(8, 128, 3).transpose(1, 0, 2).reshape(128, 24)).astype(np.float32)
    invf = np.float32(10000.0) ** (-(np.arange(32, dtype=np.float32) * np.float32(2.0) / np.float32(64)))
    return {
        "w_in": w_in, "w_out": np.ascontiguousarray(attn_w_out[0]), "ada_w": np.ascontiguousarray(ada_w),
        "vecs": np.ascontiguousarray(vecs), "hv": np.ascontiguousarray(hv),
        "cw_in": np.ascontiguousarray(conv_w_in[0]), "cwT": cwT, "cw_out": np.ascontiguousarray(conv_w_out[0]),
        "wg": np.ascontiguousarray(ffn_w_gate), "wu": np.ascontiguousarray(ffn_w_up), "wd": np.ascontiguousarray(ffn_w_down),
        "ident": np.eye(128, dtype=np.float32), "invf": np.tile(invf.astype(np.float32)[None, :], (128, 1)),
        "iota": np.tile(np.arange(512, dtype=np.float32)[None, :], (128, 1)),
    }


def kernel(x, c, positions, ada_w, ada_b, norm1_g, norm2_g, attn_w_in, attn_q_norm_g, attn_k_norm_g, idx_k_ln_g,
           idx_k_ln_b, attn_w_out, conv_w_in, conv_w, conv_w_out, ffn_w_gate, ffn_w_up, ffn_w_down):
    args = [np.asarray(a) for a in (x, c, positions, ada_w, ada_b, norm1_g, norm2_g, attn_w_in, attn_q_norm_g,
                                     attn_k_norm_g, idx_k_ln_g, idx_k_ln_b, attn_w_out, conv_w_in, conv_w, conv_w_out,
                                     ffn_w_gate, ffn_w_up, ffn_w_down)]
    x = args[0]
    B, T, _ = x.shape
    cfg = Cfg(T)
    if (T, STOP) not in _CACHE:
        _CACHE[(T, STOP)] = build(cfg, STOP)[0]
    nc = _CACHE[(T, STOP)]
    shared = _shared_inputs(*args[3:])
    ncores = 2 * B
    in_maps = [_host_inputs(cfg, r, *args, shared) for r in range(ncores)]
    res = run_bass_kernel_spmd(nc, in_maps, core_ids=list(range(ncores)))
    global LAST
    LAST = res
    out = np.empty((B, T, D), dtype=np.float32)
    for r in range(ncores):
        b, role = r // 2, r % 2
        tiles = cfg.tilesA if role == 0 else cfg.tilesB
        halo = cfg.haloA if role == 0 else cfg.haloB
        o = np.asarray(res.results[r]["out"]).reshape(cfg.NS, 128, D)
        for s, t in enumerate(tiles):
            if s == halo:
                continue
            out[b, t * 128:(t + 1) * 128, :] = o[s]
    return out
```

```python
import math
import numpy as np
import ml_dtypes
import concourse.bass as bass
import concourse.mybir as mybir
from concourse.bass_utils import run_bass_kernel_spmd

F32 = mybir.dt.float32
BF16 = mybir.dt.bfloat16
I32 = mybir.dt.int32
U8 = mybir.dt.uint8
ALU = mybir.AluOpType
AF = mybir.ActivationFunctionType
AX = mybir.AxisListType

D = 1024
FF = 2816
NFC = FF // 128
WCOLS = 2184
EPS = 1e-6
NIT = 25
TWO_PI = 2.0 * math.pi
C1 = 6.28125
C2 = TWO_PI - C1


class Tok:
    __slots__ = ("w", "r", "rd")

    def __init__(self):
        self.w = None
        self.r = {}
        self.rd = []


class Op:
    __slots__ = ("eng", "fn", "deps", "dma", "sig", "cnt", "dsem")

    def __init__(self, eng, fn, dma):
        self.eng = eng
        self.fn = fn
        self.deps = set()
        self.dma = dma
        self.sig = False
        self.cnt = 0
        self.dsem = None


class Prog:
    def __init__(self, nc):
        self.nc = nc
        self.ops = []
        self.engs = {"pe": nc.tensor, "act": nc.scalar, "dve": nc.vector,
                     "pool": nc.gpsimd, "sp": nc.sync}
        self.last = {}
        self.dmas_since_bar = []
        self.bar = {}

    def add(self, eng, fn, R=(), W=(), dma=None):
        idx = len(self.ops)
        op = Op(eng, fn, dma)
        deps = op.deps
        if eng in self.bar:
            deps.update(self.bar.pop(eng))
        for t in R:
            if t.w is not None:
                deps.add(t.w)
        for t in W:
            if t.w is not None:
                deps.add(t.w)
            deps.update(t.r.values())
            deps.update(t.rd)
        deps.discard(idx)
        for t in W:
            t.w = idx
            t.r = {}
            t.rd = []
        for t in R:
            if t.w == idx:
                continue
            if dma is not None:
                t.rd.append(idx)
            else:
                t.r[eng] = idx
        self.ops.append(op)
        if dma is None:
            self.last[eng] = idx
        else:
            self.dmas_since_bar.append(idx)
        return idx

    def barrier(self):
        s = set(self.last.values()) | set(self.dmas_since_bar)
        self.dmas_since_bar = []
        for e in self.engs:
            self.bar[e] = set(s) | self.bar.get(e, set())

    def emit(self, final_wait_eng="sp"):
        nc = self.nc
        ops = self.ops
        for op in ops:
            nd = set()
            for d in op.deps:
                dop = ops[d]
                if dop.dma is None and op.dma is None and dop.eng == op.eng == "pe":
                    continue
                nd.add(d)
                dop.sig = True
            op.deps = nd
        esem = {e: nc.semaphore("se_" + e).__enter__() for e in self.engs}
        dsem, dcnt = {}, {}
        ecnt = {e: 0 for e in self.engs}
        for op in ops:
            if op.dma is not None:
                if op.dma not in dsem:
                    dsem[op.dma] = nc.semaphore("sd_%d" % len(dsem)).__enter__()
                    dcnt[op.dma] = 0
                dcnt[op.dma] += 16
                op.cnt = dcnt[op.dma]
                op.dsem = dsem[op.dma]
                op.sig = True
            elif op.sig:
                ecnt[op.eng] += 1
                op.cnt = ecnt[op.eng]
        waited = {e: {} for e in self.engs}
        for op in ops:
            E = self.engs[op.eng]
            wd = waited[op.eng]
            need = {}
            for d in op.deps:
                dop = ops[d]
                if dop.dma is not None:
                    key, sem = ("d", dop.dma), dop.dsem
                else:
                    key, sem = ("e", dop.eng), esem[dop.eng]
                if need.get(key, (None, 0))[1] < dop.cnt:
                    need[key] = (sem, dop.cnt)
            for key, (sem, cnt) in need.items():
                if wd.get(key, 0) < cnt:
                    E.wait_ge(sem, cnt)
                    wd[key] = cnt
            ins = op.fn()
            if op.sig:
                ins.then_inc(op.dsem if op.dma is not None else esem[op.eng], 16 if op.dma is not None else 1)
        E = self.engs[final_wait_eng]
        for k, sem in dsem.items():
            E.wait_ge(sem, dcnt[k])
        for e in self.engs:
            if ecnt[e] > 0 and e != final_wait_eng:
                E.wait_ge(esem[e], ecnt[e])
        self.stats = (len(ops), ecnt, len(dsem))


def _dsize(dt):
    if dt in (F32, I32):
        return 4
    if dt == BF16:
        return 2
    return 1


class Arena:
    def __init__(self, nc, nbytes):
        self.t = nc.sbuf_tensor("arena", [128, nbytes // 4], F32).__enter__()
        self.cap = nbytes
        self.off = 0
        self.peak = 0

    def mark(self):
        return self.off

    def release(self, m):
        self.off = m

    def alloc(self, free, dt=F32, parts=128):
        if isinstance(free, int):
            free = [free]
        n = 1
        for f in free:
            n *= f
        sz = (n * _dsize(dt) + 63) // 64 * 64
        assert self.off + sz <= self.cap, "SBUF arena overflow: need %d have %d" % (self.off + sz, self.cap)
        ap = self.t[0:parts, self.off // 4:(self.off + sz) // 4]
        self.off += sz
        self.peak = max(self.peak, self.off)
        if dt != F32:
            ap = ap.bitcast(dt)
        ap = ap[:, 0:n]
        if len(free) == 2:
            ap = ap.rearrange("p (a b) -> p a b", a=free[0])
        elif len(free) == 3:
            ap = ap.rearrange("p (a b c) -> p a b c", a=free[0], b=free[1])
        return ap


class Cfg:
    def __init__(self, T):
        self.T = T
        self.NT = T // 128
        assert self.NT % 4 == 0
        CH = self.NT // 4
        self.CH = CH
        self.NS = 2 * CH + 1
        self.tilesA = list(range(CH)) + [3 * CH - 1] + list(range(3 * CH, 4 * CH))
        self.tilesB = [CH - 1] + list(range(CH, 3 * CH))
        self.ext = [max(a, b) + 1 for a, b in zip(self.tilesA, self.tilesB)]
        self.dchunk = [min(a, b) // 4 for a, b in zip(self.tilesA, self.tilesB)]
        self.haloA = CH
        self.haloB = 0
        self.topk = min(256, T // 4)


def build(cfg, stop=None):
    T, NT, NS = cfg.T, cfg.NT, cfg.NS
    nc = bass.Bass("TRN2", target_bir_lowering=False)
    P = Prog(nc)

    def din(name, shape, dt=F32):
        return nc.dram_tensor(name, list(shape), dt, kind="ExternalInput").ap()

    x_seq = din("x_seq", [T, D])
    x_own = din("x_own", [NS * 128, D])
    pos_seq = din("pos_seq", [128, NT], I32)
    pos_own = din("pos_own", [128, NS], I32)
    qpos_d = din("qpos", [128, NS])
    cT_d = din("cT", [128, 8])
    w_in = din("w_in", [D, WCOLS])
    w_out = din("w_out", [D, D])
    ada_w = din("ada_w", [2, D, 6 * D])
    vecs_d = din("vecs", [128, 128])
    hv_d = din("hv", [128, 256])
    cw_in = din("cw_in", [D, 3 * D])
    cwT_d = din("cwT", [128, 24])
    cw_out = din("cw_out", [D, D])
    wg_d = din("wg", [2, D, FF])
    wu_d = din("wu", [2, D, FF])
    wd_d = din("wd", [2, FF, D])
    ident_d = din("ident", [128, 128])
    invf_d = din("invf", [128, 32])
    iota_d = din("iota", [128, 512])
    out_d = nc.dram_tensor("out", [NS * 128, D], F32, kind="ExternalOutput").ap()
    dbg_d = nc.dram_tensor("dbg", [128, 8192], F32, kind="ExternalOutput").ap() if stop else None
    qTs = nc.dram_tensor("qTs", [NS, 128, 1024], BF16, kind="Internal").ap()
    qiTs = nc.dram_tensor("qiTs", [NS, 128, 512], BF16, kind="Internal").ap()
    x1s = nc.dram_tensor("x1s", [NS * 128, D], F32, kind="Internal").ap()
    wg_b = nc.dram_tensor("wg_b", [2, D, FF], BF16, kind="Internal").ap()
    wu_b = nc.dram_tensor("wu_b", [2, D, FF], BF16, kind="Internal").ap()
    wd_b = nc.dram_tensor("wd_b", [2, FF, D], BF16, kind="Internal").ap()
    cwin_b = nc.dram_tensor("cwin_b", [D, 3 * D], BF16, kind="Internal").ap()
    cwout_b = nc.dram_tensor("cwout_b", [D, D], BF16, kind="Internal").ap()
    t_wgb = [Tok(), Tok()]; t_wub = [Tok(), Tok()]; t_wdb = [Tok(), Tok()]; t_cwinb = Tok(); t_cwoutb = Tok()
    t_qTs = [Tok() for _ in range(NS)]
    t_qiTs = [Tok() for _ in range(NS)]
    t_x1s = [Tok() for _ in range(NS)]
    t_out = [Tok() for _ in range(NS)]

    AR = Arena(nc, 206 * 1024)
    pbank = [nc.psum_tensor("pb%d" % k, [128, 512], F32).__enter__() for k in range(8)]
    tpb = [Tok() for _ in range(8)]

    def pbf(k):
        return pbank[k][:].bitcast(BF16)

    V, S, G, PE = nc.vector, nc.scalar, nc.gpsimd, nc.tensor

    def A(eng, fn, R=(), W=()):
        P.add(eng, fn, R, W)

    dma_ctr = [0]

    def DMA(q, out, in_, R=(), W=(), key=None, **kw):
        if key is None:
            dma_ctr[0] += 1
            key = "k%d" % dma_ctr[0]
        e = {"sp": nc.sync, "pool": nc.gpsimd, "act": nc.scalar}[q]
        P.add(q, lambda: e.dma_start(out=out, in_=in_, **kw), R, W, dma=key)

    dbg_off = [0]

    def dump(ap2d, toks, n, bf=False):
        if stop.endswith("x"):
            return
        o = dbg_off[0]
        if bf:
            tmp = AR.alloc(n)
            tt = Tok()
            A("dve", lambda: V.tensor_copy(out=tmp, in_=ap2d), toks, [tt])
            DMA("sp", dbg_d[:, o:o + n], tmp, R=[tt], W=[Tok()])
        else:
            DMA("sp", dbg_d[:, o:o + n], ap2d, R=toks, W=[Tok()])
        dbg_off[0] += n

    identF = AR.alloc(128); t_identF = Tok()
    identB = AR.alloc(128, BF16); t_identB = Tok()
    onesF = AR.alloc(128); t_onesF = Tok()
    iota = AR.alloc(512); t_iota = Tok()
    vecT = AR.alloc(128); t_vecT = Tok()
    modT = AR.alloc(96); t_modT = Tok()
    gscT = AR.alloc(32); t_gscT = Tok()
    wsign = AR.alloc([NS, 8]); t_wsign = Tok()
    cwT = AR.alloc(24); t_cwT = Tok()
    qpos = AR.alloc(NS); t_qpos = Tok()
    hv = AR.alloc(256); t_hv = Tok()
    G0 = AR.alloc(1024); t_G0 = Tok()

    DMA("sp", identF, ident_d, W=[t_identF])
    DMA("sp", iota, iota_d, W=[t_iota])
    DMA("sp", cwT, cwT_d, W=[t_cwT])
    DMA("sp", qpos, qpos_d, W=[t_qpos])
    DMA("sp", hv, hv_d, W=[t_hv])
    A("dve", lambda: V.tensor_copy(out=identB, in_=identF), [t_identF], [t_identB])
    A("dve", lambda: V.memset(onesF, 1.0), [], [t_onesF])

    if stop == "pre":
        dump(identF, [t_identF], 128)
        dump(identB, [t_identB], 128, bf=True)
        dump(hv, [t_hv], 256)
        P.emit()
        return nc, P, AR
    m0 = AR.mark()
    vecs_sb = AR.alloc(128); t_vecs = Tok()
    cT_sb = AR.alloc(8); t_cT = Tok()
    cact2 = AR.alloc([8, 2]); t_cact = Tok()
    adaw = [AR.alloc([8, 512]) for _ in range(2)]
    t_adaw = [Tok(), Tok()]
    DMA("sp", vecs_sb, vecs_d, W=[t_vecs])
    DMA("sp", cT_sb, cT_d, W=[t_cT])
    A("pe", lambda: PE.transpose(out=pbank[0][:, 0:128], in_=vecs_sb, identity=identF), [t_vecs, t_identF], [tpb[0]])
    A("act", lambda: S.copy(out=vecT, in_=pbank[0][:, 0:128]), [tpb[0]], [t_vecT])
    A("act", lambda: S.activation(out=cact2[:, :, 0], in_=cT_sb, func=AF.Silu), [t_cT], [t_cact])
    A("act", lambda: S.activation(out=cact2[:, :, 1], in_=cT_sb, func=AF.Silu), [t_cT], [t_cact])
    n = 0
    for L in range(2):
        src = ada_w[L].rearrange("(k p) n -> p k n", p=128)
        for cg in range(12):
            b = n % 2
            n += 1
            DMA("sp", adaw[b], src[:, :, cg * 512:(cg + 1) * 512], W=[t_adaw[b]], key="adaw%d" % b)
            for m4 in range(4):
                m = L * 48 + cg * 4 + m4
                for k in range(8):
                    A("pe", (lambda b=b, m=m, m4=m4, k=k: PE.matmul(
                        pbank[1][:, 2 * m:2 * m + 2], lhsT=adaw[b][:, k, m4 * 128:(m4 + 1) * 128],
                        rhs=cact2[:, k, :], start=(k == 0), stop=(k == 7))),
                      [t_adaw[b], t_cact], [tpb[1]])
    if stop == "p0a":
        A("act", lambda: S.copy(out=modT, in_=pbank[1][:, 0:96]), [tpb[1]], [t_modT])
        dump(modT, [t_modT], 96)
        dump(vecT, [t_vecT], 128)
        dump(cact2.rearrange("p a b -> p (a b)"), [t_cact], 16)
        P.emit()
        return nc, P, AR
    A("dve", lambda: V.tensor_tensor(out=modT, in0=pbank[1][:, 0:192].rearrange("p (m t) -> p m t", t=2)[:, :, 0],
                                     in1=vecT[:, 0:96], op=ALU.add), [tpb[1], t_vecT], [t_modT])
    for L in range(2):
        for s in range(2):
            o = (2 * L + s) * 8
            sc = modT[:, L * 48 + (8 if s == 0 else 32): L * 48 + (16 if s == 0 else 40)]
            ng = vecT[:, 96 + s * 16 + L * 8: 96 + s * 16 + L * 8 + 8]
            A("dve", (lambda o=o, sc=sc, ng=ng: V.scalar_tensor_tensor(
                out=gscT[:, o:o + 8], in0=sc, scalar=1.0, in1=ng, op0=ALU.add, op1=ALU.mult)),
              [t_modT, t_vecT], [t_gscT])

    def mod_cols(L, which):
        return modT[:, L * 48 + which * 8: L * 48 + which * 8 + 8]

    def make_gate(gcols, dst, t_dst, dg_bufs, t_dg, banks):
        for j in range(8):
            b = j % 2
            A("dve", (lambda j=j, b=b: V.tensor_scalar(out=dg_bufs[b], in0=identF, scalar1=gcols[:, j:j + 1],
                                                      scalar2=None, op0=ALU.mult)),
              [t_identF, t_modT], [t_dg[b]])
            bk = banks[j // 4]
            A("pe", (lambda j=j, b=b, bk=bk: PE.matmul(pbank[bk][:, (j % 4) * 128:(j % 4 + 1) * 128], lhsT=onesF,
                                                      rhs=dg_bufs[b], start=True, stop=True)),
              [t_onesF, t_dg[b]], [tpb[bk]])
        for h in range(2):
            A("act", (lambda h=h: S.copy(out=dst[:, h * 512:(h + 1) * 512], in_=pbank[banks[h]][:])),
              [tpb[banks[h]]], [t_dst])

    if stop == "p0b":
        dump(modT, [t_modT], 96)
        dump(gscT, [t_gscT], 32)
        P.emit()
        return nc, P, AR
    dgb = [AR.alloc(128), AR.alloc(128)]
    t_dgb = [Tok(), Tok()]
    make_gate(mod_cols(0, 2), G0, t_G0, dgb, t_dgb, [2, 3])
    if stop == "p0c":
        dump(G0, [t_G0], 1024)
        P.emit()
        return nc, P, AR
    P.barrier()
    if stop == "p0":
        dump(modT, [t_modT], 96)
        dump(gscT, [t_gscT], 32)
        dump(G0, [t_G0], 1024)
        dump(vecT, [t_vecT], 128)
        P.emit()
        return nc, P, AR
    AR.release(m0)

    def norm_T(xt, t_xt, gs, shc, hT_dst, t_hT, scr, bank):
        junkb, t_junkb, ss, t_ss, xn, t_xn = scr
        A("act", lambda: S.activation(out=junkb, in_=xt, func=AF.Square, accum_out=ss[:, 0:1]), [t_xt], [t_junkb, t_ss])
        A("act", lambda: S.activation(out=ss[:, 1:2], in_=ss[:, 0:1], func=AF.Sqrt, scale=1.0 / D, bias=eps_t[:, 0:1]),
          [t_ss, t_eps], [t_ss])
        A("dve", lambda: V.reciprocal(out=ss[:, 2:3], in_=ss[:, 1:2]), [t_ss], [t_ss])
        A("act", lambda: S.activation(out=xn, in_=xt, func=AF.Identity, scale=ss[:, 2:3]), [t_xt, t_ss], [t_xn])
        pv = pbf(bank)
        for j in range(8):
            A("pe", (lambda j=j: PE.transpose(out=pv[:, j * 128:(j + 1) * 128], in_=xn[:, j * 128:(j + 1) * 128],
                                              identity=identB)), [t_xn, t_identB], [tpb[bank]])
        for j in range(8):
            A("act", (lambda j=j: S.activation(out=hT_dst[:, j, :], in_=pv[:, j * 128:(j + 1) * 128], func=AF.Identity,
                                               scale=gs[:, j:j + 1], bias=shc[:, j:j + 1])),
              [tpb[bank], t_gscT, t_modT], [t_hT])

    eps_t = AR.alloc(4); t_eps = Tok()
    A("dve", lambda: V.memset(eps_t, EPS), [], [t_eps])

    mA = AR.mark()
    kT_all = AR.alloc([2, T], BF16); t_kT = [Tok() for _ in range(NT)]
    Vaug = AR.alloc([NT, 4, 66], BF16); t_V = [Tok() for _ in range(NT)]
    kiT = AR.alloc(T, BF16); t_kiT = [Tok() for _ in range(NT)]
    t_Vones = Tok()
    A("pool", lambda: G.memset(Vaug[:, :, :, 64:65], 1.0), [], [t_Vones] + t_V)

    mA1 = AR.mark()
    Win = AR.alloc([8, WCOLS], BF16); t_Win = [Tok() for _ in range(8)]
    wsrc = w_in.rearrange("(k p) n -> p k n", p=128)
    for k in range(8):
        DMA("pool", Win[:, k, :], wsrc[:, k, :], W=[t_Win[k]], key="win%d" % k, max_dma_last_dim=4096)
    for L in range(2):
        DMA("pool", wg_b[L], wg_d[L], W=[t_wgb[L]], key="cwg%d" % L, max_dma_last_dim=4096)
        DMA("pool", wu_b[L], wu_d[L], W=[t_wub[L]], key="cwu%d" % L, max_dma_last_dim=4096)
        DMA("pool", wd_b[L], wd_d[L], W=[t_wdb[L]], key="cwd%d" % L, max_dma_last_dim=4096)
        if L == 0:
            DMA("pool", cwin_b, cw_in, W=[t_cwinb], key="ccwin", max_dma_last_dim=4096)
            DMA("pool", cwout_b, cw_out, W=[t_cwoutb], key="ccwout", max_dma_last_dim=4096)

    def rope_tables(pos_d, n, cos_t, sin_t, t_tab):
        m = AR.mark()
        pi_ = AR.alloc(n, I32); t_pi = Tok()
        pf = AR.alloc(n); ang = AR.alloc([n, 32]); u = AR.alloc([n, 32]); ki_ = AR.alloc([n, 32], I32)
        kf = AR.alloc([n, 32]); r = AR.alloc([n, 32]); r2 = AR.alloc([n, 32]); tmp = AR.alloc([n, 32])
        invt = AR.alloc(32)
        tk = Tok()
        DMA("sp", pi_, pos_d, W=[t_pi])
        DMA("sp", invt, invf_d, W=[tk])
        A("dve", lambda: V.tensor_copy(out=pf, in_=pi_), [t_pi], [tk])
        A("dve", lambda: V.tensor_tensor(out=ang, in0=pf.unsqueeze(2).to_broadcast([128, n, 32]),
                                         in1=invt.unsqueeze(1).to_broadcast([128, n, 32]), op=ALU.mult), [tk], [tk])
        A("dve", lambda: V.tensor_scalar(out=u, in0=ang, scalar1=1.0 / TWO_PI, scalar2=None, op0=ALU.mult), [tk], [tk])
        A("dve", lambda: V.tensor_copy(out=ki_, in_=u), [tk], [tk])
        A("dve", lambda: V.tensor_copy(out=kf, in_=ki_), [tk], [tk])
        A("dve", lambda: V.scalar_tensor_tensor(out=r, in0=kf, scalar=-C1, in1=ang, op0=ALU.mult, op1=ALU.add), [tk], [tk])
        A("dve", lambda: V.scalar_tensor_tensor(out=r, in0=kf, scalar=-C2, in1=r, op0=ALU.mult, op1=ALU.add), [tk], [tk])
        A("dve", lambda: V.tensor_scalar(out=r, in0=r, scalar1=-3.1415925, scalar2=3.1415925, op0=ALU.max, op1=ALU.min), [tk], [tk])
        A("dve", lambda: V.tensor_scalar(out=r2, in0=r, scalar1=math.pi / 2, scalar2=None, op0=ALU.add), [tk], [tk])
        A("dve", lambda: V.tensor_scalar(out=tmp, in0=r2, scalar1=math.pi, scalar2=-TWO_PI, op0=ALU.is_gt, op1=ALU.mult), [tk], [tk])
        A("dve", lambda: V.tensor_tensor(out=r2, in0=r2, in1=tmp, op=ALU.add), [tk], [tk])
        A("dve", lambda: V.tensor_scalar(out=r2, in0=r2, scalar1=-3.1415925, scalar2=3.1415925, op0=ALU.max, op1=ALU.min), [tk], [tk])
        A("act", lambda: S.activation(out=sin_t, in_=r, func=AF.Sin), [tk], [t_tab])
        A("act", lambda: S.activation(out=cos_t, in_=r2, func=AF.Sin), [tk], [t_tab])
        return m

    cos_s = AR.alloc([NT, 32]); sin_s = AR.alloc([NT, 32]); t_tabs = Tok()
    cos_o = AR.alloc([NS, 32]); sin_o = AR.alloc([NS, 32]); t_tabo = Tok()
    hn = NT // 2
    for hf in range(2):
        mm_ = rope_tables(pos_seq[:, hf * hn:(hf + 1) * hn], hn, cos_s[:, hf * hn:(hf + 1) * hn, :], sin_s[:, hf * hn:(hf + 1) * hn, :], t_tabs)
        P.barrier()
        AR.release(mm_)
    mm_ = rope_tables(pos_own, NS, cos_o, sin_o, t_tabo)
    P.barrier()
    AR.release(mm_)

    xt = [AR.alloc(1024), AR.alloc(1024)]; t_xt = [Tok(), Tok()]
    hT = [AR.alloc([8, 128], BF16), AR.alloc([8, 128], BF16)]; t_hT = [Tok(), Tok()]
    scr = (AR.alloc(1024, BF16), Tok(), AR.alloc(4), Tok(), AR.alloc(1024, BF16), Tok())
    sq = AR.alloc(512); t_sq = Tok()
    qn = AR.alloc(512); t_qn = Tok()
    r1 = AR.alloc(512); t_r1 = Tok()
    r2_ = AR.alloc(512); t_r2 = Tok()
    qb = AR.alloc(1024, BF16); t_qb = Tok()
    sm = AR.alloc(64); t_sm = Tok()
    qTt = [AR.alloc(1024, BF16), AR.alloc(1024, BF16)]; t_qTt = [Tok(), Tok()]
    qiTt = [AR.alloc(512, BF16), AR.alloc(512, BF16)]; t_qiTt = [Tok(), Tok()]
    kib = AR.alloc(128, BF16); t_kib = Tok()
    qr = AR.alloc(512); t_qr = Tok()
    wab = AR.alloc(8); t_wab = Tok()
    SS0 = (sq, t_sq, qn, t_qn, r1, t_r1, r2_, t_r2, sm, t_sm)
    SSs = []
    for _ in range(2):
        SSs.append(dict(ss=(AR.alloc(256), Tok(), AR.alloc(256), Tok(), AR.alloc(256), Tok(), AR.alloc(256), Tok(), AR.alloc(64), Tok()),
                        qb=AR.alloc(256, BF16), t_qb=Tok(), kib=AR.alloc(128, BF16), t_kib=Tok(),
                        scr=(scr[0], Tok(), AR.alloc(4), Tok(), AR.alloc(1024, BF16), Tok())))

    def headnorm_rope(src_ps, t_src, H, gcol, cosv, sinv, t_tab, outb, t_outb, ss=None):
        ss = ss or SS0
        sq, t_sq, qn, t_qn, r1, t_r1, r2_, t_r2, sm, t_sm = ss
        W_ = H * 64
        s3 = src_ps.rearrange("p (h d) -> p h d", h=H)
        A("act", lambda: S.activation(out=sq[:, 0:W_], in_=src_ps, func=AF.Square), [t_src], [t_sq])
        A("dve", lambda: V.tensor_reduce(out=sm[:, 0:H], in_=sq[:, 0:W_].rearrange("p (h d) -> p h d", h=H), axis=AX.X, op=ALU.add),
          [t_sq], [t_sm])
        A("act", lambda: S.activation(out=sm[:, 16:16 + H], in_=sm[:, 0:H], func=AF.Sqrt, scale=1.0 / 64, bias=eps_t[:, 0:1]),
          [t_sm, t_eps], [t_sm])
        A("dve", lambda: V.reciprocal(out=sm[:, 32:32 + H], in_=sm[:, 16:16 + H]), [t_sm], [t_sm])
        q3 = qn[:, 0:W_].rearrange("p (h d) -> p h d", h=H)
        A("dve", lambda: V.tensor_tensor(out=q3, in0=s3, in1=sm[:, 32:32 + H].unsqueeze(2).to_broadcast([128, H, 64]), op=ALU.mult),
          [t_src, t_sm], [t_qn])
        A("pool", lambda: G.tensor_tensor(out=q3, in0=q3, in1=hv[:, gcol:gcol + 64].unsqueeze(1).to_broadcast([128, H, 64]), op=ALU.mult),
          [t_qn, t_hv], [t_qn])
        rope(q3, t_qn, H, cosv, sinv, t_tab, outb, t_outb, ss)

    def rope(q3, t_q3, H, cosv, sinv, t_tab, outb, t_outb, ss=None):
        ss = ss or SS0
        sq, t_sq, qn, t_qn, r1, t_r1, r2_, t_r2, sm, t_sm = ss
        W_ = H * 64
        a3 = r1[:, 0:W_].rearrange("p (h d) -> p h d", h=H)
        b3 = r2_[:, 0:W_].rearrange("p (h d) -> p h d", h=H)
        o3 = outb[:, 0:W_].rearrange("p (h d) -> p h d", h=H)
        cb = cosv.unsqueeze(1).to_broadcast([128, H, 32])
        sb_ = sinv.unsqueeze(1).to_broadcast([128, H, 32])
        A("pool", lambda: G.tensor_tensor(out=a3[:, :, 0:32], in0=q3[:, :, 0:32], in1=cb, op=ALU.mult), [t_q3, t_tab], [t_r1])
        A("pool", lambda: G.tensor_tensor(out=a3[:, :, 32:64], in0=q3[:, :, 32:64], in1=cb, op=ALU.mult), [t_q3, t_tab], [t_r1])
        A("dve", lambda: V.tensor_tensor(out=b3[:, :, 0:32], in0=q3[:, :, 32:64], in1=sb_, op=ALU.mult), [t_q3, t_tab], [t_r2])
        A("dve", lambda: V.tensor_tensor(out=b3[:, :, 32:64], in0=q3[:, :, 0:32], in1=sb_, op=ALU.mult), [t_q3, t_tab], [t_r2])
        A("pool", lambda: G.tensor_tensor(out=o3[:, :, 0:32], in0=a3[:, :, 0:32], in1=b3[:, :, 0:32], op=ALU.subtract), [t_r1, t_r2], [t_outb])
        A("pool", lambda: G.tensor_tensor(out=o3[:, :, 32:64], in0=a3[:, :, 32:64], in1=b3[:, :, 32:64], op=ALU.add), [t_r1, t_r2], [t_outb])

    def proj(dst_bank, ncols, c0, hTb, t_hTb):
        for k in range(8):
            A("pe", (lambda k=k: PE.matmul(pbank[dst_bank][:, 0:ncols], lhsT=hTb[:, k, :], rhs=Win[:, k, c0:c0 + ncols],
                                           start=(k == 0), stop=(k == 7))), [t_hTb, t_Win[k]], [tpb[dst_bank]])

    gs1_0 = gscT[:, 0:8]
    sh1_0 = mod_cols(0, 0)
    def seq_tile(j):
        b = j % 2
        S_ = SSs[b]
        ss = S_["ss"]
        sq, t_sq, qn, t_qn, r1, t_r1, r2_, t_r2, sm, t_sm = ss
        qb, t_qb, kib, t_kib, scr_ = S_["qb"], S_["t_qb"], S_["kib"], S_["t_kib"], S_["scr"]
        nb, p1, p2, tb_ = (0, 1, 2, 3) if b == 0 else (7, 4, 5, 6)
        DMA("sp", xt[b], x_seq[j * 128:(j + 1) * 128, :], W=[t_xt[b]], key="xt%d" % b)
        norm_T(xt[b], t_xt[b], gs1_0, sh1_0, hT[b], t_hT[b], scr_, nb)
        proj(p1, 512, 1024, hT[b], t_hT[b])
        proj(p2, 128, 2048, hT[b], t_hT[b])
        cj, sj = cos_s[:, j, :], sin_s[:, j, :]
        headnorm_rope(pbank[p1][:, 0:256], tpb[p1], 4, 64, cj, sj, t_tabs, qb, t_qb, ss)
        A("act", lambda: S.copy(out=Vaug[:, j, :, 0:64], in_=pbank[p1][:, 256:512].rearrange("p (g d) -> p g d", g=4)),
          [tpb[p1]], [t_V[j]])
        pv3 = pbf(tb_)
        for i in range(2):
            A("pe", (lambda i=i: PE.transpose(out=pv3[:, i * 128:(i + 1) * 128], in_=qb[:, i * 128:(i + 1) * 128], identity=identB)),
              [t_qb, t_identB], [tpb[tb_]])
        A("act", lambda: S.copy(out=kT_all[:, :, j * 128:(j + 1) * 128], in_=pv3[:, 0:256].rearrange("p (i t) -> p i t", i=2)),
          [tpb[tb_]], [t_kT[j]])
        A("dve", lambda: V.bn_stats(out=sm[:, 48:54], in_=pbank[p2][:, 0:64]), [tpb[p2]], [t_sm])
        A("dve", lambda: V.bn_aggr(out=sm[:, 54:56], in_=sm[:, 48:54]), [t_sm], [t_sm])
        A("act", lambda: S.activation(out=sm[:, 56:57], in_=sm[:, 55:56], func=AF.Sqrt, scale=1.0, bias=eps_t[:, 0:1]), [t_sm, t_eps], [t_sm])
        A("dve", lambda: V.reciprocal(out=sm[:, 57:58], in_=sm[:, 56:57]), [t_sm], [t_sm])
        A("dve", lambda: V.tensor_scalar(out=qn[:, 0:64], in0=pbank[p2][:, 0:64], scalar1=sm[:, 54:55], scalar2=sm[:, 57:58],
                                         op0=ALU.subtract, op1=ALU.mult), [tpb[p2], t_sm], [t_qn])
        A("pool", lambda: G.tensor_tensor(out=qn[:, 0:64], in0=qn[:, 0:64], in1=hv[:, 128:192], op=ALU.mult), [t_qn, t_hv], [t_qn])
        A("pool", lambda: G.tensor_tensor(out=qn[:, 0:64], in0=qn[:, 0:64], in1=hv[:, 192:256], op=ALU.add), [t_qn, t_hv], [t_qn])
        rope(qn[:, 0:64].rearrange("p (h d) -> p h d", h=1), t_qn, 1, cj, sj, t_tabs, kib, t_kib, ss)
        A("pool", lambda: G.tensor_copy(out=kib[:, 64:128], in_=kib[:, 0:64]), [t_kib], [t_kib])
        A("pe", lambda: PE.transpose(out=pv3[:, 256:384], in_=kib, identity=identB), [t_kib, t_identB], [tpb[tb_]])
        A("act", lambda: S.copy(out=kiT[:, j * 128:(j + 1) * 128], in_=pv3[:, 256:384]), [tpb[tb_]], [t_kiT[j]])

    for j in range(NT):
        seq_tile(j)

    WSC = (8 ** -0.5) * (64 ** -0.5)
    for i in range(NS):
        b = i % 2
        DMA("sp", xt[b], x_own[i * 128:(i + 1) * 128, :], W=[t_xt[b]], key="xt%d" % b)
        norm_T(xt[b], t_xt[b], gs1_0, sh1_0, hT[b], t_hT[b], scr, 0)
        proj(1, 512, 0, hT[b], t_hT[b])
        proj(2, 512, 512, hT[b], t_hT[b])
        proj(4, 512, 1536, hT[b], t_hT[b])
        proj(5, 8, 2176, hT[b], t_hT[b])
        ci, si = cos_o[:, i, :], sin_o[:, i, :]
        for hh in range(2):
            headnorm_rope(pbank[1 + hh][:], tpb[1 + hh], 8, 0, ci, si, t_tabo, qb, t_qb)
            pv3 = pbf(3)
            for jj in range(4):
                A("pe", (lambda jj=jj, hh=hh: PE.transpose(out=pv3[:, (hh * 4 + jj) * 128:(hh * 4 + jj + 1) * 128],
                                                          in_=qb[:, jj * 128:(jj + 1) * 128], identity=identB)),
                  [t_qb, t_identB], [tpb[3]])
        A("act", (lambda b=b: S.copy(out=qTt[b], in_=pbf(3))), [tpb[3]], [t_qTt[b]])
        DMA("sp", qTs[i], qTt[b], R=[t_qTt[b]], W=[t_qTs[i]], key="qTs%d" % b)
        A("act", (lambda i=i: S.activation(out=wsign[:, i, :], in_=pbank[5][:, 0:8], func=AF.Sign)), [tpb[5]], [t_wsign])
        A("act", lambda: S.activation(out=wab, in_=pbank[5][:, 0:8], func=AF.Abs, scale=WSC), [tpb[5]], [t_wab])
        A("act", lambda: S.copy(out=qn[:, 0:512], in_=pbank[4][:]), [tpb[4]], [t_qn])
        rope(qn[:, 0:512].rearrange("p (h d) -> p h d", h=8), t_qn, 8, ci, si, t_tabo, qr, t_qr)
        A("dve", lambda: V.tensor_tensor(out=qb[:, 0:512].rearrange("p (h d) -> p h d", h=8),
                                         in0=qr[:, 0:512].rearrange("p (h d) -> p h d", h=8),
                                         in1=wab.unsqueeze(2).to_broadcast([128, 8, 64]), op=ALU.mult),
          [t_qr, t_wab], [t_qb])
        pv6 = pbf(6)
        for jj in range(4):
            A("pe", (lambda jj=jj: PE.transpose(out=pv6[:, jj * 128:(jj + 1) * 128], in_=qb[:, jj * 128:(jj + 1) * 128], identity=identB)),
              [t_qb, t_identB], [tpb[6]])
        A("act", (lambda b=b: S.copy(out=qiTt[b], in_=pv6[:, 0:512])), [tpb[6]], [t_qiTt[b]])
        DMA("sp", qiTs[i], qiTt[b], R=[t_qiTt[b]], W=[t_qiTs[i]], key="qiTs%d" % b)
    P.barrier()
    if stop in ("a1", "a1x"):
        dump(kT_all[:, 0, 0:512], t_kT, 512, bf=True)
        dump(kT_all[:, 1, 0:512], t_kT, 512, bf=True)
        dump(kiT[:, 0:512], t_kiT, 512, bf=True)
        dump(Vaug[:, 0:4, :, :].rearrange("p a g d -> p (a g d)"), t_V, 1056, bf=True)
        dump(qTt[(NS - 1) % 2], t_qTt, 1024, bf=True)
        dump(qiTt[(NS - 1) % 2], t_qiTt, 512, bf=True)
        dump(wsign.rearrange("p a h -> p (a h)"), [t_wsign], NS * 8)
        dump(cos_s.rearrange("p a h -> p (a h)"), [t_tabs], NT * 32)
        dump(sin_s.rearrange("p a h -> p (a h)"), [t_tabs], NT * 32)
        P.emit()
        return nc, P, AR
    AR.release(mA1)

    Wo = AR.alloc([8, 1024], BF16); t_Wo = Tok()
    for hh in range(2):
        DMA("pool", Wo[hh * 64:(hh + 1) * 64, :, :], w_out[hh * 512:(hh + 1) * 512, :].rearrange("(j d) c -> d j c", d=64),
            W=[t_Wo], key="wo", max_dma_last_dim=4096)
    I_ = AR.alloc(T); t_I = Tok()
    mask01 = AR.alloc(T, BF16); t_mask = Tok()
    junk8 = AR.alloc(T, U8); t_junk8 = Tok()
    qTb = [[AR.alloc(1024, BF16), AR.alloc(1024, BF16)] for _ in range(2)]; t_qTb = [Tok(), Tok()]
    qiTb = [[AR.alloc(512, BF16), AR.alloc(512, BF16)] for _ in range(2)]; t_qiTb = [Tok(), Tok()]
    for b_ in range(2):
        for hf_ in range(2):
            A("pool", (lambda b_=b_, hf_=hf_: G.memset(qTb[b_][hf_], 0.0)), [], [t_qTb[b_]])
            A("pool", (lambda b_=b_, hf_=hf_: G.memset(qiTb[b_][hf_], 0.0)), [], [t_qiTb[b_]])
    NR = 4
    Rb = [AR.alloc(512, BF16) for _ in range(NR)]; t_Rb = [Tok() for _ in range(NR)]
    Dh = AR.alloc([8, 128], BF16); t_Dh = Tok()
    biasb = [AR.alloc(512, BF16), AR.alloc(512, BF16)]; t_biasb = [Tok(), Tok()]
    NPB = 6
    Pexp = [AR.alloc(512, BF16) for _ in range(NPB)]; t_Pexp = [Tok() for _ in range(NPB)]
    Sel4 = AR.alloc(512, BF16); t_Sel4 = Tok()
    for r_ in range(4):
        A("pool", (lambda r_=r_: G.tensor_copy(out=Sel4[:, r_ * 128:(r_ + 1) * 128], in_=identB)), [t_identB], [t_Sel4])
    rs = AR.alloc(512); t_rs = Tok()
    bcS = AR.alloc(512); t_bcS = Tok()
    ys = AR.alloc(512); t_ys = Tok()
    numS = ys; t_numS = t_ys
    oT_all = AR.alloc([2, 512], BF16); t_oTlo = Tok(); t_oThi = Tok()
    oT_tmp = AR.alloc([2, 512], BF16); t_oTtmp = Tok()
    xa = AR.alloc(512); t_xa = Tok()
    ta = AR.alloc(512); t_ta = Tok()
    bs = AR.alloc(16); t_bs = Tok()
    tr_banks = [0, 1, 2]
    trc = [0]

    def tbank():
        k = tr_banks[trc[0] % 3]
        trc[0] += 1
        return k

    rbc = [0]
    SCALE = 64 ** -0.5

    def indexer(i):
        b = i % 2
        E = cfg.ext[i]
        nch = (E + 3) // 4
        for hf_ in range(2):
            DMA("sp", qiTb[b][hf_][hf_ * 64:(hf_ + 1) * 64, :], qiTs[i][hf_ * 64:(hf_ + 1) * 64, :], R=[t_qiTs[i]], W=[t_qiTb[b]],
                key="qiTb%d" % b)
        for hf_ in range(2):
            DMA("sp", qTb[b][hf_][hf_ * 64:(hf_ + 1) * 64, :], qTs[i][hf_ * 64:(hf_ + 1) * 64, :], R=[t_qTs[i]], W=[t_qTb[b]],
                key="qTb%d" % b)
        for h in range(8):
            A("pool", (lambda h=h, i=i: G.tensor_scalar(out=Dh[:, h, :], in0=identB, scalar1=wsign[:, i, h:h + 1], scalar2=None, op0=ALU.mult)),
              [t_identB, t_wsign], [t_Dh])
        for c in range(nch):
            kts = [t_kiT[jj] for jj in range(c * 4, c * 4 + 4)]
            need_bias = c >= cfg.dchunk[i]
            if need_bias:
                bb = c % 2
                A("dve", (lambda c=c, i=i: V.tensor_scalar(out=bs[:, 6:7], in0=qpos[:, i:i + 1], scalar1=float(-512 * c), scalar2=None, op0=ALU.add)),
                  [t_qpos], [t_bs])
                A("dve", (lambda bb=bb: V.tensor_scalar(out=biasb[bb], in0=iota, scalar1=bs[:, 6:7], scalar2=-1e30, op0=ALU.is_gt, op1=ALU.mult)),
                  [t_iota, t_bs], [t_biasb[bb]])
            banks = []
            LA = 2

            def acc(hh, nb=need_bias):
                A("pe", (lambda hh=hh, rbb=banks[hh], nb=nb: PE.matmul(pbank[3][:], lhsT=Dh[:, hh, :], rhs=Rb[rbb], start=(hh == 0),
                                                                     stop=(hh == 7 and not nb))),
                  [t_Dh, t_Rb[banks[hh]]], [tpb[3]])

            for h in range(8):
                bk = tbank()
                hp, pr = h % 2, h // 2
                A("pe", (lambda bk=bk, hp=hp, pr=pr, c=c, b=b: PE.matmul(
                    pbank[bk][:], lhsT=qiTb[b][hp][:, pr * 128:(pr + 1) * 128],
                    rhs=kiT[:, c * 512:(c + 1) * 512], start=True, stop=True)),
                  [t_qiTb[b]] + kts, [tpb[bk]])
                rb = rbc[0] % NR
                rbc[0] += 1
                A("act", (lambda bk=bk, rb=rb: S.activation(out=Rb[rb], in_=pbank[bk][:], func=AF.Relu)), [tpb[bk]], [t_Rb[rb]])
                banks.append(rb)
                if h >= LA:
                    acc(h - LA)
            for hh in range(8 - LA, 8):
                acc(hh)
            if need_bias:
                A("pe", (lambda bb=bb: PE.matmul(pbank[3][:], lhsT=identB, rhs=biasb[bb], start=False, stop=True)),
                  [t_identB, t_biasb[bb]], [tpb[3]])
            A("act", (lambda c=c: S.copy(out=I_[:, c * 512:(c + 1) * 512], in_=pbank[3][:])), [tpb[3]], [t_I])

    def topk(i):
        E = cfg.ext[i]
        Sn = ((E + 3) // 4) * 512
        K0 = min(256, Sn)
        A("dve", lambda: V.tensor_reduce(out=bs[:, 0:1], in_=I_[:, 0:Sn], axis=AX.X, op=ALU.max), [t_I], [t_bs])
        A("dve", lambda: V.tensor_reduce(out=bs[:, 1:2], in_=I_[:, 0:K0], axis=AX.X, op=ALU.min), [t_I], [t_bs])
        A("dve", lambda: V.tensor_scalar(out=bs[:, 1:2], in0=bs[:, 1:2], scalar1=-1e29, scalar2=None, op0=ALU.max), [t_bs], [t_bs])
        A("dve", lambda: V.tensor_tensor(out=bs[:, 2:3], in0=bs[:, 0:1], in1=bs[:, 1:2], op=ALU.subtract), [t_bs], [t_bs])
        for it in range(NIT):
            f = 2.0 ** (-(it + 1))
            A("dve", (lambda f=f: V.tensor_scalar(out=bs[:, 3:4], in0=bs[:, 2:3], scalar1=f, scalar2=bs[:, 1:2], op0=ALU.mult, op1=ALU.add)),
              [t_bs], [t_bs])
            A("dve", lambda: V.tensor_scalar(out=junk8[:, 0:Sn], in0=I_[:, 0:Sn], scalar1=bs[:, 3:4], scalar2=None, op0=ALU.is_ge,
                                             op1=ALU.add, accum_out=bs[:, 4:5]), [t_I, t_bs], [t_junk8, t_bs])
            A("dve", (lambda f=f: V.tensor_scalar(out=bs[:, 5:6], in0=bs[:, 4:5], scalar1=cfg.topk - 0.5, scalar2=f, op0=ALU.is_gt, op1=ALU.mult)),
              [t_bs], [t_bs])
            A("dve", lambda: V.tensor_scalar(out=bs[:, 1:2], in0=bs[:, 5:6], scalar1=bs[:, 2:3], scalar2=bs[:, 1:2], op0=ALU.mult, op1=ALU.add),
              [t_bs], [t_bs])
        A("dve", lambda: V.tensor_scalar(out=mask01[:, 0:E * 128], in0=I_[:, 0:E * 128], scalar1=bs[:, 1:2], scalar2=-30000.0,
                                         op0=ALU.is_lt, op1=ALU.mult), [t_I, t_bs], [t_mask])

    pbc = [0]

    def attention(i):
        b = i % 2
        E = cfg.ext[i]
        steps = [(kb, g) for kb in range(E) for g in range(4)]
        DL = 3
        assert NPB > DL
        pmof = {}

        def front(n):
            kb, g = steps[n]
            hp, gi = g // 2, g % 2
            bk = tbank()
            A("pe", (lambda bk=bk, hp=hp, gi=gi, kb=kb, b=b: PE.matmul(
                pbank[bk][:], lhsT=kT_all[:, gi, kb * 128:(kb + 1) * 128],
                rhs=qTb[b][hp][:, gi * 512:(gi + 1) * 512], start=True, stop=False)),
              [t_kT[kb], t_qTb[b]], [tpb[bk]])
            A("pe", (lambda bk=bk, kb=kb: PE.matmul(pbank[bk][:], lhsT=mask01[:, kb * 128:(kb + 1) * 128], rhs=Sel4, start=False, stop=True)),
              [t_mask, t_Sel4], [tpb[bk]])
            pe_ = pbc[0] % NPB
            pbc[0] += 1
            pmof[n] = pe_
            A("act", (lambda bk=bk, pe_=pe_: S.activation(out=Pexp[pe_], in_=pbank[bk][:], func=AF.Exp, scale=SCALE)),
              [tpb[bk]], [t_Pexp[pe_]])

        def back(n):
            kb, g = steps[n]
            pe_ = pmof[n]
            A("pe", (lambda g=g, kb=kb, pe_=pe_, E=E: PE.matmul(pbank[4 + g][0:65, :], lhsT=Vaug[:, kb, g, 0:65], rhs=Pexp[pe_],
                                                               start=(kb == 0), stop=(kb == E - 1))),
              [t_V[kb], t_Vones, t_Pexp[pe_]], [tpb[4 + g]])

        for n in range(len(steps) + DL):
            if n < len(steps):
                front(n)
            if n - DL >= 0:
                back(n - DL)
        if ATT_PARTS < 2:
            return
        for g in range(4):
            A("act", (lambda g=g: S.activation(out=rs[64:65, :], in_=pbank[4 + g][64:65, :], func=AF.Ln)), [tpb[4 + g]], [t_rs])
            A("act", lambda: S.activation(out=rs[64:65, :], in_=rs[64:65, :], func=AF.Exp, scale=-1.0), [t_rs], [t_rs])
            bk = tbank()
            A("pe", (lambda bk=bk: PE.matmul(pbank[bk][0:64, :], lhsT=onesF[64:65, 0:64], rhs=rs[64:65, :], start=True, stop=True)),
              [t_onesF, t_rs], [tpb[bk]])
            A("act", (lambda bk=bk: S.copy(out=bcS[0:64, :], in_=pbank[bk][0:64, :])), [tpb[bk]], [t_bcS])
            A("act", (lambda g=g: S.copy(out=numS[0:64, :], in_=pbank[4 + g][0:64, :])), [tpb[4 + g]], [t_numS])
            if g < 2:
                A("pool", (lambda g=g: G.tensor_tensor(out=oT_all[0:64, g, :], in0=numS[0:64, :], in1=bcS[0:64, :], op=ALU.mult)),
                  [t_numS, t_bcS], [t_oTlo])
            else:
                A("pool", (lambda g=g: G.tensor_tensor(out=oT_tmp[0:64, g - 2, :], in0=numS[0:64, :], in1=bcS[0:64, :], op=ALU.mult)),
                  [t_numS, t_bcS], [t_oTtmp])
        if ATT_PARTS < 3:
            return
        DMA("sp", oT_all[64:128, :, :], oT_tmp[0:64, :, :], R=[t_oTtmp], W=[t_oThi], key="oThi")
        if ATT_PARTS < 4:
            return
        for ch in range(2):
            DMA("sp", xa, x_own[i * 128:(i + 1) * 128, ch * 512:(ch + 1) * 512], W=[t_xa], key="xa")
            bk = tbank()
            for gi in range(2):
                for r in range(4):
                    j = gi * 4 + r
                    A("pe", (lambda bk=bk, gi=gi, r=r, j=j, ch=ch: PE.matmul(
                        pbank[bk][:], lhsT=oT_all[:, gi, r * 128:(r + 1) * 128], rhs=Wo[:, j, ch * 512:(ch + 1) * 512],
                        start=(j == 0), stop=(j == 7))), [t_oTlo, t_oThi, t_Wo], [tpb[bk]])
            A("act", (lambda bk=bk: S.copy(out=ys, in_=pbank[bk][:])), [tpb[bk]], [t_ys])
            A("pool", (lambda ch=ch: G.tensor_tensor(out=ta, in0=ys, in1=G0[:, ch * 512:(ch + 1) * 512], op=ALU.mult)), [t_ys, t_G0], [t_ta])
            A("pool", lambda: G.tensor_tensor(out=ta, in0=ta, in1=xa, op=ALU.add), [t_ta, t_xa], [t_ta])
            DMA("sp", x1s[i * 128:(i + 1) * 128, ch * 512:(ch + 1) * 512], ta, R=[t_ta], W=[t_x1s[i]], key="x1s")

    indexer(0)
    if stop in ("a2i", "a2t", "a2a"):
        if stop in ("a2t", "a2a"):
            topk(0)
        if stop == "a2a":
            attention(0)
        dump(I_[:, 0:1024], [t_I], 1024)
        dump(mask01[:, 0:1024], [t_mask], 1024, bf=True)
        dump(bs, [t_bs], 16)
        dump(ta, [t_ta], 512)
        P.emit()
        return nc, P, AR
    for i in range(NS):
        topk(i)
        if i + 1 < NS:
            indexer(i + 1)
        attention(i)
    P.barrier()
    if stop in ("a2", "a2x"):
        dump(I_[:, 0:1024], [t_I], 1024)
        dump(mask01[:, 0:1024], [t_mask], 1024, bf=True)
        dump(bs, [t_bs], 16)
        dump(ta, [t_ta], 512)
        P.emit()
        return nc, P, AR
    AR.release(mA)

    Gt = [AR.alloc(1024) for _ in range(3)]; t_Gt = [Tok() for _ in range(3)]
    dgb = [AR.alloc(128), AR.alloc(128)]
    make_gate(mod_cols(0, 5), Gt[0], t_Gt[0], dgb, t_dgb, [0, 1])
    make_gate(mod_cols(1, 2), Gt[1], t_Gt[1], dgb, t_dgb, [2, 3])
    make_gate(mod_cols(1, 5), Gt[2], t_Gt[2], dgb, t_dgb, [0, 1])
    MS = 4
    xm = AR.alloc([MS, 1024]); t_xm = [Tok() for _ in range(MS)]
    hTm = AR.alloc([8, MS * 128], BF16); t_hTm = Tok()
    scrB2 = [(AR.alloc(1024, BF16), Tok(), AR.alloc(4), Tok(), AR.alloc(1024, BF16), Tok()) for _ in range(2)]
    t_hTm_s = [Tok() for _ in range(MS)]
    aT = AR.alloc([NFC, MS * 128], BF16); t_aT = Tok()
    Wd = AR.alloc([NFC, 1024], BF16); t_Wd = [Tok() for _ in range(4)]
    NWB = 2
    WA = [AR.alloc([8, 512], BF16) for _ in range(NWB)]; t_WA = [Tok() for _ in range(NWB)]
    WB = [AR.alloc([8, 512], BF16) for _ in range(NWB)]; t_WB = [Tok() for _ in range(NWB)]
    WC = [AR.alloc([8, 512], BF16) for _ in range(NWB)]; t_WC = [Tok() for _ in range(NWB)]
    sg = [AR.alloc(MS * 128), AR.alloc(MS * 128)]; t_sg = [Tok(), Tok()]
    zb = AR.alloc(MS * 128 + 2); t_zb = Tok()
    zc = AR.alloc(MS * 128); t_zc = Tok()
    carry = AR.alloc([8, 2]); t_carry = Tok()
    tb = AR.alloc(512); t_tb = Tok()
    A("dve", lambda: V.memset(carry, 0.0), [], [t_carry])
    wbc = [0]

    def norm_macro(ns, gs, shc):
        for s in range(ns):
            norm_T(xm[:, s, :], t_xm[s], gs, shc, hTm[:, :, s * 128:(s + 1) * 128], t_hTm_s[s], scrB2[s % 2], 6 + s % 2)

    def down(ns, nchunks, Gate, t_Gate):
        N = ns * 128
        for s in range(ns):
            for ch in range(2):
                bk = 4 + (s * 2 + ch) % 2
                for j in range(nchunks):
                    A("pe", (lambda bk=bk, j=j, s=s, ch=ch: PE.matmul(pbank[bk][:], lhsT=aT[:, j, s * 128:(s + 1) * 128],
                                                                     rhs=Wd[:, j, ch * 512:(ch + 1) * 512],
                                                                     start=(j == 0), stop=(j == nchunks - 1))),
                      [t_aT, t_Wd[j // 6]], [tpb[bk]])
                A("dve", (lambda bk=bk, ch=ch: V.tensor_tensor(out=tb, in0=pbank[bk][:], in1=Gate[:, ch * 512:(ch + 1) * 512], op=ALU.mult)),
                  [tpb[bk], t_Gate], [t_tb])
                A("pool", (lambda s=s, ch=ch: G.tensor_tensor(out=xm[:, s, ch * 512:(ch + 1) * 512], in0=xm[:, s, ch * 512:(ch + 1) * 512],
                                                             in1=tb, op=ALU.add)), [t_tb, t_xm[s]], [t_xm[s]])

    def ffn(L, ns, Gate, t_Gate):
        N = ns * 128
        dsrc = wd_b[L].rearrange("(j p) c -> p j c", p=128)
        for q4 in range(0, NFC, 6):
            q5 = min(NFC, q4 + 6)
            DMA("act", Wd[:, q4:q5, :], dsrc[:, q4:q5, :], R=[t_wdb[L]], W=[t_Wd[q4 // 6]], key="Wd%d" % (q4 // 6))
        norm_macro(ns, gscT[:, (2 * L + 1) * 8:(2 * L + 1) * 8 + 8], mod_cols(L, 3))
        gsrc = wg_b[L].rearrange("(k p) f -> p k f", p=128)
        usrc = wu_b[L].rearrange("(k p) f -> p k f", p=128)
        for fg in range(6):
            f0 = fg * 512
            fw = min(512, FF - f0)
            wb = wbc[0] % NWB
            wbc[0] += 1
            DMA("sp", WA[wb][:, :, 0:fw], gsrc[:, :, f0:f0 + fw], R=[t_wgb[L]], W=[t_WA[wb]], key="WA%d" % wb)
            DMA("sp", WB[wb][:, :, 0:fw], usrc[:, :, f0:f0 + fw], R=[t_wub[L]], W=[t_WB[wb]], key="WB%d" % wb)
            for fc in range(fw // 128):
                j = fg * 4 + fc
                bg, bu = (0, 1) if j % 2 == 0 else (2, 3)
                for k in range(8):
                    A("pe", (lambda bg=bg, k=k, fc=fc, wb=wb: PE.matmul(pbank[bg][:, 0:N], lhsT=WA[wb][:, k, fc * 128:(fc + 1) * 128],
                                                                       rhs=hTm[:, k, 0:N], start=(k == 0), stop=(k == 7))),
                      [t_WA[wb]] + t_hTm_s[0:ns], [tpb[bg]])
                for k in range(8):
                    A("pe", (lambda bu=bu, k=k, fc=fc, wb=wb: PE.matmul(pbank[bu][:, 0:N], lhsT=WB[wb][:, k, fc * 128:(fc + 1) * 128],
                                                                       rhs=hTm[:, k, 0:N], start=(k == 0), stop=(k == 7))),
                      [t_WB[wb]] + t_hTm_s[0:ns], [tpb[bu]])
                sb_ = j % 2
                A("act", (lambda bg=bg, sb_=sb_: S.activation(out=sg[sb_][:, 0:N], in_=pbank[bg][:, 0:N], func=AF.Silu)),
                  [tpb[bg]], [t_sg[sb_]])
                A("dve", (lambda bu=bu, sb_=sb_, j=j: V.tensor_tensor(out=aT[:, j, 0:N], in0=pbank[bu][:, 0:N], in1=sg[sb_][:, 0:N], op=ALU.mult)),
                  [tpb[bu], t_sg[sb_]], [t_aT])
        down(ns, NFC, Gate, t_Gate)

    def convmix(ns):
        N = ns * 128
        osrc = cwout_b.rearrange("(j p) c -> p j c", p=128)
        DMA("act", Wd[:, 0:6, :], osrc[:, 0:6, :], R=[t_cwoutb], W=[t_Wd[0]], key="Wd0")
        DMA("act", Wd[:, 6:8, :], osrc[:, 6:8, :], R=[t_cwoutb], W=[t_Wd[1]], key="Wd1")
        norm_macro(ns, gscT[:, 16:24], mod_cols(1, 0))
        src = cwin_b.rearrange("(k p) f -> p k f", p=128)
        for cg in range(2):
            wb = wbc[0] % NWB
            wbc[0] += 1
            DMA("sp", WA[wb], src[:, :, cg * 512:(cg + 1) * 512], R=[t_cwinb], W=[t_WA[wb]], key="WA%d" % wb)
            DMA("sp", WB[wb], src[:, :, 1024 + cg * 512:1024 + (cg + 1) * 512], R=[t_cwinb], W=[t_WB[wb]], key="WB%d" % wb)
            DMA("sp", WC[wb], src[:, :, 2048 + cg * 512:2048 + (cg + 1) * 512], R=[t_cwinb], W=[t_WC[wb]], key="WC%d" % wb)
            for cc in range(4):
                cj = cg * 4 + cc
                for (bk, Wt, tW) in ((0, WA, t_WA), (1, WB, t_WB), (2, WC, t_WC)):
                    for k in range(8):
                        A("pe", (lambda bk=bk, Wt=Wt, k=k, cc=cc, wb=wb: PE.matmul(
                            pbank[bk][:, 0:N], lhsT=Wt[wb][:, k, cc * 128:(cc + 1) * 128], rhs=hTm[:, k, 0:N],
                            start=(k == 0), stop=(k == 7))), [tW[wb]] + t_hTm_s[0:ns], [tpb[bk]])
                A("act", lambda: S.copy(out=sg[0][:, 0:N], in_=pbank[1][:, 0:N]), [tpb[1]], [t_sg[0]])
                A("dve", (lambda cj=cj: V.tensor_copy(out=zb[:, 0:2], in_=carry[:, cj, :])), [t_carry], [t_zb])
                A("dve", lambda: V.tensor_tensor(out=zb[:, 2:2 + N], in0=pbank[2][:, 0:N], in1=sg[0][:, 0:N], op=ALU.mult),
                  [tpb[2], t_sg[0]], [t_zb])
                A("dve", (lambda cj=cj: V.tensor_copy(out=carry[:, cj, :], in_=zb[:, N:N + 2])), [t_zb], [t_carry])
                A("dve", (lambda cj=cj: V.tensor_scalar(out=zc[:, 0:N], in0=zb[:, 2:2 + N], scalar1=cwT[:, cj * 3 + 2:cj * 3 + 3],
                                                       scalar2=None, op0=ALU.mult)), [t_zb, t_cwT], [t_zc])
                A("dve", (lambda cj=cj: V.scalar_tensor_tensor(out=zc[:, 0:N], in0=zb[:, 1:1 + N], scalar=cwT[:, cj * 3 + 1:cj * 3 + 2],
                                                              in1=zc[:, 0:N], op0=ALU.mult, op1=ALU.add)), [t_zb, t_cwT, t_zc], [t_zc])
                A("dve", (lambda cj=cj: V.scalar_tensor_tensor(out=zc[:, 0:N], in0=zb[:, 0:N], scalar=cwT[:, cj * 3:cj * 3 + 1],
                                                              in1=zc[:, 0:N], op0=ALU.mult, op1=ALU.add)), [t_zb, t_cwT, t_zc], [t_zc])
                A("dve", (lambda cj=cj: V.tensor_tensor(out=aT[:, cj, 0:N], in0=pbank[0][:, 0:N], in1=zc[:, 0:N], op=ALU.mult)),
                  [tpb[0], t_zc], [t_aT])
        down(ns, 8, Gt[1], t_Gt[1])

    nmac = (NS + MS - 1) // MS
    for m in range(nmac):
        s0 = m * MS
        ns = min(MS, NS - s0)
        for s in range(ns):
            DMA("sp", xm[:, s, :], x1s[(s0 + s) * 128:(s0 + s + 1) * 128, :], R=[t_x1s[s0 + s]], W=[t_xm[s]], key="xm%d" % s)
        ffn(0, ns, Gt[0], t_Gt[0])
        convmix(ns)
        ffn(1, ns, Gt[2], t_Gt[2])
        for s in range(ns):
            DMA("sp", out_d[(s0 + s) * 128:(s0 + s + 1) * 128, :], xm[:, s, :], R=[t_xm[s]], W=[t_out[s0 + s]], key="out%d" % s)

    P.emit()
    return nc, P, AR


import os
ATT_PARTS = int(os.environ.get('ATT_PARTS', '9'))
_CACHE = {}
STOP = None
LAST = None


def _host_inputs(cfg, r, x, c, positions, ada_w, ada_b, norm1_g, norm2_g, attn_w_in, attn_q_norm_g, attn_k_norm_g,
                 idx_k_ln_g, idx_k_ln_b, attn_w_out, conv_w_in, conv_w, conv_w_out, ffn_w_gate, ffn_w_up, ffn_w_down,
                 shared):
    b, role = r // 2, r % 2
    tiles = cfg.tilesA if role == 0 else cfg.tilesB
    T, NT, NS = cfg.T, cfg.NT, cfg.NS
    xs = np.ascontiguousarray(x[b])
    xo = np.ascontiguousarray(xs.reshape(NT, 128, D)[tiles].reshape(NS * 128, D))
    ps = np.ascontiguousarray(positions[b].reshape(NT, 128).T)
    po = np.ascontiguousarray(positions[b].reshape(NT, 128)[tiles].T)
    tok = np.arange(T, dtype=np.float32).reshape(NT, 128)
    qp = np.ascontiguousarray(tok[tiles].T)
    cT = np.ascontiguousarray(c[b].reshape(8, 128).T)
    d = dict(shared)
    d.update({"x_seq": xs, "x_own": xo, "pos_seq": ps.astype(np.int32), "pos_own": po.astype(np.int32), "qpos": qp, "cT": cT})
    return d


def _shared_inputs(ada_w, ada_b, norm1_g, norm2_g, attn_w_in, attn_q_norm_g, attn_k_norm_g, idx_k_ln_g, idx_k_ln_b,
                   attn_w_out, conv_w_in, conv_w, conv_w_out, ffn_w_gate, ffn_w_up, ffn_w_down):
    w = attn_w_in[0]
    qc = w[:, 0:1024].reshape(D, 16, 64)
    qperm = np.stack([qc[:, [j, 8 + j], :] for j in range(8)], axis=1).reshape(D, 1024)
    kc = w[:, 1024:1280].reshape(D, 4, 64)
    kperm = np.concatenate([kc[:, 0], kc[:, 2], kc[:, 1], kc[:, 3]], axis=1)
    vcol = w[:, 1280:1536]
    qic = w[:, 1536:2048]
    kic = w[:, 2048:2112]
    wic = w[:, 2112:2120]
    w_in = np.ascontiguousarray(np.concatenate([qperm, kperm, vcol, qic, kic, kic, wic], axis=1), dtype=np.float32)
    assert w_in.shape[1] == WCOLS
    vecs = np.concatenate([ada_b[0].reshape(48, 128), ada_b[1].reshape(48, 128), norm1_g.reshape(16, 128),
                           norm2_g.reshape(16, 128)], axis=0).astype(np.float32)
    hv = np.tile(np.concatenate([attn_q_norm_g[0], attn_k_norm_g[0], idx_k_ln_g[0], idx_k_ln_b[0]])[None, :], (128, 1)).astype(np.float32)
    cwT = np.ascontiguousarray(conv_w[0].T.reshape(8, 128, 3).transpose(1, 0, 2).reshape(128, 24)).astype(np.float32)
    invf = np.float32(10000.0) ** (-(np.arange(32, dtype=np.float32) * np.float32(2.0) / np.float32(64)))
    return {
        "w_in": w_in, "w_out": np.ascontiguousarray(attn_w_out[0]), "ada_w": np.ascontiguousarray(ada_w),
        "vecs": np.ascontiguousarray(vecs), "hv": np.ascontiguousarray(hv),
        "cw_in": np.ascontiguousarray(conv_w_in[0]), "cwT": cwT, "cw_out": np.ascontiguousarray(conv_w_out[0]),
        "wg": np.ascontiguousarray(ffn_w_gate), "wu": np.ascontiguousarray(ffn_w_up), "wd": np.ascontiguousarray(ffn_w_down),
        "ident": np.eye(128, dtype=np.float32), "invf": np.tile(invf.astype(np.float32)[None, :], (128, 1)),
        "iota": np.tile(np.arange(512, dtype=np.float32)[None, :], (128, 1)),
    }


def kernel(x, c, positions, ada_w, ada_b, norm1_g, norm2_g, attn_w_in, attn_q_norm_g, attn_k_norm_g, idx_k_ln_g,
           idx_k_ln_b, attn_w_out, conv_w_in, conv_w, conv_w_out, ffn_w_gate, ffn_w_up, ffn_w_down):
    args = [np.asarray(a) for a in (x, c, positions, ada_w, ada_b, norm1_g, norm2_g, attn_w_in, attn_q_norm_g,
                                     attn_k_norm_g, idx_k_ln_g, idx_k_ln_b, attn_w_out, conv_w_in, conv_w, conv_w_out,
                                     ffn_w_gate, ffn_w_up, ffn_w_down)]
    x = args[0]
    B, T, _ = x.shape
    cfg = Cfg(T)
    if (T, STOP) not in _CACHE:
        _CACHE[(T, STOP)] = build(cfg, STOP)[0]
    nc = _CACHE[(T, STOP)]
    shared = _shared_inputs(*args[3:])
    ncores = 2 * B
    in_maps = [_host_inputs(cfg, r, *args, shared) for r in range(ncores)]
    res = run_bass_kernel_spmd(nc, in_maps, core_ids=list(range(ncores)))
    global LAST
    LAST = res
    out = np.empty((B, T, D), dtype=np.float32)
    for r in range(ncores):
        b, role = r // 2, r % 2
        tiles = cfg.tilesA if role == 0 else cfg.tilesB
        halo = cfg.haloA if role == 0 else cfg.haloB
        o = np.asarray(res.results[r]["out"]).reshape(cfg.NS, 128, D)
        for s, t in enumerate(tiles):
            if s == halo:
                continue
            out[b, t * 128:(t + 1) * 128, :] = o[s]
    return out
```
